# Optimizing a Trainium2 kernel written in Bass

```python
import math
import numpy as np
import jax
import jax.numpy as jnp
from jax import lax

D_MODEL = 1024
BATCH = 8
SEQ = 8192
DEPTH = 2

PLE_DIM = 256
D_FF = 4 * D_MODEL
HEAD_DIM = 64
ROT_DIM = HEAD_DIM // 4
ROPE_THETA = 500000.0
NORM_EPS = 1e-6
NEG_INF = -1e30
MAX_POS_OFFSET = 4096
CONV_WIDTH = 4
FOX_HEADS = (D_MODEL // 2) // HEAD_DIM
FOX_W = FOX_HEADS * HEAD_DIM
FOX_QBLOCK = 128
GDN_HEAD_DIM = 128
GDN_HEADS = (D_MODEL // 2) // GDN_HEAD_DIM
GDN_W = GDN_HEADS * GDN_HEAD_DIM
GDN_CHUNK = 64
NSA_HEADS = (D_MODEL // 2) // HEAD_DIM
NSA_W = NSA_HEADS * HEAD_DIM
NSA_KV_GROUPS = 2
NSA_KV_W = NSA_KV_GROUPS * HEAD_DIM
CMP_BLOCK = 32
CMP_STRIDE = 16
CMP_HIDDEN = 128
SEL_BLOCK = 64
SEL_TOPK = 16
WINDOW = 512
NSA_QBLOCK = 64
LRU_WIDTH = D_MODEL // 2
LRU_BLOCKS = 8
RG_C = 8.0

EVEN_SPLITS = [FOX_W, FOX_W, FOX_W, FOX_HEADS, GDN_W, GDN_W, GDN_W, GDN_W, GDN_HEADS, GDN_HEADS]
ODD_SPLITS = [NSA_W] + [NSA_KV_W] * 6 + [NSA_HEADS * 3, LRU_WIDTH, LRU_WIDTH]
EVEN_IN = sum(EVEN_SPLITS)
ODD_IN = sum(ODD_SPLITS)

kernel_name = "hybrid_fox_gdn_nsa_rglru_trunk"


def rms_norm(x, w):
    xf = x.astype(jnp.float32)
    y = xf * lax.rsqrt(jnp.mean(xf * xf, axis=-1, keepdims=True) + NORM_EPS)
    return (y * w.astype(jnp.float32)).astype(x.dtype)


def split_cols(y, sizes):
    return jnp.split(y, [int(c) for c in np.cumsum(sizes)[:-1]], axis=-1)


def masked_softmax(s, mask):
    p = jax.nn.softmax(jnp.where(mask, s, NEG_INF), axis=-1)
    return jnp.where(mask, p, 0.0)


def rope_tables(positions):
    half = ROT_DIM // 2
    inv_freq = ROPE_THETA ** (-jnp.arange(half, dtype=jnp.float32) * (2.0 / ROT_DIM))
    ang = positions.astype(jnp.float32)[..., None] * inv_freq
    return jnp.cos(ang)[:, :, None, :], jnp.sin(ang)[:, :, None, :]


def apply_partial_rope(x, cos, sin):
    half = ROT_DIM // 2
    x1 = x[..., :half].astype(jnp.float32)
    x2 = x[..., half:ROT_DIM].astype(jnp.float32)
    rot = jnp.concatenate([x1 * cos - x2 * sin, x2 * cos + x1 * sin], axis=-1).astype(x.dtype)
    return jnp.concatenate([rot, x[..., ROT_DIM:]], axis=-1)


def causal_depthwise_conv(x, w):
    K = w.shape[0]
    S = x.shape[1]
    xp = jnp.pad(x, ((0, 0), (K - 1, 0), (0, 0)))
    y = xp[:, 0:S] * w[0]
    for j in range(1, K):
        y = y + xp[:, j:j + S] * w[j]
    return y


def fox_attention(q, k, v, f_logit):
    B, S, H, Dh = q.shape
    nb = S // FOX_QBLOCK
    logf = jax.nn.log_sigmoid(f_logit.astype(jnp.float32))
    F = jnp.cumsum(logf, axis=1).transpose(0, 2, 1)
    q_blocks = q.reshape(B, nb, FOX_QBLOCK, H, Dh).transpose(1, 0, 2, 3, 4)
    F_blocks = F.reshape(B, H, nb, FOX_QBLOCK).transpose(2, 0, 1, 3)
    key_pos = jnp.arange(S)
    scale = Dh ** -0.5

    def one_block(args):
        i, q_i, F_i = args
        t = i * FOX_QBLOCK + jnp.arange(FOX_QBLOCK)
        s = jnp.einsum('bqhd,bkhd->bhqk', q_i, k).astype(jnp.float32) * scale
        s = s + F_i[..., :, None] - F[:, :, None, :]
        p = masked_softmax(s, key_pos[None, :] <= t[:, None])
        return jnp.einsum('bhqk,bkhd->bqhd', p.astype(v.dtype), v)

    out = lax.map(one_block, (jnp.arange(nb), q_blocks, F_blocks))
    return out.transpose(1, 0, 2, 3, 4).reshape(B, S, H, Dh)


def gated_deltanet(q, k, v, z, b_logit, a_logit, a_log, dt_bias, norm_w):
    B, S, H, Dk = q.shape
    Dv = v.shape[-1]
    C = GDN_CHUNK
    N = S // C
    f32 = jnp.float32

    def l2n(t):
        tf = t.astype(f32)
        return tf * lax.rsqrt(jnp.sum(tf * tf, axis=-1, keepdims=True) + 1e-6)

    qf = l2n(q) * (Dk ** -0.5)
    kf = l2n(k)
    vf = v.astype(f32)
    beta = jax.nn.sigmoid(b_logit.astype(f32))
    g = -jnp.exp(a_log.astype(f32)) * jax.nn.softplus(a_logit.astype(f32) + dt_bias.astype(f32))

    def chunks(t):
        return t.reshape(B, N, C, H, -1).transpose(0, 3, 1, 2, 4)

    qc, kc, vc = chunks(qf), chunks(kf), chunks(vf)
    beta_c = chunks(beta[..., None])[..., 0]
    g_cum = jnp.cumsum(chunks(g[..., None])[..., 0], axis=-1)
    idx = jnp.arange(C)
    incl = idx[:, None] >= idx[None, :]
    strict = idx[:, None] > idx[None, :]
    decay = jnp.where(incl, jnp.exp(jnp.where(incl, g_cum[..., :, None] - g_cum[..., None, :], 0.0)), 0.0)
    k_beta = kc * beta_c[..., None]
    L = jnp.where(strict, jnp.einsum('bhnid,bhnjd->bhnij', k_beta, kc) * decay, 0.0)
    lhs = L + jnp.eye(C, dtype=f32)
    rhs = jnp.concatenate([vc * beta_c[..., None], k_beta * jnp.exp(g_cum)[..., None]], axis=-1)
    sol = lax.linalg.triangular_solve(lhs, rhs, left_side=True, lower=True, unit_diagonal=True)
    u, w = sol[..., :Dv], sol[..., Dv:]
    attn_intra = jnp.where(incl, jnp.einsum('bhnid,bhnjd->bhnij', qc, kc) * decay, 0.0)

    def step(state, inp):
        q_i, k_i, u_i, w_i, g_i, a_i = inp
        v_new = u_i - jnp.einsum('bhck,bhkv->bhcv', w_i, state)
        o_i = (jnp.einsum('bhck,bhkv->bhcv', q_i * jnp.exp(g_i)[..., None], state)
               + jnp.einsum('bhij,bhjv->bhiv', a_i, v_new))
        g_last = g_i[..., -1:]
        state = (state * jnp.exp(g_last)[..., None]
                 + jnp.einsum('bhck,bhcv->bhkv', k_i * jnp.exp(g_last - g_i)[..., None], v_new))
        return state, o_i

    xs = tuple(jnp.moveaxis(t, 2, 0) for t in (qc, kc, u, w, g_cum, attn_intra))
    _, o = lax.scan(step, jnp.zeros((B, H, Dk, Dv), f32), xs)
    o = o.transpose(1, 0, 3, 2, 4).reshape(B, S, H, Dv)
    o = o * lax.rsqrt(jnp.mean(o * o, axis=-1, keepdims=True) + NORM_EPS) * norm_w.astype(f32)
    o = o * jax.nn.silu(z.astype(f32))
    return o.astype(q.dtype)


def compress_tokens(x, pe, w1, w2):
    S = x.shape[1]
    n_cmp = (S - CMP_BLOCK) // CMP_STRIDE + 1
    idx = (np.arange(n_cmp) * CMP_STRIDE)[:, None] + np.arange(CMP_BLOCK)[None, :]
    blocks = x[:, idx] + pe[None, None, :, None, :]
    B, n, L, G, Dh = blocks.shape
    flat = blocks.transpose(0, 1, 3, 2, 4).reshape(B, n, G, L * Dh)
    hid = jax.nn.gelu(jnp.einsum('bngf,fh->bngh', flat, w1))
    return jnp.einsum('bngh,hd->bngd', hid, w2)


def nsa_attention(q, k_cmp, v_cmp, k_slc, v_slc, k_win, v_win, gate_logit):
    B, S, H, Dh = q.shape
    G = k_slc.shape[2]
    hpg = H // G
    f32 = jnp.float32
    scale = Dh ** -0.5
    n_cmp = k_cmp.shape[1]
    n_sel = S // SEL_BLOCK
    top_k = min(SEL_TOPK, n_sel)
    QB = NSA_QBLOCK
    nb = S // QB
    cmp_end = jnp.arange(n_cmp) * CMP_STRIDE + CMP_BLOCK - 1
    cs = np.arange(n_cmp) * CMP_STRIDE
    ss = np.arange(n_sel) * SEL_BLOCK
    overlap = jnp.asarray(((cs[:, None] <= ss[None, :] + SEL_BLOCK - 1)
                           & (cs[:, None] + CMP_BLOCK - 1 >= ss[None, :])).astype(np.float32))
    sel_k = k_slc.reshape(B, n_sel, SEL_BLOCK, G, Dh).transpose(0, 3, 1, 2, 4)
    sel_v = v_slc.reshape(B, n_sel, SEL_BLOCK, G, Dh).transpose(0, 3, 1, 2, 4)
    k_win_p = jnp.pad(k_win, ((0, 0), (WINDOW, 0), (0, 0), (0, 0)))
    v_win_p = jnp.pad(v_win, ((0, 0), (WINDOW, 0), (0, 0), (0, 0)))
    gates = jax.nn.sigmoid(gate_logit.astype(f32)).reshape(B, S, H, 3)
    q_blocks = q.reshape(B, nb, QB, G, hpg, Dh).transpose(1, 0, 3, 4, 2, 5)
    gate_blocks = gates.reshape(B, nb, QB, H, 3).transpose(1, 0, 2, 3, 4)
    gather_blocks = jax.vmap(jax.vmap(lambda blk, ix: blk[ix]))
    blk_ids = jnp.arange(n_sel)
    in_blk = jnp.arange(SEL_BLOCK)
    win_off = jnp.arange(WINDOW + QB) - WINDOW

    def one_block(args):
        i, q_i, g_i = args
        t = i * QB + jnp.arange(QB)
        s_c = jnp.einsum('bghqd,bngd->bghqn', q_i, k_cmp).astype(f32) * scale
        p_c = masked_softmax(s_c, cmp_end[None, :] <= t[:, None])
        o_c = jnp.einsum('bghqn,bngd->bghqd', p_c.astype(v_cmp.dtype), v_cmp)
        imp = jnp.einsum('bgqn,nm->bgqm', p_c.sum(axis=2), overlap)
        cur = t // SEL_BLOCK
        forced = ((blk_ids[None, :] == 0) | (blk_ids[None, :] == cur[:, None])
                  | (blk_ids[None, :] == cur[:, None] - 1))
        future = blk_ids[None, :] * SEL_BLOCK > t[:, None]
        imp = jnp.where(future, NEG_INF, jnp.where(forced, -NEG_INF, imp))
        _, sel = lax.top_k(imp, top_k)
        k_sel = gather_blocks(sel_k, sel).reshape(B, G, QB, top_k * SEL_BLOCK, Dh)
        v_sel = gather_blocks(sel_v, sel).reshape(B, G, QB, top_k * SEL_BLOCK, Dh)
        sel_pos = (sel[..., None] * SEL_BLOCK + in_blk).reshape(B, G, QB, top_k * SEL_BLOCK)
        s_s = jnp.einsum('bghqd,bgqmd->bghqm', q_i, k_sel).astype(f32) * scale
        p_s = masked_softmax(s_s, (sel_pos <= t[:, None])[:, :, None])
        o_s = jnp.einsum('bghqm,bgqmd->bghqd', p_s.astype(v_slc.dtype), v_sel)
        kw = lax.dynamic_slice_in_dim(k_win_p, i * QB, WINDOW + QB, axis=1)
        vw = lax.dynamic_slice_in_dim(v_win_p, i * QB, WINDOW + QB, axis=1)
        wpos = i * QB + win_off
        mask_w = ((wpos[None, :] <= t[:, None]) & (t[:, None] - wpos[None, :] < WINDOW)
                  & (wpos[None, :] >= 0))
        s_w = jnp.einsum('bghqd,bkgd->bghqk', q_i, kw).astype(f32) * scale
        p_w = masked_softmax(s_w, mask_w)
        o_w = jnp.einsum('bghqk,bkgd->bghqd', p_w.astype(v_win.dtype), vw)
        gg = g_i.reshape(B, QB, G, hpg, 3).transpose(0, 2, 3, 1, 4)
        o = gg[..., 0:1] * o_c + gg[..., 1:2] * o_s + gg[..., 2:3] * o_w
        return o.astype(q.dtype)

    out = lax.map(one_block, (jnp.arange(nb), q_blocks, gate_blocks))
    return out.transpose(1, 0, 4, 2, 3, 5).reshape(B, S, H, Dh)


def rg_lru_block(gate_in, x_in, conv_w, conv_b, wa, ba, wx, bx, lam):
    f32 = jnp.float32
    x = causal_depthwise_conv(x_in, conv_w) + conv_b
    B, S, W = x.shape
    nblk = wa.shape[0]
    xb = x.reshape(B, S, nblk, W // nblk)
    r = jax.nn.sigmoid((jnp.einsum('bsnk,nkj->bsnj', xb, wa).reshape(B, S, W) + ba).astype(f32))
    ig = jax.nn.sigmoid((jnp.einsum('bsnk,nkj->bsnj', xb, wx).reshape(B, S, W) + bx).astype(f32))
    log_a = -RG_C * jax.nn.softplus(-lam.astype(f32)) * r
    a = jnp.exp(log_a)
    b = jnp.sqrt(-jnp.expm1(2.0 * log_a)) * (ig * x.astype(f32))

    def combine(left, right):
        a_l, b_l = left
        a_r, b_r = right
        return a_l * a_r, a_r * b_l + b_r

    _, h = lax.associative_scan(combine, (a, b), axis=1)
    y = h * jax.nn.gelu(gate_in.astype(f32))
    return y.astype(x_in.dtype)


def even_mixer(h, w_in, fox_bf, gdn_conv_w, gdn_a_log, gdn_dt_bias, gdn_norm_w, w_out):
    B, S, _ = h.shape
    y = h @ w_in
    fq, fk, fv, ff, gq, gk, gv, gz, gb, ga = split_cols(y, EVEN_SPLITS)
    fox = fox_attention(fq.reshape(B, S, FOX_HEADS, HEAD_DIM), fk.reshape(B, S, FOX_HEADS, HEAD_DIM),
                        fv.reshape(B, S, FOX_HEADS, HEAD_DIM), ff + fox_bf)
    qkv = jax.nn.silu(causal_depthwise_conv(jnp.concatenate([gq, gk, gv], axis=-1), gdn_conv_w))
    cq, ck, cv = jnp.split(qkv, 3, axis=-1)
    hd = lambda t: t.reshape(B, S, GDN_HEADS, GDN_HEAD_DIM)
    gdn = gated_deltanet(hd(cq), hd(ck), hd(cv), hd(gz), gb, ga, gdn_a_log, gdn_dt_bias, gdn_norm_w)
    mixed = jnp.concatenate([fox.reshape(B, S, FOX_W), gdn.reshape(B, S, GDN_W)], axis=-1)
    return mixed @ w_out


def odd_mixer(h, cos, sin, w_in, k_pe, k_w1, k_w2, v_pe, v_w1, v_w2,
              conv_w, conv_b, wa, ba, wx, bx, lam, w_out):
    B, S, _ = h.shape
    y = h @ w_in
    nq, kc, vc, ksl, vsl, kwn, vwn, ng, rg, rx = split_cols(y, ODD_SPLITS)
    hd = lambda t, n: t.reshape(B, S, n, HEAD_DIM)
    G = NSA_KV_GROUPS
    q = apply_partial_rope(hd(nq, NSA_HEADS), cos, sin)
    k_c = compress_tokens(apply_partial_rope(hd(kc, G), cos, sin), k_pe, k_w1, k_w2)
    v_c = compress_tokens(hd(vc, G), v_pe, v_w1, v_w2)
    nsa = nsa_attention(q, k_c, v_c, apply_partial_rope(hd(ksl, G), cos, sin), hd(vsl, G),
                        apply_partial_rope(hd(kwn, G), cos, sin), hd(vwn, G), ng)
    lru = rg_lru_block(rg, rx, conv_w, conv_b, wa, ba, wx, bx, lam)
    mixed = jnp.concatenate([nsa.reshape(B, S, NSA_W), lru], axis=-1)
    return mixed @ w_out


def setup_inputs(seed: int = 0) -> dict:
    key = jax.random.key(seed)
    ks = iter(jax.random.split(key, 48))
    f32 = jnp.float32
    nrm = lambda shape, scale: jax.random.normal(next(ks), shape, f32) * scale
    gain = lambda shape: 1.0 + nrm(shape, 0.02)
    ne, no = (DEPTH + 1) // 2, DEPTH // 2
    D = D_MODEL
    x = nrm((BATCH, SEQ, D), 1.0)
    p = nrm((DEPTH, BATCH, SEQ, PLE_DIM), 1.0)
    positions = (jax.random.randint(next(ks), (BATCH, 1), 0, MAX_POS_OFFSET, jnp.int32)
                 + jnp.arange(SEQ, dtype=jnp.int32)[None, :])
    even_norm_mix = gain((ne, D))
    even_w_in = nrm((ne, D, EVEN_IN), D ** -0.5)
    even_fox_bf = 2.0 + nrm((ne, FOX_HEADS), 0.1)
    even_gdn_conv_w = nrm((ne, CONV_WIDTH, 3 * GDN_W), CONV_WIDTH ** -0.5)
    even_gdn_a_log = jnp.log(jax.random.uniform(next(ks), (ne, GDN_HEADS), f32, 1.0, 16.0))
    dt = jnp.exp(jax.random.uniform(next(ks), (ne, GDN_HEADS), f32, math.log(1e-3), math.log(1e-1)))
    even_gdn_dt_bias = dt + jnp.log(-jnp.expm1(-dt))
    even_gdn_norm_w = gain((ne, GDN_HEAD_DIM))
    even_w_out = nrm((ne, FOX_W + GDN_W, D), (FOX_W + GDN_W) ** -0.5)
    odd_norm_mix = gain((no, D))
    odd_w_in = nrm((no, D, ODD_IN), D ** -0.5)
    odd_cmp_k_pe = nrm((no, CMP_BLOCK, HEAD_DIM), 0.02)
    odd_cmp_k_w1 = nrm((no, CMP_BLOCK * HEAD_DIM, CMP_HIDDEN), (CMP_BLOCK * HEAD_DIM) ** -0.5)
    odd_cmp_k_w2 = nrm((no, CMP_HIDDEN, HEAD_DIM), CMP_HIDDEN ** -0.5)
    odd_cmp_v_pe = nrm((no, CMP_BLOCK, HEAD_DIM), 0.02)
    odd_cmp_v_w1 = nrm((no, CMP_BLOCK * HEAD_DIM, CMP_HIDDEN), (CMP_BLOCK * HEAD_DIM) ** -0.5)
    odd_cmp_v_w2 = nrm((no, CMP_HIDDEN, HEAD_DIM), CMP_HIDDEN ** -0.5)
    odd_rg_conv_w = nrm((no, CONV_WIDTH, LRU_WIDTH), CONV_WIDTH ** -0.5)
    odd_rg_conv_b = nrm((no, LRU_WIDTH), 0.01)
    bw = LRU_WIDTH // LRU_BLOCKS
    odd_rg_wa = nrm((no, LRU_BLOCKS, bw, bw), bw ** -0.5)
    odd_rg_ba = nrm((no, LRU_WIDTH), 0.01)
    odd_rg_wx = nrm((no, LRU_BLOCKS, bw, bw), bw ** -0.5)
    odd_rg_bx = nrm((no, LRU_WIDTH), 0.01)
    a_c = jax.random.uniform(next(ks), (no, LRU_WIDTH), f32, 0.9, 0.999)
    a0 = a_c ** (1.0 / RG_C)
    odd_rg_lambda = jnp.log(a0) - jnp.log1p(-a0)
    odd_w_out = nrm((no, NSA_W + LRU_WIDTH, D), (NSA_W + LRU_WIDTH) ** -0.5)
    mlp_norm = gain((DEPTH, D))
    mlp_w_up = nrm((DEPTH, D, D_FF), D ** -0.5)
    mlp_w_down = nrm((DEPTH, D_FF, D), D_FF ** -0.5)
    ple_norm = gain((DEPTH, D))
    ple_w_gate = nrm((DEPTH, D, D), D ** -0.5)
    ple_w_proj = nrm((DEPTH, PLE_DIM, D), PLE_DIM ** -0.5)
    final_norm = gain((D,))
    return {"x": x, "p": p, "positions": positions,
            "even_norm_mix": even_norm_mix, "even_w_in": even_w_in, "even_fox_bf": even_fox_bf,
            "even_gdn_conv_w": even_gdn_conv_w, "even_gdn_a_log": even_gdn_a_log,
            "even_gdn_dt_bias": even_gdn_dt_bias, "even_gdn_norm_w": even_gdn_norm_w,
            "even_w_out": even_w_out,
            "odd_norm_mix": odd_norm_mix, "odd_w_in": odd_w_in,
            "odd_cmp_k_pe": odd_cmp_k_pe, "odd_cmp_k_w1": odd_cmp_k_w1, "odd_cmp_k_w2": odd_cmp_k_w2,
            "odd_cmp_v_pe": odd_cmp_v_pe, "odd_cmp_v_w1": odd_cmp_v_w1, "odd_cmp_v_w2": odd_cmp_v_w2,
            "odd_rg_conv_w": odd_rg_conv_w, "odd_rg_conv_b": odd_rg_conv_b,
            "odd_rg_wa": odd_rg_wa, "odd_rg_ba": odd_rg_ba, "odd_rg_wx": odd_rg_wx, "odd_rg_bx": odd_rg_bx,
            "odd_rg_lambda": odd_rg_lambda, "odd_w_out": odd_w_out,
            "mlp_norm": mlp_norm, "mlp_w_up": mlp_w_up, "mlp_w_down": mlp_w_down,
            "ple_norm": ple_norm, "ple_w_gate": ple_w_gate, "ple_w_proj": ple_w_proj,
            "final_norm": final_norm}


def reference(x, p, positions,
              even_norm_mix, even_w_in, even_fox_bf, even_gdn_conv_w, even_gdn_a_log,
              even_gdn_dt_bias, even_gdn_norm_w, even_w_out,
              odd_norm_mix, odd_w_in, odd_cmp_k_pe, odd_cmp_k_w1, odd_cmp_k_w2,
              odd_cmp_v_pe, odd_cmp_v_w1, odd_cmp_v_w2, odd_rg_conv_w, odd_rg_conv_b,
              odd_rg_wa, odd_rg_ba, odd_rg_wx, odd_rg_bx, odd_rg_lambda, odd_w_out,
              mlp_norm, mlp_w_up, mlp_w_down, ple_norm, ple_w_gate, ple_w_proj, final_norm):
    cos, sin = rope_tables(positions)
    h = x
    for i in range(DEPTH):
        j = i // 2
        if i % 2 == 0:
            hn = rms_norm(h, even_norm_mix[j])
            h = h + even_mixer(hn, even_w_in[j], even_fox_bf[j], even_gdn_conv_w[j], even_gdn_a_log[j],
                               even_gdn_dt_bias[j], even_gdn_norm_w[j], even_w_out[j])
        else:
            hn = rms_norm(h, odd_norm_mix[j])
            h = h + odd_mixer(hn, cos, sin, odd_w_in[j], odd_cmp_k_pe[j], odd_cmp_k_w1[j], odd_cmp_k_w2[j],
                              odd_cmp_v_pe[j], odd_cmp_v_w1[j], odd_cmp_v_w2[j], odd_rg_conv_w[j],
                              odd_rg_conv_b[j], odd_rg_wa[j], odd_rg_ba[j], odd_rg_wx[j], odd_rg_bx[j],
                              odd_rg_lambda[j], odd_w_out[j])
        hn = rms_norm(h, mlp_norm[i])
        u = jax.nn.relu(hn @ mlp_w_up[i])
        h = h + (u * u) @ mlp_w_down[i]
        hn = rms_norm(h, ple_norm[i])
        h = h + jax.nn.sigmoid(hn @ ple_w_gate[i]) * (p[i] @ ple_w_proj[i])
    return rms_norm(h, final_norm)
```

```python
from contextlib import ExitStack
import concourse.bass as bass
import concourse.mybir as mybir

F32 = mybir.dt.float32
BF16 = mybir.dt.bfloat16
I32 = mybir.dt.int32
ALU = mybir.AluOpType
AF = mybir.ActivationFunctionType
AX = mybir.AxisListType
ENG = ['pe', 'act', 'dve', 'pool', 'sp']
NDS = 90


class Buf:
    def __init__(self, t, name):
        self.t = t
        self.name = name
        self.w = {}
        self.r = {}
        self.dsem = None
        self.psum = False
        self.wr = {}

    def __getitem__(self, k):
        return self.t[k]


class KCtx:
    def __init__(self, nc):
        self.nc = nc
        self.e = dict(pe=nc.tensor, act=nc.scalar, dve=nc.vector, pool=nc.gpsimd, sp=nc.sync)
        self.gstack = ExitStack()
        self.csem = {n: self.gstack.enter_context(nc.semaphore('c_' + n)) for n in ENG}
        self.cnt = {n: 0 for n in ENG}
        self.dsems = [self.gstack.enter_context(nc.semaphore('d%d' % i)) for i in range(NDS)]
        self.dcnt = [0] * NDS
        self.free_ds = list(range(NDS))
        self.seen = {n: {} for n in ENG}
        self.pstack = None
        self.phase_bufs = []
        self.nwaits = 0
        self.uid = 0

    def phase_begin(self):
        self.pstack = ExitStack()
        self.phase_bufs = []

    def phase_end(self):
        self.barrier()
        for b in self.phase_bufs:
            if b.dsem is not None:
                self.free_ds.append(b.dsem)
                b.dsem = None
        self.pstack.close()
        self.pstack = None

    def sb(self, name, shape, dt):
        self.uid += 1
        t = self.pstack.enter_context(self.nc.sbuf_tensor('%s_%d' % (name, self.uid), list(shape), dt))
        b = Buf(t, name)
        self.phase_bufs.append(b)
        return b

    def ps(self, name, shape, dt=F32):
        self.uid += 1
        t = self.pstack.enter_context(self.nc.psum_tensor('%s_%d' % (name, self.uid), list(shape), dt))
        b = Buf(t, name)
        b.psum = True
        self.phase_bufs.append(b)
        return b

    def _sem(self, k):
        return self.csem[k] if isinstance(k, str) else self.dsems[k]

    def _wait(self, eng, deps, force_self=False):
        for k, v in deps.items():
            if v <= self.seen[eng].get(k, 0):
                continue
            if k == eng and not force_self:
                if eng == 'pe':
                    continue
                if v < self.cnt[eng] - 1:
                    continue
            self.e[eng].wait_ge(self._sem(k), v)
            self.nwaits += 1
            self.seen[eng][k] = v

    @staticmethod
    def _merge(d, s):
        for k, v in s.items():
            if v > d.get(k, 0):
                d[k] = v

    def _deps(self, reads, writes, nowaw, eng=None):
        deps = {}
        for b in reads:
            self._merge(deps, b.w)
            if b.psum:
                self._merge(deps, {k: v for k, v in b.r.items() if k != eng})
        for b in writes:
            if b not in nowaw:
                self._merge(deps, b.w)
            else:
                self._merge(deps, b.wr)
            self._merge(deps, b.r)
        return deps

    def op(self, eng, fn, reads=(), writes=(), nowaw=()):
        self._wait(eng, self._deps(reads, writes, nowaw, eng))
        ins = fn(self.e[eng])
        self.cnt[eng] += 1
        c = self.cnt[eng]
        ins.then_inc(self.csem[eng], 1)
        for b in reads:
            if c > b.r.get(eng, 0):
                b.r[eng] = c
        for b in writes:
            if b in nowaw:
                b.w[eng] = c
            else:
                b.w = {eng: c}
                b.wr = {eng: c}
                b.r = {}
        return ins

    def dma(self, q, out, in_, reads=(), writes=(), nowaw=(), **kw):
        self._wait(q, self._deps(reads, writes, nowaw))
        b0 = (list(writes) + list(reads))[0]
        if b0.dsem is None:
            assert self.free_ds, "out of DMA semaphores"
            b0.dsem = self.free_ds.pop(0)
        i = b0.dsem
        self.e[q].dma_start(out=out, in_=in_, **kw).then_inc(self.dsems[i], 16)
        self.dcnt[i] += 16
        v = self.dcnt[i]
        for b in reads:
            b.r[i] = v
        for b in writes:
            if b in nowaw:
                b.w[i] = v
            else:
                b.w = {i: v}
                b.wr = {i: v}
                b.r = {}

    def barrier(self):
        deps = {n: self.cnt[n] for n in ENG if self.cnt[n] > 0}
        for i in range(NDS):
            if self.dcnt[i] > 0:
                deps[i] = self.dcnt[i]
        for eng in ENG:
            self._wait(eng, deps, force_self=True)

    def finish(self):
        self.barrier()
        self.gstack.close()

    def mm(self, out_b, out_ap, lhsT_b, lhsT_ap, rhs_b, rhs_ap, start=True, stop=True, extra_reads=()):
        return self.op('pe', lambda e: e.matmul(out_ap, lhsT_ap, rhs_ap, start=start, stop=stop),
                       reads=[lhsT_b, rhs_b] + list(extra_reads), writes=[out_b])

    def tr(self, out_b, out_ap, in_b, in_ap, id_b, id_ap):
        return self.op('pe', lambda e: e.transpose(out_ap, in_ap, id_ap), reads=[in_b, id_b], writes=[out_b])

import numpy as np
from concourse.bass_utils import run_bass_kernel_spmd

D = 1024
EPS = 1e-6


def make_ident(K, dt, name):
    ones = K.sb(name + '_ones', [128, 128], dt)
    ident = K.sb(name, [128, 128], dt)
    K.op('pool', lambda e: e.memset(ones[:], 1.0), writes=[ones])
    K.op('pool', lambda e: e.affine_select(out=ident[:], in_=ones[:], pattern=[[-1, 128]],
                                           compare_op=ALU.is_equal, fill=0.0, base=0, channel_multiplier=1),
         reads=[ones], writes=[ident])
    return ident


def load_weight_bf16(K, Wb, src, kchunks, ncols, gain=None, stage_name='wst', q='sp', cb=1024):
    cb = min(cb, ncols)
    st = [K.sb(stage_name + str(i), [128, cb], F32) for i in range(4)]
    n = 0
    for k in range(kchunks):
        for c0 in range(0, ncols, cb):
            c1 = min(ncols, c0 + cb)
            s = st[n % 4]
            eng = 'dve' if n % 2 == 0 else 'act'
            n += 1
            K.dma(q, s[:, 0:c1 - c0], src[k * 128:(k + 1) * 128, c0:c1], writes=[s])
            if eng == 'act':
                if gain is not None:
                    K.op('act', lambda e, s=s, k=k, c0=c0, c1=c1: e.activation(out=Wb[:, k, c0:c1], in_=s[:, 0:c1 - c0], func=AF.Copy,
                                                                               scale=gain[:, k:k + 1]),
                         reads=[s, gain], writes=[Wb], nowaw=[Wb])
                else:
                    K.op('act', lambda e, s=s, k=k, c0=c0, c1=c1: e.copy(out=Wb[:, k, c0:c1], in_=s[:, 0:c1 - c0]),
                         reads=[s], writes=[Wb], nowaw=[Wb])
            elif gain is not None:
                K.op(eng, lambda e, s=s, k=k, c0=c0, c1=c1: e.tensor_scalar(out=Wb[:, k, c0:c1], in0=s[:, 0:c1 - c0], scalar1=gain[:, k:k + 1],
                                                                scalar2=None, op0=ALU.mult),
                     reads=[s, gain], writes=[Wb], nowaw=[Wb])
            else:
                K.op(eng, lambda e, s=s, k=k, c0=c0, c1=c1: e.tensor_copy(out=Wb[:, k, c0:c1], in_=s[:, 0:c1 - c0]),
                     reads=[s], writes=[Wb], nowaw=[Wb])


class NormT:
    def __init__(self, K, identb, nbuf_x=3, gt=4, nbuf_n=2):
        self.K = K
        self.identb = identb
        self.nn = nbuf_n
        self.xb = [K.sb('xb%d' % i, [128, D], F32) for i in range(nbuf_x)]
        self.xn = [K.sb('xn%d' % i, [128, D], BF16) for i in range(nbuf_n)]
        self.st = [K.sb('nst%d' % i, [128, 4], F32) for i in range(nbuf_n)]
        self.pT = [K.ps('pT%d' % i, [128, D], BF16) for i in range(2)]
        self.hnT = [K.sb('hnT%d' % i, [128, 8, gt * 128], BF16) for i in range(2)]
        self.n = 0

    def prep(self, src_rows):
        K = self.K
        n = self.n
        self.n += 1
        xb = self.xb[n % len(self.xb)]
        xn = self.xn[n % self.nn]
        st = self.st[n % self.nn]
        K.dma('sp', xb[:], src_rows, writes=[xb])
        K.op('act', lambda e: e.activation(out=xn[:], in_=xb[:], func=AF.Square, accum_out=st[:, 0:1]),
             reads=[xb], writes=[xn, st])
        K.op('act', lambda e: e.activation(out=st[:, 1:2], in_=st[:, 0:1], func=AF.Sqrt, scale=1.0 / D, bias=K.eps_t[:, 0:1]),
             reads=[st, K.eps_t], writes=[st])
        K.op('dve', lambda e: e.reciprocal(out=st[:, 2:3], in_=st[:, 1:2]), reads=[st], writes=[st])
        K.op('dve', lambda e: e.tensor_scalar(out=xn[:], in0=xb[:], scalar1=st[:, 2:3], scalar2=None, op0=ALU.mult),
             reads=[xb, st], writes=[xn])
        return (n, xb)

    def finish(self, tok, hnT, j):
        K = self.K
        n, xb = tok
        xn = self.xn[n % self.nn]
        pT = self.pT[n % 2]
        for k in range(8):
            K.tr(pT, pT[:, k * 128:(k + 1) * 128], xn, xn[:, k * 128:(k + 1) * 128], self.identb, self.identb[:])
        K.op('act', lambda e: e.copy(out=hnT[:, :, j * 128:(j + 1) * 128],
                                     in_=pT[:].rearrange("p (k t) -> p k t", k=8)),
             reads=[pT], writes=[hnT], nowaw=[hnT])
        return xb

    def tile(self, src_rows, hnT, j, keep_x=None):
        return self.finish(self.prep(src_rows), hnT, j)


def consts(K):
    K.eps_t = K.sb('eps', [128, 1], F32)
    K.op('pool', lambda e: e.memset(K.eps_t[:], EPS), writes=[K.eps_t])
    K.one_t = K.sb('one', [128, 1], F32)
    K.op('pool', lambda e: e.memset(K.one_t[:], 1.0), writes=[K.one_t])
    K.mhalf_t = K.sb('mhalf', [128, 1], F32)
    K.op('pool', lambda e: e.memset(K.mhalf_t[:], -0.5), writes=[K.mhalf_t])


E_FQ, E_FK, E_FV, E_FF, E_GQ, E_GK, E_GV, E_GZ, E_GB, E_GA = 0, 512, 1024, 1536, 1544, 2056, 2568, 3080, 3592, 3596


def phase_A0(K, d, S):
    NG = S // 512
    K.phase_begin()
    consts(K)
    identb = make_ident(K, BF16, 'identb')
    gain = K.sb('gain', [128, 8], F32)
    K.dma('sp', gain[:], d['even_norm_mix'].rearrange("(k p) -> p k", p=128), writes=[gain],
          allow_slow_non_contiguous=True)
    cw = K.sb('cw', [128, 4, 12], F32)
    for j in range(4):
        K.dma('sp', cw[:, j, :], d['even_gdn_conv_w'][j, :].rearrange("(c p) -> p c", p=128), writes=[cw], nowaw=[cw],
              allow_slow_non_contiguous=True)
    W = K.sb('W', [128, 8, 3600], BF16)
    load_weight_bf16(K, W, d['even_w_in'], 8, 3600, gain=gain)
    NT = NormT(K, identb, nbuf_x=8, nbuf_n=8)
    xpad = [K.sb('xpad%d' % c, [128, 515], F32) for c in range(12)]
    for c in range(12):
        K.op('pool', lambda e, c=c: e.memset(xpad[c][:, 0:3], 0.0), writes=[xpad[c]])
    acc = [K.sb('acc%d' % i, [128, 512], F32) for i in range(2)]
    cout = [K.sb('cout%d' % i, [128, 512], F32) for i in range(3)]
    qk = [K.sb('qk%d' % i, [128, 512], BF16) for i in range(3)]
    vt = [K.sb('vt%d' % i, [128, 512], BF16) for i in range(2)]
    zt = [K.sb('zt%d' % i, [128, 512], F32) for i in range(2)]
    sm = [K.sb('sm%d' % i, [8, 512], F32) for i in range(2)]
    sm2 = [K.sb('smb%d' % i, [8, 512], F32) for i in range(2)]
    ptm = [K.ps('ptm%d' % i, [128, 512], F32) for i in range(2)]
    pfm = [K.ps('pfm%d' % i, [128, 512], F32) for i in range(3)]
    nfm = 0
    ntm = 0
    pend_silu = [None]
    toks = [NT.prep(d['x'][u * 128:(u + 1) * 128, :]) for u in range(4)]
    for g in range(NG):
        hnT = NT.hnT[g % 2]
        for j in range(4):
            t = g * 4 + j
            NT.finish(toks.pop(0), hnT, j)
            if j == 0:
                for u in range(t + 4, min(t + 8, NG * 4)):
                    toks.append(NT.prep(d['x'][u * 128:(u + 1) * 128, :]))
            for which, col0 in ((0, E_FV), (1, E_GZ)):
                p = ptm[ntm % 2]
                ntm += 1
                for k in range(8):
                    K.mm(p, p[:], hnT, hnT[:, k, j * 128:(j + 1) * 128], W, W[:, k, col0:col0 + 512],
                         start=(k == 0), stop=(k == 7))
                if which == 0:
                    o = vt[t % 2]
                    K.op('dve', lambda e, o=o, p=p: e.tensor_copy(out=o[:], in_=p[:]), reads=[p], writes=[o])
                    K.dma('pool', d['Vf'][t * 128:(t + 1) * 128, :], o[:], reads=[o])
                else:
                    o = zt[t % 2]
                    K.op('act', lambda e, o=o, p=p: e.activation(out=o[:], in_=p[:], func=AF.Silu), reads=[p], writes=[o])
                    K.dma('pool', d['GZ'][t * 128:(t + 1) * 128, :], o[:], reads=[o])
        chunks = ([('q', c, E_FQ + c * 128, 128) for c in range(4)] + [('k', c, E_FK + c * 128, 128) for c in range(4)]
                  + [('g', c, E_GQ + c * 128, 128) for c in range(12)] + [('s', 0, None, 16)])
        for kind, c, col0, M in chunks:
            p = pfm[nfm % 3]
            nfm += 1
            if kind == 's':
                for k in range(8):
                    K.mm(p, p[0:8, :], W, W[:, k, E_FF:E_FF + 8], hnT, hnT[:, k, :], start=(k == 0), stop=(k == 7))
                o = sm[g % 2]
                K.op('act', lambda e, o=o, p=p: e.copy(out=o[0:8, :], in_=p[0:8, :]), reads=[p], writes=[o])
                p2 = pfm[nfm % 3]
                nfm += 1
                for k in range(8):
                    K.mm(p2, p2[0:8, :], W, W[:, k, E_GB:E_GB + 8], hnT, hnT[:, k, :], start=(k == 0), stop=(k == 7))
                o2 = sm2[g % 2]
                K.op('act', lambda e, o2=o2, p2=p2: e.copy(out=o2[:], in_=p2[0:8, :]), reads=[p2], writes=[o2])
                K.dma('pool', d['smallT'][0:8, g * 512:(g + 1) * 512], o[0:8, :], reads=[o])
                K.dma('pool', d['smallT'][8:16, g * 512:(g + 1) * 512], o2[:], reads=[o2])
                continue
            for k in range(8):
                K.mm(p, p[:], W, W[:, k, col0:col0 + 128], hnT, hnT[:, k, :], start=(k == 0), stop=(k == 7))
            if kind in ('q', 'k'):
                o = qk[nfm % 3]
                K.op('dve', lambda e, o=o, p=p: e.tensor_copy(out=o[:], in_=p[:]), reads=[p], writes=[o])
                dst = d['QfT'] if kind == 'q' else d['KfT']
                K.dma('pool', dst[c * 128:(c + 1) * 128, g * 512:(g + 1) * 512], o[:], reads=[o])
            else:
                xp = xpad[c]
                K.op('act', lambda e, xp=xp, p=p: e.copy(out=xp[:, 3:515], in_=p[:]), reads=[p], writes=[xp], nowaw=[xp])
                a = acc[c % 2]
                K.op('dve', lambda e, a=a, xp=xp, c=c: e.tensor_scalar(out=a[:], in0=xp[:, 0:512], scalar1=cw[:, 0, c:c + 1],
                                                                        scalar2=None, op0=ALU.mult),
                     reads=[xp, cw], writes=[a])
                for j in range(1, 4):
                    K.op('dve', lambda e, a=a, xp=xp, c=c, j=j: e.scalar_tensor_tensor(
                        out=a[:], in0=xp[:, j:j + 512], scalar=cw[:, j, c:c + 1], in1=a[:], op0=ALU.mult, op1=ALU.add),
                        reads=[xp, cw, a], writes=[a])
                K.op('pool', lambda e, xp=xp: e.tensor_copy(out=xp[:, 0:3], in_=xp[:, 512:515]), reads=[xp], writes=[xp])
                o = cout[nfm % 3]

                def silu_store(o=o, a=a, c=c, g=g):
                    K.op('act', lambda e: e.activation(out=o[:], in_=a[:], func=AF.Silu), reads=[a], writes=[o])
                    K.dma('pool', d['GCT'][c * 128:(c + 1) * 128, g * 512:(g + 1) * 512], o[:], reads=[o])
                if pend_silu[0] is not None:
                    pend_silu[0]()
                pend_silu[0] = silu_store
    if pend_silu[0] is not None:
        pend_silu[0]()
    K.phase_end()


INPUT_SHAPES = {
    "even_norm_mix": [1024], "even_w_in": [1024, 3600], "even_fox_bf": [8], "even_gdn_conv_w": [4, 1536],
    "even_gdn_a_log": [4], "even_gdn_dt_bias": [4], "even_gdn_norm_w": [128], "even_w_out": [1024, 1024],
    "odd_norm_mix": [1024], "odd_w_in": [1024, 2328], "odd_cmp_k_pe": [32, 64], "odd_cmp_k_w1": [2048, 128],
    "odd_cmp_k_w2": [128, 64], "odd_cmp_v_pe": [32, 64], "odd_cmp_v_w1": [2048, 128], "odd_cmp_v_w2": [128, 64],
    "odd_rg_conv_w": [4, 512], "odd_rg_conv_b": [512], "odd_rg_wa": [8, 64, 64], "odd_rg_ba": [512],
    "odd_rg_wx": [8, 64, 64], "odd_rg_bx": [512], "odd_rg_lambda": [512], "odd_w_out": [1024, 1024],
    "mlp_norm": [2, 1024], "mlp_w_up": [2, 1024, 4096], "mlp_w_down": [2, 4096, 1024], "ple_norm": [2, 1024],
    "ple_w_gate": [2, 1024, 1024], "ple_w_proj": [2, 256, 1024], "final_norm": [1024],
}


def scratch_specs(S):
    return {
        "h": ([S, 1024], F32),
        "QfT": ([512, S], BF16), "KfT": ([512, S], BF16), "Vf": ([S, 512], BF16),
        "GZ": ([S, 512], F32), "GCT": ([1536, S], F32), "smallT": ([16, S], F32),
        "gcd": ([4, S], F32), "ngcd": ([4, S], F32), "e1d": ([4, S], F32), "egld": ([4, S // 128], F32),
        "tokSd": ([128, (S // 128) * 16], F32),
        "csd": ([128, (S // 128) * 16], F32), "QT1": ([512, S], BF16), "KcT": ([128, S], BF16), "VcT": ([128, S], BF16),
        "KsT": ([128, S], BF16), "KwT": ([128, S], BF16), "Vs": ([S, 128], BF16), "Vw": ([S, 128], BF16),
        "gates": ([S, 24], F32), "RGX": ([1024, S], F32),
        "KcmpT": ([128, S // 16], BF16), "Vcmp": ([S // 16, 128], BF16),
        "nsa0": ([S, 512], F32), "nsa1": ([S, 512], F32), "nsa2": ([S, 512], F32), "selT": ([2, 128, S], BF16),
        "FQ": ([8, 3, S], BF16), "nF": ([128, (S // 128) * 8], F32), "mixed": ([S, 1024], BF16),
    }


def build(S, phases, dbg=(), h_input=False):
    nc = bass.Bass("TRN2", target_bir_lowering=False)
    d = {}
    d['x'] = nc.dram_tensor("x", [S, 1024], F32, kind="ExternalInput").ap()
    d['p'] = nc.dram_tensor("p", [2, S, 256], F32, kind="ExternalInput").ap()
    d['positions'] = nc.dram_tensor("positions", [1, S], I32, kind="ExternalInput").ap()
    for n, shp in INPUT_SHAPES.items():
        d[n] = nc.dram_tensor(n, shp, F32, kind="ExternalInput").ap()
    d['out'] = nc.dram_tensor("out", [S, 1024], F32, kind="ExternalOutput").ap()
    for n, (shp, dt) in scratch_specs(S).items():
        kind = "ExternalOutput" if n in dbg else "Internal"
        if n == 'h' and h_input:
            kind = "ExternalInput"
        d[n] = nc.dram_tensor(n, shp, dt, kind=kind).ap()
    K = KCtx(nc)
    import os
    K.stop = int(os.environ.get('KSTOP', '0')) or None
    for ph in phases:
        ph(K, d, S)
    K.finish()
    return nc, K


def phase_Fprep(K, d, S):
    NT_ = S // 128
    K.phase_begin()
    consts(K)
    identf = make_ident(K, F32, 'identf')
    ff = K.sb('ff', [8, S], F32)
    sp = K.sb('spl', [8, S], F32)
    ones = K.sb('ones8', [8, S], F32)
    nF = K.sb('negF', [8, S], F32)
    bf = K.sb('bf', [8, 2], F32)
    K.dma('sp', ff[:], d['smallT'][0:8, :], writes=[ff])
    K.dma('sp', bf[:, 0:1], d['even_fox_bf'].rearrange("(h o) -> h o", o=1), writes=[bf])
    K.op('dve', lambda e: e.tensor_scalar(out=bf[:, 1:2], in0=bf[:, 0:1], scalar1=-1.0, scalar2=None, op0=ALU.mult),
         reads=[bf], writes=[bf])
    K.op('pool', lambda e: e.memset(ones[:], 1.0), writes=[ones])
    K.op('act', lambda e: e.activation(out=sp[:], in_=ff[:], func=AF.Exp, scale=-1.0, bias=bf[:, 1:2]),
         reads=[ff, bf], writes=[sp])
    K.op('act', lambda e: e.activation(out=sp[:], in_=sp[:], func=AF.Ln, scale=1.0, bias=K.one_t[0:8, 0:1]),
         reads=[sp, K.one_t], writes=[sp])
    K.op('dve', lambda e: e.tensor_tensor_scan(out=nF[:], data0=ones[:], data1=sp[:], initial=0.0,
                                               op0=ALU.mult, op1=ALU.add), reads=[ones, sp], writes=[nF])
    q8 = ff
    K.op('dve', lambda e: e.tensor_scalar(out=q8[:], in0=nF[:], scalar1=-8.0, scalar2=None, op0=ALU.mult),
         reads=[nF], writes=[q8])
    parts = [K.sb('fq%d' % i, [8, S], BF16) for i in range(3)]
    for i in range(3):
        K.op('dve', lambda e, i=i: e.tensor_copy(out=parts[i][:], in_=q8[:]), reads=[q8], writes=[parts[i]])
        if i < 2:
            K.op('dve', lambda e, i=i: e.tensor_tensor(out=q8[:], in0=q8[:], in1=parts[i][:], op=ALU.subtract),
                 reads=[q8, parts[i]], writes=[q8])
        K.dma('sp', d['FQ'][:, i, :], parts[i][:], reads=[parts[i]])
    pt = K.ps('ptr', [128, 512], F32)
    nft = K.sb('nft', [128, NT_ * 8], F32)
    for t0 in range(0, NT_, 64):
        n = min(64, NT_ - t0)
        for t in range(n):
            K.tr(pt, pt[:, t * 8:(t + 1) * 8], nF, nF[0:8, (t0 + t) * 128:(t0 + t + 1) * 128], identf, identf[0:8, 0:8])
        K.op('act', lambda e, t0=t0, n=n: e.copy(out=nft[:, t0 * 8:(t0 + n) * 8], in_=pt[:, 0:n * 8]), reads=[pt], writes=[nft])
    K.dma('sp', d['nF'][:, :], nft[:], reads=[nft])
    K.phase_end()


def phase_fox(K, d, S, heads=range(8), L=3):
    NT_ = S // 128
    NQ = S // 512
    K.phase_begin()
    consts(K)
    identf = make_ident(K, F32, 'identf')
    nF = K.sb('nF', [128, NT_ * 8], F32)
    K.dma('sp', nF[:], d['nF'][:, :], writes=[nF])
    KT = [K.sb('KT%d' % i, [67, S], BF16) for i in range(2)]
    QT = [K.sb('QT%d' % i, [67, S], BF16) for i in range(2)]
    V = [K.sb('V%d' % i, [128, NT_, 65], BF16) for i in range(2)]
    for i in range(2):
        K.op('pool', lambda e, i=i: e.memset(KT[i][64:67, :], 1.0), writes=[KT[i]])
        K.op('pool', lambda e, i=i: e.memset(V[i][:, :, 64:65], 1.0), writes=[V[i]])
    ps = [K.ps('ps%d' % i, [128, 512], F32) for i in range(3)]
    po = [K.ps('po%d' % i, [128, 512], F32) for i in range(2)]
    pq = [K.ps('pq%d' % i, [128, 4, 65], F32) for i in range(2)]
    pt = [K.sb('pt%d' % i, [128, 512], BF16) for i in range(4)]
    osb = [K.sb('osb%d' % i, [65, 512], F32) for i in range(2)]
    rc = [K.sb('rc%d' % i, [128, 4], F32) for i in range(2)]
    mo = [K.sb('mo%d' % i, [128, 4, 64], BF16) for i in range(2)]

    def load_head(hi, h):
        s = hi % 2
        K.dma('sp', KT[s][0:64, :], d['KfT'][h * 64:(h + 1) * 64, :], writes=[KT[s]])
        K.dma('sp', QT[s][0:64, :], d['QfT'][h * 64:(h + 1) * 64, :], writes=[QT[s]])
        K.dma('sp', QT[s][64:67, :], d['FQ'][h, :, :], writes=[QT[s]], nowaw=[QT[s]])
        K.dma('sp', V[s][:, :, 0:64], d['Vf'][:, h * 64:(h + 1) * 64].rearrange("(t p) c -> p t c", p=128),
              writes=[V[s]])

    heads = list(heads)
    units = []
    for hi, h in enumerate(heads):
        for qg in range(NQ):
            nk = 4 * qg + 4
            for kt in range(nk):
                units.append((hi, h, qg, kt, kt == nk - 1))
    load_head(0, heads[0])
    NU = len(units)
    for i in range(NU + L):
        if i < NU:
            hi, h, qg, kt, last = units[i]
            s = hi % 2
            r = kt - 4 * qg
            c0 = 128 * max(r, 0)
            p_s = ps[i % 3]
            p_t = pt[i % 4]
            K.mm(p_s, p_s[:, c0:512], KT[s], KT[s][0:67, kt * 128:(kt + 1) * 128], QT[s],
                 QT[s][0:67, qg * 512 + c0:(qg + 1) * 512])
            K.op('act', lambda e, p_s=p_s, p_t=p_t, c0=c0, kt=kt, h=h: e.activation(
                out=p_t[:, c0:512], in_=p_s[:, c0:512], func=AF.Exp, scale=0.125, bias=nF[:, kt * 8 + h:kt * 8 + h + 1]),
                reads=[p_s, nF], writes=[p_t])
            if r >= 0:
                K.op('pool', lambda e, p_t=p_t, c0=c0: e.affine_select(
                    out=p_t[:, c0:c0 + 128], in_=p_t[:, c0:c0 + 128], pattern=[[1, 128]], compare_op=ALU.is_ge,
                    fill=0.0, base=0, channel_multiplier=-1), reads=[p_t], writes=[p_t])
        if i - L >= 0:
            hi, h, qg, kt, last = units[i - L]
            if qg == 0 and kt == 0 and hi + 1 < len(heads):
                load_head(hi + 1, heads[hi + 1])
            s = hi % 2
            r = kt - 4 * qg
            c0 = 128 * max(r, 0)
            p_t = pt[(i - L) % 4]
            gi = hi * NQ + qg
            p_o = po[gi % 2]
            K.mm(p_o, p_o[0:65, c0:512], V[s], V[s][:, kt, 0:65], p_t, p_t[:, c0:512], start=(kt == 0), stop=last)
            if last:
                o_s = osb[gi % 2]
                p_q = pq[gi % 2]
                K.op('act', lambda e, o_s=o_s, p_o=p_o: e.copy(out=o_s[:], in_=p_o[0:65, :]), reads=[p_o], writes=[o_s])
                for j in range(4):
                    K.tr(p_q, p_q[:, j, :], o_s, o_s[0:65, j * 128:(j + 1) * 128], identf, identf[0:65, 0:65])
                r_c = rc[gi % 2]
                m_o = mo[gi % 2]
                K.op('dve', lambda e, r_c=r_c, p_q=p_q: e.reciprocal(out=r_c[:], in_=p_q[:, :, 64]), reads=[p_q], writes=[r_c])
                for j in range(4):
                    K.op('dve', lambda e, j=j, r_c=r_c, p_q=p_q, m_o=m_o: e.tensor_scalar(
                        out=m_o[:, j, :], in0=p_q[:, j, 0:64], scalar1=r_c[:, j:j + 1], scalar2=None, op0=ALU.mult),
                        reads=[p_q, r_c], writes=[m_o])
                K.dma('pool', d['mixed'][qg * 512:(qg + 1) * 512, h * 64:(h + 1) * 64].rearrange("(j p) c -> p j c", p=128),
                      m_o[:], reads=[m_o])
    K.phase_end()


NEGM = -30000.0


def phase_Gprep(K, d, S):
    NC = S // 128
    K.phase_begin()
    consts(K)
    identf = make_ident(K, F32, 'identf')
    A = K.sb('gpA', [4, S], F32)
    B = K.sb('gpB', [4, S], F32)
    C = K.sb('gpC', [4, S], F32)
    Dt = K.sb('gpD', [4, S], F32)
    E = K.sb('gpE', [4, S], F32)
    K.dma('sp', A[:], d['smallT'][8:12, :], writes=[A])
    K.dma('sp', B[:], d['smallT'][12:16, :], writes=[B])
    pr = K.sb('pr', [4, 4], F32)
    K.dma('sp', pr[:, 0:1], d['even_gdn_a_log'].rearrange("(h o) -> h o", o=1), writes=[pr])
    K.dma('sp', pr[:, 1:2], d['even_gdn_dt_bias'].rearrange("(h o) -> h o", o=1), writes=[pr], nowaw=[pr])
    K.op('act', lambda e: e.activation(out=pr[:, 2:3], in_=pr[:, 0:1], func=AF.Exp), reads=[pr], writes=[pr])
    K.op('pool', lambda e: e.memset(C[:], 1.0), writes=[C])
    K.op('pool', lambda e: e.memset(C[:].rearrange("p (c t) -> p c t", t=128)[:, :, 0:1], 0.0), writes=[C])
    K.op('act', lambda e: e.activation(out=Dt[:], in_=B[:], func=AF.Exp, bias=pr[:, 1:2]), reads=[B, pr], writes=[Dt])
    K.op('act', lambda e: e.activation(out=Dt[:], in_=Dt[:], func=AF.Ln, bias=K.one_t[0:4, 0:1]), reads=[Dt, K.one_t], writes=[Dt])
    K.op('dve', lambda e: e.tensor_scalar(out=B[:], in0=Dt[:], scalar1=pr[:, 2:3], scalar2=-1.0, op0=ALU.mult, op1=ALU.mult),
         reads=[Dt, pr], writes=[B])
    gc = E
    K.op('dve', lambda e: e.tensor_tensor_scan(out=gc[:], data0=C[:], data1=B[:], initial=0.0, op0=ALU.mult, op1=ALU.add),
         reads=[C, B], writes=[gc])
    K.dma('sp', d['gcd'][:, :], gc[:], reads=[gc])
    K.op('dve', lambda e: e.tensor_scalar(out=Dt[:], in0=gc[:], scalar1=-1.0, scalar2=None, op0=ALU.mult), reads=[gc], writes=[Dt])
    K.dma('sp', d['ngcd'][:, :], Dt[:], reads=[Dt])
    beta = A
    K.op('act', lambda e: e.activation(out=beta[:], in_=A[:], func=AF.Sigmoid), reads=[A], writes=[beta])
    e1 = B
    K.op('act', lambda e: e.activation(out=e1[:], in_=gc[:], func=AF.Exp), reads=[gc], writes=[e1])
    K.dma('sp', d['e1d'][:, :], e1[:], reads=[e1])
    be = C
    K.op('dve', lambda e: e.tensor_tensor(out=be[:], in0=beta[:], in1=e1[:], op=ALU.mult), reads=[beta, e1], writes=[be])
    gc3 = gc[:].rearrange("p (c t) -> p c t", t=128)
    e2 = Dt
    K.op('dve', lambda e: e.tensor_tensor(out=e2[:].rearrange("p (c t) -> p c t", t=128),
                                          in0=gc3[:, :, 127:128].to_broadcast([4, NC, 128]), in1=gc3, op=ALU.subtract),
         reads=[gc], writes=[e2])
    K.op('act', lambda e: e.activation(out=e2[:], in_=e2[:], func=AF.Exp), reads=[e2], writes=[e2])
    egl = K.sb('egl', [4, NC], F32)
    K.op('act', lambda e: e.activation(out=egl[:], in_=gc3[:, :, 127], func=AF.Exp), reads=[gc], writes=[egl])
    K.dma('sp', d['egld'][:, :], egl[:], reads=[egl])
    pt = [K.ps('ptg%d' % i, [128, 32, 16], F32) for i in range(2)]
    tokS = K.sb('tokS', [128, NC, 16], F32)
    K.op('pool', lambda e: e.memset(tokS[:], 0.0), writes=[tokS])
    for t0 in range(0, NC, 32):
        n = min(32, NC - t0)
        p = pt[(t0 // 32) % 2]
        for t in range(n):
            for qi, src in enumerate((beta, be, e2, gc)):
                K.tr(p, p[:, t, qi * 4:(qi + 1) * 4], src, src[0:4, (t0 + t) * 128:(t0 + t + 1) * 128], identf, identf[0:4, 0:4])
        K.op('act', lambda e, p=p, t0=t0, n=n: e.copy(out=tokS[:, t0:t0 + n, 0:16], in_=p[:, 0:n, 0:16]), reads=[p], writes=[tokS],
             nowaw=[tokS])
    K.dma('sp', d['tokSd'][:, :], tokS[:].rearrange("p c q -> p (c q)"), reads=[tokS])
    K.phase_end()


class StopPhase(Exception):
    pass


def chk(K, n):
    if getattr(K, 'stop', None) == n:
        raise StopPhase()


def phase_gdn(K, d, S, heads=range(4)):
    try:
        _phase_gdn(K, d, S, heads)
    except StopPhase:
        pass
    K.phase_end()


def _phase_gdn(K, d, S, heads=range(4)):
    NC = S // 128
    NG = S // 512
    K.phase_begin()
    consts(K)
    identf = make_ident(K, F32, 'identf')
    onesf = K.sb('onesf', [128, 128], F32)
    K.op('pool', lambda e: e.memset(onesf[:], 1.0), writes=[onesf])
    maskT = K.sb('maskT', [128, 128], F32)
    maskTs = K.sb('maskTs', [128, 128], F32)
    zer = K.sb('zer', [128, 128], F32)
    K.op('pool', lambda e: e.memset(zer[:], 0.0), writes=[zer])
    K.op('pool', lambda e: e.affine_select(out=maskT[:], in_=zer[:], pattern=[[1, 128]], compare_op=ALU.is_ge, fill=NEGM,
                                           base=0, channel_multiplier=-1), reads=[zer], writes=[maskT])
    K.op('pool', lambda e: e.affine_select(out=maskTs[:], in_=zer[:], pattern=[[1, 128]], compare_op=ALU.is_gt, fill=NEGM,
                                           base=0, channel_multiplier=-1), reads=[zer], writes=[maskTs])
    nwb = K.sb('nwb', [128, 128], F32)
    K.dma('sp', nwb[:], d['even_gdn_norm_w'].partition_broadcast(128), writes=[nwb])
    tokS = K.sb('tokS', [128, NC, 16], F32)
    K.dma('sp', tokS[:].rearrange("p c q -> p (c q)"), d['tokSd'][:, :], writes=[tokS])
    c128 = K.sb('c128', [128, 1], F32)
    K.op('pool', lambda e: e.memset(c128[:], 128.0 * 1e-6), writes=[c128])
    G2L = K.sb('G2L', [2, S], F32)
    G2R = K.sb('G2R', [2, S], F32)
    e1row = K.sb('e1row', [1, S], F32)
    eglrow = K.sb('eglrow', [1, NC], F32)
    eglb = K.sb('eglb', [128, NC], F32)
    Sst = K.sb('Sst', [128, 128], F32)
    qT = [K.sb('gqT%d' % i, [128, 512], F32) for i in range(2)]
    kT = [K.sb('gkT%d' % i, [128, 512], F32) for i in range(2)]
    vT = [K.sb('gvT%d' % i, [128, 512], F32) for i in range(2)]
    sq = K.sb('gsq', [128, 512], F32)
    rr = K.sb('grr', [128, 512], F32)
    knT = K.sb('knT', [128, 512], F32)
    qnT = K.sb('qnT', [128, 512], F32)
    qgT = K.sb('qgT', [128, 512], F32)
    zt = [K.sb('gz%d' % i, [128, 128], F32) for i in range(2)]
    nz = K.sb('nz', [128, 128], F32)
    Kbg = K.sb('Kbg', [128, 128], F32)
    Kd = K.sb('Kd', [128, 128], F32)
    Vb = K.sb('Vb', [128, 128], F32)
    dm1 = K.sb('dm1', [128, 128], F32)
    dm2 = K.sb('dm2', [128, 128], F32)
    attnT = K.sb('attnT', [128, 128], F32)
    Mt = K.sb('Mt', [128, 128], F32)
    P = [K.sb('Pn%d' % i, [128, 128], F32) for i in range(2)]
    PT = [K.sb('PTn%d' % i, [128, 128], F32) for i in range(2)]
    X = K.sb('Xn', [128, 128], F32)
    U = K.sb('Un', [128, 128], F32)
    WT = K.sb('WTn', [128, 128], F32)
    vnew = K.sb('vnew', [128, 128], F32)
    ost = K.sb('gost', [128, 4], F32)
    ojunk = K.sb('gojunk', [128, 128], F32)
    ob = [K.sb('gob%d' % i, [128, 128], BF16) for i in range(2)]
    bA = K.ps('bA', [128, 512], F32)
    bB = K.ps('bB', [128, 512], F32)
    bC = K.ps('bC', [128, 512], F32)
    bD = K.ps('bD', [128, 512], F32)
    bE = K.ps('bE', [128, 512], F32)
    bF = K.ps('bF', [128, 512], F32)
    bG = K.ps('bG', [128, 512], F32)
    bH = K.ps('bH', [128, 512], F32)
    sl = lambda i: slice(i * 128, (i + 1) * 128)
    chk(K, 1)
    for h in heads:
        K.op('pool', lambda e: e.memset(G2L[:], 1.0), writes=[G2L])
        K.op('pool', lambda e: e.memset(G2R[:], 1.0), writes=[G2R])
        K.dma('sp', G2L[0:1, :], d['ngcd'][h:h + 1, :], writes=[G2L])
        K.dma('sp', G2R[1:2, :], d['gcd'][h:h + 1, :], writes=[G2R])
        K.dma('sp', e1row[:], d['e1d'][h:h + 1, :], writes=[e1row])
        K.dma('sp', eglrow[:], d['egld'][h:h + 1, :], writes=[eglrow])
        K.mm(bA, bA[:, 0:NC], onesf, onesf[0:1, :], eglrow, eglrow[0:1, :])
        K.op('act', lambda e: e.copy(out=eglb[:], in_=bA[:, 0:NC]), reads=[bA], writes=[eglb])
        K.op('pool', lambda e: e.memset(Sst[:], 0.0), writes=[Sst])
        chk(K, 2)
        for g in range(NG):
            cs = slice(g * 512, (g + 1) * 512)
            q_, k_, v_ = qT[g % 2], kT[g % 2], vT[g % 2]
            K.dma('sp', q_[:], d['GCT'][h * 128:(h + 1) * 128, cs], writes=[q_])
            K.dma('sp', k_[:], d['GCT'][512 + h * 128:512 + (h + 1) * 128, cs], writes=[k_])
            K.dma('sp', v_[:], d['GCT'][1024 + h * 128:1024 + (h + 1) * 128, cs], writes=[v_])
            K.op('act', lambda e: e.activation(out=sq[:], in_=k_[:], func=AF.Square), reads=[k_], writes=[sq])
            K.mm(bA, bA[:], onesf, onesf[:], sq, sq[:])
            K.op('act', lambda e: e.activation(out=rr[:], in_=bA[:], func=AF.Sqrt, bias=K.eps_t[:, 0:1]), reads=[bA, K.eps_t], writes=[rr])
            K.op('dve', lambda e: e.reciprocal(out=rr[:], in_=rr[:]), reads=[rr], writes=[rr])
            K.op('dve', lambda e: e.tensor_tensor(out=knT[:], in0=k_[:], in1=rr[:], op=ALU.mult), reads=[k_, rr], writes=[knT])
            K.op('act', lambda e: e.activation(out=sq[:], in_=q_[:], func=AF.Square), reads=[q_], writes=[sq])
            K.mm(bA, bA[:], onesf, onesf[:], sq, sq[:])
            K.op('act', lambda e: e.activation(out=rr[:], in_=bA[:], func=AF.Sqrt, scale=128.0, bias=c128[:, 0:1]),
                 reads=[bA, c128], writes=[rr])
            K.op('dve', lambda e: e.reciprocal(out=rr[:], in_=rr[:]), reads=[rr], writes=[rr])
            K.op('dve', lambda e: e.tensor_tensor(out=qnT[:], in0=q_[:], in1=rr[:], op=ALU.mult), reads=[q_, rr], writes=[qnT])
            K.mm(bA, bA[:], onesf, onesf[0:1, :], e1row, e1row[0:1, cs])
            K.op('dve', lambda e: e.tensor_tensor(out=qgT[:], in0=qnT[:], in1=bA[:], op=ALU.mult), reads=[qnT, bA], writes=[qgT])
            chk(K, 3)
            for tt in range(4):
                c = g * 4 + tt
                ts_ = sl(tt)
                tok = slice(c * 128, (c + 1) * 128)
                z = zt[c % 2]
                K.dma('sp', z[:], d['GZ'][tok, h * 128:(h + 1) * 128], writes=[z])
                K.op('pool', lambda e, z=z: e.tensor_tensor(out=nz[:], in0=z[:], in1=nwb[:], op=ALU.mult), reads=[z, nwb], writes=[nz])
                beta_c = tokS[:, c, h:h + 1]
                be_c = tokS[:, c, 4 + h:5 + h]
                e2_c = tokS[:, c, 8 + h:9 + h]
                K.tr(bB, bB[:, 0:128], knT, knT[:, ts_], identf, identf[:])
                K.tr(bB, bB[:, 128:256], v_, v_[:, ts_], identf, identf[:])
                K.op('dve', lambda e: e.tensor_scalar(out=Kbg[:], in0=bB[:, 0:128], scalar1=be_c, scalar2=None, op0=ALU.mult),
                     reads=[bB, tokS], writes=[Kbg])
                K.op('act', lambda e: e.activation(out=Kd[:], in_=bB[:, 0:128], func=AF.Copy, scale=e2_c), reads=[bB, tokS], writes=[Kd])
                K.op('dve', lambda e: e.tensor_scalar(out=Vb[:], in0=bB[:, 128:256], scalar1=beta_c, scalar2=None, op0=ALU.mult),
                     reads=[bB, tokS], writes=[Vb])
                chk(K, 4)
                K.mm(bC, bC[:, 0:128], knT, knT[:, ts_], knT, knT[:, ts_])
                K.mm(bC, bC[:, 128:256], knT, knT[:, ts_], qnT, qnT[:, ts_])
                K.mm(bC, bC[:, 256:384], G2L, G2L[0:2, tok], G2R, G2R[0:2, tok])
                K.op('dve', lambda e: e.tensor_tensor(out=dm1[:], in0=bC[:, 256:384], in1=maskT[:], op=ALU.add), reads=[bC, maskT], writes=[dm1])
                K.op('dve', lambda e: e.tensor_tensor(out=dm2[:], in0=bC[:, 256:384], in1=maskTs[:], op=ALU.add), reads=[bC, maskTs], writes=[dm2])
                K.op('act', lambda e: e.activation(out=dm1[:], in_=dm1[:], func=AF.Exp), reads=[dm1], writes=[dm1])
                K.op('act', lambda e: e.activation(out=dm2[:], in_=dm2[:], func=AF.Exp), reads=[dm2], writes=[dm2])
                K.op('dve', lambda e: e.tensor_tensor(out=attnT[:], in0=bC[:, 128:256], in1=dm1[:], op=ALU.mult), reads=[bC, dm1], writes=[attnT])
                K.op('dve', lambda e: e.tensor_tensor(out=Mt[:], in0=bC[:, 0:128], in1=dm2[:], op=ALU.mult), reads=[bC, dm2], writes=[Mt])
                chk(K, 5)
                K.tr(bD, bD[:, 0:128], Mt, Mt[:], identf, identf[:])
                K.op('dve', lambda e: e.tensor_scalar(out=P[0][:], in0=bD[:, 0:128], scalar1=beta_c, scalar2=-1.0, op0=ALU.mult, op1=ALU.mult),
                     reads=[bD, tokS], writes=[P[0]])
                K.tr(bD, bD[:, 128:256], P[0], P[0][:], identf, identf[:])
                K.op('act', lambda e: e.copy(out=PT[0][:], in_=bD[:, 128:256]), reads=[bD], writes=[PT[0]])
                K.op('dve', lambda e: e.tensor_tensor(out=X[:], in0=bD[:, 128:256], in1=identf[:], op=ALU.add), reads=[bD, identf], writes=[X])
                for n in range(6):
                    a, b = n % 2, (n + 1) % 2
                    K.mm(bE, bE[:, 128:256], PT[a], PT[a][:], P[a], P[a][:])
                    K.mm(bE, bE[:, 256:384], P[a], P[a][:], PT[a], PT[a][:])
                    K.op('act', lambda e, b=b: e.copy(out=P[b][:], in_=bE[:, 128:256]), reads=[bE], writes=[P[b]])
                    K.op('act', lambda e, b=b: e.copy(out=PT[b][:], in_=bE[:, 256:384]), reads=[bE], writes=[PT[b]])
                    K.mm(bF, bF[:, 0:128], P[b], P[b][:], X, X[:])
                    K.op('dve', lambda e: e.tensor_tensor(out=X[:], in0=X[:], in1=bF[:, 0:128], op=ALU.add), reads=[X, bF], writes=[X])
                chk(K, 6)
                K.mm(bG, bG[:, 0:128], X, X[:], Vb, Vb[:])
                K.mm(bG, bG[:, 128:256], Kbg, Kbg[:], X, X[:])
                K.op('act', lambda e: e.copy(out=U[:], in_=bG[:, 0:128]), reads=[bG], writes=[U])
                K.op('act', lambda e: e.copy(out=WT[:], in_=bG[:, 128:256]), reads=[bG], writes=[WT])
                chk(K, 7)
                K.mm(bH, bH[:, 0:128], WT, WT[:], Sst, Sst[:])
                K.op('dve', lambda e: e.tensor_tensor(out=vnew[:], in0=U[:], in1=bH[:, 0:128], op=ALU.subtract), reads=[U, bH], writes=[vnew])
                K.mm(bH, bH[:, 128:256], qgT, qgT[:, ts_], Sst, Sst[:], start=True, stop=False)
                K.mm(bH, bH[:, 128:256], attnT, attnT[:], vnew, vnew[:], start=False, stop=True)
                K.mm(bH, bH[:, 256:384], Kd, Kd[:], vnew, vnew[:])
                K.op('dve', lambda e, c=c: e.scalar_tensor_tensor(out=Sst[:], in0=Sst[:], scalar=eglb[:, c:c + 1], in1=bH[:, 256:384],
                                                                   op0=ALU.mult, op1=ALU.add), reads=[Sst, eglb, bH], writes=[Sst])
                chk(K, 8)
                K.op('act', lambda e: e.activation(out=ojunk[:], in_=bH[:, 128:256], func=AF.Square, accum_out=ost[:, 0:1]),
                     reads=[bH], writes=[ojunk, ost])
                K.op('act', lambda e: e.activation(out=ost[:, 1:2], in_=ost[:, 0:1], func=AF.Sqrt, scale=1.0 / 128, bias=K.eps_t[:, 0:1]),
                     reads=[ost, K.eps_t], writes=[ost])
                K.op('dve', lambda e: e.reciprocal(out=ost[:, 2:3], in_=ost[:, 1:2]), reads=[ost], writes=[ost])
                o_b = ob[c % 2]
                K.op('dve', lambda e, o_b=o_b: e.scalar_tensor_tensor(out=o_b[:], in0=bH[:, 128:256], scalar=ost[:, 2:3], in1=nz[:],
                                                                       op0=ALU.mult, op1=ALU.mult), reads=[bH, ost, nz], writes=[o_b])
                K.dma('pool', d['mixed'][tok, 512 + h * 128:512 + (h + 1) * 128], o_b[:], reads=[o_b])


def load_gain(K, name, src1d):
    g = K.sb(name, [128, 8], F32)
    K.dma('sp', g[:], src1d.rearrange("(k p) -> p k", p=128), writes=[g], allow_slow_non_contiguous=True)
    return g


def phase_O(K, d, S, src, w_out, dst='h'):
    NT_ = S // 128
    K.phase_begin()
    consts(K)
    identb = make_ident(K, BF16, 'identb')
    W = K.sb('Wo', [128, 8, 1024], BF16)
    load_weight_bf16(K, W, w_out, 8, 1024)
    mx = [K.sb('mx%d' % i, [128, 1024], BF16) for i in range(2)]
    xr = [K.sb('xr%d' % i, [128, 1024], F32) for i in range(2)]
    mT = [K.sb('mT%d' % i, [128, 8, 128], BF16) for i in range(2)]
    ho = [K.sb('ho%d' % i, [128, 1024], F32) for i in range(2)]
    pT = [K.ps('opT%d' % i, [128, 1024], BF16) for i in range(2)]
    po = [K.ps('opo%d' % i, [128, 512], F32) for i in range(4)]
    for t in range(NT_):
        rows = slice(t * 128, (t + 1) * 128)
        m_, x_, mT_, h_, p_ = mx[t % 2], xr[t % 2], mT[t % 2], ho[t % 2], pT[t % 2]
        K.dma('sp', m_[:], d['mixed'][rows, :], writes=[m_])
        K.dma('sp', x_[:], src[rows, :], writes=[x_])
        for k in range(8):
            K.tr(p_, p_[:, k * 128:(k + 1) * 128], m_, m_[:, k * 128:(k + 1) * 128], identb, identb[:])
        K.op('act', lambda e: e.copy(out=mT_[:], in_=p_[:].rearrange("p (k t) -> p k t", k=8)), reads=[p_], writes=[mT_])
        for half in range(2):
            p2 = po[(t * 2 + half) % 4]
            for k in range(8):
                K.mm(p2, p2[:], mT_, mT_[:, k, :], W, W[:, k, half * 512:(half + 1) * 512], start=(k == 0), stop=(k == 7))
            K.op('dve', lambda e, p2=p2, half=half: e.tensor_tensor(out=h_[:, half * 512:(half + 1) * 512], in0=p2[:],
                                                                   in1=x_[:, half * 512:(half + 1) * 512], op=ALU.add),
                 reads=[p2, x_], writes=[h_], nowaw=[h_])
        K.dma('pool', d[dst][rows, :], h_[:], reads=[h_])
    K.phase_end()


def phase_MLP(K, d, S, li):
    GT = 2
    NG = S // (GT * 128)
    K.phase_begin()
    consts(K)
    identb = make_ident(K, BF16, 'identb')
    gain = load_gain(K, 'gain', d['mlp_norm'][li, :])
    Wu = K.sb('Wu', [128, 8, 4096], BF16)
    load_weight_bf16(K, Wu, d['mlp_w_up'][li], 8, 4096, gain=gain, stage_name='wsu', cb=512)
    Wd = K.sb('Wd', [128, 32, 1024], BF16)
    load_weight_bf16(K, Wd, d['mlp_w_down'][li], 32, 1024, stage_name='wsd', cb=512)
    NT = NormT(K, identb, nbuf_x=2 * GT, gt=GT)
    uT = K.sb('uT', [128, 32, GT * 128], BF16)
    rl = [K.sb('rl%d' % i, [128, GT * 128], F32) for i in range(2)]
    ho = [K.sb('mho%d' % i, [128, 1024], F32) for i in range(2)]
    pu = [K.ps('pu%d' % i, [128, 512], F32) for i in range(2)]
    pd = [K.ps('pd%d' % i, [128, 512], F32) for i in range(4)]
    toks = [NT.prep(d['h'][j * 128:(j + 1) * 128, :]) for j in range(GT)]
    for g in range(NG):
        hnT = NT.hnT[g % 2]
        xbs = []
        for j in range(GT):
            xbs.append(NT.finish(toks[j], hnT, j))
        for fc in range(32):
            if fc == 8 and g + 1 < NG:
                toks = [NT.prep(d['h'][((g + 1) * GT + j) * 128:((g + 1) * GT + j + 1) * 128, :]) for j in range(GT)]
            p = pu[fc % 2]
            for k in range(8):
                K.mm(p, p[:, 0:GT * 128], Wu, Wu[:, k, fc * 128:(fc + 1) * 128], hnT, hnT[:, k, :], start=(k == 0), stop=(k == 7))
            r = rl[fc % 2]
            K.op('act', lambda e, r=r, p=p: e.activation(out=r[:], in_=p[:, 0:GT * 128], func=AF.Relu), reads=[p], writes=[r])
            K.op('dve' if fc % 2 == 0 else 'pool', lambda e, r=r, fc=fc: e.tensor_tensor(out=uT[:, fc, :], in0=r[:], in1=r[:], op=ALU.mult),
                 reads=[r], writes=[uT], nowaw=[uT])
        for j in range(GT):
            t = g * GT + j
            h_ = ho[t % 2]
            for half in range(2):
                p2 = pd[(t * 2 + half) % 4]
                for fc in range(32):
                    K.mm(p2, p2[:], uT, uT[:, fc, j * 128:(j + 1) * 128], Wd, Wd[:, fc, half * 512:(half + 1) * 512],
                         start=(fc == 0), stop=(fc == 31))
                K.op('dve', lambda e, p2=p2, half=half, x_=xbs[j]: e.tensor_tensor(
                    out=h_[:, half * 512:(half + 1) * 512], in0=p2[:], in1=x_[:, half * 512:(half + 1) * 512], op=ALU.add),
                    reads=[p2, xbs[j]], writes=[h_], nowaw=[h_])
            K.dma('pool', d['h'][t * 128:(t + 1) * 128, :], h_[:], reads=[h_])
    K.phase_end()


def phase_PLE(K, d, S, li, final=False):
    NT_ = S // 128
    K.phase_begin()
    consts(K)
    identb = make_ident(K, BF16, 'identb')
    gain = load_gain(K, 'gain', d['ple_norm'][li, :])
    Wg = K.sb('Wg', [128, 8, 1024], BF16)
    load_weight_bf16(K, Wg, d['ple_w_gate'][li], 8, 1024, gain=gain)
    Wp = K.sb('Wp', [128, 2, 1024], BF16)
    load_weight_bf16(K, Wp, d['ple_w_proj'][li], 2, 1024)
    NT = NormT(K, identb, nbuf_x=9, gt=1, nbuf_n=8)
    pin = [K.sb('pin%d' % i, [128, 256], F32) for i in range(2)]
    pbf = [K.sb('pbf%d' % i, [128, 256], BF16) for i in range(2)]
    ppT = [K.sb('ppT%d' % i, [128, 2, 128], BF16) for i in range(2)]
    sg = [K.sb('sg%d' % i, [128, 1024], F32) for i in range(2)]
    ho = [K.sb('pho%d' % i, [128, 1024], F32) for i in range(2)]
    ptp = K.ps('ptp', [128, 1024], BF16)
    pg = [K.ps('pg%d' % i, [128, 512], F32) for i in range(2)]
    pp = [K.ps('pp%d' % i, [128, 512], F32) for i in range(2)]
    if final:
        fg = K.sb('fg', [128, 1024], F32)
        K.dma('sp', fg[:], d['final_norm'].partition_broadcast(128), writes=[fg])
        fst = [K.sb('fst%d' % i, [128, 4], F32) for i in range(2)]
        fo = [K.sb('fo%d' % i, [128, 1024], F32) for i in range(2)]
    toks = [NT.prep(d['h'][u * 128:(u + 1) * 128, :]) for u in range(min(4, NT_))]
    for t in range(NT_):
        rows = slice(t * 128, (t + 1) * 128)
        hnT = NT.hnT[t % 2]
        x_ = NT.finish(toks.pop(0), hnT, 0)
        if t % 4 == 0:
            for u in range(t + 4, min(t + 8, NT_)):
                toks.append(NT.prep(d['h'][u * 128:(u + 1) * 128, :]))
        pi_, pb_, pT_, sg_, h_ = pin[t % 2], pbf[t % 2], ppT[t % 2], sg[t % 2], ho[t % 2]
        K.dma('sp', pi_[:], d['p'][li, rows, :], writes=[pi_])
        K.op('dve', lambda e: e.tensor_copy(out=pb_[:], in_=pi_[:]), reads=[pi_], writes=[pb_])
        for k in range(2):
            K.tr(ptp, ptp[:, k * 128:(k + 1) * 128], pb_, pb_[:, k * 128:(k + 1) * 128], identb, identb[:])
        K.op('act', lambda e: e.copy(out=pT_[:], in_=ptp[:, 0:256].rearrange("p (k t) -> p k t", k=2)), reads=[ptp], writes=[pT_])
        for half in range(2):
            hs = slice(half * 512, (half + 1) * 512)
            g_, p_ = pg[half], pp[half]
            for k in range(8):
                K.mm(g_, g_[:], hnT, hnT[:, k, :], Wg, Wg[:, k, hs], start=(k == 0), stop=(k == 7))
            for k in range(2):
                K.mm(p_, p_[:], pT_, pT_[:, k, :], Wp, Wp[:, k, hs], start=(k == 0), stop=(k == 1))
            K.op('act', lambda e, g_=g_, hs=hs: e.activation(out=sg_[:, hs], in_=g_[:], func=AF.Sigmoid), reads=[g_], writes=[sg_], nowaw=[sg_])
            K.op('dve', lambda e, p_=p_, hs=hs: e.tensor_tensor(out=sg_[:, hs], in0=sg_[:, hs], in1=p_[:], op=ALU.mult),
                 reads=[sg_, p_], writes=[sg_])
        K.op('pool', lambda e: e.tensor_tensor(out=h_[:], in0=sg_[:], in1=x_[:], op=ALU.add), reads=[sg_, x_], writes=[h_])
        if not final:
            K.dma('pool', d['h'][rows, :], h_[:], reads=[h_])
        else:
            st, o_ = fst[t % 2], fo[t % 2]
            K.op('act', lambda e: e.activation(out=o_[:], in_=h_[:], func=AF.Square, accum_out=st[:, 0:1]), reads=[h_], writes=[o_, st])
            K.op('act', lambda e: e.activation(out=st[:, 1:2], in_=st[:, 0:1], func=AF.Sqrt, scale=1.0 / D, bias=K.eps_t[:, 0:1]),
                 reads=[st, K.eps_t], writes=[st])
            K.op('dve', lambda e: e.reciprocal(out=st[:, 2:3], in_=st[:, 1:2]), reads=[st], writes=[st])
            K.op('dve', lambda e: e.scalar_tensor_tensor(out=o_[:], in0=h_[:], scalar=st[:, 2:3], in1=fg[:], op0=ALU.mult, op1=ALU.mult),
                 reads=[h_, st, fg], writes=[o_])
            K.dma('pool', d['out'][rows, :], o_[:], reads=[o_])
    K.phase_end()


O_NQ, O_KC, O_VC, O_KSL, O_VSL, O_KWN, O_VWN, O_NG, O_RG, O_RX = 0, 512, 640, 768, 896, 1024, 1152, 1280, 1304, 1816
TWO_PI = 6.283185307179586
CW1 = 6.28125
CW2 = TWO_PI - CW1
MAGIC = 12582912.0


def phase_Rprep(K, d, S):
    NT_ = S // 128
    inv_freq = (np.float32(500000.0) ** (-np.arange(8, dtype=np.float32) * np.float32(2.0 / 16))).astype(np.float32)
    K.phase_begin()
    consts(K)
    posi = K.sb('posi', [128, NT_], I32)
    K.dma('sp', posi[:], d['positions'].rearrange("o (t p) -> p (o t)", p=128), writes=[posi], allow_slow_non_contiguous=True)
    posf = K.sb('posf', [128, NT_], F32)
    K.op('dve', lambda e: e.tensor_copy(out=posf[:], in_=posi[:]), reads=[posi], writes=[posf])
    ang = K.sb('ang', [128, NT_, 8], F32)
    for i in range(8):
        K.op('dve', lambda e, i=i: e.tensor_scalar(out=ang[:, :, i], in0=posf[:], scalar1=float(inv_freq[i]), scalar2=None, op0=ALU.mult),
             reads=[posf], writes=[ang], nowaw=[ang])
    cs = K.sb('cs', [128, NT_, 16], F32)
    a2 = K.sb('a2', [128, NT_, 8], F32)
    kk = K.sb('kk', [128, NT_, 8], F32)
    rr = K.sb('rr', [128, NT_, 8], F32)
    for which in range(2):
        src = ang
        if which == 0:
            K.op('dve', lambda e: e.tensor_scalar(out=a2[:], in0=ang[:], scalar1=float(np.pi / 2), scalar2=None, op0=ALU.add), reads=[ang], writes=[a2])
            src = a2
        K.op('dve', lambda e, src=src: e.tensor_scalar(out=kk[:], in0=src[:], scalar1=float(1.0 / TWO_PI), scalar2=MAGIC, op0=ALU.mult, op1=ALU.add),
             reads=[src], writes=[kk])
        K.op('dve', lambda e: e.tensor_scalar(out=kk[:], in0=kk[:], scalar1=-MAGIC, scalar2=None, op0=ALU.add), reads=[kk], writes=[kk])
        K.op('dve', lambda e, src=src: e.scalar_tensor_tensor(out=rr[:], in0=kk[:], scalar=-CW1, in1=src[:], op0=ALU.mult, op1=ALU.add),
             reads=[kk, src], writes=[rr])
        K.op('dve', lambda e: e.scalar_tensor_tensor(out=rr[:], in0=kk[:], scalar=-CW2, in1=rr[:], op0=ALU.mult, op1=ALU.add),
             reads=[kk, rr], writes=[rr])
        K.op('dve', lambda e: e.tensor_scalar(out=rr[:], in0=rr[:], scalar1=3.141592, scalar2=-3.141592, op0=ALU.min, op1=ALU.max),
             reads=[rr], writes=[rr])
        K.op('act', lambda e, which=which: e.activation(out=cs[:, :, which * 8:(which + 1) * 8], in_=rr[:], func=AF.Sin),
             reads=[rr], writes=[cs], nowaw=[cs])
    K.dma('sp', d['csd'][:, :], cs[:].rearrange("p t c -> p (t c)"), reads=[cs])
    K.phase_end()


def phase_A1(K, d, S):
    NG = S // 512
    NT_ = S // 128
    K.phase_begin()
    consts(K)
    identb = make_ident(K, BF16, 'identb')
    gain = load_gain(K, 'gain', d['odd_norm_mix'])
    W = K.sb('W1', [128, 8, 2328], BF16)
    load_weight_bf16(K, W, d['odd_w_in'], 8, 2328, gain=gain)
    cs = K.sb('cs', [128, NT_, 16], F32)
    K.dma('sp', cs[:].rearrange("p t c -> p (t c)"), d['csd'][:, :], writes=[cs])
    NT = NormT(K, identb, nbuf_x=8, nbuf_n=8)
    yq = [K.sb('yq%d' % i, [128, 512], F32) for i in range(2)]
    yb = [K.sb('yb%d' % i, [128, 512], F32) for i in range(2)]
    yc = [K.sb('yc%d' % i, [128, 280], F32) for i in range(2)]
    gt_ = [K.sb('gt%d' % i, [128, 24], F32) for i in range(2)]
    qb = [K.sb('qb%d' % i, [128, 512], BF16) for i in range(2)]
    kb = [K.sb('kb%d' % i, [128, 4, 128], BF16) for i in range(2)]
    vb = [K.sb('vb%d' % i, [128, 2, 128], BF16) for i in range(2)]
    tA = K.sb('tA', [128, 8, 8], F32)
    tB = K.sb('tB', [128, 8, 8], F32)
    qTs = [K.sb('qTs%d' % i, [128, 4, 128], BF16) for i in range(2)]
    kTs = [K.sb('kTs%d' % i, [128, 4, 128], BF16) for i in range(2)]
    rgo = [K.sb('rgo%d' % i, [128, 512], F32) for i in range(3)]
    ptm = [K.ps('p1tm%d' % i, [128, 512], F32) for i in range(3)]
    pfm = [K.ps('p1fm%d' % i, [128, 512], F32) for i in range(2)]
    ptr = K.ps('p1tr', [128, 1024], BF16)
    nfm = 0

    def rope(eng, src, nh, dst, t):
        cosb = cs[:, t:t + 1, 0:8].to_broadcast([128, nh, 8])
        sinb = cs[:, t:t + 1, 8:16].to_broadcast([128, nh, 8])
        x1, x2 = src[:, :, 0:8], src[:, :, 8:16]
        a, b = tA[:, 0:nh, :], tB[:, 0:nh, :]
        return [
            lambda e: e.tensor_tensor(out=a, in0=x1, in1=cosb, op=ALU.mult),
            lambda e: e.tensor_tensor(out=b, in0=x2, in1=sinb, op=ALU.mult),
            lambda e: e.tensor_tensor(out=dst[:, :, 0:8], in0=a, in1=b, op=ALU.subtract),
            lambda e: e.tensor_tensor(out=a, in0=x2, in1=cosb, op=ALU.mult),
            lambda e: e.tensor_tensor(out=b, in0=x1, in1=sinb, op=ALU.mult),
            lambda e: e.tensor_tensor(out=dst[:, :, 8:16], in0=a, in1=b, op=ALU.add),
        ]

    toks = [NT.prep(d['h'][u * 128:(u + 1) * 128, :]) for u in range(4)]
    pending = [None]
    for g in range(NG):
        hnT = NT.hnT[g % 2]
        for j in range(4):
            t = g * 4 + j
            rows = slice(t * 128, (t + 1) * 128)
            NT.finish(toks.pop(0), hnT, j)
            if j == 0:
                for u in range(t + 4, min(t + 8, NG * 4)):
                    toks.append(NT.prep(d['h'][u * 128:(u + 1) * 128, :]))
            yq_, yb_, yc_, g_, qb_, kb_, vb_, qT_, kT_ = (yq[t % 2], yb[t % 2], yc[t % 2], gt_[t % 2], qb[t % 2], kb[t % 2], vb[t % 2],
                                                         qTs[t % 2], kTs[t % 2])
            for bi, (c0, c1, dst) in enumerate(((0, 512, yq_), (512, 1024, yb_), (1024, 1304, yc_))):
                p = ptm[bi]
                for k in range(8):
                    K.mm(p, p[:, 0:c1 - c0], hnT, hnT[:, k, j * 128:(j + 1) * 128], W, W[:, k, c0:c1], start=(k == 0), stop=(k == 7))
                K.op('act', lambda e, p=p, dst=dst, n=c1 - c0: e.copy(out=dst[:, 0:n], in_=p[:, 0:n]), reads=[p], writes=[dst])
            K.op('act', lambda e: e.activation(out=g_[:], in_=yc_[:, 256:280], func=AF.Sigmoid), reads=[yc_], writes=[g_])
            K.dma('pool', d['gates'][rows, :], g_[:], reads=[g_])
            K.op('dve', lambda e: e.tensor_copy(out=qb_[:], in_=yq_[:]), reads=[yq_], writes=[qb_])
            K.op('dve', lambda e: e.tensor_copy(out=kb_[:, 0:2, :], in_=yb_[:, 0:256].rearrange("p (a c) -> p a c", a=2)), reads=[yb_], writes=[kb_])
            K.op('dve', lambda e: e.tensor_copy(out=kb_[:, 2, :], in_=yb_[:, 256:384]), reads=[yb_], writes=[kb_])
            K.op('dve', lambda e: e.tensor_copy(out=kb_[:, 3, :], in_=yc_[:, 0:128]), reads=[yc_], writes=[kb_])
            K.op('pool', lambda e: e.tensor_copy(out=vb_[:, 0, :], in_=yb_[:, 384:512]), reads=[yb_], writes=[vb_])
            K.op('pool', lambda e: e.tensor_copy(out=vb_[:, 1, :], in_=yc_[:, 128:256]), reads=[yc_], writes=[vb_])
            K.dma('pool', d['Vs'][rows, :], vb_[:, 0, :], reads=[vb_])
            K.dma('pool', d['Vw'][rows, :], vb_[:, 1, :], reads=[vb_])
            jobs = [(yq_, yq_[:].rearrange("p (h c) -> p h c", c=64), 8, qb_, qb_[:].rearrange("p (h c) -> p h c", c=64)),
                    (yb_, yb_[:, 0:128].rearrange("p (h c) -> p h c", c=64), 2, kb_, kb_[:, 0, :].rearrange("p (h c) -> p h c", c=64)),
                    (yb_, yb_[:, 256:384].rearrange("p (h c) -> p h c", c=64), 2, kb_, kb_[:, 2, :].rearrange("p (h c) -> p h c", c=64)),
                    (yc_, yc_[:, 0:128].rearrange("p (h c) -> p h c", c=64), 2, kb_, kb_[:, 3, :].rearrange("p (h c) -> p h c", c=64))]
            for sb_, sap, nh, db_, dap in jobs:
                fns = rope('dve', sap, nh, dap, t)
                rw = [([sb_, cs], [tA]), ([sb_, cs], [tB]), ([tA, tB], [db_]), ([sb_, cs], [tA]), ([sb_, cs], [tB]), ([tA, tB], [db_])]
                for fn, (rd, wr) in zip(fns, rw):
                    K.op('dve', fn, reads=rd, writes=wr)
            def post(qb_=qb_, kb_=kb_, qT_=qT_, kT_=kT_, rows=rows):
                for c in range(4):
                    K.tr(ptr, ptr[:, c * 128:(c + 1) * 128], qb_, qb_[:, c * 128:(c + 1) * 128], identb, identb[:])
                for c in range(4):
                    K.tr(ptr, ptr[:, 512 + c * 128:512 + (c + 1) * 128], kb_, kb_[:, c, :], identb, identb[:])
                K.op('act', lambda e: e.copy(out=qT_[:], in_=ptr[:, 0:512].rearrange("p (c t) -> p c t", c=4)), reads=[ptr], writes=[qT_])
                K.op('act', lambda e: e.copy(out=kT_[:], in_=ptr[:, 512:1024].rearrange("p (c t) -> p c t", c=4)), reads=[ptr], writes=[kT_])
                K.dma('sp', d['QT1'][:, rows].rearrange("(c p) t -> p c t", p=128), qT_[:], reads=[qT_])
                for c, nm in enumerate(('KcT', 'VcT', 'KsT', 'KwT')):
                    K.dma('sp', d[nm][:, rows], kT_[:, c, :], reads=[kT_])
            if pending[0] is not None:
                pending[0]()
            pending[0] = post
        for c in range(8):
            p = pfm[nfm % 2]
            o = rgo[nfm % 3]
            nfm += 1
            for k in range(8):
                K.mm(p, p[:], W, W[:, k, O_RG + c * 128:O_RG + (c + 1) * 128], hnT, hnT[:, k, :], start=(k == 0), stop=(k == 7))
            K.op('act', lambda e, o=o, p=p: e.copy(out=o[:], in_=p[:]), reads=[p], writes=[o])
            K.dma('pool', d['RGX'][c * 128:(c + 1) * 128, g * 512:(g + 1) * 512], o[:], reads=[o])
    pending[0]()
    K.phase_end()


GELU_C = 1.5957691216057308


def gelu_tanh(K, xs, tmp, out_ap, out_b, eng2='pool'):
    K.op(eng2, lambda e: e.tensor_tensor(out=tmp[:], in0=xs[:], in1=xs[:], op=ALU.mult), reads=[xs], writes=[tmp])
    K.op('dve', lambda e: e.tensor_scalar(out=tmp[:], in0=tmp[:], scalar1=0.044715, scalar2=1.0, op0=ALU.mult, op1=ALU.add),
         reads=[tmp], writes=[tmp])
    K.op('dve', lambda e: e.tensor_tensor(out=tmp[:], in0=tmp[:], in1=xs[:], op=ALU.mult), reads=[tmp, xs], writes=[tmp])
    K.op('act', lambda e: e.activation(out=tmp[:], in_=tmp[:], func=AF.Sigmoid, scale=GELU_C), reads=[tmp], writes=[tmp])
    K.op('dve', lambda e: e.tensor_tensor(out=out_ap, in0=tmp[:], in1=xs[:], op=ALU.mult), reads=[tmp, xs], writes=[out_b])


def phase_cmp(K, d, S):
    NCP = S // 16
    NCMP = NCP - 1
    K.phase_begin()
    consts(K)
    XT = K.sb('XT', [128, S], BF16)
    w1s = K.sb('w1s', [128, 32, 128], F32)
    W1 = K.sb('W1c', [128, 32, 128], BF16)
    pes = K.sb('pes', [128, 32], F32)
    peT = K.sb('peT', [128, 32], BF16)
    w2s = K.sb('w2s', [128, 64], F32)
    W2 = K.sb('W2c', [128, 64], BF16)
    cvec = K.sb('cvec', [128, 1], F32)
    xs = K.sb('cxs', [128, NCP], F32)
    tmp = K.sb('ctmp', [128, NCP], F32)
    hidT = K.sb('hidT', [128, NCP], BF16)
    ko = K.sb('ko', [64, NCP], BF16)
    vo = K.sb('vo', [128, 64], BF16)
    ph = K.ps('cph', [128, 512], F32)
    pc = K.ps('cpc', [128, 512], F32)
    po = K.ps('cpo', [128, 512], F32)
    for which, (src, pe, w1, w2) in enumerate((('KcT', 'odd_cmp_k_pe', 'odd_cmp_k_w1', 'odd_cmp_k_w2'),
                                                ('VcT', 'odd_cmp_v_pe', 'odd_cmp_v_w1', 'odd_cmp_v_w2'))):
        K.dma('sp', XT[:], d[src][:, :], writes=[XT])
        for hf in range(2):
            K.dma('sp', w1s[hf * 64:(hf + 1) * 64, :, :], d[w1].rearrange("(l c) h -> c l h", c=64), writes=[w1s], nowaw=[w1s])
            K.dma('sp', pes[hf * 64:(hf + 1) * 64, :], d[pe].rearrange("l c -> c l"), writes=[pes], nowaw=[pes],
                  allow_slow_non_contiguous=True)
        K.dma('sp', w2s[:], d[w2][:, :], writes=[w2s])
        K.op('dve', lambda e: e.tensor_copy(out=W1[:], in_=w1s[:]), reads=[w1s], writes=[W1])
        K.op('dve', lambda e: e.tensor_copy(out=peT[:], in_=pes[:]), reads=[pes], writes=[peT])
        K.op('dve', lambda e: e.tensor_copy(out=W2[:], in_=w2s[:]), reads=[w2s], writes=[W2])
        for l in range(32):
            K.mm(pc, pc[:, 0:1], W1, W1[0:64, l, :], peT, peT[0:64, l:l + 1], start=(l == 0), stop=(l == 31))
        K.op('act', lambda e: e.copy(out=cvec[:], in_=pc[:, 0:1]), reads=[pc], writes=[cvec])
        X3 = XT[:].rearrange("p (n s) -> p n s", s=16)
        for g in range(2):
            gs = slice(g * 64, (g + 1) * 64)
            for l in range(32):
                rhs = X3[gs, 0:NCMP, l] if l < 16 else X3[gs, 1:NCP, l - 16]
                K.mm(ph, ph[:, 0:NCMP], W1, W1[gs, l, :], XT, rhs, start=(l == 0), stop=(l == 31))
            K.op('pool', lambda e: e.memset(xs[:], 0.0), writes=[xs])
            K.op('act', lambda e: e.activation(out=xs[:, 0:NCMP], in_=ph[:, 0:NCMP], func=AF.Identity, bias=cvec[:, 0:1]),
                 reads=[ph, cvec], writes=[xs])
            gelu_tanh(K, xs, tmp, hidT[:], hidT)
            if which == 0:
                for c0 in range(0, NCP, 512):
                    n = min(512, NCP - c0)
                    K.mm(po, po[0:64, 0:n], W2, W2[:, :], hidT, hidT[:, c0:c0 + n])
                    K.op('act', lambda e, c0=c0, n=n: e.copy(out=ko[:, c0:c0 + n], in_=po[0:64, 0:n]), reads=[po], writes=[ko])
                K.dma('sp', d['KcmpT'][gs, :], ko[:], reads=[ko])
            else:
                for c0 in range(0, NCP, 128):
                    K.mm(po, po[:, 0:64], hidT, hidT[:, c0:c0 + 128], W2, W2[:, :])
                    K.op('act', lambda e: e.copy(out=vo[:], in_=po[:, 0:64]), reads=[po], writes=[vo])
                    K.dma('sp', d['Vcmp'][c0:c0 + 128, gs], vo[:], reads=[vo])
    K.phase_end()


def attn_core(K, S, heads, B, load_head, units_fn, KR, s_extra, exp_bias, emit_masks, pv_extra, finalize, L=3):
    NQ = S // 512
    units = []
    for hi, h in enumerate(heads):
        for qg in range(NQ):
            us = units_fn(qg)
            for ui, u in enumerate(us):
                units.append((hi, h, qg, u, ui == 0, ui == len(us) - 1))
    load_head(0, heads[0])
    NU = len(units)
    deferred = []
    for i in range(NU + L):
        if deferred and deferred[0][0] <= i:
            deferred.pop(0)[1]()
        if i < NU:
            hi, h, qg, (kt, c0, c1), first, last = units[i]
            s = hi % 2
            p_s = B['ps'][i % len(B['ps'])]
            p_t = B['pt'][i % 4]
            KT = B['KT'][s]
            QT = B['QTsel'](s, kt) if 'QTsel' in B else B['QT'][s]
            ex = s_extra(h, qg, kt, c0, c1) if s_extra else None
            K.mm(p_s, p_s[:, c0:c1], KT, KT[0:KR, kt * 128:(kt + 1) * 128], QT, QT[0:KR, qg * 512 + c0:qg * 512 + c1],
                 start=True, stop=(ex is None))
            if ex is not None:
                K.mm(p_s, p_s[:, c0:c1], ex[0], ex[1], ex[2], ex[3], start=False, stop=True)
            bb, bap = exp_bias(h, kt) if exp_bias else (K.zero_t, K.zero_t[:, 0:1])
            K.op('act', lambda e, p_s=p_s, p_t=p_t, c0=c0, c1=c1, bap=bap: e.activation(
                out=p_t[:, c0:c1], in_=p_s[:, c0:c1], func=AF.Exp, scale=0.125, bias=bap), reads=[p_s, bb], writes=[p_t])
            emit_masks(p_t, h, qg, kt, c0, c1)
        if i - L >= 0:
            hi, h, qg, (kt, c0, c1), first, last = units[i - L]
            if first and qg == 0 and hi + 1 < len(heads):
                load_head(hi + 1, heads[hi + 1])
            s = hi % 2
            p_t = B['pt'][(i - L) % 4]
            gi = hi * NQ + qg
            p_o = B['po'][gi % 2]
            V = B['V'][s]
            K.mm(p_o, p_o[0:65, c0:c1], V, V[:, kt, 0:65], p_t, p_t[:, c0:c1], start=first, stop=last)
            if pv_extra:
                pv_extra(p_t, hi, h, qg, kt, c0, c1, first, last)
            if last:
                rest = finalize(hi, h, qg, gi, p_o)
                if rest is not None:
                    deferred.append((i + 2, rest))
    for _, rest in deferred:
        rest()


def causal_sel(K, p_t, c):
    K.op('pool', lambda e: e.affine_select(out=p_t[:, c:c + 128], in_=p_t[:, c:c + 128], pattern=[[1, 128]], compare_op=ALU.is_ge,
                                           fill=0.0, base=0, channel_multiplier=-1), reads=[p_t], writes=[p_t])


class NsaCommon:
    def __init__(self, K, d, S, nkt, br, out_name, n_ps=3):
        self.K, self.d, self.S, self.br, self.out_name = K, d, S, br, out_name
        NT_ = S // 128
        consts(K)
        K.zero_t = K.sb('zero', [128, 1], F32)
        K.op('pool', lambda e: e.memset(K.zero_t[:], 0.0), writes=[K.zero_t])
        self.identf = make_ident(K, F32, 'identf')
        self.gates = K.sb('gates', [128, NT_, 24], F32)
        K.dma('sp', self.gates[:], d['gates'].rearrange("(t p) c -> p t c", p=128), writes=[self.gates])
        self.B = dict(
            KT=[K.sb('nKT%d' % i, [128, nkt * 128], BF16) for i in range(2)],
            QT=[K.sb('nQT%d' % i, [128, S], BF16) for i in range(2)],
            V=[K.sb('nV%d' % i, [128, nkt, 65], BF16) for i in range(2)],
            ps=[K.ps('nps%d' % i, [128, 512], F32) for i in range(n_ps)],
            po=[K.ps('npo%d' % i, [128, 512], F32) for i in range(2)],
            pt=[K.sb('npt%d' % i, [128, 512], BF16) for i in range(4)],
        )
        for i in range(2):
            K.op('pool', lambda e, i=i: e.memset(self.B['V'][i][:, :, 64:65], 1.0), writes=[self.B['V'][i]])
            K.op('pool', lambda e, i=i: e.memset(self.B['KT'][i][64:128, :], 0.0), writes=[self.B['KT'][i]])
            K.op('pool', lambda e, i=i: e.memset(self.B['QT'][i][64:128, :], 0.0), writes=[self.B['QT'][i]])
        self.pq = K.ps('npq', [128, 4, 65], F32)
        self.osb = [K.sb('nosb%d' % i, [65, 512], F32) for i in range(2)]
        self.rc = [K.sb('nrc%d' % i, [128, 4], F32) for i in range(2)]
        self.mo = [K.sb('nmo%d' % i, [128, 4, 64], F32) for i in range(2)]

    def load_head(self, ksrc, vsrc, nk):
        K, d, B = self.K, self.d, self.B

        def f(hi, h):
            s, g = hi % 2, h // 4
            K.dma('sp', B['KT'][s][0:64, 0:nk], d[ksrc][g * 64:(g + 1) * 64, 0:nk], writes=[B['KT'][s]])
            K.dma('sp', B['QT'][s][0:64, :], d['QT1'][h * 64:(h + 1) * 64, :], writes=[B['QT'][s]])
            K.dma('sp', B['V'][s][:, :, 0:64], d[vsrc][0:nk, g * 64:(g + 1) * 64].rearrange("(t p) c -> p t c", p=128),
                  writes=[B['V'][s]])
        return f

    def finalize(self, hi, h, qg, gi, p_o, pre=None):
        K, d = self.K, self.d
        o_s, r_c, m_o, p_q = self.osb[gi % 2], self.rc[gi % 2], self.mo[gi % 2], self.pq
        K.op('act', lambda e: e.copy(out=o_s[:], in_=p_o[0:65, :]), reads=[p_o], writes=[o_s])
        if pre:
            pre(o_s)

        def rest():
            for j in range(4):
                K.tr(p_q, p_q[:, j, :], o_s, o_s[0:65, j * 128:(j + 1) * 128], self.identf, self.identf[0:65, 0:65])
            K.op('dve', lambda e: e.tensor_scalar(out=r_c[:], in0=p_q[:, :, 64], scalar1=1e-30, scalar2=None, op0=ALU.add),
                 reads=[p_q], writes=[r_c])
            K.op('dve', lambda e: e.reciprocal(out=r_c[:], in_=r_c[:]), reads=[r_c], writes=[r_c])
            col = h * 3 + self.br
            K.op('dve', lambda e: e.tensor_tensor(out=r_c[:], in0=r_c[:], in1=self.gates[:, qg * 4:(qg + 1) * 4, col], op=ALU.mult),
                 reads=[r_c, self.gates], writes=[r_c])
            for j in range(4):
                K.op('dve', lambda e, j=j: e.tensor_scalar(out=m_o[:, j, :], in0=p_q[:, j, 0:64], scalar1=r_c[:, j:j + 1], scalar2=None,
                                                           op0=ALU.mult), reads=[p_q, r_c], writes=[m_o])
            K.dma('pool', d[self.out_name][qg * 512:(qg + 1) * 512, h * 64:(h + 1) * 64].rearrange("(j p) c -> p j c", p=128),
                  m_o[:], reads=[m_o])
        return rest


def phase_nsa_win(K, d, S, heads=range(8)):
    K.phase_begin()
    C = NsaCommon(K, d, S, S // 128, 2, 'nsa2')

    def units_fn(qg):
        us = []
        for kt in range(max(0, 4 * qg - 4), 4 * qg + 4):
            i0 = max(kt, 4 * qg)
            i1 = min(kt + 4, 4 * qg + 3)
            us.append((kt, (i0 - 4 * qg) * 128, (i1 - 4 * qg + 1) * 128))
        return us

    def masks(p_t, h, qg, kt, c0, c1):
        if kt >= 4 * qg:
            causal_sel(K, p_t, (kt - 4 * qg) * 128)
        if kt + 4 <= 4 * qg + 3 and kt + 4 >= 4 * qg:
            c = (kt + 4 - 4 * qg) * 128
            K.op('pool', lambda e: e.affine_select(out=p_t[:, c:c + 128], in_=p_t[:, c:c + 128], pattern=[[-1, 128]], compare_op=ALU.is_gt,
                                                   fill=0.0, base=0, channel_multiplier=1), reads=[p_t], writes=[p_t])

    attn_core(K, S, list(heads), C.B, C.load_head('KwT', 'Vw', S), units_fn, 128, None, None, masks, None, C.finalize)
    K.phase_end()


def phase_nsa_slc(K, d, S, heads=range(8)):
    NT_ = S // 128
    NA = max(1, S // 4096)
    K.phase_begin()
    C = NsaCommon(K, d, S, NT_, 1, 'nsa1')
    QT2 = [[C.B['QT'][i] for i in range(2)]] + [[K.sb('nQTa%d_%d' % (a, i), [128, S], BF16) for i in range(2)] for a in range(1, NA)]
    for i in range(2):
        kt_ = C.B['KT'][i]
        K.op('pool', lambda e, kt_=kt_: e.memset(kt_[64:128, :], 262144.0), writes=[kt_])
        MB = min(64, S // 64)
        K.op('pool', lambda e, kt_=kt_: e.affine_select(
            out=kt_[64:128, :].rearrange("p (a m k) -> p a m k", m=MB, k=64), in_=kt_[64:128, :].rearrange("p (a m k) -> p a m k", m=MB, k=64),
            pattern=[[0, S // (MB * 64)], [1, MB], [0, 64]], compare_op=ALU.is_equal, fill=0.0, base=0,
            channel_multiplier=-1), reads=[kt_], writes=[kt_])
    C.B['QTsel'] = lambda s_, kt: QT2[kt // 32][s_]

    def load_head(hi, h):
        s_, g = hi % 2, h // 4
        B = C.B
        K.dma('sp', B['KT'][s_][0:64, :], d['KsT'][g * 64:(g + 1) * 64, :], writes=[B['KT'][s_]], nowaw=[B['KT'][s_]])
        for a in range(NA):
            qt = QT2[a][s_]
            K.dma('sp', qt[0:64, :], d['QT1'][h * 64:(h + 1) * 64, :], writes=[qt])
            K.dma('sp', qt[64:128, :], d['selT'][g, 64 * a:64 * a + 64, :], writes=[qt], nowaw=[qt])
        K.dma('sp', B['V'][s_][:, :, 0:64], d['Vs'][:, g * 64:(g + 1) * 64].rearrange("(t p) c -> p t c", p=128), writes=[B['V'][s_]])

    def units_fn(qg):
        return [(kt, 128 * max(kt - 4 * qg, 0), 512) for kt in range(4 * qg + 4)]

    def masks(p_t, h, qg, kt, c0, c1):
        if kt >= 4 * qg:
            causal_sel(K, p_t, (kt - 4 * qg) * 128)

    attn_core(K, S, list(heads), C.B, load_head, units_fn, 128, None, None, masks, None, C.finalize)
    K.phase_end()


def phase_nsa_cmp(K, d, S, heads=range(8)):
    NT_ = S // 128
    NQ = S // 512
    NCP = S // 16
    NKT = (NCP + 127) // 128
    NKP = NKT * 128
    K.phase_begin()
    C = NsaCommon(K, d, S, NKT, 0, 'nsa0', n_ps=2)
    identf = C.identf
    identb = make_ident(K, BF16, 'identb')
    onesf = K.sb('onesf', [128, 128], F32)
    K.op('pool', lambda e: e.memset(onesf[:], 1.0), writes=[onesf])
    ovf = K.sb('ovf', [128, 128], F32)
    ovf2 = K.sb('ovf2', [128, 128], F32)
    ovl = K.sb('ovl', [128, NKT, 128], BF16)
    for kt in range(NKT):
        K.op('pool', lambda e, kt=kt: e.affine_select(out=ovf[:], in_=onesf[:], pattern=[[64, 128]], compare_op=ALU.is_ge, fill=0.0,
                                                       base=63 - 2048 * kt, channel_multiplier=-16), reads=[onesf], writes=[ovf])
        K.op('pool', lambda e, kt=kt: e.affine_select(out=ovf2[:], in_=ovf[:], pattern=[[-64, 128]], compare_op=ALU.is_ge, fill=0.0,
                                                       base=2048 * kt + 31, channel_multiplier=16), reads=[ovf], writes=[ovf2])
        K.op('pool', lambda e, kt=kt: e.tensor_copy(out=ovl[:, kt, :], in_=ovf2[:]), reads=[ovf2], writes=[ovl], nowaw=[ovl])
    cst = K.sb('cst', [128, 8], F32)
    K.op('pool', lambda e: e.memset(cst[:, 0:1], 1e30), writes=[cst])
    K.op('pool', lambda e: e.memset(cst[:, 1:2], 2e30), writes=[cst])
    K.op('pool', lambda e: e.memset(cst[0:64, 2:3], 3e30), writes=[cst])
    K.op('pool', lambda e: e.memset(cst[64:128, 2:3], 0.0), writes=[cst])
    K.op('pool', lambda e: e.memset(cst[0:64, 3:4], 0.0), writes=[cst])
    K.op('pool', lambda e: e.memset(cst[64:128, 3:4], 1.0), writes=[cst])
    K.op('pool', lambda e: e.memset(cst[0:64, 4:5], -1e30), writes=[cst])
    K.op('pool', lambda e: e.memset(cst[64:128, 4:5], 4e30), writes=[cst])
    po2 = K.ps('npo2', [128, 512], F32)
    pb = K.ps('npb', [128, 512], F32)
    impT = K.sb('impT', [128, S], F32)
    rrow = K.sb('rrow', [65, 512], F32)
    rbs = K.sb('rbs', [128, 512], F32)
    tmpi = K.sb('tmpi', [128, 512], F32)
    work = K.sb('work', [128, 128], F32)
    work2 = K.sb('work2', [128, 128], F32)
    m8 = K.sb('m8', [128, 16], F32)
    selm = K.sb('selm', [128, 128], F32)
    selb = K.sb('selb', [128, 128], BF16)
    sTs = [K.sb('sTs%d' % i, [128, 128], BF16) for i in range(2)]

    def units_fn(qg):
        kmax = min(NKT - 1, (32 * qg + 30) // 128)
        return [(kt, 0, 512) for kt in range(kmax + 1)]

    def masks(p_t, h, qg, kt, c0, c1):
        K.op('pool', lambda e: e.affine_select(out=p_t[:], in_=p_t[:], pattern=[[1, 512]], compare_op=ALU.is_ge, fill=0.0,
                                               base=512 * qg - 2048 * kt - 31, channel_multiplier=-16), reads=[p_t], writes=[p_t])

    def pv_extra(p_t, hi, h, qg, kt, c0, c1, first, last):
        K.mm(po2, po2[:], ovl, ovl[:, kt, :], p_t, p_t[:], start=first, stop=last)

    def topk_pass(g):
        for i in range(NT_):
            W = 2 * i + 2
            cols = slice(i * 128, (i + 1) * 128)
            K.op('pool', lambda e: e.memset(selm[:], 0.0), writes=[selm])
            if W <= 16:
                K.op('pool', lambda e, W=W: e.memset(selm[:, 0:W], 1.0), writes=[selm])
            else:
                K.tr(pb, pb[:, 0:128], impT, impT[:, cols], identf, identf[:])
                K.op('act', lambda e, W=W: e.copy(out=work[:, 0:W], in_=pb[:, 0:W]), reads=[pb], writes=[work])
                K.op('dve', lambda e: e.tensor_copy(out=work[:, 0:1], in_=cst[:, 0:1]), reads=[cst], writes=[work])
                K.op('dve', lambda e, i=i: e.tensor_copy(out=work[:, 2 * i:2 * i + 1], in_=cst[:, 1:2]), reads=[cst], writes=[work])
                K.op('dve', lambda e, i=i: e.tensor_scalar(out=work[:, 2 * i - 1:2 * i], in0=work[:, 2 * i - 1:2 * i], scalar1=cst[:, 3:4],
                                                           scalar2=cst[:, 2:3], op0=ALU.mult, op1=ALU.add), reads=[work, cst], writes=[work])
                K.op('dve', lambda e, i=i: e.tensor_copy(out=work[:, 2 * i + 1:2 * i + 2], in_=cst[:, 4:5]), reads=[cst], writes=[work])
                K.op('dve', lambda e, W=W: e.max(out=m8[:, 0:8], in_=work[:, 0:W]), reads=[work], writes=[m8])
                K.op('dve', lambda e, W=W: e.match_replace(out=work2[:, 0:W], in_to_replace=m8[:, 0:8], in_values=work[:, 0:W],
                                                           imm_value=-3.0e38), reads=[work, m8], writes=[work2])
                K.op('dve', lambda e, W=W: e.max(out=m8[:, 8:16], in_=work2[:, 0:W]), reads=[work2], writes=[m8])
                K.op('dve', lambda e, W=W: e.tensor_scalar(out=selm[:, 0:W], in0=work[:, 0:W], scalar1=m8[:, 15:16], scalar2=None,
                                                           op0=ALU.is_ge), reads=[work, m8], writes=[selm])
            K.op('dve', lambda e: e.tensor_scalar(out=selb[:], in0=selm[:], scalar1=-1.0, scalar2=None, op0=ALU.add), reads=[selm], writes=[selb])
            pbb = pbT
            K.tr(pbb, pbb[:, 0:128], selb, selb[:], identb, identb[:])
            sT = sTs[i % 2]
            K.op('act', lambda e, sT=sT: e.copy(out=sT[:], in_=pbb[:, 0:128]), reads=[pbb], writes=[sT])
            K.dma('pool', d['selT'][g, :, cols], sT[:], reads=[sT])

    pbT = K.ps('npbT', [128, 1024], BF16)

    def finalize(hi, h, qg, gi, p_o):
        cs_ = slice(qg * 512, (qg + 1) * 512)

        def pre(o_s):
            K.op('dve', lambda e: e.tensor_scalar(out=rrow[64:65, :], in0=o_s[64:65, :], scalar1=1e-30, scalar2=None, op0=ALU.add),
                 reads=[o_s], writes=[rrow])
            K.op('dve', lambda e: e.reciprocal(out=rrow[64:65, :], in_=rrow[64:65, :]), reads=[rrow], writes=[rrow])
            K.mm(pb, pb[:], onesf, onesf[64:65, :], rrow, rrow[64:65, :])
            K.op('act', lambda e: e.copy(out=rbs[:], in_=pb[:]), reads=[pb], writes=[rbs])
            if h % 4 == 0:
                K.op('dve', lambda e: e.tensor_tensor(out=impT[:, cs_], in0=po2[:], in1=rbs[:], op=ALU.mult), reads=[po2, rbs], writes=[impT])
            else:
                K.op('dve', lambda e: e.tensor_tensor(out=tmpi[:], in0=po2[:], in1=rbs[:], op=ALU.mult), reads=[po2, rbs], writes=[tmpi])
                K.op('pool', lambda e: e.tensor_tensor(out=impT[:, cs_], in0=impT[:, cs_], in1=tmpi[:], op=ALU.add), reads=[impT, tmpi], writes=[impT])
        rest0 = C.finalize(hi, h, qg, gi, p_o, pre=pre)

        def rest():
            rest0()
            if h % 4 == 3 and qg == NQ - 1:
                topk_pass(h // 4)
        return rest

    attn_core(K, S, list(heads), C.B, C.load_head('KcmpT', 'Vcmp', NCP), units_fn, 128, None, None, masks, pv_extra, finalize)
    K.phase_end()


def phase_nsa_combine(K, d, S):
    NT_ = S // 128
    K.phase_begin()
    a = [K.sb('ca%d' % i, [128, 512], F32) for i in range(2)]
    b = [K.sb('cb%d' % i, [128, 512], F32) for i in range(2)]
    c = [K.sb('cc%d' % i, [128, 512], F32) for i in range(2)]
    o = [K.sb('co%d' % i, [128, 512], BF16) for i in range(2)]
    for t in range(NT_):
        rows = slice(t * 128, (t + 1) * 128)
        a_, b_, c_, o_ = a[t % 2], b[t % 2], c[t % 2], o[t % 2]
        K.dma('sp', a_[:], d['nsa0'][rows, :], writes=[a_])
        K.dma('sp', b_[:], d['nsa1'][rows, :], writes=[b_])
        K.dma('sp', c_[:], d['nsa2'][rows, :], writes=[c_])
        K.op('dve', lambda e: e.tensor_tensor(out=a_[:], in0=a_[:], in1=b_[:], op=ALU.add), reads=[a_, b_], writes=[a_])
        K.op('dve', lambda e: e.tensor_tensor(out=o_[:], in0=a_[:], in1=c_[:], op=ALU.add), reads=[a_, c_], writes=[o_])
        K.dma('pool', d['mixed'][rows, 0:512], o_[:], reads=[o_])
    K.phase_end()


def phase_lru(K, d, S):
    SEG = min(S, 2048)
    NS = S // SEG
    K.phase_begin()
    consts(K)
    identb = make_ident(K, BF16, 'identb')
    prm = K.sb('lprm', [128, 12, 4], F32)
    for j in range(4):
        K.dma('sp', prm[:, j, :], d['odd_rg_conv_w'][j, :].rearrange("(c p) -> p c", p=128), writes=[prm], nowaw=[prm],
              allow_slow_non_contiguous=True)
    for idx, nm in ((4, 'odd_rg_conv_b'), (5, 'odd_rg_ba'), (6, 'odd_rg_bx'), (7, 'odd_rg_lambda')):
        K.dma('sp', prm[:, idx, :], d[nm].rearrange("(c p) -> p c", p=128), writes=[prm], nowaw=[prm], allow_slow_non_contiguous=True)
    K.op('act', lambda e: e.activation(out=prm[:, 8, :], in_=prm[:, 7, :], func=AF.Exp, scale=-1.0), reads=[prm], writes=[prm])
    K.op('act', lambda e: e.activation(out=prm[:, 8, :], in_=prm[:, 8, :], func=AF.Ln, bias=K.one_t[:, 0:1]), reads=[prm, K.one_t], writes=[prm])
    K.op('dve', lambda e: e.tensor_scalar(out=prm[:, 8, :], in0=prm[:, 8, :], scalar1=-8.0, scalar2=None, op0=ALU.mult), reads=[prm], writes=[prm])
    wst = K.sb('lwst', [128, 128], F32)
    WA = K.sb('WA', [128, 4, 128], BF16)
    WX = K.sb('WX', [128, 4, 128], BF16)
    for Wt, nm in ((WA, 'odd_rg_wa'), (WX, 'odd_rg_wx')):
        for c in range(4):
            K.op('pool', lambda e: e.memset(wst[:], 0.0), writes=[wst])
            K.dma('sp', wst[0:64, 0:64], d[nm][2 * c, :, :], writes=[wst])
            K.dma('sp', wst[64:128, 64:128], d[nm][2 * c + 1, :, :], writes=[wst])
            K.op('dve', lambda e, Wt=Wt, c=c: e.tensor_copy(out=Wt[:, c, :], in_=wst[:]), reads=[wst], writes=[Wt], nowaw=[Wt])
    names = [('xpad', SEG + 3, F32), ('x', SEG, F32), ('xbf', SEG, BF16), ('r', SEG, F32), ('ig', SEG, F32), ('a', SEG, F32), ('t', SEG, F32),
             ('hh', SEG, F32), ('rg', SEG, F32), ('tmp', SEG, F32), ('gg', SEG, F32), ('ybf', SEG, BF16)]
    TL = [{nm: K.sb('l%s%d' % (nm, i), [128, w], dt) for nm, w, dt in names} for i in range(2)]
    hc = K.sb('lhc', [128, 1], F32)
    yo = [K.sb('lyo%d' % i, [128, 4, 128], BF16) for i in range(2)]
    pr = [K.ps('lpr%d' % i, [128, 512], F32) for i in range(2)]
    pi = [K.ps('lpi%d' % i, [128, 512], F32) for i in range(2)]
    ptr = [K.ps('lptr%d' % i, [128, 1024], BF16) for i in range(2)]
    nt = 0
    it = 0
    for c in range(4):
        K.op('pool', lambda e: e.memset(hc[:], 0.0), writes=[hc])
        for s in range(NS):
            cols = slice(s * SEG, (s + 1) * SEG)
            T_ = TL[it % 2]
            Tn = TL[(it + 1) % 2]
            it += 1
            xpad, x, xbf, r, ig, a, t, hh, rg, tmp, gg, ybf = (T_[k] for k in ('xpad', 'x', 'xbf', 'r', 'ig', 'a', 't', 'hh', 'rg', 'tmp', 'gg', 'ybf'))
            if s == 0:
                K.op('pool', lambda e, xpad=xpad: e.memset(xpad[:, 0:3], 0.0), writes=[xpad])
            K.dma('sp', xpad[:, 3:3 + SEG], d['RGX'][512 + c * 128:512 + (c + 1) * 128, cols], writes=[xpad], nowaw=[xpad])
            K.dma('sp', rg[:], d['RGX'][c * 128:(c + 1) * 128, cols], writes=[rg])
            K.op('dve', lambda e: e.tensor_scalar(out=x[:], in0=xpad[:, 0:SEG], scalar1=prm[:, 0, c:c + 1], scalar2=prm[:, 4, c:c + 1],
                                                  op0=ALU.mult, op1=ALU.add), reads=[xpad, prm], writes=[x])
            for j in range(1, 4):
                K.op('dve', lambda e, j=j: e.scalar_tensor_tensor(out=x[:], in0=xpad[:, j:j + SEG], scalar=prm[:, j, c:c + 1], in1=x[:],
                                                                   op0=ALU.mult, op1=ALU.add), reads=[xpad, prm, x], writes=[x])
            K.op('pool', lambda e, xpad=xpad, xnx=Tn['xpad']: e.tensor_copy(out=xnx[:, 0:3], in_=xpad[:, SEG:SEG + 3]), reads=[xpad], writes=[Tn['xpad']])
            K.op('pool', lambda e: e.tensor_copy(out=xbf[:], in_=x[:]), reads=[x], writes=[xbf])
            for pc in range(SEG // 512):
                ps_ = slice(pc * 512, (pc + 1) * 512)
                p1, p2 = pr[pc % 2], pi[pc % 2]
                K.mm(p1, p1[:], WA, WA[:, c, :], xbf, xbf[:, ps_])
                K.mm(p2, p2[:], WX, WX[:, c, :], xbf, xbf[:, ps_])
                K.op('act', lambda e, p1=p1, ps_=ps_: e.activation(out=r[:, ps_], in_=p1[:], func=AF.Sigmoid, bias=prm[:, 5, c:c + 1]),
                     reads=[p1, prm], writes=[r], nowaw=[r])
                K.op('act', lambda e, p2=p2, ps_=ps_: e.activation(out=ig[:, ps_], in_=p2[:], func=AF.Sigmoid, bias=prm[:, 6, c:c + 1]),
                     reads=[p2, prm], writes=[ig], nowaw=[ig])
            K.op('act', lambda e: e.activation(out=a[:], in_=r[:], func=AF.Exp, scale=prm[:, 8, c:c + 1]), reads=[r, prm], writes=[a])
            K.op('dve', lambda e: e.tensor_tensor(out=t[:], in0=a[:], in1=a[:], op=ALU.mult), reads=[a], writes=[t])
            K.op('dve', lambda e: e.tensor_scalar(out=t[:], in0=t[:], scalar1=-1.0, scalar2=1.0, op0=ALU.mult, op1=ALU.add), reads=[t], writes=[t])
            K.op('act', lambda e: e.activation(out=t[:], in_=t[:], func=AF.Sqrt), reads=[t], writes=[t])
            K.op('dve', lambda e: e.tensor_tensor(out=t[:], in0=t[:], in1=ig[:], op=ALU.mult), reads=[t, ig], writes=[t])
            K.op('dve', lambda e: e.tensor_tensor(out=t[:], in0=t[:], in1=x[:], op=ALU.mult), reads=[t, x], writes=[t])
            K.op('dve', lambda e: e.tensor_tensor_scan(out=hh[:], data0=a[:], data1=t[:], initial=hc[:, 0:1], op0=ALU.mult, op1=ALU.add),
                 reads=[a, t, hc], writes=[hh])
            K.op('pool', lambda e: e.tensor_copy(out=hc[:], in_=hh[:, SEG - 1:SEG]), reads=[hh], writes=[hc])
            gelu_tanh(K, rg, tmp, gg[:], gg)
            K.op('dve', lambda e: e.tensor_tensor(out=ybf[:], in0=hh[:], in1=gg[:], op=ALU.mult), reads=[hh, gg], writes=[ybf])
            for q4 in range(SEG // 512):
                p_ = ptr[nt % 2]
                y_ = yo[nt % 2]
                nt += 1
                for j in range(4):
                    tc_ = slice(q4 * 512 + j * 128, q4 * 512 + (j + 1) * 128)
                    K.tr(p_, p_[:, j * 128:(j + 1) * 128], ybf, ybf[:, tc_], identb, identb[:])
                K.op('act', lambda e, p_=p_, y_=y_: e.copy(out=y_[:], in_=p_[:, 0:512].rearrange("p (j c) -> p j c", j=4)), reads=[p_], writes=[y_])
                t0 = s * SEG + q4 * 512
                K.dma('pool', d['mixed'][t0:t0 + 512, 512 + c * 128:512 + (c + 1) * 128].rearrange("(j p) c -> p j c", p=128), y_[:],
                      reads=[y_])
    K.phase_end()


def all_phases():
    return [
        phase_A0, phase_Fprep, phase_fox, phase_Gprep, phase_gdn2,
        lambda K, d, S: phase_O(K, d, S, d['x'], d['even_w_out']),
        lambda K, d, S: phase_MLP(K, d, S, 0),
        lambda K, d, S: phase_PLE(K, d, S, 0),
        phase_Rprep, phase_A1, phase_cmp, phase_nsa_cmp, phase_nsa_slc, phase_nsa_win, phase_nsa_combine, phase_lru,
        lambda K, d, S: phase_O(K, d, S, d['h'], d['odd_w_out']),
        lambda K, d, S: phase_MLP(K, d, S, 1),
        lambda K, d, S: phase_PLE(K, d, S, 1, final=True),
    ]


SEQ = 8192
NCORES = 8


def kernel(**inputs):
    S = SEQ
    nc, K = build(S, all_phases())
    in_maps = []
    shared = {}
    for n, shp in INPUT_SHAPES.items():
        a = np.asarray(inputs[n])
        if list(a.shape) != shp:
            a = a.reshape(shp)
        shared[n] = np.ascontiguousarray(a.astype(np.float32, copy=False))
    x = np.asarray(inputs['x'])
    p = np.asarray(inputs['p'])
    pos = np.asarray(inputs['positions'])
    for b in range(NCORES):
        m = dict(shared)
        m['x'] = np.ascontiguousarray(x[b])
        m['p'] = np.ascontiguousarray(p[:, b])
        m['positions'] = np.ascontiguousarray(pos[b:b + 1]).astype(np.int32, copy=False)
        in_maps.append(m)
    res = run_bass_kernel_spmd(nc, in_maps, core_ids=list(range(NCORES)))
    out = np.stack([np.asarray(res.results[b]['out']) for b in range(NCORES)], axis=0)
    return out.astype(np.float32, copy=False)


def phase_gdn2(K, d, S, heads=range(4)):
    NC = S // 128
    NG = S // 512
    heads = list(heads)
    K.phase_begin()
    consts(K)
    identf = make_ident(K, F32, 'identf')
    onesf = K.sb('onesf', [128, 128], F32)
    K.op('pool', lambda e: e.memset(onesf[:], 1.0), writes=[onesf])
    maskT = K.sb('maskT', [128, 128], F32)
    maskTs = K.sb('maskTs', [128, 128], F32)
    zer = K.sb('zer', [128, 128], F32)
    K.op('pool', lambda e: e.memset(zer[:], 0.0), writes=[zer])
    K.op('pool', lambda e: e.affine_select(out=maskT[:], in_=zer[:], pattern=[[1, 128]], compare_op=ALU.is_ge, fill=NEGM,
                                           base=0, channel_multiplier=-1), reads=[zer], writes=[maskT])
    K.op('pool', lambda e: e.affine_select(out=maskTs[:], in_=zer[:], pattern=[[1, 128]], compare_op=ALU.is_gt, fill=NEGM,
                                           base=0, channel_multiplier=-1), reads=[zer], writes=[maskTs])
    nwb = K.sb('nwb', [128, 128], F32)
    K.dma('sp', nwb[:], d['even_gdn_norm_w'].partition_broadcast(128), writes=[nwb])
    tokS = K.sb('tokS', [128, NC, 16], F32)
    K.dma('sp', tokS[:].rearrange("p c q -> p (c q)"), d['tokSd'][:, :], writes=[tokS])
    c128 = K.sb('c128', [128, 1], F32)
    K.op('pool', lambda e: e.memset(c128[:], 128.0 * 1e-6), writes=[c128])
    onesb = K.sb('onesb', [128, 128], BF16)
    K.op('pool', lambda e: e.memset(onesb[:], 1.0), writes=[onesb])
    lnt = [K.sb('gln%d' % i, [128, 512], F32) for i in range(2)]
    ngl = [0]

    class H:
        pass
    HB = {}
    f32t = lambda nm, h: K.sb('%s_h%d' % (nm, h), [128, 128], F32)
    bf_t = lambda nm, h: K.sb('%s_h%d' % (nm, h), [128, 128], BF16)
    TB = {}
    for h in heads:
        for par in range(2):
            t = H()
            t.bk = K.ps('gbk%d_%d' % (h, par), [128, 512], F32)
            t.z = K.sb('gz_%d_%d' % (h, par), [128, 128], F32)
            t.nz = K.sb('nz_%d_%d' % (h, par), [128, 128], F32)
            for nm in ('Kbg', 'Kd', 'Vb', 'attnT', 'Xb', 'WT', 'vnew', 'ob'):
                setattr(t, nm, K.sb('%s_%d_%d' % (nm, h, par), [128, 128], BF16))
            for nm in ('dm1', 'dm2', 'Mt', 'U', 'osb', 'ojunk'):
                setattr(t, nm, K.sb('%s_%d_%d' % (nm, h, par), [128, 128], F32))
            t.P = [K.sb('P%d_%d_%d' % (i, h, par), [128, 128], F32) for i in range(2)]
            t.RX = K.sb('RX_%d_%d' % (h, par), [128, 256], F32)
            t.ost = K.sb('ost_%d_%d' % (h, par), [128, 4], F32)
            TB[(h, par)] = t
    for h in heads:
        b = H()
        b.gcrow = K.sb('gcrow%d' % h, [1, 512], F32)
        b.e1row = K.sb('e1row%d' % h, [1, 512], F32)
        b.gcb = [K.sb('gcb%d_%d' % (h, i), [128, 512], F32) for i in range(2)]
        b.sqk = K.sb('sqk%d' % h, [128, 512], BF16)
        b.sqq = K.sb('sqq%d' % h, [128, 512], BF16)
        b.eglrow = K.sb('eglrow%d' % h, [1, NC], F32)
        b.eglb = K.sb('eglb%d' % h, [128, NC], F32)
        b.Sf = f32t('Sf', h)
        b.Sb = bf_t('Sb', h)
        b.q_ = K.sb('gq%d' % h, [128, 512], F32)
        b.k_ = K.sb('gk%d' % h, [128, 512], F32)
        b.v_ = [K.sb('gv%d_%d' % (h, i), [128, 512], F32) for i in range(2)]
        b.knf = [K.sb('knf%d_%d' % (h, i), [128, 512], F32) for i in range(2)]
        b.knb = [K.sb('knb%d_%d' % (h, i), [128, 512], BF16) for i in range(2)]
        b.qnf = K.sb('qnf%d' % h, [128, 512], F32)
        b.qnb = [K.sb('qnb%d_%d' % (h, i), [128, 512], BF16) for i in range(2)]
        b.qgb = [K.sb('qgb%d_%d' % (h, i), [128, 512], BF16) for i in range(2)]
        HB[h] = b
    sl = lambda i: slice(i * 128, (i + 1) * 128)
    import os
    F32R = mybir.dt.float32r
    R = (lambda ap: ap.bitcast(F32R)) if os.environ.get('GDN_F32R', '0') == '1' else (lambda ap: ap)

    for h in heads:
        b = HB[h]
        K.dma('sp', b.eglrow[:], d['egld'][h:h + 1, :], writes=[b.eglrow])
        pg0 = TB[(h, 0)].bk
        K.mm(pg0, pg0[:, 0:NC], onesf, onesf[0:1, :], b.eglrow, b.eglrow[0:1, :])
        K.op('act', lambda e: e.copy(out=b.eglb[:], in_=pg0[:, 0:NC]), reads=[pg0], writes=[b.eglb])
        K.op('pool', lambda e: e.memset(b.Sf[:], 0.0), writes=[b.Sf])
        K.op('pool', lambda e: e.memset(b.Sb[:], 0.0), writes=[b.Sb])

    def prep_a(h, g):
        b = HB[h]
        cs = slice(g * 512, (g + 1) * 512)
        v_ = b.v_[g % 2]
        K.dma('sp', b.q_[:], d['GCT'][h * 128:(h + 1) * 128, cs], writes=[b.q_])
        K.dma('sp', b.k_[:], d['GCT'][512 + h * 128:512 + (h + 1) * 128, cs], writes=[b.k_])
        K.dma('sp', v_[:], d['GCT'][1024 + h * 128:1024 + (h + 1) * 128, cs], writes=[v_])
        K.dma('sp', b.gcrow[:], d['gcd'][h:h + 1, cs], writes=[b.gcrow])
        K.dma('sp', b.e1row[:], d['e1d'][h:h + 1, cs], writes=[b.e1row])
        K.op('pool', lambda e: e.tensor_tensor(out=b.sqk[:], in0=b.k_[:], in1=b.k_[:], op=ALU.mult), reads=[b.k_], writes=[b.sqk])
        K.op('pool', lambda e: e.tensor_tensor(out=b.sqq[:], in0=b.q_[:], in1=b.q_[:], op=ALU.mult), reads=[b.q_], writes=[b.sqq])

    def prep_b(h, g):
        b = HB[h]
        b0, b1 = TB[(h, 0)].bk, TB[(h, 1)].bk
        i = ngl[0] % 2
        ngl[0] += 1
        lk, lq = lnt[0], lnt[1]
        knf, knb, qnb, qgb, gcb = b.knf[g % 2], b.knb[g % 2], b.qnb[g % 2], b.qgb[g % 2], b.gcb[g % 2]
        K.mm(b0, b0[:], onesb, onesb[:], b.sqk, b.sqk[:])
        K.mm(b1, b1[:], onesb, onesb[:], b.sqq, b.sqq[:])
        K.op('act', lambda e: e.activation(out=lk[:], in_=b0[:], func=AF.Ln, bias=K.eps_t[:, 0:1]), reads=[b0, K.eps_t], writes=[lk])
        K.op('act', lambda e: e.activation(out=lq[:], in_=b1[:], func=AF.Ln, scale=128.0, bias=c128[:, 0:1]), reads=[b1, c128], writes=[lq])
        K.mm(b0, b0[:], onesf, onesf[0:1, :], b.gcrow, b.gcrow[0:1, :])
        K.mm(b1, b1[:], onesf, onesf[0:1, :], b.e1row, b.e1row[0:1, :])
        K.op('act', lambda e: e.activation(out=lk[:], in_=lk[:], func=AF.Exp, scale=-0.5), reads=[lk], writes=[lk])
        K.op('act', lambda e: e.activation(out=lq[:], in_=lq[:], func=AF.Exp, scale=-0.5), reads=[lq], writes=[lq])
        K.op('act', lambda e: e.copy(out=gcb[:], in_=b0[:]), reads=[b0], writes=[gcb])
        K.op('dve', lambda e: e.tensor_tensor(out=knf[:], in0=b.k_[:], in1=lk[:], op=ALU.mult), reads=[b.k_, lk], writes=[knf])
        K.op('pool', lambda e: e.tensor_copy(out=knb[:], in_=knf[:]), reads=[knf], writes=[knb])
        K.op('dve', lambda e: e.tensor_tensor(out=b.qnf[:], in0=b.q_[:], in1=lq[:], op=ALU.mult), reads=[b.q_, lq], writes=[b.qnf])
        K.op('pool', lambda e: e.tensor_copy(out=qnb[:], in_=b.qnf[:]), reads=[b.qnf], writes=[qnb])
        K.op('dve', lambda e: e.tensor_tensor(out=qgb[:], in0=b.qnf[:], in1=b1[:], op=ALU.mult), reads=[b.qnf, b1], writes=[qgb])

    class View:
        def __init__(self, t, hb):
            self._t, self._hb = t, hb

        def __getattr__(self, n):
            t = object.__getattribute__(self, '_t')
            if hasattr(t, n):
                return getattr(t, n)
            return getattr(object.__getattribute__(self, '_hb'), n)
    VW = {(h, par): View(TB[(h, par)], HB[h]) for h in heads for par in range(2)}

    def st1(h, g, tt):
        b = VW[(h, tt % 2)]
        c = g * 4 + tt
        ts_, tok = sl(tt), slice(c * 128, (c + 1) * 128)
        bk, v_ = b.bk, b.v_[g % 2]
        z = b.z
        K.dma('sp', z[:], d['GZ'][tok, h * 128:(h + 1) * 128], writes=[z])
        K.op('pool', lambda e: e.tensor_tensor(out=b.nz[:], in0=z[:], in1=nwb[:], op=ALU.mult), reads=[z, nwb], writes=[b.nz])
        beta_c, be_c, e2_c, gc_c = tokS[:, c, h:h + 1], tokS[:, c, 4 + h:5 + h], tokS[:, c, 8 + h:9 + h], tokS[:, c, 12 + h:13 + h]
        knf, knb, qnb, gcb = b.knf[g % 2], b.knb[g % 2], b.qnb[g % 2], b.gcb[g % 2]
        K.tr(bk, bk[:, 0:128], knf, knf[:, ts_], identf, identf[:])
        K.tr(bk, bk[:, 128:256], v_, v_[:, ts_], identf, identf[:])
        K.mm(bk, bk[:, 256:384], knb, knb[:, ts_], knb, knb[:, ts_])
        K.mm(bk, bk[:, 384:512], knb, knb[:, ts_], qnb, qnb[:, ts_])
        K.op('dve', lambda e: e.tensor_scalar(out=b.Kbg[:], in0=bk[:, 0:128], scalar1=be_c, scalar2=None, op0=ALU.mult), reads=[bk, tokS], writes=[b.Kbg])
        K.op('dve', lambda e: e.tensor_scalar(out=b.Kd[:], in0=bk[:, 0:128], scalar1=e2_c, scalar2=None, op0=ALU.mult), reads=[bk, tokS], writes=[b.Kd])
        K.op('dve', lambda e: e.tensor_scalar(out=b.Vb[:], in0=bk[:, 128:256], scalar1=beta_c, scalar2=None, op0=ALU.mult), reads=[bk, tokS], writes=[b.Vb])
        K.op('dve', lambda e: e.scalar_tensor_tensor(out=b.dm1[:], in0=gcb[:, ts_], scalar=gc_c, in1=maskT[:], op0=ALU.subtract, op1=ALU.add),
             reads=[gcb, tokS, maskT], writes=[b.dm1])
        K.op('dve', lambda e: e.scalar_tensor_tensor(out=b.dm2[:], in0=gcb[:, ts_], scalar=gc_c, in1=maskTs[:], op0=ALU.subtract, op1=ALU.add),
             reads=[gcb, tokS, maskTs], writes=[b.dm2])

    def st1b(h, g, tt):
        b = VW[(h, tt % 2)]
        bk = b.bk
        K.op('act', lambda e: e.activation(out=b.dm1[:], in_=b.dm1[:], func=AF.Exp), reads=[b.dm1], writes=[b.dm1])
        K.op('act', lambda e: e.activation(out=b.dm2[:], in_=b.dm2[:], func=AF.Exp), reads=[b.dm2], writes=[b.dm2])
        K.op('dve', lambda e: e.tensor_tensor(out=b.attnT[:], in0=bk[:, 384:512], in1=b.dm1[:], op=ALU.mult), reads=[bk, b.dm1], writes=[b.attnT])
        K.op('dve', lambda e: e.tensor_tensor(out=b.Mt[:], in0=bk[:, 256:384], in1=b.dm2[:], op=ALU.mult), reads=[bk, b.dm2], writes=[b.Mt])

    def st2(h, g, tt):
        b = VW[(h, tt % 2)]
        c = g * 4 + tt
        bk = b.bk
        beta_c = tokS[:, c, h:h + 1]
        K.tr(bk, bk[:, 0:128], b.Mt, b.Mt[:], identf, identf[:])
        K.op('dve', lambda e: e.tensor_scalar(out=b.P[0][:], in0=bk[:, 0:128], scalar1=beta_c, scalar2=-1.0, op0=ALU.mult, op1=ALU.mult),
             reads=[bk, tokS], writes=[b.P[0]])

    def st2b(h, g, tt):
        b = VW[(h, tt % 2)]
        bk = b.bk
        K.tr(bk, bk[:, 128:256], b.P[0], b.P[0][:], identf, identf[:])
        K.op('act', lambda e: e.copy(out=b.RX[:, 0:128], in_=bk[:, 128:256]), reads=[bk], writes=[b.RX])
        K.op('dve', lambda e: e.tensor_tensor(out=b.RX[:, 128:256], in0=bk[:, 128:256], in1=identf[:], op=ALU.add), reads=[bk, identf],
             writes=[b.RX], nowaw=[b.RX])

    def lvl_first(h, tt):
        b = VW[(h, tt % 2)]
        bk = b.bk
        K.mm(bk, bk[:, 256:384], b.RX, b.RX[:, 0:128], b.P[0], b.P[0][:])
        K.mm(bk, bk[:, 384:512], b.P[0], b.P[0][:], b.RX, b.RX[:, 0:128])
        K.op('act', lambda e: e.copy(out=b.P[1][:], in_=bk[:, 256:384]), reads=[bk], writes=[b.P[1]])
        K.op('act', lambda e: e.copy(out=b.RX[:, 0:128], in_=bk[:, 384:512]), reads=[bk], writes=[b.RX], nowaw=[b.RX])

    def lvl_mid(h, n, tt):
        b = VW[(h, tt % 2)]
        bk = b.bk
        Pn, Pn1 = b.P[n % 2], b.P[(n + 1) % 2]
        if n < 5:
            K.mm(bk, bk[:, 0:256], Pn, Pn[:], b.RX, b.RX[:, 0:256])
        else:
            K.mm(bk, bk[:, 128:256], Pn, Pn[:], b.RX, b.RX[:, 128:256])
        K.mm(bk, bk[:, 256:384], b.RX, b.RX[:, 0:128], Pn, Pn[:])
        K.op('dve', lambda e: e.tensor_tensor(out=b.RX[:, 128:256], in0=b.RX[:, 128:256], in1=bk[:, 128:256], op=ALU.add),
             reads=[b.RX, bk], writes=[b.RX], nowaw=[b.RX])
        if n < 5:
            K.op('act', lambda e: e.copy(out=b.RX[:, 0:128], in_=bk[:, 0:128]), reads=[bk], writes=[b.RX], nowaw=[b.RX])
        K.op('act', lambda e: e.copy(out=Pn1[:], in_=bk[:, 256:384]), reads=[bk], writes=[Pn1])

    def lvl_last(h, tt):
        b = VW[(h, tt % 2)]
        bk = b.bk
        K.mm(bk, bk[:, 0:128], b.P[0], b.P[0][:], b.RX, b.RX[:, 128:256])
        K.op('dve', lambda e: e.tensor_tensor(out=b.RX[:, 128:256], in0=b.RX[:, 128:256], in1=bk[:, 0:128], op=ALU.add),
             reads=[b.RX, bk], writes=[b.RX], nowaw=[b.RX])

    def st3(h, g, tt):
        b = VW[(h, tt % 2)]
        bk = b.bk
        K.op('pool', lambda e: e.tensor_copy(out=b.Xb[:], in_=b.RX[:, 128:256]), reads=[b.RX], writes=[b.Xb])
        K.mm(bk, bk[:, 0:128], b.Xb, b.Xb[:], b.Vb, b.Vb[:])
        K.mm(bk, bk[:, 128:256], b.Kbg, b.Kbg[:], b.Xb, b.Xb[:])
        K.op('act', lambda e: e.copy(out=b.U[:], in_=bk[:, 0:128]), reads=[bk], writes=[b.U])
        K.op('act', lambda e: e.copy(out=b.WT[:], in_=bk[:, 128:256]), reads=[bk], writes=[b.WT])

    def st4(h, g, tt):
        b = VW[(h, tt % 2)]
        c = g * 4 + tt
        ts_ = sl(tt)
        bk = b.bk
        K.mm(bk, bk[:, 256:384], b.WT, b.WT[:], b.Sb, b.Sb[:])
        K.op('dve', lambda e: e.tensor_tensor(out=b.vnew[:], in0=b.U[:], in1=bk[:, 256:384], op=ALU.subtract), reads=[b.U, bk], writes=[b.vnew])
        K.mm(bk, bk[:, 384:512], b.qgb[g % 2], b.qgb[g % 2][:, ts_], b.Sb, b.Sb[:], start=True, stop=False)
        K.mm(bk, bk[:, 384:512], b.attnT, b.attnT[:], b.vnew, b.vnew[:], start=False, stop=True)
        K.mm(bk, bk[:, 0:128], b.Kd, b.Kd[:], b.vnew, b.vnew[:])
        K.op('dve', lambda e: e.scalar_tensor_tensor(out=b.Sb[:], in0=b.Sf[:], scalar=b.eglb[:, c:c + 1], in1=bk[:, 0:128],
                                                     op0=ALU.mult, op1=ALU.add), reads=[b.Sf, b.eglb, bk], writes=[b.Sb])
        K.op('dve', lambda e: e.scalar_tensor_tensor(out=b.Sf[:], in0=b.Sf[:], scalar=b.eglb[:, c:c + 1], in1=bk[:, 0:128],
                                                     op0=ALU.mult, op1=ALU.add), reads=[b.Sf, b.eglb, bk], writes=[b.Sf])
        K.op('act', lambda e: e.copy(out=b.osb[:], in_=bk[:, 384:512]), reads=[bk], writes=[b.osb])

    def st5(h, g, tt):
        b = VW[(h, tt % 2)]
        c = g * 4 + tt
        tok = slice(c * 128, (c + 1) * 128)
        K.op('dve', lambda e: e.scalar_tensor_tensor(out=b.ojunk[:], in0=b.osb[:], scalar=1.0 / 128, in1=b.osb[:], op0=ALU.mult, op1=ALU.mult,
                                                     accum_out=b.ost[:, 0:1]), reads=[b.osb], writes=[b.ojunk, b.ost])
        K.op('act', lambda e: e.activation(out=b.ost[:, 1:2], in_=b.ost[:, 0:1], func=AF.Ln, bias=K.eps_t[:, 0:1]), reads=[b.ost, K.eps_t], writes=[b.ost])
        K.op('act', lambda e: e.activation(out=b.ost[:, 2:3], in_=b.ost[:, 1:2], func=AF.Exp, scale=-0.5), reads=[b.ost], writes=[b.ost])
        o_b = b.ob
        K.op('dve', lambda e: e.scalar_tensor_tensor(out=o_b[:], in0=b.osb[:], scalar=b.ost[:, 2:3], in1=b.nz[:], op0=ALU.mult, op1=ALU.mult),
             reads=[b.osb, b.ost, b.nz], writes=[o_b])
        K.dma('pool', d['mixed'][tok, 512 + h * 128:512 + (h + 1) * 128], o_b[:], reads=[o_b])

    for h in heads:
        prep_a(h, 0)
    for h in heads:
        prep_b(h, 0)
    for g in range(NG):
        if g + 1 < NG:
            for h in heads:
                prep_a(h, g + 1)
        for tp in range(0, 4, 2):
            tts = (tp, tp + 1)
            if tp == 2 and g + 1 < NG:
                for h in heads:
                    prep_b(h, g + 1)
            for stage in (st1, st1b, st2, st2b):
                for tt in tts:
                    for h in heads:
                        stage(h, g, tt)
            for tt in tts:
                for h in heads:
                    lvl_first(h, tt)
            for n in range(1, 6):
                for tt in tts:
                    for h in heads:
                        lvl_mid(h, n, tt)
            for tt in tts:
                for h in heads:
                    lvl_last(h, tt)
            for tt in tts:
                for h in heads:
                    st3(h, g, tt)
            for tt in tts:
                for h in heads:
                    st4(h, g, tt)
            for tt in tts:
                for h in heads:
                    st5(h, g, tt)
    K.phase_end()
```

```python
from contextlib import ExitStack
import concourse.bass as bass
import concourse.mybir as mybir

F32 = mybir.dt.float32
BF16 = mybir.dt.bfloat16
I32 = mybir.dt.int32
ALU = mybir.AluOpType
AF = mybir.ActivationFunctionType
AX = mybir.AxisListType
ENG = ['pe', 'act', 'dve', 'pool', 'sp']
NDS = 90


class Buf:
    def __init__(self, t, name):
        self.t = t
        self.name = name
        self.w = {}
        self.r = {}
        self.dsem = None
        self.psum = False
        self.wr = {}

    def __getitem__(self, k):
        return self.t[k]


class KCtx:
    def __init__(self, nc):
        self.nc = nc
        self.e = dict(pe=nc.tensor, act=nc.scalar, dve=nc.vector, pool=nc.gpsimd, sp=nc.sync)
        self.gstack = ExitStack()
        self.csem = {n: self.gstack.enter_context(nc.semaphore('c_' + n)) for n in ENG}
        self.cnt = {n: 0 for n in ENG}
        self.dsems = [self.gstack.enter_context(nc.semaphore('d%d' % i)) for i in range(NDS)]
        self.dcnt = [0] * NDS
        self.free_ds = list(range(NDS))
        self.seen = {n: {} for n in ENG}
        self.pstack = None
        self.phase_bufs = []
        self.nwaits = 0
        self.uid = 0

    def phase_begin(self):
        self.pstack = ExitStack()
        self.phase_bufs = []

    def phase_end(self):
        self.barrier()
        for b in self.phase_bufs:
            if b.dsem is not None:
                self.free_ds.append(b.dsem)
                b.dsem = None
        self.pstack.close()
        self.pstack = None

    def sb(self, name, shape, dt):
        self.uid += 1
        t = self.pstack.enter_context(self.nc.sbuf_tensor('%s_%d' % (name, self.uid), list(shape), dt))
        b = Buf(t, name)
        self.phase_bufs.append(b)
        return b

    def ps(self, name, shape, dt=F32):
        self.uid += 1
        t = self.pstack.enter_context(self.nc.psum_tensor('%s_%d' % (name, self.uid), list(shape), dt))
        b = Buf(t, name)
        b.psum = True
        self.phase_bufs.append(b)
        return b

    def _sem(self, k):
        return self.csem[k] if isinstance(k, str) else self.dsems[k]

    def _wait(self, eng, deps, force_self=False):
        for k, v in deps.items():
            if v <= self.seen[eng].get(k, 0):
                continue
            if k == eng and not force_self:
                if eng == 'pe':
                    continue
                if v < self.cnt[eng] - 1:
                    continue
            self.e[eng].wait_ge(self._sem(k), v)
            self.nwaits += 1
            self.seen[eng][k] = v

    @staticmethod
    def _merge(d, s):
        for k, v in s.items():
            if v > d.get(k, 0):
                d[k] = v

    def _deps(self, reads, writes, nowaw, eng=None):
        deps = {}
        for b in reads:
            self._merge(deps, b.w)
            if b.psum:
                self._merge(deps, {k: v for k, v in b.r.items() if k != eng})
        for b in writes:
            if b not in nowaw:
                self._merge(deps, b.w)
            else:
                self._merge(deps, b.wr)
            self._merge(deps, b.r)
        return deps

    def op(self, eng, fn, reads=(), writes=(), nowaw=()):
        self._wait(eng, self._deps(reads, writes, nowaw, eng))
        ins = fn(self.e[eng])
        self.cnt[eng] += 1
        c = self.cnt[eng]
        ins.then_inc(self.csem[eng], 1)
        for b in reads:
            if c > b.r.get(eng, 0):
                b.r[eng] = c
        for b in writes:
            if b in nowaw:
                b.w[eng] = c
            else:
                b.w = {eng: c}
                b.wr = {eng: c}
                b.r = {}
        return ins

    def dma(self, q, out, in_, reads=(), writes=(), nowaw=(), **kw):
        self._wait(q, self._deps(reads, writes, nowaw))
        b0 = (list(writes) + list(reads))[0]
        if b0.dsem is None:
            assert self.free_ds, "out of DMA semaphores"
            b0.dsem = self.free_ds.pop(0)
        i = b0.dsem
        self.e[q].dma_start(out=out, in_=in_, **kw).then_inc(self.dsems[i], 16)
        self.dcnt[i] += 16
        v = self.dcnt[i]
        for b in reads:
            b.r[i] = v
        for b in writes:
            if b in nowaw:
                b.w[i] = v
            else:
                b.w = {i: v}
                b.wr = {i: v}
                b.r = {}

    def barrier(self):
        deps = {n: self.cnt[n] for n in ENG if self.cnt[n] > 0}
        for i in range(NDS):
            if self.dcnt[i] > 0:
                deps[i] = self.dcnt[i]
        for eng in ENG:
            self._wait(eng, deps, force_self=True)

    def finish(self):
        self.barrier()
        self.gstack.close()

    def mm(self, out_b, out_ap, lhsT_b, lhsT_ap, rhs_b, rhs_ap, start=True, stop=True, extra_reads=()):
        return self.op('pe', lambda e: e.matmul(out_ap, lhsT_ap, rhs_ap, start=start, stop=stop),
                       reads=[lhsT_b, rhs_b] + list(extra_reads), writes=[out_b])

    def tr(self, out_b, out_ap, in_b, in_ap, id_b, id_ap):
        return self.op('pe', lambda e: e.transpose(out_ap, in_ap, id_ap), reads=[in_b, id_b], writes=[out_b])

import numpy as np
from concourse.bass_utils import run_bass_kernel_spmd

D = 1024
EPS = 1e-6


def make_ident(K, dt, name):
    ones = K.sb(name + '_ones', [128, 128], dt)
    ident = K.sb(name, [128, 128], dt)
    K.op('pool', lambda e: e.memset(ones[:], 1.0), writes=[ones])
    K.op('pool', lambda e: e.affine_select(out=ident[:], in_=ones[:], pattern=[[-1, 128]],
                                           compare_op=ALU.is_equal, fill=0.0, base=0, channel_multiplier=1),
         reads=[ones], writes=[ident])
    return ident


def load_weight_bf16(K, Wb, src, kchunks, ncols, gain=None, stage_name='wst', q='sp', cb=1024):
    cb = min(cb, ncols)
    st = [K.sb(stage_name + str(i), [128, cb], F32) for i in range(4)]
    n = 0
    for k in range(kchunks):
        for c0 in range(0, ncols, cb):
            c1 = min(ncols, c0 + cb)
            s = st[n % 4]
            eng = 'dve' if n % 2 == 0 else 'act'
            n += 1
            K.dma(q, s[:, 0:c1 - c0], src[k * 128:(k + 1) * 128, c0:c1], writes=[s])
            if eng == 'act':
                if gain is not None:
                    K.op('act', lambda e, s=s, k=k, c0=c0, c1=c1: e.activation(out=Wb[:, k, c0:c1], in_=s[:, 0:c1 - c0], func=AF.Copy,
                                                                               scale=gain[:, k:k + 1]),
                         reads=[s, gain], writes=[Wb], nowaw=[Wb])
                else:
                    K.op('act', lambda e, s=s, k=k, c0=c0, c1=c1: e.copy(out=Wb[:, k, c0:c1], in_=s[:, 0:c1 - c0]),
                         reads=[s], writes=[Wb], nowaw=[Wb])
            elif gain is not None:
                K.op(eng, lambda e, s=s, k=k, c0=c0, c1=c1: e.tensor_scalar(out=Wb[:, k, c0:c1], in0=s[:, 0:c1 - c0], scalar1=gain[:, k:k + 1],
                                                                scalar2=None, op0=ALU.mult),
                     reads=[s, gain], writes=[Wb], nowaw=[Wb])
            else:
                K.op(eng, lambda e, s=s, k=k, c0=c0, c1=c1: e.tensor_copy(out=Wb[:, k, c0:c1], in_=s[:, 0:c1 - c0]),
                     reads=[s], writes=[Wb], nowaw=[Wb])


class NormT:
    def __init__(self, K, identb, nbuf_x=3, gt=4, nbuf_n=2):
        self.K = K
        self.identb = identb
        self.nn = nbuf_n
        self.xb = [K.sb('xb%d' % i, [128, D], F32) for i in range(nbuf_x)]
        self.xn = [K.sb('xn%d' % i, [128, D], BF16) for i in range(nbuf_n)]
        self.st = [K.sb('nst%d' % i, [128, 4], F32) for i in range(nbuf_n)]
        self.pT = [K.ps('pT%d' % i, [128, D], BF16) for i in range(2)]
        self.hnT = [K.sb('hnT%d' % i, [128, 8, gt * 128], BF16) for i in range(2)]
        self.n = 0

    def prep(self, src_rows):
        K = self.K
        n = self.n
        self.n += 1
        xb = self.xb[n % len(self.xb)]
        xn = self.xn[n % self.nn]
        st = self.st[n % self.nn]
        K.dma('sp', xb[:], src_rows, writes=[xb])
        K.op('act', lambda e: e.activation(out=xn[:], in_=xb[:], func=AF.Square, accum_out=st[:, 0:1]),
             reads=[xb], writes=[xn, st])
        K.op('act', lambda e: e.activation(out=st[:, 1:2], in_=st[:, 0:1], func=AF.Sqrt, scale=1.0 / D, bias=K.eps_t[:, 0:1]),
             reads=[st, K.eps_t], writes=[st])
        K.op('dve', lambda e: e.reciprocal(out=st[:, 2:3], in_=st[:, 1:2]), reads=[st], writes=[st])
        K.op('dve', lambda e: e.tensor_scalar(out=xn[:], in0=xb[:], scalar1=st[:, 2:3], scalar2=None, op0=ALU.mult),
             reads=[xb, st], writes=[xn])
        return (n, xb)

    def finish(self, tok, hnT, j):
        K = self.K
        n, xb = tok
        xn = self.xn[n % self.nn]
        pT = self.pT[n % 2]
        for k in range(8):
            K.tr(pT, pT[:, k * 128:(k + 1) * 128], xn, xn[:, k * 128:(k + 1) * 128], self.identb, self.identb[:])
        K.op('act', lambda e: e.copy(out=hnT[:, :, j * 128:(j + 1) * 128],
                                     in_=pT[:].rearrange("p (k t) -> p k t", k=8)),
             reads=[pT], writes=[hnT], nowaw=[hnT])
        return xb

    def tile(self, src_rows, hnT, j, keep_x=None):
        return self.finish(self.prep(src_rows), hnT, j)


def consts(K):
    K.eps_t = K.sb('eps', [128, 1], F32)
    K.op('pool', lambda e: e.memset(K.eps_t[:], EPS), writes=[K.eps_t])
    K.one_t = K.sb('one', [128, 1], F32)
    K.op('pool', lambda e: e.memset(K.one_t[:], 1.0), writes=[K.one_t])
    K.mhalf_t = K.sb('mhalf', [128, 1], F32)
    K.op('pool', lambda e: e.memset(K.mhalf_t[:], -0.5), writes=[K.mhalf_t])


E_FQ, E_FK, E_FV, E_FF, E_GQ, E_GK, E_GV, E_GZ, E_GB, E_GA = 0, 512, 1024, 1536, 1544, 2056, 2568, 3080, 3592, 3596


def phase_A0(K, d, S):
    NG = S // 512
    K.phase_begin()
    consts(K)
    identb = make_ident(K, BF16, 'identb')
    gain = K.sb('gain', [128, 8], F32)
    K.dma('sp', gain[:], d['even_norm_mix'].rearrange("(k p) -> p k", p=128), writes=[gain],
          allow_slow_non_contiguous=True)
    cw = K.sb('cw', [128, 4, 12], F32)
    for j in range(4):
        K.dma('sp', cw[:, j, :], d['even_gdn_conv_w'][j, :].rearrange("(c p) -> p c", p=128), writes=[cw], nowaw=[cw],
              allow_slow_non_contiguous=True)
    W = K.sb('W', [128, 8, 3600], BF16)
    load_weight_bf16(K, W, d['even_w_in'], 8, 3600, gain=gain)
    NT = NormT(K, identb, nbuf_x=8, nbuf_n=8)
    xpad = [K.sb('xpad%d' % c, [128, 515], F32) for c in range(12)]
    for c in range(12):
        K.op('pool', lambda e, c=c: e.memset(xpad[c][:, 0:3], 0.0), writes=[xpad[c]])
    acc = [K.sb('acc%d' % i, [128, 512], F32) for i in range(2)]
    cout = [K.sb('cout%d' % i, [128, 512], F32) for i in range(3)]
    qk = [K.sb('qk%d' % i, [128, 512], BF16) for i in range(3)]
    vt = [K.sb('vt%d' % i, [128, 512], BF16) for i in range(2)]
    zt = [K.sb('zt%d' % i, [128, 512], F32) for i in range(2)]
    sm = [K.sb('sm%d' % i, [8, 512], F32) for i in range(2)]
    sm2 = [K.sb('smb%d' % i, [8, 512], F32) for i in range(2)]
    ptm = [K.ps('ptm%d' % i, [128, 512], F32) for i in range(2)]
    pfm = [K.ps('pfm%d' % i, [128, 512], F32) for i in range(3)]
    nfm = 0
    ntm = 0
    pend_silu = [None]
    toks = [NT.prep(d['x'][u * 128:(u + 1) * 128, :]) for u in range(4)]
    for g in range(NG):
        hnT = NT.hnT[g % 2]
        for j in range(4):
            t = g * 4 + j
            NT.finish(toks.pop(0), hnT, j)
            if j == 0:
                for u in range(t + 4, min(t + 8, NG * 4)):
                    toks.append(NT.prep(d['x'][u * 128:(u + 1) * 128, :]))
            for which, col0 in ((0, E_FV), (1, E_GZ)):
                p = ptm[ntm % 2]
                ntm += 1
                for k in range(8):
                    K.mm(p, p[:], hnT, hnT[:, k, j * 128:(j + 1) * 128], W, W[:, k, col0:col0 + 512],
                         start=(k == 0), stop=(k == 7))
                if which == 0:
                    o = vt[t % 2]
                    K.op('dve', lambda e, o=o, p=p: e.tensor_copy(out=o[:], in_=p[:]), reads=[p], writes=[o])
                    K.dma('pool', d['Vf'][t * 128:(t + 1) * 128, :], o[:], reads=[o])
                else:
                    o = zt[t % 2]
                    K.op('act', lambda e, o=o, p=p: e.activation(out=o[:], in_=p[:], func=AF.Silu), reads=[p], writes=[o])
                    K.dma('pool', d['GZ'][t * 128:(t + 1) * 128, :], o[:], reads=[o])
        chunks = ([('q', c, E_FQ + c * 128, 128) for c in range(4)] + [('k', c, E_FK + c * 128, 128) for c in range(4)]
                  + [('g', c, E_GQ + c * 128, 128) for c in range(12)] + [('s', 0, None, 16)])
        for kind, c, col0, M in chunks:
            p = pfm[nfm % 3]
            nfm += 1
            if kind == 's':
                for k in range(8):
                    K.mm(p, p[0:8, :], W, W[:, k, E_FF:E_FF + 8], hnT, hnT[:, k, :], start=(k == 0), stop=(k == 7))
                o = sm[g % 2]
                K.op('act', lambda e, o=o, p=p: e.copy(out=o[0:8, :], in_=p[0:8, :]), reads=[p], writes=[o])
                p2 = pfm[nfm % 3]
                nfm += 1
                for k in range(8):
                    K.mm(p2, p2[0:8, :], W, W[:, k, E_GB:E_GB + 8], hnT, hnT[:, k, :], start=(k == 0), stop=(k == 7))
                o2 = sm2[g % 2]
                K.op('act', lambda e, o2=o2, p2=p2: e.copy(out=o2[:], in_=p2[0:8, :]), reads=[p2], writes=[o2])
                K.dma('pool', d['smallT'][0:8, g * 512:(g + 1) * 512], o[0:8, :], reads=[o])
                K.dma('pool', d['smallT'][8:16, g * 512:(g + 1) * 512], o2[:], reads=[o2])
                continue
            for k in range(8):
                K.mm(p, p[:], W, W[:, k, col0:col0 + 128], hnT, hnT[:, k, :], start=(k == 0), stop=(k == 7))
            if kind in ('q', 'k'):
                o = qk[nfm % 3]
                K.op('dve', lambda e, o=o, p=p: e.tensor_copy(out=o[:], in_=p[:]), reads=[p], writes=[o])
                dst = d['QfT'] if kind == 'q' else d['KfT']
                K.dma('pool', dst[c * 128:(c + 1) * 128, g * 512:(g + 1) * 512], o[:], reads=[o])
            else:
                xp = xpad[c]
                K.op('act', lambda e, xp=xp, p=p: e.copy(out=xp[:, 3:515], in_=p[:]), reads=[p], writes=[xp], nowaw=[xp])
                a = acc[c % 2]
                K.op('dve', lambda e, a=a, xp=xp, c=c: e.tensor_scalar(out=a[:], in0=xp[:, 0:512], scalar1=cw[:, 0, c:c + 1],
                                                                        scalar2=None, op0=ALU.mult),
                     reads=[xp, cw], writes=[a])
                for j in range(1, 4):
                    K.op('dve', lambda e, a=a, xp=xp, c=c, j=j: e.scalar_tensor_tensor(
                        out=a[:], in0=xp[:, j:j + 512], scalar=cw[:, j, c:c + 1], in1=a[:], op0=ALU.mult, op1=ALU.add),
                        reads=[xp, cw, a], writes=[a])
                K.op('pool', lambda e, xp=xp: e.tensor_copy(out=xp[:, 0:3], in_=xp[:, 512:515]), reads=[xp], writes=[xp])
                o = cout[nfm % 3]

                def silu_store(o=o, a=a, c=c, g=g):
                    K.op('act', lambda e: e.activation(out=o[:], in_=a[:], func=AF.Silu), reads=[a], writes=[o])
                    K.dma('pool', d['GCT'][c * 128:(c + 1) * 128, g * 512:(g + 1) * 512], o[:], reads=[o])
                if pend_silu[0] is not None:
                    pend_silu[0]()
                pend_silu[0] = silu_store
    if pend_silu[0] is not None:
        pend_silu[0]()
    K.phase_end()


INPUT_SHAPES = {
    "even_norm_mix": [1024], "even_w_in": [1024, 3600], "even_fox_bf": [8], "even_gdn_conv_w": [4, 1536],
    "even_gdn_a_log": [4], "even_gdn_dt_bias": [4], "even_gdn_norm_w": [128], "even_w_out": [1024, 1024],
    "odd_norm_mix": [1024], "odd_w_in": [1024, 2328], "odd_cmp_k_pe": [32, 64], "odd_cmp_k_w1": [2048, 128],
    "odd_cmp_k_w2": [128, 64], "odd_cmp_v_pe": [32, 64], "odd_cmp_v_w1": [2048, 128], "odd_cmp_v_w2": [128, 64],
    "odd_rg_conv_w": [4, 512], "odd_rg_conv_b": [512], "odd_rg_wa": [8, 64, 64], "odd_rg_ba": [512],
    "odd_rg_wx": [8, 64, 64], "odd_rg_bx": [512], "odd_rg_lambda": [512], "odd_w_out": [1024, 1024],
    "mlp_norm": [2, 1024], "mlp_w_up": [2, 1024, 4096], "mlp_w_down": [2, 4096, 1024], "ple_norm": [2, 1024],
    "ple_w_gate": [2, 1024, 1024], "ple_w_proj": [2, 256, 1024], "final_norm": [1024],
}


def scratch_specs(S):
    return {
        "h": ([S, 1024], F32),
        "QfT": ([512, S], BF16), "KfT": ([512, S], BF16), "Vf": ([S, 512], BF16),
        "GZ": ([S, 512], F32), "GCT": ([1536, S], F32), "smallT": ([16, S], F32),
        "gcd": ([4, S], F32), "ngcd": ([4, S], F32), "e1d": ([4, S], F32), "egld": ([4, S // 128], F32),
        "tokSd": ([128, (S // 128) * 16], F32),
        "csd": ([128, (S // 128) * 16], F32), "QT1": ([512, S], BF16), "KcT": ([128, S], BF16), "VcT": ([128, S], BF16),
        "KsT": ([128, S], BF16), "KwT": ([128, S], BF16), "Vs": ([S, 128], BF16), "Vw": ([S, 128], BF16),
        "gates": ([S, 24], F32), "RGX": ([1024, S], F32),
        "KcmpT": ([128, S // 16], BF16), "Vcmp": ([S // 16, 128], BF16),
        "nsa0": ([S, 512], F32), "nsa1": ([S, 512], F32), "nsa2": ([S, 512], F32), "selT": ([2, 128, S], BF16),
        "FQ": ([8, 3, S], BF16), "nF": ([128, (S // 128) * 8], F32), "mixed": ([S, 1024], BF16),
    }


def build(S, phases, dbg=(), h_input=False):
    nc = bass.Bass("TRN2", target_bir_lowering=False)
    d = {}
    d['x'] = nc.dram_tensor("x", [S, 1024], F32, kind="ExternalInput").ap()
    d['p'] = nc.dram_tensor("p", [2, S, 256], F32, kind="ExternalInput").ap()
    d['positions'] = nc.dram_tensor("positions", [1, S], I32, kind="ExternalInput").ap()
    for n, shp in INPUT_SHAPES.items():
        d[n] = nc.dram_tensor(n, shp, F32, kind="ExternalInput").ap()
    d['out'] = nc.dram_tensor("out", [S, 1024], F32, kind="ExternalOutput").ap()
    for n, (shp, dt) in scratch_specs(S).items():
        kind = "ExternalOutput" if n in dbg else "Internal"
        if n == 'h' and h_input:
            kind = "ExternalInput"
        d[n] = nc.dram_tensor(n, shp, dt, kind=kind).ap()
    K = KCtx(nc)
    import os
    K.stop = int(os.environ.get('KSTOP', '0')) or None
    for ph in phases:
        ph(K, d, S)
    K.finish()
    return nc, K


def phase_Fprep(K, d, S):
    NT_ = S // 128
    K.phase_begin()
    consts(K)
    identf = make_ident(K, F32, 'identf')
    ff = K.sb('ff', [8, S], F32)
    sp = K.sb('spl', [8, S], F32)
    ones = K.sb('ones8', [8, S], F32)
    nF = K.sb('negF', [8, S], F32)
    bf = K.sb('bf', [8, 2], F32)
    K.dma('sp', ff[:], d['smallT'][0:8, :], writes=[ff])
    K.dma('sp', bf[:, 0:1], d['even_fox_bf'].rearrange("(h o) -> h o", o=1), writes=[bf])
    K.op('dve', lambda e: e.tensor_scalar(out=bf[:, 1:2], in0=bf[:, 0:1], scalar1=-1.0, scalar2=None, op0=ALU.mult),
         reads=[bf], writes=[bf])
    K.op('pool', lambda e: e.memset(ones[:], 1.0), writes=[ones])
    K.op('act', lambda e: e.activation(out=sp[:], in_=ff[:], func=AF.Exp, scale=-1.0, bias=bf[:, 1:2]),
         reads=[ff, bf], writes=[sp])
    K.op('act', lambda e: e.activation(out=sp[:], in_=sp[:], func=AF.Ln, scale=1.0, bias=K.one_t[0:8, 0:1]),
         reads=[sp, K.one_t], writes=[sp])
    K.op('dve', lambda e: e.tensor_tensor_scan(out=nF[:], data0=ones[:], data1=sp[:], initial=0.0,
                                               op0=ALU.mult, op1=ALU.add), reads=[ones, sp], writes=[nF])
    q8 = ff
    K.op('dve', lambda e: e.tensor_scalar(out=q8[:], in0=nF[:], scalar1=-8.0, scalar2=None, op0=ALU.mult),
         reads=[nF], writes=[q8])
    parts = [K.sb('fq%d' % i, [8, S], BF16) for i in range(3)]
    for i in range(3):
        K.op('dve', lambda e, i=i: e.tensor_copy(out=parts[i][:], in_=q8[:]), reads=[q8], writes=[parts[i]])
        if i < 2:
            K.op('dve', lambda e, i=i: e.tensor_tensor(out=q8[:], in0=q8[:], in1=parts[i][:], op=ALU.subtract),
                 reads=[q8, parts[i]], writes=[q8])
        K.dma('sp', d['FQ'][:, i, :], parts[i][:], reads=[parts[i]])
    pt = K.ps('ptr', [128, 512], F32)
    nft = K.sb('nft', [128, NT_ * 8], F32)
    for t0 in range(0, NT_, 64):
        n = min(64, NT_ - t0)
        for t in range(n):
            K.tr(pt, pt[:, t * 8:(t + 1) * 8], nF, nF[0:8, (t0 + t) * 128:(t0 + t + 1) * 128], identf, identf[0:8, 0:8])
        K.op('act', lambda e, t0=t0, n=n: e.copy(out=nft[:, t0 * 8:(t0 + n) * 8], in_=pt[:, 0:n * 8]), reads=[pt], writes=[nft])
    K.dma('sp', d['nF'][:, :], nft[:], reads=[nft])
    K.phase_end()


def phase_fox(K, d, S, heads=range(8), L=3):
    NT_ = S // 128
    NQ = S // 512
    K.phase_begin()
    consts(K)
    identf = make_ident(K, F32, 'identf')
    nF = K.sb('nF', [128, NT_ * 8], F32)
    K.dma('sp', nF[:], d['nF'][:, :], writes=[nF])
    KT = [K.sb('KT%d' % i, [67, S], BF16) for i in range(2)]
    QT = [K.sb('QT%d' % i, [67, S], BF16) for i in range(2)]
    V = [K.sb('V%d' % i, [128, NT_, 65], BF16) for i in range(2)]
    for i in range(2):
        K.op('pool', lambda e, i=i: e.memset(KT[i][64:67, :], 1.0), writes=[KT[i]])
        K.op('pool', lambda e, i=i: e.memset(V[i][:, :, 64:65], 1.0), writes=[V[i]])
    ps = [K.ps('ps%d' % i, [128, 512], F32) for i in range(3)]
    po = [K.ps('po%d' % i, [128, 512], F32) for i in range(2)]
    pq = [K.ps('pq%d' % i, [128, 4, 65], F32) for i in range(2)]
    pt = [K.sb('pt%d' % i, [128, 512], BF16) for i in range(4)]
    osb = [K.sb('osb%d' % i, [65, 512], F32) for i in range(2)]
    rc = [K.sb('rc%d' % i, [128, 4], F32) for i in range(2)]
    mo = [K.sb('mo%d' % i, [128, 4, 64], BF16) for i in range(2)]

    def load_head(hi, h):
        s = hi % 2
        K.dma('sp', KT[s][0:64, :], d['KfT'][h * 64:(h + 1) * 64, :], writes=[KT[s]])
        K.dma('sp', QT[s][0:64, :], d['QfT'][h * 64:(h + 1) * 64, :], writes=[QT[s]])
        K.dma('sp', QT[s][64:67, :], d['FQ'][h, :, :], writes=[QT[s]], nowaw=[QT[s]])
        K.dma('sp', V[s][:, :, 0:64], d['Vf'][:, h * 64:(h + 1) * 64].rearrange("(t p) c -> p t c", p=128),
              writes=[V[s]])

    heads = list(heads)
    units = []
    for hi, h in enumerate(heads):
        for qg in range(NQ):
            nk = 4 * qg + 4
            for kt in range(nk):
                units.append((hi, h, qg, kt, kt == nk - 1))
    load_head(0, heads[0])
    NU = len(units)
    for i in range(NU + L):
        if i < NU:
            hi, h, qg, kt, last = units[i]
            s = hi % 2
            r = kt - 4 * qg
            c0 = 128 * max(r, 0)
            p_s = ps[i % 3]
            p_t = pt[i % 4]
            K.mm(p_s, p_s[:, c0:512], KT[s], KT[s][0:67, kt * 128:(kt + 1) * 128], QT[s],
                 QT[s][0:67, qg * 512 + c0:(qg + 1) * 512])
            K.op('act', lambda e, p_s=p_s, p_t=p_t, c0=c0, kt=kt, h=h: e.activation(
                out=p_t[:, c0:512], in_=p_s[:, c0:512], func=AF.Exp, scale=0.125, bias=nF[:, kt * 8 + h:kt * 8 + h + 1]),
                reads=[p_s, nF], writes=[p_t])
            if r >= 0:
                K.op('pool', lambda e, p_t=p_t, c0=c0: e.affine_select(
                    out=p_t[:, c0:c0 + 128], in_=p_t[:, c0:c0 + 128], pattern=[[1, 128]], compare_op=ALU.is_ge,
                    fill=0.0, base=0, channel_multiplier=-1), reads=[p_t], writes=[p_t])
        if i - L >= 0:
            hi, h, qg, kt, last = units[i - L]
            if qg == 0 and kt == 0 and hi + 1 < len(heads):
                load_head(hi + 1, heads[hi + 1])
            s = hi % 2
            r = kt - 4 * qg
            c0 = 128 * max(r, 0)
            p_t = pt[(i - L) % 4]
            gi = hi * NQ + qg
            p_o = po[gi % 2]
            K.mm(p_o, p_o[0:65, c0:512], V[s], V[s][:, kt, 0:65], p_t, p_t[:, c0:512], start=(kt == 0), stop=last)
            if last:
                o_s = osb[gi % 2]
                p_q = pq[gi % 2]
                K.op('act', lambda e, o_s=o_s, p_o=p_o: e.copy(out=o_s[:], in_=p_o[0:65, :]), reads=[p_o], writes=[o_s])
                for j in range(4):
                    K.tr(p_q, p_q[:, j, :], o_s, o_s[0:65, j * 128:(j + 1) * 128], identf, identf[0:65, 0:65])
                r_c = rc[gi % 2]
                m_o = mo[gi % 2]
                K.op('dve', lambda e, r_c=r_c, p_q=p_q: e.reciprocal(out=r_c[:], in_=p_q[:, :, 64]), reads=[p_q], writes=[r_c])
                for j in range(4):
                    K.op('dve', lambda e, j=j, r_c=r_c, p_q=p_q, m_o=m_o: e.tensor_scalar(
                        out=m_o[:, j, :], in0=p_q[:, j, 0:64], scalar1=r_c[:, j:j + 1], scalar2=None, op0=ALU.mult),
                        reads=[p_q, r_c], writes=[m_o])
                K.dma('pool', d['mixed'][qg * 512:(qg + 1) * 512, h * 64:(h + 1) * 64].rearrange("(j p) c -> p j c", p=128),
                      m_o[:], reads=[m_o])
    K.phase_end()


NEGM = -30000.0


def phase_Gprep(K, d, S):
    NC = S // 128
    K.phase_begin()
    consts(K)
    identf = make_ident(K, F32, 'identf')
    A = K.sb('gpA', [4, S], F32)
    B = K.sb('gpB', [4, S], F32)
    C = K.sb('gpC', [4, S], F32)
    Dt = K.sb('gpD', [4, S], F32)
    E = K.sb('gpE', [4, S], F32)
    K.dma('sp', A[:], d['smallT'][8:12, :], writes=[A])
    K.dma('sp', B[:], d['smallT'][12:16, :], writes=[B])
    pr = K.sb('pr', [4, 4], F32)
    K.dma('sp', pr[:, 0:1], d['even_gdn_a_log'].rearrange("(h o) -> h o", o=1), writes=[pr])
    K.dma('sp', pr[:, 1:2], d['even_gdn_dt_bias'].rearrange("(h o) -> h o", o=1), writes=[pr], nowaw=[pr])
    K.op('act', lambda e: e.activation(out=pr[:, 2:3], in_=pr[:, 0:1], func=AF.Exp), reads=[pr], writes=[pr])
    K.op('pool', lambda e: e.memset(C[:], 1.0), writes=[C])
    K.op('pool', lambda e: e.memset(C[:].rearrange("p (c t) -> p c t", t=128)[:, :, 0:1], 0.0), writes=[C])
    K.op('act', lambda e: e.activation(out=Dt[:], in_=B[:], func=AF.Exp, bias=pr[:, 1:2]), reads=[B, pr], writes=[Dt])
    K.op('act', lambda e: e.activation(out=Dt[:], in_=Dt[:], func=AF.Ln, bias=K.one_t[0:4, 0:1]), reads=[Dt, K.one_t], writes=[Dt])
    K.op('dve', lambda e: e.tensor_scalar(out=B[:], in0=Dt[:], scalar1=pr[:, 2:3], scalar2=-1.0, op0=ALU.mult, op1=ALU.mult),
         reads=[Dt, pr], writes=[B])
    gc = E
    K.op('dve', lambda e: e.tensor_tensor_scan(out=gc[:], data0=C[:], data1=B[:], initial=0.0, op0=ALU.mult, op1=ALU.add),
         reads=[C, B], writes=[gc])
    K.dma('sp', d['gcd'][:, :], gc[:], reads=[gc])
    K.op('dve', lambda e: e.tensor_scalar(out=Dt[:], in0=gc[:], scalar1=-1.0, scalar2=None, op0=ALU.mult), reads=[gc], writes=[Dt])
    K.dma('sp', d['ngcd'][:, :], Dt[:], reads=[Dt])
    beta = A
    K.op('act', lambda e: e.activation(out=beta[:], in_=A[:], func=AF.Sigmoid), reads=[A], writes=[beta])
    e1 = B
    K.op('act', lambda e: e.activation(out=e1[:], in_=gc[:], func=AF.Exp), reads=[gc], writes=[e1])
    K.dma('sp', d['e1d'][:, :], e1[:], reads=[e1])
    be = C
    K.op('dve', lambda e: e.tensor_tensor(out=be[:], in0=beta[:], in1=e1[:], op=ALU.mult), reads=[beta, e1], writes=[be])
    gc3 = gc[:].rearrange("p (c t) -> p c t", t=128)
    e2 = Dt
    K.op('dve', lambda e: e.tensor_tensor(out=e2[:].rearrange("p (c t) -> p c t", t=128),
                                          in0=gc3[:, :, 127:128].to_broadcast([4, NC, 128]), in1=gc3, op=ALU.subtract),
         reads=[gc], writes=[e2])
    K.op('act', lambda e: e.activation(out=e2[:], in_=e2[:], func=AF.Exp), reads=[e2], writes=[e2])
    egl = K.sb('egl', [4, NC], F32)
    K.op('act', lambda e: e.activation(out=egl[:], in_=gc3[:, :, 127], func=AF.Exp), reads=[gc], writes=[egl])
    K.dma('sp', d['egld'][:, :], egl[:], reads=[egl])
    pt = [K.ps('ptg%d' % i, [128, 32, 16], F32) for i in range(2)]
    tokS = K.sb('tokS', [128, NC, 16], F32)
    K.op('pool', lambda e: e.memset(tokS[:], 0.0), writes=[tokS])
    for t0 in range(0, NC, 32):
        n = min(32, NC - t0)
        p = pt[(t0 // 32) % 2]
        for t in range(n):
            for qi, src in enumerate((beta, be, e2, gc)):
                K.tr(p, p[:, t, qi * 4:(qi + 1) * 4], src, src[0:4, (t0 + t) * 128:(t0 + t + 1) * 128], identf, identf[0:4, 0:4])
        K.op('act', lambda e, p=p, t0=t0, n=n: e.copy(out=tokS[:, t0:t0 + n, 0:16], in_=p[:, 0:n, 0:16]), reads=[p], writes=[tokS],
             nowaw=[tokS])
    K.dma('sp', d['tokSd'][:, :], tokS[:].rearrange("p c q -> p (c q)"), reads=[tokS])
    K.phase_end()


class StopPhase(Exception):
    pass


def chk(K, n):
    if getattr(K, 'stop', None) == n:
        raise StopPhase()


def phase_gdn(K, d, S, heads=range(4)):
    try:
        _phase_gdn(K, d, S, heads)
    except StopPhase:
        pass
    K.phase_end()


def _phase_gdn(K, d, S, heads=range(4)):
    NC = S // 128
    NG = S // 512
    K.phase_begin()
    consts(K)
    identf = make_ident(K, F32, 'identf')
    onesf = K.sb('onesf', [128, 128], F32)
    K.op('pool', lambda e: e.memset(onesf[:], 1.0), writes=[onesf])
    maskT = K.sb('maskT', [128, 128], F32)
    maskTs = K.sb('maskTs', [128, 128], F32)
    zer = K.sb('zer', [128, 128], F32)
    K.op('pool', lambda e: e.memset(zer[:], 0.0), writes=[zer])
    K.op('pool', lambda e: e.affine_select(out=maskT[:], in_=zer[:], pattern=[[1, 128]], compare_op=ALU.is_ge, fill=NEGM,
                                           base=0, channel_multiplier=-1), reads=[zer], writes=[maskT])
    K.op('pool', lambda e: e.affine_select(out=maskTs[:], in_=zer[:], pattern=[[1, 128]], compare_op=ALU.is_gt, fill=NEGM,
                                           base=0, channel_multiplier=-1), reads=[zer], writes=[maskTs])
    nwb = K.sb('nwb', [128, 128], F32)
    K.dma('sp', nwb[:], d['even_gdn_norm_w'].partition_broadcast(128), writes=[nwb])
    tokS = K.sb('tokS', [128, NC, 16], F32)
    K.dma('sp', tokS[:].rearrange("p c q -> p (c q)"), d['tokSd'][:, :], writes=[tokS])
    c128 = K.sb('c128', [128, 1], F32)
    K.op('pool', lambda e: e.memset(c128[:], 128.0 * 1e-6), writes=[c128])
    G2L = K.sb('G2L', [2, S], F32)
    G2R = K.sb('G2R', [2, S], F32)
    e1row = K.sb('e1row', [1, S], F32)
    eglrow = K.sb('eglrow', [1, NC], F32)
    eglb = K.sb('eglb', [128, NC], F32)
    Sst = K.sb('Sst', [128, 128], F32)
    qT = [K.sb('gqT%d' % i, [128, 512], F32) for i in range(2)]
    kT = [K.sb('gkT%d' % i, [128, 512], F32) for i in range(2)]
    vT = [K.sb('gvT%d' % i, [128, 512], F32) for i in range(2)]
    sq = K.sb('gsq', [128, 512], F32)
    rr = K.sb('grr', [128, 512], F32)
    knT = K.sb('knT', [128, 512], F32)
    qnT = K.sb('qnT', [128, 512], F32)
    qgT = K.sb('qgT', [128, 512], F32)
    zt = [K.sb('gz%d' % i, [128, 128], F32) for i in range(2)]
    nz = K.sb('nz', [128, 128], F32)
    Kbg = K.sb('Kbg', [128, 128], F32)
    Kd = K.sb('Kd', [128, 128], F32)
    Vb = K.sb('Vb', [128, 128], F32)
    dm1 = K.sb('dm1', [128, 128], F32)
    dm2 = K.sb('dm2', [128, 128], F32)
    attnT = K.sb('attnT', [128, 128], F32)
    Mt = K.sb('Mt', [128, 128], F32)
    P = [K.sb('Pn%d' % i, [128, 128], F32) for i in range(2)]
    PT = [K.sb('PTn%d' % i, [128, 128], F32) for i in range(2)]
    X = K.sb('Xn', [128, 128], F32)
    U = K.sb('Un', [128, 128], F32)
    WT = K.sb('WTn', [128, 128], F32)
    vnew = K.sb('vnew', [128, 128], F32)
    ost = K.sb('gost', [128, 4], F32)
    ojunk = K.sb('gojunk', [128, 128], F32)
    ob = [K.sb('gob%d' % i, [128, 128], BF16) for i in range(2)]
    bA = K.ps('bA', [128, 512], F32)
    bB = K.ps('bB', [128, 512], F32)
    bC = K.ps('bC', [128, 512], F32)
    bD = K.ps('bD', [128, 512], F32)
    bE = K.ps('bE', [128, 512], F32)
    bF = K.ps('bF', [128, 512], F32)
    bG = K.ps('bG', [128, 512], F32)
    bH = K.ps('bH', [128, 512], F32)
    sl = lambda i: slice(i * 128, (i + 1) * 128)
    chk(K, 1)
    for h in heads:
        K.op('pool', lambda e: e.memset(G2L[:], 1.0), writes=[G2L])
        K.op('pool', lambda e: e.memset(G2R[:], 1.0), writes=[G2R])
        K.dma('sp', G2L[0:1, :], d['ngcd'][h:h + 1, :], writes=[G2L])
        K.dma('sp', G2R[1:2, :], d['gcd'][h:h + 1, :], writes=[G2R])
        K.dma('sp', e1row[:], d['e1d'][h:h + 1, :], writes=[e1row])
        K.dma('sp', eglrow[:], d['egld'][h:h + 1, :], writes=[eglrow])
        K.mm(bA, bA[:, 0:NC], onesf, onesf[0:1, :], eglrow, eglrow[0:1, :])
        K.op('act', lambda e: e.copy(out=eglb[:], in_=bA[:, 0:NC]), reads=[bA], writes=[eglb])
        K.op('pool', lambda e: e.memset(Sst[:], 0.0), writes=[Sst])
        chk(K, 2)
        for g in range(NG):
            cs = slice(g * 512, (g + 1) * 512)
            q_, k_, v_ = qT[g % 2], kT[g % 2], vT[g % 2]
            K.dma('sp', q_[:], d['GCT'][h * 128:(h + 1) * 128, cs], writes=[q_])
            K.dma('sp', k_[:], d['GCT'][512 + h * 128:512 + (h + 1) * 128, cs], writes=[k_])
            K.dma('sp', v_[:], d['GCT'][1024 + h * 128:1024 + (h + 1) * 128, cs], writes=[v_])
            K.op('act', lambda e: e.activation(out=sq[:], in_=k_[:], func=AF.Square), reads=[k_], writes=[sq])
            K.mm(bA, bA[:], onesf, onesf[:], sq, sq[:])
            K.op('act', lambda e: e.activation(out=rr[:], in_=bA[:], func=AF.Sqrt, bias=K.eps_t[:, 0:1]), reads=[bA, K.eps_t], writes=[rr])
            K.op('dve', lambda e: e.reciprocal(out=rr[:], in_=rr[:]), reads=[rr], writes=[rr])
            K.op('dve', lambda e: e.tensor_tensor(out=knT[:], in0=k_[:], in1=rr[:], op=ALU.mult), reads=[k_, rr], writes=[knT])
            K.op('act', lambda e: e.activation(out=sq[:], in_=q_[:], func=AF.Square), reads=[q_], writes=[sq])
            K.mm(bA, bA[:], onesf, onesf[:], sq, sq[:])
            K.op('act', lambda e: e.activation(out=rr[:], in_=bA[:], func=AF.Sqrt, scale=128.0, bias=c128[:, 0:1]),
                 reads=[bA, c128], writes=[rr])
            K.op('dve', lambda e: e.reciprocal(out=rr[:], in_=rr[:]), reads=[rr], writes=[rr])
            K.op('dve', lambda e: e.tensor_tensor(out=qnT[:], in0=q_[:], in1=rr[:], op=ALU.mult), reads=[q_, rr], writes=[qnT])
            K.mm(bA, bA[:], onesf, onesf[0:1, :], e1row, e1row[0:1, cs])
            K.op('dve', lambda e: e.tensor_tensor(out=qgT[:], in0=qnT[:], in1=bA[:], op=ALU.mult), reads=[qnT, bA], writes=[qgT])
            chk(K, 3)
            for tt in range(4):
                c = g * 4 + tt
                ts_ = sl(tt)
                tok = slice(c * 128, (c + 1) * 128)
                z = zt[c % 2]
                K.dma('sp', z[:], d['GZ'][tok, h * 128:(h + 1) * 128], writes=[z])
                K.op('pool', lambda e, z=z: e.tensor_tensor(out=nz[:], in0=z[:], in1=nwb[:], op=ALU.mult), reads=[z, nwb], writes=[nz])
                beta_c = tokS[:, c, h:h + 1]
                be_c = tokS[:, c, 4 + h:5 + h]
                e2_c = tokS[:, c, 8 + h:9 + h]
                K.tr(bB, bB[:, 0:128], knT, knT[:, ts_], identf, identf[:])
                K.tr(bB, bB[:, 128:256], v_, v_[:, ts_], identf, identf[:])
                K.op('dve', lambda e: e.tensor_scalar(out=Kbg[:], in0=bB[:, 0:128], scalar1=be_c, scalar2=None, op0=ALU.mult),
                     reads=[bB, tokS], writes=[Kbg])
                K.op('act', lambda e: e.activation(out=Kd[:], in_=bB[:, 0:128], func=AF.Copy, scale=e2_c), reads=[bB, tokS], writes=[Kd])
                K.op('dve', lambda e: e.tensor_scalar(out=Vb[:], in0=bB[:, 128:256], scalar1=beta_c, scalar2=None, op0=ALU.mult),
                     reads=[bB, tokS], writes=[Vb])
                chk(K, 4)
                K.mm(bC, bC[:, 0:128], knT, knT[:, ts_], knT, knT[:, ts_])
                K.mm(bC, bC[:, 128:256], knT, knT[:, ts_], qnT, qnT[:, ts_])
                K.mm(bC, bC[:, 256:384], G2L, G2L[0:2, tok], G2R, G2R[0:2, tok])
                K.op('dve', lambda e: e.tensor_tensor(out=dm1[:], in0=bC[:, 256:384], in1=maskT[:], op=ALU.add), reads=[bC, maskT], writes=[dm1])
                K.op('dve', lambda e: e.tensor_tensor(out=dm2[:], in0=bC[:, 256:384], in1=maskTs[:], op=ALU.add), reads=[bC, maskTs], writes=[dm2])
                K.op('act', lambda e: e.activation(out=dm1[:], in_=dm1[:], func=AF.Exp), reads=[dm1], writes=[dm1])
                K.op('act', lambda e: e.activation(out=dm2[:], in_=dm2[:], func=AF.Exp), reads=[dm2], writes=[dm2])
                K.op('dve', lambda e: e.tensor_tensor(out=attnT[:], in0=bC[:, 128:256], in1=dm1[:], op=ALU.mult), reads=[bC, dm1], writes=[attnT])
                K.op('dve', lambda e: e.tensor_tensor(out=Mt[:], in0=bC[:, 0:128], in1=dm2[:], op=ALU.mult), reads=[bC, dm2], writes=[Mt])
                chk(K, 5)
                K.tr(bD, bD[:, 0:128], Mt, Mt[:], identf, identf[:])
                K.op('dve', lambda e: e.tensor_scalar(out=P[0][:], in0=bD[:, 0:128], scalar1=beta_c, scalar2=-1.0, op0=ALU.mult, op1=ALU.mult),
                     reads=[bD, tokS], writes=[P[0]])
                K.tr(bD, bD[:, 128:256], P[0], P[0][:], identf, identf[:])
                K.op('act', lambda e: e.copy(out=PT[0][:], in_=bD[:, 128:256]), reads=[bD], writes=[PT[0]])
                K.op('dve', lambda e: e.tensor_tensor(out=X[:], in0=bD[:, 128:256], in1=identf[:], op=ALU.add), reads=[bD, identf], writes=[X])
                for n in range(6):
                    a, b = n % 2, (n + 1) % 2
                    K.mm(bE, bE[:, 128:256], PT[a], PT[a][:], P[a], P[a][:])
                    K.mm(bE, bE[:, 256:384], P[a], P[a][:], PT[a], PT[a][:])
                    K.op('act', lambda e, b=b: e.copy(out=P[b][:], in_=bE[:, 128:256]), reads=[bE], writes=[P[b]])
                    K.op('act', lambda e, b=b: e.copy(out=PT[b][:], in_=bE[:, 256:384]), reads=[bE], writes=[PT[b]])
                    K.mm(bF, bF[:, 0:128], P[b], P[b][:], X, X[:])
                    K.op('dve', lambda e: e.tensor_tensor(out=X[:], in0=X[:], in1=bF[:, 0:128], op=ALU.add), reads=[X, bF], writes=[X])
                chk(K, 6)
                K.mm(bG, bG[:, 0:128], X, X[:], Vb, Vb[:])
                K.mm(bG, bG[:, 128:256], Kbg, Kbg[:], X, X[:])
                K.op('act', lambda e: e.copy(out=U[:], in_=bG[:, 0:128]), reads=[bG], writes=[U])
                K.op('act', lambda e: e.copy(out=WT[:], in_=bG[:, 128:256]), reads=[bG], writes=[WT])
                chk(K, 7)
                K.mm(bH, bH[:, 0:128], WT, WT[:], Sst, Sst[:])
                K.op('dve', lambda e: e.tensor_tensor(out=vnew[:], in0=U[:], in1=bH[:, 0:128], op=ALU.subtract), reads=[U, bH], writes=[vnew])
                K.mm(bH, bH[:, 128:256], qgT, qgT[:, ts_], Sst, Sst[:], start=True, stop=False)
                K.mm(bH, bH[:, 128:256], attnT, attnT[:], vnew, vnew[:], start=False, stop=True)
                K.mm(bH, bH[:, 256:384], Kd, Kd[:], vnew, vnew[:])
                K.op('dve', lambda e, c=c: e.scalar_tensor_tensor(out=Sst[:], in0=Sst[:], scalar=eglb[:, c:c + 1], in1=bH[:, 256:384],
                                                                   op0=ALU.mult, op1=ALU.add), reads=[Sst, eglb, bH], writes=[Sst])
                chk(K, 8)
                K.op('act', lambda e: e.activation(out=ojunk[:], in_=bH[:, 128:256], func=AF.Square, accum_out=ost[:, 0:1]),
                     reads=[bH], writes=[ojunk, ost])
                K.op('act', lambda e: e.activation(out=ost[:, 1:2], in_=ost[:, 0:1], func=AF.Sqrt, scale=1.0 / 128, bias=K.eps_t[:, 0:1]),
                     reads=[ost, K.eps_t], writes=[ost])
                K.op('dve', lambda e: e.reciprocal(out=ost[:, 2:3], in_=ost[:, 1:2]), reads=[ost], writes=[ost])
                o_b = ob[c % 2]
                K.op('dve', lambda e, o_b=o_b: e.scalar_tensor_tensor(out=o_b[:], in0=bH[:, 128:256], scalar=ost[:, 2:3], in1=nz[:],
                                                                       op0=ALU.mult, op1=ALU.mult), reads=[bH, ost, nz], writes=[o_b])
                K.dma('pool', d['mixed'][tok, 512 + h * 128:512 + (h + 1) * 128], o_b[:], reads=[o_b])


def load_gain(K, name, src1d):
    g = K.sb(name, [128, 8], F32)
    K.dma('sp', g[:], src1d.rearrange("(k p) -> p k", p=128), writes=[g], allow_slow_non_contiguous=True)
    return g


def phase_O(K, d, S, src, w_out, dst='h'):
    NT_ = S // 128
    K.phase_begin()
    consts(K)
    identb = make_ident(K, BF16, 'identb')
    W = K.sb('Wo', [128, 8, 1024], BF16)
    load_weight_bf16(K, W, w_out, 8, 1024)
    mx = [K.sb('mx%d' % i, [128, 1024], BF16) for i in range(2)]
    xr = [K.sb('xr%d' % i, [128, 1024], F32) for i in range(2)]
    mT = [K.sb('mT%d' % i, [128, 8, 128], BF16) for i in range(2)]
    ho = [K.sb('ho%d' % i, [128, 1024], F32) for i in range(2)]
    pT = [K.ps('opT%d' % i, [128, 1024], BF16) for i in range(2)]
    po = [K.ps('opo%d' % i, [128, 512], F32) for i in range(4)]
    for t in range(NT_):
        rows = slice(t * 128, (t + 1) * 128)
        m_, x_, mT_, h_, p_ = mx[t % 2], xr[t % 2], mT[t % 2], ho[t % 2], pT[t % 2]
        K.dma('sp', m_[:], d['mixed'][rows, :], writes=[m_])
        K.dma('sp', x_[:], src[rows, :], writes=[x_])
        for k in range(8):
            K.tr(p_, p_[:, k * 128:(k + 1) * 128], m_, m_[:, k * 128:(k + 1) * 128], identb, identb[:])
        K.op('act', lambda e: e.copy(out=mT_[:], in_=p_[:].rearrange("p (k t) -> p k t", k=8)), reads=[p_], writes=[mT_])
        for half in range(2):
            p2 = po[(t * 2 + half) % 4]
            for k in range(8):
                K.mm(p2, p2[:], mT_, mT_[:, k, :], W, W[:, k, half * 512:(half + 1) * 512], start=(k == 0), stop=(k == 7))
            K.op('dve', lambda e, p2=p2, half=half: e.tensor_tensor(out=h_[:, half * 512:(half + 1) * 512], in0=p2[:],
                                                                   in1=x_[:, half * 512:(half + 1) * 512], op=ALU.add),
                 reads=[p2, x_], writes=[h_], nowaw=[h_])
        K.dma('pool', d[dst][rows, :], h_[:], reads=[h_])
    K.phase_end()


def phase_MLP(K, d, S, li):
    GT = 2
    NG = S // (GT * 128)
    K.phase_begin()
    consts(K)
    identb = make_ident(K, BF16, 'identb')
    gain = load_gain(K, 'gain', d['mlp_norm'][li, :])
    Wu = K.sb('Wu', [128, 8, 4096], BF16)
    load_weight_bf16(K, Wu, d['mlp_w_up'][li], 8, 4096, gain=gain, stage_name='wsu', cb=512)
    Wd = K.sb('Wd', [128, 32, 1024], BF16)
    load_weight_bf16(K, Wd, d['mlp_w_down'][li], 32, 1024, stage_name='wsd', cb=512)
    NT = NormT(K, identb, nbuf_x=2 * GT, gt=GT)
    uT = K.sb('uT', [128, 32, GT * 128], BF16)
    rl = [K.sb('rl%d' % i, [128, GT * 128], F32) for i in range(2)]
    ho = [K.sb('mho%d' % i, [128, 1024], F32) for i in range(2)]
    pu = [K.ps('pu%d' % i, [128, 512], F32) for i in range(2)]
    pd = [K.ps('pd%d' % i, [128, 512], F32) for i in range(4)]
    toks = [NT.prep(d['h'][j * 128:(j + 1) * 128, :]) for j in range(GT)]
    for g in range(NG):
        hnT = NT.hnT[g % 2]
        xbs = []
        for j in range(GT):
            xbs.append(NT.finish(toks[j], hnT, j))
        for fc in range(32):
            if fc == 8 and g + 1 < NG:
                toks = [NT.prep(d['h'][((g + 1) * GT + j) * 128:((g + 1) * GT + j + 1) * 128, :]) for j in range(GT)]
            p = pu[fc % 2]
            for k in range(8):
                K.mm(p, p[:, 0:GT * 128], Wu, Wu[:, k, fc * 128:(fc + 1) * 128], hnT, hnT[:, k, :], start=(k == 0), stop=(k == 7))
            r = rl[fc % 2]
            K.op('act', lambda e, r=r, p=p: e.activation(out=r[:], in_=p[:, 0:GT * 128], func=AF.Relu), reads=[p], writes=[r])
            K.op('dve' if fc % 2 == 0 else 'pool', lambda e, r=r, fc=fc: e.tensor_tensor(out=uT[:, fc, :], in0=r[:], in1=r[:], op=ALU.mult),
                 reads=[r], writes=[uT], nowaw=[uT])
        for j in range(GT):
            t = g * GT + j
            h_ = ho[t % 2]
            for half in range(2):
                p2 = pd[(t * 2 + half) % 4]
                for fc in range(32):
                    K.mm(p2, p2[:], uT, uT[:, fc, j * 128:(j + 1) * 128], Wd, Wd[:, fc, half * 512:(half + 1) * 512],
                         start=(fc == 0), stop=(fc == 31))
                K.op('dve', lambda e, p2=p2, half=half, x_=xbs[j]: e.tensor_tensor(
                    out=h_[:, half * 512:(half + 1) * 512], in0=p2[:], in1=x_[:, half * 512:(half + 1) * 512], op=ALU.add),
                    reads=[p2, xbs[j]], writes=[h_], nowaw=[h_])
            K.dma('pool', d['h'][t * 128:(t + 1) * 128, :], h_[:], reads=[h_])
    K.phase_end()


def phase_PLE(K, d, S, li, final=False):
    NT_ = S // 128
    K.phase_begin()
    consts(K)
    identb = make_ident(K, BF16, 'identb')
    gain = load_gain(K, 'gain', d['ple_norm'][li, :])
    Wg = K.sb('Wg', [128, 8, 1024], BF16)
    load_weight_bf16(K, Wg, d['ple_w_gate'][li], 8, 1024, gain=gain)
    Wp = K.sb('Wp', [128, 2, 1024], BF16)
    load_weight_bf16(K, Wp, d['ple_w_proj'][li], 2, 1024)
    NT = NormT(K, identb, nbuf_x=9, gt=1, nbuf_n=8)
    pin = [K.sb('pin%d' % i, [128, 256], F32) for i in range(2)]
    pbf = [K.sb('pbf%d' % i, [128, 256], BF16) for i in range(2)]
    ppT = [K.sb('ppT%d' % i, [128, 2, 128], BF16) for i in range(2)]
    sg = [K.sb('sg%d' % i, [128, 1024], F32) for i in range(2)]
    ho = [K.sb('pho%d' % i, [128, 1024], F32) for i in range(3)]
    ptp = K.ps('ptp', [128, 1024], BF16)
    pg = [K.ps('pg%d' % i, [128, 512], F32) for i in range(2)]
    pp = [K.ps('pp%d' % i, [128, 512], F32) for i in range(2)]
    if final:
        fg = K.sb('fg', [128, 1024], F32)
        K.dma('sp', fg[:], d['final_norm'].partition_broadcast(128), writes=[fg])
        fst = [K.sb('fst%d' % i, [128, 4], F32) for i in range(2)]
        fo = [K.sb('fo%d' % i, [128, 1024], F32) for i in range(2)]
    pend_fin = [None]
    toks = [NT.prep(d['h'][u * 128:(u + 1) * 128, :]) for u in range(min(4, NT_))]
    for t in range(NT_):
        rows = slice(t * 128, (t + 1) * 128)
        hnT = NT.hnT[t % 2]
        x_ = NT.finish(toks.pop(0), hnT, 0)
        if t % 4 == 0:
            for u in range(t + 4, min(t + 8, NT_)):
                toks.append(NT.prep(d['h'][u * 128:(u + 1) * 128, :]))
        pi_, pb_, pT_, sg_, h_ = pin[t % 2], pbf[t % 2], ppT[t % 2], sg[t % 2], ho[t % 3]
        K.dma('sp', pi_[:], d['p'][li, rows, :], writes=[pi_])
        K.op('dve', lambda e: e.tensor_copy(out=pb_[:], in_=pi_[:]), reads=[pi_], writes=[pb_])
        for k in range(2):
            K.tr(ptp, ptp[:, k * 128:(k + 1) * 128], pb_, pb_[:, k * 128:(k + 1) * 128], identb, identb[:])
        K.op('act', lambda e: e.copy(out=pT_[:], in_=ptp[:, 0:256].rearrange("p (k t) -> p k t", k=2)), reads=[ptp], writes=[pT_])
        for half in range(2):
            hs = slice(half * 512, (half + 1) * 512)
            g_, p_ = pg[half], pp[half]
            for k in range(8):
                K.mm(g_, g_[:], hnT, hnT[:, k, :], Wg, Wg[:, k, hs], start=(k == 0), stop=(k == 7))
            for k in range(2):
                K.mm(p_, p_[:], pT_, pT_[:, k, :], Wp, Wp[:, k, hs], start=(k == 0), stop=(k == 1))
            K.op('act', lambda e, g_=g_, hs=hs: e.activation(out=sg_[:, hs], in_=g_[:], func=AF.Sigmoid), reads=[g_], writes=[sg_], nowaw=[sg_])
            K.op('dve', lambda e, p_=p_, hs=hs: e.tensor_tensor(out=sg_[:, hs], in0=sg_[:, hs], in1=p_[:], op=ALU.mult),
                 reads=[sg_, p_], writes=[sg_])
        K.op('pool', lambda e: e.tensor_tensor(out=h_[:], in0=sg_[:], in1=x_[:], op=ALU.add), reads=[sg_, x_], writes=[h_])
        if not final:
            K.dma('pool', d['h'][rows, :], h_[:], reads=[h_])
        else:
            def fin(h_=h_, st=fst[t % 2], o_=fo[t % 2], rows=rows):
                K.op('act', lambda e: e.activation(out=o_[:], in_=h_[:], func=AF.Square, accum_out=st[:, 0:1]), reads=[h_], writes=[o_, st])
                K.op('act', lambda e: e.activation(out=st[:, 1:2], in_=st[:, 0:1], func=AF.Sqrt, scale=1.0 / D, bias=K.eps_t[:, 0:1]),
                     reads=[st, K.eps_t], writes=[st])
                K.op('dve', lambda e: e.reciprocal(out=st[:, 2:3], in_=st[:, 1:2]), reads=[st], writes=[st])
                K.op('dve', lambda e: e.scalar_tensor_tensor(out=o_[:], in0=h_[:], scalar=st[:, 2:3], in1=fg[:], op0=ALU.mult, op1=ALU.mult),
                     reads=[h_, st, fg], writes=[o_])
                K.dma('pool', d['out'][rows, :], o_[:], reads=[o_])
            if pend_fin[0] is not None:
                pend_fin[0]()
            pend_fin[0] = fin
    if final and pend_fin[0] is not None:
        pend_fin[0]()
    K.phase_end()


O_NQ, O_KC, O_VC, O_KSL, O_VSL, O_KWN, O_VWN, O_NG, O_RG, O_RX = 0, 512, 640, 768, 896, 1024, 1152, 1280, 1304, 1816
TWO_PI = 6.283185307179586
CW1 = 6.28125
CW2 = TWO_PI - CW1
MAGIC = 12582912.0


def phase_Rprep(K, d, S):
    NT_ = S // 128
    inv_freq = (np.float32(500000.0) ** (-np.arange(8, dtype=np.float32) * np.float32(2.0 / 16))).astype(np.float32)
    K.phase_begin()
    consts(K)
    posi = K.sb('posi', [128, NT_], I32)
    K.dma('sp', posi[:], d['positions'].rearrange("o (t p) -> p (o t)", p=128), writes=[posi], allow_slow_non_contiguous=True)
    posf = K.sb('posf', [128, NT_], F32)
    K.op('dve', lambda e: e.tensor_copy(out=posf[:], in_=posi[:]), reads=[posi], writes=[posf])
    ang = K.sb('ang', [128, NT_, 8], F32)
    for i in range(8):
        K.op('dve', lambda e, i=i: e.tensor_scalar(out=ang[:, :, i], in0=posf[:], scalar1=float(inv_freq[i]), scalar2=None, op0=ALU.mult),
             reads=[posf], writes=[ang], nowaw=[ang])
    cs = K.sb('cs', [128, NT_, 16], F32)
    a2 = K.sb('a2', [128, NT_, 8], F32)
    kk = K.sb('kk', [128, NT_, 8], F32)
    rr = K.sb('rr', [128, NT_, 8], F32)
    for which in range(2):
        src = ang
        if which == 0:
            K.op('dve', lambda e: e.tensor_scalar(out=a2[:], in0=ang[:], scalar1=float(np.pi / 2), scalar2=None, op0=ALU.add), reads=[ang], writes=[a2])
            src = a2
        K.op('dve', lambda e, src=src: e.tensor_scalar(out=kk[:], in0=src[:], scalar1=float(1.0 / TWO_PI), scalar2=MAGIC, op0=ALU.mult, op1=ALU.add),
             reads=[src], writes=[kk])
        K.op('dve', lambda e: e.tensor_scalar(out=kk[:], in0=kk[:], scalar1=-MAGIC, scalar2=None, op0=ALU.add), reads=[kk], writes=[kk])
        K.op('dve', lambda e, src=src: e.scalar_tensor_tensor(out=rr[:], in0=kk[:], scalar=-CW1, in1=src[:], op0=ALU.mult, op1=ALU.add),
             reads=[kk, src], writes=[rr])
        K.op('dve', lambda e: e.scalar_tensor_tensor(out=rr[:], in0=kk[:], scalar=-CW2, in1=rr[:], op0=ALU.mult, op1=ALU.add),
             reads=[kk, rr], writes=[rr])
        K.op('dve', lambda e: e.tensor_scalar(out=rr[:], in0=rr[:], scalar1=3.141592, scalar2=-3.141592, op0=ALU.min, op1=ALU.max),
             reads=[rr], writes=[rr])
        K.op('act', lambda e, which=which: e.activation(out=cs[:, :, which * 8:(which + 1) * 8], in_=rr[:], func=AF.Sin),
             reads=[rr], writes=[cs], nowaw=[cs])
    K.dma('sp', d['csd'][:, :], cs[:].rearrange("p t c -> p (t c)"), reads=[cs])
    K.phase_end()


def phase_A1(K, d, S):
    NG = S // 512
    NT_ = S // 128
    K.phase_begin()
    consts(K)
    identb = make_ident(K, BF16, 'identb')
    gain = load_gain(K, 'gain', d['odd_norm_mix'])
    W = K.sb('W1', [128, 8, 2328], BF16)
    load_weight_bf16(K, W, d['odd_w_in'], 8, 2328, gain=gain)
    cs = K.sb('cs', [128, NT_, 16], F32)
    K.dma('sp', cs[:].rearrange("p t c -> p (t c)"), d['csd'][:, :], writes=[cs])
    NT = NormT(K, identb, nbuf_x=8, nbuf_n=8)
    yq = [K.sb('yq%d' % i, [128, 512], F32) for i in range(2)]
    yb = [K.sb('yb%d' % i, [128, 512], F32) for i in range(2)]
    yc = [K.sb('yc%d' % i, [128, 280], F32) for i in range(2)]
    gt_ = [K.sb('gt%d' % i, [128, 24], F32) for i in range(2)]
    qb = [K.sb('qb%d' % i, [128, 512], BF16) for i in range(2)]
    kb = [K.sb('kb%d' % i, [128, 4, 128], BF16) for i in range(2)]
    vb = [K.sb('vb%d' % i, [128, 2, 128], BF16) for i in range(2)]
    tA = K.sb('tA', [128, 8, 8], F32)
    tB = K.sb('tB', [128, 8, 8], F32)
    qTs = [K.sb('qTs%d' % i, [128, 4, 128], BF16) for i in range(2)]
    kTs = [K.sb('kTs%d' % i, [128, 4, 128], BF16) for i in range(2)]
    rgo = [K.sb('rgo%d' % i, [128, 512], F32) for i in range(3)]
    ptm = [K.ps('p1tm%d' % i, [128, 512], F32) for i in range(3)]
    pfm = [K.ps('p1fm%d' % i, [128, 512], F32) for i in range(2)]
    ptr = K.ps('p1tr', [128, 1024], BF16)
    nfm = 0

    def rope(eng, src, nh, dst, t):
        cosb = cs[:, t:t + 1, 0:8].to_broadcast([128, nh, 8])
        sinb = cs[:, t:t + 1, 8:16].to_broadcast([128, nh, 8])
        x1, x2 = src[:, :, 0:8], src[:, :, 8:16]
        a, b = tA[:, 0:nh, :], tB[:, 0:nh, :]
        return [
            lambda e: e.tensor_tensor(out=a, in0=x1, in1=cosb, op=ALU.mult),
            lambda e: e.tensor_tensor(out=b, in0=x2, in1=sinb, op=ALU.mult),
            lambda e: e.tensor_tensor(out=dst[:, :, 0:8], in0=a, in1=b, op=ALU.subtract),
            lambda e: e.tensor_tensor(out=a, in0=x2, in1=cosb, op=ALU.mult),
            lambda e: e.tensor_tensor(out=b, in0=x1, in1=sinb, op=ALU.mult),
            lambda e: e.tensor_tensor(out=dst[:, :, 8:16], in0=a, in1=b, op=ALU.add),
        ]

    toks = [NT.prep(d['h'][u * 128:(u + 1) * 128, :]) for u in range(4)]
    pending = [None]
    for g in range(NG):
        hnT = NT.hnT[g % 2]
        for j in range(4):
            t = g * 4 + j
            rows = slice(t * 128, (t + 1) * 128)
            NT.finish(toks.pop(0), hnT, j)
            if j == 0:
                for u in range(t + 4, min(t + 8, NG * 4)):
                    toks.append(NT.prep(d['h'][u * 128:(u + 1) * 128, :]))
            yq_, yb_, yc_, g_, qb_, kb_, vb_, qT_, kT_ = (yq[t % 2], yb[t % 2], yc[t % 2], gt_[t % 2], qb[t % 2], kb[t % 2], vb[t % 2],
                                                         qTs[t % 2], kTs[t % 2])
            for bi, (c0, c1, dst) in enumerate(((0, 512, yq_), (512, 1024, yb_), (1024, 1304, yc_))):
                p = ptm[bi]
                for k in range(8):
                    K.mm(p, p[:, 0:c1 - c0], hnT, hnT[:, k, j * 128:(j + 1) * 128], W, W[:, k, c0:c1], start=(k == 0), stop=(k == 7))
                K.op('act', lambda e, p=p, dst=dst, n=c1 - c0: e.copy(out=dst[:, 0:n], in_=p[:, 0:n]), reads=[p], writes=[dst])
            K.op('act', lambda e: e.activation(out=g_[:], in_=yc_[:, 256:280], func=AF.Sigmoid), reads=[yc_], writes=[g_])
            K.dma('pool', d['gates'][rows, :], g_[:], reads=[g_])
            K.op('dve', lambda e: e.tensor_copy(out=qb_[:], in_=yq_[:]), reads=[yq_], writes=[qb_])
            K.op('dve', lambda e: e.tensor_copy(out=kb_[:, 0:2, :], in_=yb_[:, 0:256].rearrange("p (a c) -> p a c", a=2)), reads=[yb_], writes=[kb_])
            K.op('dve', lambda e: e.tensor_copy(out=kb_[:, 2, :], in_=yb_[:, 256:384]), reads=[yb_], writes=[kb_])
            K.op('dve', lambda e: e.tensor_copy(out=kb_[:, 3, :], in_=yc_[:, 0:128]), reads=[yc_], writes=[kb_])
            K.op('pool', lambda e: e.tensor_copy(out=vb_[:, 0, :], in_=yb_[:, 384:512]), reads=[yb_], writes=[vb_])
            K.op('pool', lambda e: e.tensor_copy(out=vb_[:, 1, :], in_=yc_[:, 128:256]), reads=[yc_], writes=[vb_])
            K.dma('pool', d['Vs'][rows, :], vb_[:, 0, :], reads=[vb_])
            K.dma('pool', d['Vw'][rows, :], vb_[:, 1, :], reads=[vb_])
            jobs = [(yq_, yq_[:].rearrange("p (h c) -> p h c", c=64), 8, qb_, qb_[:].rearrange("p (h c) -> p h c", c=64)),
                    (yb_, yb_[:, 0:128].rearrange("p (h c) -> p h c", c=64), 2, kb_, kb_[:, 0, :].rearrange("p (h c) -> p h c", c=64)),
                    (yb_, yb_[:, 256:384].rearrange("p (h c) -> p h c", c=64), 2, kb_, kb_[:, 2, :].rearrange("p (h c) -> p h c", c=64)),
                    (yc_, yc_[:, 0:128].rearrange("p (h c) -> p h c", c=64), 2, kb_, kb_[:, 3, :].rearrange("p (h c) -> p h c", c=64))]
            for sb_, sap, nh, db_, dap in jobs:
                fns = rope('dve', sap, nh, dap, t)
                rw = [([sb_, cs], [tA]), ([sb_, cs], [tB]), ([tA, tB], [db_]), ([sb_, cs], [tA]), ([sb_, cs], [tB]), ([tA, tB], [db_])]
                for fn, (rd, wr) in zip(fns, rw):
                    K.op('dve', fn, reads=rd, writes=wr)
            def post(qb_=qb_, kb_=kb_, qT_=qT_, kT_=kT_, rows=rows):
                for c in range(4):
                    K.tr(ptr, ptr[:, c * 128:(c + 1) * 128], qb_, qb_[:, c * 128:(c + 1) * 128], identb, identb[:])
                for c in range(4):
                    K.tr(ptr, ptr[:, 512 + c * 128:512 + (c + 1) * 128], kb_, kb_[:, c, :], identb, identb[:])
                K.op('act', lambda e: e.copy(out=qT_[:], in_=ptr[:, 0:512].rearrange("p (c t) -> p c t", c=4)), reads=[ptr], writes=[qT_])
                K.op('act', lambda e: e.copy(out=kT_[:], in_=ptr[:, 512:1024].rearrange("p (c t) -> p c t", c=4)), reads=[ptr], writes=[kT_])
                K.dma('sp', d['QT1'][:, rows].rearrange("(c p) t -> p c t", p=128), qT_[:], reads=[qT_])
                for c, nm in enumerate(('KcT', 'VcT', 'KsT', 'KwT')):
                    K.dma('sp', d[nm][:, rows], kT_[:, c, :], reads=[kT_])
            if pending[0] is not None:
                pending[0]()
            pending[0] = post
        for c in range(8):
            p = pfm[nfm % 2]
            o = rgo[nfm % 3]
            nfm += 1
            for k in range(8):
                K.mm(p, p[:], W, W[:, k, O_RG + c * 128:O_RG + (c + 1) * 128], hnT, hnT[:, k, :], start=(k == 0), stop=(k == 7))
            K.op('act', lambda e, o=o, p=p: e.copy(out=o[:], in_=p[:]), reads=[p], writes=[o])
            K.dma('pool', d['RGX'][c * 128:(c + 1) * 128, g * 512:(g + 1) * 512], o[:], reads=[o])
    pending[0]()
    K.phase_end()


GELU_C = 1.5957691216057308


def gelu_tanh(K, xs, tmp, out_ap, out_b, eng2='pool'):
    K.op(eng2, lambda e: e.tensor_tensor(out=tmp[:], in0=xs[:], in1=xs[:], op=ALU.mult), reads=[xs], writes=[tmp])
    K.op('dve', lambda e: e.tensor_scalar(out=tmp[:], in0=tmp[:], scalar1=0.044715, scalar2=1.0, op0=ALU.mult, op1=ALU.add),
         reads=[tmp], writes=[tmp])
    K.op('dve', lambda e: e.tensor_tensor(out=tmp[:], in0=tmp[:], in1=xs[:], op=ALU.mult), reads=[tmp, xs], writes=[tmp])
    K.op('act', lambda e: e.activation(out=tmp[:], in_=tmp[:], func=AF.Sigmoid, scale=GELU_C), reads=[tmp], writes=[tmp])
    K.op('dve', lambda e: e.tensor_tensor(out=out_ap, in0=tmp[:], in1=xs[:], op=ALU.mult), reads=[tmp, xs], writes=[out_b])


def phase_cmp(K, d, S):
    NCP = S // 16
    NCMP = NCP - 1
    K.phase_begin()
    consts(K)
    XT = K.sb('XT', [128, S], BF16)
    w1s = K.sb('w1s', [128, 32, 128], F32)
    W1 = K.sb('W1c', [128, 32, 128], BF16)
    pes = K.sb('pes', [128, 32], F32)
    peT = K.sb('peT', [128, 32], BF16)
    w2s = K.sb('w2s', [128, 64], F32)
    W2 = K.sb('W2c', [128, 64], BF16)
    cvec = K.sb('cvec', [128, 1], F32)
    xs = K.sb('cxs', [128, NCP], F32)
    tmp = K.sb('ctmp', [128, NCP], F32)
    hidT = K.sb('hidT', [128, NCP], BF16)
    ko = K.sb('ko', [64, NCP], BF16)
    vo = K.sb('vo', [128, 64], BF16)
    ph = K.ps('cph', [128, 512], F32)
    pc = K.ps('cpc', [128, 512], F32)
    po = K.ps('cpo', [128, 512], F32)
    for which, (src, pe, w1, w2) in enumerate((('KcT', 'odd_cmp_k_pe', 'odd_cmp_k_w1', 'odd_cmp_k_w2'),
                                                ('VcT', 'odd_cmp_v_pe', 'odd_cmp_v_w1', 'odd_cmp_v_w2'))):
        K.dma('sp', XT[:], d[src][:, :], writes=[XT])
        for hf in range(2):
            K.dma('sp', w1s[hf * 64:(hf + 1) * 64, :, :], d[w1].rearrange("(l c) h -> c l h", c=64), writes=[w1s], nowaw=[w1s])
            K.dma('sp', pes[hf * 64:(hf + 1) * 64, :], d[pe].rearrange("l c -> c l"), writes=[pes], nowaw=[pes],
                  allow_slow_non_contiguous=True)
        K.dma('sp', w2s[:], d[w2][:, :], writes=[w2s])
        K.op('dve', lambda e: e.tensor_copy(out=W1[:], in_=w1s[:]), reads=[w1s], writes=[W1])
        K.op('dve', lambda e: e.tensor_copy(out=peT[:], in_=pes[:]), reads=[pes], writes=[peT])
        K.op('dve', lambda e: e.tensor_copy(out=W2[:], in_=w2s[:]), reads=[w2s], writes=[W2])
        for l in range(32):
            K.mm(pc, pc[:, 0:1], W1, W1[0:64, l, :], peT, peT[0:64, l:l + 1], start=(l == 0), stop=(l == 31))
        K.op('act', lambda e: e.copy(out=cvec[:], in_=pc[:, 0:1]), reads=[pc], writes=[cvec])
        X3 = XT[:].rearrange("p (n s) -> p n s", s=16)
        for g in range(2):
            gs = slice(g * 64, (g + 1) * 64)
            for l in range(32):
                rhs = X3[gs, 0:NCMP, l] if l < 16 else X3[gs, 1:NCP, l - 16]
                K.mm(ph, ph[:, 0:NCMP], W1, W1[gs, l, :], XT, rhs, start=(l == 0), stop=(l == 31))
            K.op('pool', lambda e: e.memset(xs[:], 0.0), writes=[xs])
            K.op('act', lambda e: e.activation(out=xs[:, 0:NCMP], in_=ph[:, 0:NCMP], func=AF.Identity, bias=cvec[:, 0:1]),
                 reads=[ph, cvec], writes=[xs])
            gelu_tanh(K, xs, tmp, hidT[:], hidT)
            if which == 0:
                for c0 in range(0, NCP, 512):
                    n = min(512, NCP - c0)
                    K.mm(po, po[0:64, 0:n], W2, W2[:, :], hidT, hidT[:, c0:c0 + n])
                    K.op('act', lambda e, c0=c0, n=n: e.copy(out=ko[:, c0:c0 + n], in_=po[0:64, 0:n]), reads=[po], writes=[ko])
                K.dma('sp', d['KcmpT'][gs, :], ko[:], reads=[ko])
            else:
                for c0 in range(0, NCP, 128):
                    K.mm(po, po[:, 0:64], hidT, hidT[:, c0:c0 + 128], W2, W2[:, :])
                    K.op('act', lambda e: e.copy(out=vo[:], in_=po[:, 0:64]), reads=[po], writes=[vo])
                    K.dma('sp', d['Vcmp'][c0:c0 + 128, gs], vo[:], reads=[vo])
    K.phase_end()


def attn_core(K, S, heads, B, load_head, units_fn, KR, s_extra, exp_bias, emit_masks, pv_extra, finalize, L=3):
    NQ = S // 512
    units = []
    for hi, h in enumerate(heads):
        for qg in range(NQ):
            us = units_fn(qg)
            for ui, u in enumerate(us):
                units.append((hi, h, qg, u, ui == 0, ui == len(us) - 1))
    load_head(0, heads[0])
    NU = len(units)
    deferred = []
    for i in range(NU + L):
        if deferred and deferred[0][0] <= i:
            deferred.pop(0)[1]()
        if i < NU:
            hi, h, qg, (kt, c0, c1), first, last = units[i]
            s = hi % 2
            p_s = B['ps'][i % len(B['ps'])]
            p_t = B['pt'][i % 4]
            KT = B['KT'][s]
            QT = B['QTsel'](s, kt) if 'QTsel' in B else B['QT'][s]
            ex = s_extra(h, qg, kt, c0, c1) if s_extra else None
            K.mm(p_s, p_s[:, c0:c1], KT, KT[0:KR, kt * 128:(kt + 1) * 128], QT, QT[0:KR, qg * 512 + c0:qg * 512 + c1],
                 start=True, stop=(ex is None))
            if ex is not None:
                K.mm(p_s, p_s[:, c0:c1], ex[0], ex[1], ex[2], ex[3], start=False, stop=True)
            bb, bap = exp_bias(h, kt) if exp_bias else (K.zero_t, K.zero_t[:, 0:1])
            K.op('act', lambda e, p_s=p_s, p_t=p_t, c0=c0, c1=c1, bap=bap: e.activation(
                out=p_t[:, c0:c1], in_=p_s[:, c0:c1], func=AF.Exp, scale=0.125, bias=bap), reads=[p_s, bb], writes=[p_t])
            emit_masks(p_t, h, qg, kt, c0, c1)
        if i - L >= 0:
            hi, h, qg, (kt, c0, c1), first, last = units[i - L]
            if first and qg == 0 and hi + 1 < len(heads):
                load_head(hi + 1, heads[hi + 1])
            s = hi % 2
            p_t = B['pt'][(i - L) % 4]
            gi = hi * NQ + qg
            p_o = B['po'][gi % 2]
            V = B['V'][s]
            K.mm(p_o, p_o[0:65, c0:c1], V, V[:, kt, 0:65], p_t, p_t[:, c0:c1], start=first, stop=last)
            if pv_extra:
                pv_extra(p_t, hi, h, qg, kt, c0, c1, first, last)
            if last:
                rest = finalize(hi, h, qg, gi, p_o)
                if rest is not None:
                    deferred.append((i + 2, rest))
    for _, rest in deferred:
        rest()


def causal_sel(K, p_t, c):
    K.op('pool', lambda e: e.affine_select(out=p_t[:, c:c + 128], in_=p_t[:, c:c + 128], pattern=[[1, 128]], compare_op=ALU.is_ge,
                                           fill=0.0, base=0, channel_multiplier=-1), reads=[p_t], writes=[p_t])


class NsaCommon:
    def __init__(self, K, d, S, nkt, br, out_name, n_ps=3):
        self.K, self.d, self.S, self.br, self.out_name = K, d, S, br, out_name
        NT_ = S // 128
        consts(K)
        K.zero_t = K.sb('zero', [128, 1], F32)
        K.op('pool', lambda e: e.memset(K.zero_t[:], 0.0), writes=[K.zero_t])
        self.identf = make_ident(K, F32, 'identf')
        self.gates = K.sb('gates', [128, NT_, 24], F32)
        K.dma('sp', self.gates[:], d['gates'].rearrange("(t p) c -> p t c", p=128), writes=[self.gates])
        self.B = dict(
            KT=[K.sb('nKT%d' % i, [128, nkt * 128], BF16) for i in range(2)],
            QT=[K.sb('nQT%d' % i, [128, S], BF16) for i in range(2)],
            V=[K.sb('nV%d' % i, [128, nkt, 65], BF16) for i in range(2)],
            ps=[K.ps('nps%d' % i, [128, 512], F32) for i in range(n_ps)],
            po=[K.ps('npo%d' % i, [128, 512], F32) for i in range(2)],
            pt=[K.sb('npt%d' % i, [128, 512], BF16) for i in range(4)],
        )
        for i in range(2):
            K.op('pool', lambda e, i=i: e.memset(self.B['V'][i][:, :, 64:65], 1.0), writes=[self.B['V'][i]])
            K.op('pool', lambda e, i=i: e.memset(self.B['KT'][i][64:128, :], 0.0), writes=[self.B['KT'][i]])
            K.op('pool', lambda e, i=i: e.memset(self.B['QT'][i][64:128, :], 0.0), writes=[self.B['QT'][i]])
        self.pq = K.ps('npq', [128, 4, 65], F32)
        self.osb = [K.sb('nosb%d' % i, [65, 512], F32) for i in range(2)]
        self.rc = [K.sb('nrc%d' % i, [128, 4], F32) for i in range(2)]
        self.mo = [K.sb('nmo%d' % i, [128, 4, 64], F32) for i in range(2)]

    def load_head(self, ksrc, vsrc, nk):
        K, d, B = self.K, self.d, self.B

        def f(hi, h):
            s, g = hi % 2, h // 4
            K.dma('sp', B['KT'][s][0:64, 0:nk], d[ksrc][g * 64:(g + 1) * 64, 0:nk], writes=[B['KT'][s]])
            K.dma('sp', B['QT'][s][0:64, :], d['QT1'][h * 64:(h + 1) * 64, :], writes=[B['QT'][s]])
            K.dma('sp', B['V'][s][:, :, 0:64], d[vsrc][0:nk, g * 64:(g + 1) * 64].rearrange("(t p) c -> p t c", p=128),
                  writes=[B['V'][s]])
        return f

    def finalize(self, hi, h, qg, gi, p_o, pre=None, defer_pre=False):
        K, d = self.K, self.d
        o_s, r_c, m_o, p_q = self.osb[gi % 2], self.rc[gi % 2], self.mo[gi % 2], self.pq
        K.op('act', lambda e: e.copy(out=o_s[:], in_=p_o[0:65, :]), reads=[p_o], writes=[o_s])
        if pre and not defer_pre:
            pre(o_s)

        def rest():
            if pre and defer_pre:
                pre(o_s)
            for j in range(4):
                K.tr(p_q, p_q[:, j, :], o_s, o_s[0:65, j * 128:(j + 1) * 128], self.identf, self.identf[0:65, 0:65])
            K.op('dve', lambda e: e.tensor_scalar(out=r_c[:], in0=p_q[:, :, 64], scalar1=1e-30, scalar2=None, op0=ALU.add),
                 reads=[p_q], writes=[r_c])
            K.op('dve', lambda e: e.reciprocal(out=r_c[:], in_=r_c[:]), reads=[r_c], writes=[r_c])
            col = h * 3 + self.br
            K.op('dve', lambda e: e.tensor_tensor(out=r_c[:], in0=r_c[:], in1=self.gates[:, qg * 4:(qg + 1) * 4, col], op=ALU.mult),
                 reads=[r_c, self.gates], writes=[r_c])
            for j in range(4):
                K.op('dve', lambda e, j=j: e.tensor_scalar(out=m_o[:, j, :], in0=p_q[:, j, 0:64], scalar1=r_c[:, j:j + 1], scalar2=None,
                                                           op0=ALU.mult), reads=[p_q, r_c], writes=[m_o])
            K.dma('pool', d[self.out_name][qg * 512:(qg + 1) * 512, h * 64:(h + 1) * 64].rearrange("(j p) c -> p j c", p=128),
                  m_o[:], reads=[m_o])
        return rest


def phase_nsa_win(K, d, S, heads=range(8)):
    K.phase_begin()
    C = NsaCommon(K, d, S, S // 128, 2, 'nsa2')

    def units_fn(qg):
        us = []
        for kt in range(max(0, 4 * qg - 4), 4 * qg + 4):
            i0 = max(kt, 4 * qg)
            i1 = min(kt + 4, 4 * qg + 3)
            us.append((kt, (i0 - 4 * qg) * 128, (i1 - 4 * qg + 1) * 128))
        return us

    def masks(p_t, h, qg, kt, c0, c1):
        if kt >= 4 * qg:
            causal_sel(K, p_t, (kt - 4 * qg) * 128)
        if kt + 4 <= 4 * qg + 3 and kt + 4 >= 4 * qg:
            c = (kt + 4 - 4 * qg) * 128
            K.op('pool', lambda e: e.affine_select(out=p_t[:, c:c + 128], in_=p_t[:, c:c + 128], pattern=[[-1, 128]], compare_op=ALU.is_gt,
                                                   fill=0.0, base=0, channel_multiplier=1), reads=[p_t], writes=[p_t])

    attn_core(K, S, list(heads), C.B, C.load_head('KwT', 'Vw', S), units_fn, 128, None, None, masks, None, C.finalize)
    K.phase_end()


def phase_nsa_slc(K, d, S, heads=range(8)):
    NT_ = S // 128
    NA = max(1, S // 4096)
    K.phase_begin()
    C = NsaCommon(K, d, S, NT_, 1, 'nsa1')
    QT2 = [[C.B['QT'][i] for i in range(2)]] + [[K.sb('nQTa%d_%d' % (a, i), [128, S], BF16) for i in range(2)] for a in range(1, NA)]
    for i in range(2):
        kt_ = C.B['KT'][i]
        K.op('pool', lambda e, kt_=kt_: e.memset(kt_[64:128, :], 262144.0), writes=[kt_])
        MB = min(64, S // 64)
        K.op('pool', lambda e, kt_=kt_: e.affine_select(
            out=kt_[64:128, :].rearrange("p (a m k) -> p a m k", m=MB, k=64), in_=kt_[64:128, :].rearrange("p (a m k) -> p a m k", m=MB, k=64),
            pattern=[[0, S // (MB * 64)], [1, MB], [0, 64]], compare_op=ALU.is_equal, fill=0.0, base=0,
            channel_multiplier=-1), reads=[kt_], writes=[kt_])
    C.B['QTsel'] = lambda s_, kt: QT2[kt // 32][s_]

    def load_head(hi, h):
        s_, g = hi % 2, h // 4
        B = C.B
        K.dma('sp', B['KT'][s_][0:64, :], d['KsT'][g * 64:(g + 1) * 64, :], writes=[B['KT'][s_]], nowaw=[B['KT'][s_]])
        for a in range(NA):
            qt = QT2[a][s_]
            K.dma('sp', qt[0:64, :], d['QT1'][h * 64:(h + 1) * 64, :], writes=[qt])
            K.dma('sp', qt[64:128, :], d['selT'][g, 64 * a:64 * a + 64, :], writes=[qt], nowaw=[qt])
        K.dma('sp', B['V'][s_][:, :, 0:64], d['Vs'][:, g * 64:(g + 1) * 64].rearrange("(t p) c -> p t c", p=128), writes=[B['V'][s_]])

    def units_fn(qg):
        return [(kt, 128 * max(kt - 4 * qg, 0), 512) for kt in range(4 * qg + 4)]

    def masks(p_t, h, qg, kt, c0, c1):
        if kt >= 4 * qg:
            causal_sel(K, p_t, (kt - 4 * qg) * 128)

    attn_core(K, S, list(heads), C.B, load_head, units_fn, 128, None, None, masks, None, C.finalize)
    K.phase_end()


def phase_nsa_cmp(K, d, S, heads=range(8)):
    NT_ = S // 128
    NQ = S // 512
    NCP = S // 16
    NKT = (NCP + 127) // 128
    NKP = NKT * 128
    K.phase_begin()
    C = NsaCommon(K, d, S, NKT, 0, 'nsa0', n_ps=2)
    identf = C.identf
    identb = make_ident(K, BF16, 'identb')
    onesf = K.sb('onesf', [128, 128], F32)
    K.op('pool', lambda e: e.memset(onesf[:], 1.0), writes=[onesf])
    ovf = K.sb('ovf', [128, 128], F32)
    ovf2 = K.sb('ovf2', [128, 128], F32)
    ovl = K.sb('ovl', [128, NKT, 128], BF16)
    for kt in range(NKT):
        K.op('pool', lambda e, kt=kt: e.affine_select(out=ovf[:], in_=onesf[:], pattern=[[64, 128]], compare_op=ALU.is_ge, fill=0.0,
                                                       base=63 - 2048 * kt, channel_multiplier=-16), reads=[onesf], writes=[ovf])
        K.op('pool', lambda e, kt=kt: e.affine_select(out=ovf2[:], in_=ovf[:], pattern=[[-64, 128]], compare_op=ALU.is_ge, fill=0.0,
                                                       base=2048 * kt + 31, channel_multiplier=16), reads=[ovf], writes=[ovf2])
        K.op('pool', lambda e, kt=kt: e.tensor_copy(out=ovl[:, kt, :], in_=ovf2[:]), reads=[ovf2], writes=[ovl], nowaw=[ovl])
    cst = K.sb('cst', [128, 8], F32)
    K.op('pool', lambda e: e.memset(cst[:, 0:1], 1e30), writes=[cst])
    K.op('pool', lambda e: e.memset(cst[:, 1:2], 2e30), writes=[cst])
    K.op('pool', lambda e: e.memset(cst[0:64, 2:3], 3e30), writes=[cst])
    K.op('pool', lambda e: e.memset(cst[64:128, 2:3], 0.0), writes=[cst])
    K.op('pool', lambda e: e.memset(cst[0:64, 3:4], 0.0), writes=[cst])
    K.op('pool', lambda e: e.memset(cst[64:128, 3:4], 1.0), writes=[cst])
    K.op('pool', lambda e: e.memset(cst[0:64, 4:5], -1e30), writes=[cst])
    K.op('pool', lambda e: e.memset(cst[64:128, 4:5], 4e30), writes=[cst])
    po2s = [K.ps('npo2_%d' % i, [128, 512], F32) for i in range(2)]
    pb = K.ps('npb', [128, 512], F32)
    impT = K.sb('impT', [128, S], F32)
    rrow = K.sb('rrow', [65, 512], F32)
    rbs = K.sb('rbs', [128, 512], F32)
    tmpi = K.sb('tmpi', [128, 512], F32)
    work = K.sb('work', [128, 128], F32)
    work2 = K.sb('work2', [128, 128], F32)
    m8 = K.sb('m8', [128, 16], F32)
    selm = K.sb('selm', [128, 128], F32)
    selb = K.sb('selb', [128, 128], F32)
    sTs = [K.sb('sTs%d' % i, [128, 128], BF16) for i in range(2)]

    def units_fn(qg):
        kmax = min(NKT - 1, (32 * qg + 30) // 128)
        return [(kt, 0, 512) for kt in range(kmax + 1)]

    def masks(p_t, h, qg, kt, c0, c1):
        K.op('pool', lambda e: e.affine_select(out=p_t[:], in_=p_t[:], pattern=[[1, 512]], compare_op=ALU.is_ge, fill=0.0,
                                               base=512 * qg - 2048 * kt - 31, channel_multiplier=-16), reads=[p_t], writes=[p_t])

    def pv_extra(p_t, hi, h, qg, kt, c0, c1, first, last):
        po2 = po2s[(hi * NQ + qg) % 2]
        K.mm(po2, po2[:], ovl, ovl[:, kt, :], p_t, p_t[:], start=first, stop=last)

    def topk_pass(g):
        for i in range(NT_):
            W = 2 * i + 2
            cols = slice(i * 128, (i + 1) * 128)
            K.op('pool', lambda e: e.memset(selm[:], 0.0), writes=[selm])
            if W <= 16:
                K.op('pool', lambda e, W=W: e.memset(selm[:, 0:W], 1.0), writes=[selm])
            else:
                K.tr(pb, pb[:, 0:128], impT, impT[:, cols], identf, identf[:])
                K.op('act', lambda e, W=W: e.copy(out=work[:, 0:W], in_=pb[:, 0:W]), reads=[pb], writes=[work])
                K.op('dve', lambda e: e.tensor_copy(out=work[:, 0:1], in_=cst[:, 0:1]), reads=[cst], writes=[work])
                K.op('dve', lambda e, i=i: e.tensor_copy(out=work[:, 2 * i:2 * i + 1], in_=cst[:, 1:2]), reads=[cst], writes=[work])
                K.op('dve', lambda e, i=i: e.tensor_scalar(out=work[:, 2 * i - 1:2 * i], in0=work[:, 2 * i - 1:2 * i], scalar1=cst[:, 3:4],
                                                           scalar2=cst[:, 2:3], op0=ALU.mult, op1=ALU.add), reads=[work, cst], writes=[work])
                K.op('dve', lambda e, i=i: e.tensor_copy(out=work[:, 2 * i + 1:2 * i + 2], in_=cst[:, 4:5]), reads=[cst], writes=[work])
                K.op('dve', lambda e, W=W: e.max(out=m8[:, 0:8], in_=work[:, 0:W]), reads=[work], writes=[m8])
                K.op('dve', lambda e, W=W: e.match_replace(out=work2[:, 0:W], in_to_replace=m8[:, 0:8], in_values=work[:, 0:W],
                                                           imm_value=-3.0e38), reads=[work, m8], writes=[work2])
                K.op('dve', lambda e, W=W: e.max(out=m8[:, 8:16], in_=work2[:, 0:W]), reads=[work2], writes=[m8])
                K.op('dve', lambda e, W=W: e.tensor_scalar(out=selm[:, 0:W], in0=work[:, 0:W], scalar1=m8[:, 15:16], scalar2=None,
                                                           op0=ALU.is_ge), reads=[work, m8], writes=[selm])
            K.op('dve', lambda e: e.tensor_scalar(out=selb[:], in0=selm[:], scalar1=-1.0, scalar2=None, op0=ALU.add), reads=[selm], writes=[selb])
            K.tr(pb, pb[:, 128:256], selb, selb[:], identf, identf[:])
            sT = sTs[i % 2]
            K.op('act', lambda e, sT=sT: e.copy(out=sT[:], in_=pb[:, 128:256]), reads=[pb], writes=[sT])
            K.dma('pool', d['selT'][g, :, cols], sT[:], reads=[sT])


    def finalize(hi, h, qg, gi, p_o):
        cs_ = slice(qg * 512, (qg + 1) * 512)
        po2 = po2s[gi % 2]

        def pre(o_s):
            K.op('dve', lambda e: e.tensor_scalar(out=rrow[64:65, :], in0=o_s[64:65, :], scalar1=1e-30, scalar2=None, op0=ALU.add),
                 reads=[o_s], writes=[rrow])
            K.op('dve', lambda e: e.reciprocal(out=rrow[64:65, :], in_=rrow[64:65, :]), reads=[rrow], writes=[rrow])
            K.mm(pb, pb[:], onesf, onesf[64:65, :], rrow, rrow[64:65, :])
            K.op('act', lambda e: e.copy(out=rbs[:], in_=pb[:]), reads=[pb], writes=[rbs])
            if h % 4 == 0:
                K.op('dve', lambda e: e.tensor_tensor(out=impT[:, cs_], in0=po2[:], in1=rbs[:], op=ALU.mult), reads=[po2, rbs], writes=[impT])
            else:
                K.op('dve', lambda e: e.tensor_tensor(out=tmpi[:], in0=po2[:], in1=rbs[:], op=ALU.mult), reads=[po2, rbs], writes=[tmpi])
                K.op('pool', lambda e: e.tensor_tensor(out=impT[:, cs_], in0=impT[:, cs_], in1=tmpi[:], op=ALU.add), reads=[impT, tmpi], writes=[impT])
        rest0 = C.finalize(hi, h, qg, gi, p_o, pre=pre, defer_pre=True)

        def rest():
            rest0()
            if h % 4 == 3 and qg == NQ - 1:
                topk_pass(h // 4)
        return rest

    attn_core(K, S, list(heads), C.B, C.load_head('KcmpT', 'Vcmp', NCP), units_fn, 128, None, None, masks, pv_extra, finalize)
    K.phase_end()


def phase_nsa_combine(K, d, S):
    NT_ = S // 128
    K.phase_begin()
    a = [K.sb('ca%d' % i, [128, 512], F32) for i in range(2)]
    b = [K.sb('cb%d' % i, [128, 512], F32) for i in range(2)]
    c = [K.sb('cc%d' % i, [128, 512], F32) for i in range(2)]
    o = [K.sb('co%d' % i, [128, 512], BF16) for i in range(2)]
    for t in range(NT_):
        rows = slice(t * 128, (t + 1) * 128)
        a_, b_, c_, o_ = a[t % 2], b[t % 2], c[t % 2], o[t % 2]
        K.dma('sp', a_[:], d['nsa0'][rows, :], writes=[a_])
        K.dma('sp', b_[:], d['nsa1'][rows, :], writes=[b_])
        K.dma('sp', c_[:], d['nsa2'][rows, :], writes=[c_])
        K.op('dve', lambda e: e.tensor_tensor(out=a_[:], in0=a_[:], in1=b_[:], op=ALU.add), reads=[a_, b_], writes=[a_])
        K.op('dve', lambda e: e.tensor_tensor(out=o_[:], in0=a_[:], in1=c_[:], op=ALU.add), reads=[a_, c_], writes=[o_])
        K.dma('pool', d['mixed'][rows, 0:512], o_[:], reads=[o_])
    K.phase_end()


def phase_lru(K, d, S):
    SEG = min(S, 2048)
    NS = S // SEG
    K.phase_begin()
    consts(K)
    identb = make_ident(K, BF16, 'identb')
    prm = K.sb('lprm', [128, 12, 4], F32)
    for j in range(4):
        K.dma('sp', prm[:, j, :], d['odd_rg_conv_w'][j, :].rearrange("(c p) -> p c", p=128), writes=[prm], nowaw=[prm],
              allow_slow_non_contiguous=True)
    for idx, nm in ((4, 'odd_rg_conv_b'), (5, 'odd_rg_ba'), (6, 'odd_rg_bx'), (7, 'odd_rg_lambda')):
        K.dma('sp', prm[:, idx, :], d[nm].rearrange("(c p) -> p c", p=128), writes=[prm], nowaw=[prm], allow_slow_non_contiguous=True)
    K.op('act', lambda e: e.activation(out=prm[:, 8, :], in_=prm[:, 7, :], func=AF.Exp, scale=-1.0), reads=[prm], writes=[prm])
    K.op('act', lambda e: e.activation(out=prm[:, 8, :], in_=prm[:, 8, :], func=AF.Ln, bias=K.one_t[:, 0:1]), reads=[prm, K.one_t], writes=[prm])
    K.op('dve', lambda e: e.tensor_scalar(out=prm[:, 8, :], in0=prm[:, 8, :], scalar1=-8.0, scalar2=None, op0=ALU.mult), reads=[prm], writes=[prm])
    wst = K.sb('lwst', [128, 128], F32)
    WA = K.sb('WA', [128, 4, 128], BF16)
    WX = K.sb('WX', [128, 4, 128], BF16)
    for Wt, nm in ((WA, 'odd_rg_wa'), (WX, 'odd_rg_wx')):
        for c in range(4):
            K.op('pool', lambda e: e.memset(wst[:], 0.0), writes=[wst])
            K.dma('sp', wst[0:64, 0:64], d[nm][2 * c, :, :], writes=[wst])
            K.dma('sp', wst[64:128, 64:128], d[nm][2 * c + 1, :, :], writes=[wst])
            K.op('dve', lambda e, Wt=Wt, c=c: e.tensor_copy(out=Wt[:, c, :], in_=wst[:]), reads=[wst], writes=[Wt], nowaw=[Wt])
    names = [('xpad', SEG + 3, F32), ('x', SEG, F32), ('xbf', SEG, BF16), ('r', SEG, F32), ('ig', SEG, F32), ('a', SEG, F32), ('t', SEG, F32),
             ('hh', SEG, F32), ('rg', SEG, F32), ('tmp', SEG, F32), ('gg', SEG, F32), ('ybf', SEG, BF16)]
    TL = [{nm: K.sb('l%s%d' % (nm, i), [128, w], dt) for nm, w, dt in names} for i in range(2)]
    hc = K.sb('lhc', [128, 1], F32)
    yo = [K.sb('lyo%d' % i, [128, 4, 128], BF16) for i in range(2)]
    pr = [K.ps('lpr%d' % i, [128, 512], F32) for i in range(2)]
    pi = [K.ps('lpi%d' % i, [128, 512], F32) for i in range(2)]
    ptr = [K.ps('lptr%d' % i, [128, 1024], BF16) for i in range(2)]
    nt = 0
    it = 0
    for c in range(4):
        K.op('pool', lambda e: e.memset(hc[:], 0.0), writes=[hc])
        for s in range(NS):
            cols = slice(s * SEG, (s + 1) * SEG)
            T_ = TL[it % 2]
            Tn = TL[(it + 1) % 2]
            it += 1
            xpad, x, xbf, r, ig, a, t, hh, rg, tmp, gg, ybf = (T_[k] for k in ('xpad', 'x', 'xbf', 'r', 'ig', 'a', 't', 'hh', 'rg', 'tmp', 'gg', 'ybf'))
            if s == 0:
                K.op('pool', lambda e, xpad=xpad: e.memset(xpad[:, 0:3], 0.0), writes=[xpad])
            K.dma('sp', xpad[:, 3:3 + SEG], d['RGX'][512 + c * 128:512 + (c + 1) * 128, cols], writes=[xpad], nowaw=[xpad])
            K.dma('sp', rg[:], d['RGX'][c * 128:(c + 1) * 128, cols], writes=[rg])
            K.op('dve', lambda e: e.tensor_scalar(out=x[:], in0=xpad[:, 0:SEG], scalar1=prm[:, 0, c:c + 1], scalar2=prm[:, 4, c:c + 1],
                                                  op0=ALU.mult, op1=ALU.add), reads=[xpad, prm], writes=[x])
            for j in range(1, 4):
                K.op('dve', lambda e, j=j: e.scalar_tensor_tensor(out=x[:], in0=xpad[:, j:j + SEG], scalar=prm[:, j, c:c + 1], in1=x[:],
                                                                   op0=ALU.mult, op1=ALU.add), reads=[xpad, prm, x], writes=[x])
            K.op('pool', lambda e, xpad=xpad, xnx=Tn['xpad']: e.tensor_copy(out=xnx[:, 0:3], in_=xpad[:, SEG:SEG + 3]), reads=[xpad], writes=[Tn['xpad']])
            K.op('pool', lambda e: e.tensor_copy(out=xbf[:], in_=x[:]), reads=[x], writes=[xbf])
            for pc in range(SEG // 512):
                ps_ = slice(pc * 512, (pc + 1) * 512)
                p1, p2 = pr[pc % 2], pi[pc % 2]
                K.mm(p1, p1[:], WA, WA[:, c, :], xbf, xbf[:, ps_])
                K.mm(p2, p2[:], WX, WX[:, c, :], xbf, xbf[:, ps_])
                K.op('act', lambda e, p1=p1, ps_=ps_: e.activation(out=r[:, ps_], in_=p1[:], func=AF.Sigmoid, bias=prm[:, 5, c:c + 1]),
                     reads=[p1, prm], writes=[r], nowaw=[r])
                K.op('act', lambda e, p2=p2, ps_=ps_: e.activation(out=ig[:, ps_], in_=p2[:], func=AF.Sigmoid, bias=prm[:, 6, c:c + 1]),
                     reads=[p2, prm], writes=[ig], nowaw=[ig])
            K.op('act', lambda e: e.activation(out=a[:], in_=r[:], func=AF.Exp, scale=prm[:, 8, c:c + 1]), reads=[r, prm], writes=[a])
            K.op('dve', lambda e: e.tensor_tensor(out=t[:], in0=a[:], in1=a[:], op=ALU.mult), reads=[a], writes=[t])
            K.op('dve', lambda e: e.tensor_scalar(out=t[:], in0=t[:], scalar1=-1.0, scalar2=1.0, op0=ALU.mult, op1=ALU.add), reads=[t], writes=[t])
            K.op('act', lambda e: e.activation(out=t[:], in_=t[:], func=AF.Sqrt), reads=[t], writes=[t])
            K.op('dve', lambda e: e.tensor_tensor(out=t[:], in0=t[:], in1=ig[:], op=ALU.mult), reads=[t, ig], writes=[t])
            K.op('dve', lambda e: e.tensor_tensor(out=t[:], in0=t[:], in1=x[:], op=ALU.mult), reads=[t, x], writes=[t])
            K.op('dve', lambda e: e.tensor_tensor_scan(out=hh[:], data0=a[:], data1=t[:], initial=hc[:, 0:1], op0=ALU.mult, op1=ALU.add),
                 reads=[a, t, hc], writes=[hh])
            K.op('pool', lambda e: e.tensor_copy(out=hc[:], in_=hh[:, SEG - 1:SEG]), reads=[hh], writes=[hc])
            gelu_tanh(K, rg, tmp, gg[:], gg)
            K.op('dve', lambda e: e.tensor_tensor(out=ybf[:], in0=hh[:], in1=gg[:], op=ALU.mult), reads=[hh, gg], writes=[ybf])
            for q4 in range(SEG // 512):
                p_ = ptr[nt % 2]
                y_ = yo[nt % 2]
                nt += 1
                for j in range(4):
                    tc_ = slice(q4 * 512 + j * 128, q4 * 512 + (j + 1) * 128)
                    K.tr(p_, p_[:, j * 128:(j + 1) * 128], ybf, ybf[:, tc_], identb, identb[:])
                K.op('act', lambda e, p_=p_, y_=y_: e.copy(out=y_[:], in_=p_[:, 0:512].rearrange("p (j c) -> p j c", j=4)), reads=[p_], writes=[y_])
                t0 = s * SEG + q4 * 512
                K.dma('pool', d['mixed'][t0:t0 + 512, 512 + c * 128:512 + (c + 1) * 128].rearrange("(j p) c -> p j c", p=128), y_[:],
                      reads=[y_])
    K.phase_end()


def all_phases():
    return [
        phase_A0, phase_Fprep, phase_fox, phase_Gprep, phase_gdn2,
        lambda K, d, S: phase_O(K, d, S, d['x'], d['even_w_out']),
        lambda K, d, S: phase_MLP(K, d, S, 0),
        lambda K, d, S: phase_PLE(K, d, S, 0),
        phase_Rprep, phase_A1, phase_cmp, phase_nsa_cmp, phase_nsa_slc, phase_nsa_win, phase_nsa_combine, phase_lru,
        lambda K, d, S: phase_O(K, d, S, d['h'], d['odd_w_out']),
        lambda K, d, S: phase_MLP(K, d, S, 1),
        lambda K, d, S: phase_PLE(K, d, S, 1, final=True),
    ]


SEQ = 8192
NCORES = 8


def kernel(**inputs):
    S = SEQ
    nc, K = build(S, all_phases())
    in_maps = []
    shared = {}
    for n, shp in INPUT_SHAPES.items():
        a = np.asarray(inputs[n])
        if list(a.shape) != shp:
            a = a.reshape(shp)
        shared[n] = np.ascontiguousarray(a.astype(np.float32, copy=False))
    x = np.asarray(inputs['x'])
    p = np.asarray(inputs['p'])
    pos = np.asarray(inputs['positions'])
    for b in range(NCORES):
        m = dict(shared)
        m['x'] = np.ascontiguousarray(x[b])
        m['p'] = np.ascontiguousarray(p[:, b])
        m['positions'] = np.ascontiguousarray(pos[b:b + 1]).astype(np.int32, copy=False)
        in_maps.append(m)
    res = run_bass_kernel_spmd(nc, in_maps, core_ids=list(range(NCORES)))
    out = np.stack([np.asarray(res.results[b]['out']) for b in range(NCORES)], axis=0)
    return out.astype(np.float32, copy=False)


def phase_gdn2(K, d, S, heads=range(4)):
    NC = S // 128
    NG = S // 512
    heads = list(heads)
    K.phase_begin()
    consts(K)
    identf = make_ident(K, F32, 'identf')
    onesf = K.sb('onesf', [128, 128], F32)
    K.op('pool', lambda e: e.memset(onesf[:], 1.0), writes=[onesf])
    maskT = K.sb('maskT', [128, 128], F32)
    maskTs = K.sb('maskTs', [128, 128], F32)
    zer = K.sb('zer', [128, 128], F32)
    K.op('pool', lambda e: e.memset(zer[:], 0.0), writes=[zer])
    K.op('pool', lambda e: e.affine_select(out=maskT[:], in_=zer[:], pattern=[[1, 128]], compare_op=ALU.is_ge, fill=NEGM,
                                           base=0, channel_multiplier=-1), reads=[zer], writes=[maskT])
    K.op('pool', lambda e: e.affine_select(out=maskTs[:], in_=zer[:], pattern=[[1, 128]], compare_op=ALU.is_gt, fill=NEGM,
                                           base=0, channel_multiplier=-1), reads=[zer], writes=[maskTs])
    nwb = K.sb('nwb', [128, 128], F32)
    K.dma('sp', nwb[:], d['even_gdn_norm_w'].partition_broadcast(128), writes=[nwb])
    tokS = K.sb('tokS', [128, NC, 16], F32)
    K.dma('sp', tokS[:].rearrange("p c q -> p (c q)"), d['tokSd'][:, :], writes=[tokS])
    c128 = K.sb('c128', [128, 1], F32)
    K.op('pool', lambda e: e.memset(c128[:], 128.0 * 1e-6), writes=[c128])
    onesb = K.sb('onesb', [128, 128], BF16)
    K.op('pool', lambda e: e.memset(onesb[:], 1.0), writes=[onesb])
    lnt = [K.sb('gln%d' % i, [128, 512], F32) for i in range(2)]
    ngl = [0]

    class H:
        pass
    HB = {}
    f32t = lambda nm, h: K.sb('%s_h%d' % (nm, h), [128, 128], F32)
    bf_t = lambda nm, h: K.sb('%s_h%d' % (nm, h), [128, 128], BF16)
    TB = {}
    for h in heads:
        for par in range(2):
            t = H()
            t.bk = K.ps('gbk%d_%d' % (h, par), [128, 512], F32)
            t.z = K.sb('gz_%d_%d' % (h, par), [128, 128], F32)
            t.nz = K.sb('nz_%d_%d' % (h, par), [128, 128], F32)
            for nm in ('Kbg', 'Kd', 'Vb', 'attnT', 'Xb', 'WT', 'vnew', 'ob'):
                setattr(t, nm, K.sb('%s_%d_%d' % (nm, h, par), [128, 128], BF16))
            for nm in ('dm1', 'dm2', 'Mt', 'U', 'osb', 'ojunk'):
                setattr(t, nm, K.sb('%s_%d_%d' % (nm, h, par), [128, 128], F32))
            t.P = [K.sb('P%d_%d_%d' % (i, h, par), [128, 128], F32) for i in range(2)]
            t.RX = K.sb('RX_%d_%d' % (h, par), [128, 256], F32)
            t.ost = K.sb('ost_%d_%d' % (h, par), [128, 4], F32)
            TB[(h, par)] = t
    for h in heads:
        b = H()
        b.gcrow = K.sb('gcrow%d' % h, [1, 512], F32)
        b.e1row = K.sb('e1row%d' % h, [1, 512], F32)
        b.gcb = [K.sb('gcb%d_%d' % (h, i), [128, 512], F32) for i in range(2)]
        b.sqk = K.sb('sqk%d' % h, [128, 512], BF16)
        b.sqq = K.sb('sqq%d' % h, [128, 512], BF16)
        b.eglrow = K.sb('eglrow%d' % h, [1, NC], F32)
        b.eglb = K.sb('eglb%d' % h, [128, NC], F32)
        b.Sf = f32t('Sf', h)
        b.Sb = bf_t('Sb', h)
        b.q_ = K.sb('gq%d' % h, [128, 512], F32)
        b.k_ = K.sb('gk%d' % h, [128, 512], F32)
        b.v_ = [K.sb('gv%d_%d' % (h, i), [128, 512], F32) for i in range(2)]
        b.knf = [K.sb('knf%d_%d' % (h, i), [128, 512], F32) for i in range(2)]
        b.knb = [K.sb('knb%d_%d' % (h, i), [128, 512], BF16) for i in range(2)]
        b.qnf = K.sb('qnf%d' % h, [128, 512], F32)
        b.qnb = [K.sb('qnb%d_%d' % (h, i), [128, 512], BF16) for i in range(2)]
        b.qgb = [K.sb('qgb%d_%d' % (h, i), [128, 512], BF16) for i in range(2)]
        HB[h] = b
    sl = lambda i: slice(i * 128, (i + 1) * 128)
    import os
    F32R = mybir.dt.float32r
    R = (lambda ap: ap.bitcast(F32R)) if os.environ.get('GDN_F32R', '0') == '1' else (lambda ap: ap)

    for h in heads:
        b = HB[h]
        K.dma('sp', b.eglrow[:], d['egld'][h:h + 1, :], writes=[b.eglrow])
        pg0 = TB[(h, 0)].bk
        K.mm(pg0, pg0[:, 0:NC], onesf, onesf[0:1, :], b.eglrow, b.eglrow[0:1, :])
        K.op('act', lambda e: e.copy(out=b.eglb[:], in_=pg0[:, 0:NC]), reads=[pg0], writes=[b.eglb])
        K.op('pool', lambda e: e.memset(b.Sf[:], 0.0), writes=[b.Sf])
        K.op('pool', lambda e: e.memset(b.Sb[:], 0.0), writes=[b.Sb])

    def prep_a(h, g):
        b = HB[h]
        cs = slice(g * 512, (g + 1) * 512)
        v_ = b.v_[g % 2]
        K.dma('sp', b.q_[:], d['GCT'][h * 128:(h + 1) * 128, cs], writes=[b.q_])
        K.dma('sp', b.k_[:], d['GCT'][512 + h * 128:512 + (h + 1) * 128, cs], writes=[b.k_])
        K.dma('sp', v_[:], d['GCT'][1024 + h * 128:1024 + (h + 1) * 128, cs], writes=[v_])
        K.dma('sp', b.gcrow[:], d['gcd'][h:h + 1, cs], writes=[b.gcrow])
        K.dma('sp', b.e1row[:], d['e1d'][h:h + 1, cs], writes=[b.e1row])
        K.op('pool', lambda e: e.tensor_tensor(out=b.sqk[:], in0=b.k_[:], in1=b.k_[:], op=ALU.mult), reads=[b.k_], writes=[b.sqk])
        K.op('pool', lambda e: e.tensor_tensor(out=b.sqq[:], in0=b.q_[:], in1=b.q_[:], op=ALU.mult), reads=[b.q_], writes=[b.sqq])

    def prep_b(h, g):
        b = HB[h]
        b0, b1 = TB[(h, 0)].bk, TB[(h, 1)].bk
        i = ngl[0] % 2
        ngl[0] += 1
        lk, lq = lnt[0], lnt[1]
        knf, knb, qnb, qgb, gcb = b.knf[g % 2], b.knb[g % 2], b.qnb[g % 2], b.qgb[g % 2], b.gcb[g % 2]
        K.mm(b0, b0[:], onesb, onesb[:], b.sqk, b.sqk[:])
        K.mm(b1, b1[:], onesb, onesb[:], b.sqq, b.sqq[:])
        K.op('act', lambda e: e.activation(out=lk[:], in_=b0[:], func=AF.Ln, bias=K.eps_t[:, 0:1]), reads=[b0, K.eps_t], writes=[lk])
        K.op('act', lambda e: e.activation(out=lq[:], in_=b1[:], func=AF.Ln, scale=128.0, bias=c128[:, 0:1]), reads=[b1, c128], writes=[lq])
        K.mm(b0, b0[:], onesf, onesf[0:1, :], b.gcrow, b.gcrow[0:1, :])
        K.mm(b1, b1[:], onesf, onesf[0:1, :], b.e1row, b.e1row[0:1, :])
        K.op('act', lambda e: e.activation(out=lk[:], in_=lk[:], func=AF.Exp, scale=-0.5), reads=[lk], writes=[lk])
        K.op('act', lambda e: e.activation(out=lq[:], in_=lq[:], func=AF.Exp, scale=-0.5), reads=[lq], writes=[lq])
        K.op('act', lambda e: e.copy(out=gcb[:], in_=b0[:]), reads=[b0], writes=[gcb])
        K.op('dve', lambda e: e.tensor_tensor(out=knf[:], in0=b.k_[:], in1=lk[:], op=ALU.mult), reads=[b.k_, lk], writes=[knf])
        K.op('pool', lambda e: e.tensor_copy(out=knb[:], in_=knf[:]), reads=[knf], writes=[knb])
        K.op('dve', lambda e: e.tensor_tensor(out=b.qnf[:], in0=b.q_[:], in1=lq[:], op=ALU.mult), reads=[b.q_, lq], writes=[b.qnf])
        K.op('pool', lambda e: e.tensor_copy(out=qnb[:], in_=b.qnf[:]), reads=[b.qnf], writes=[qnb])
        K.op('dve', lambda e: e.tensor_tensor(out=qgb[:], in0=b.qnf[:], in1=b1[:], op=ALU.mult), reads=[b.qnf, b1], writes=[qgb])

    class View:
        def __init__(self, t, hb):
            self._t, self._hb = t, hb

        def __getattr__(self, n):
            t = object.__getattribute__(self, '_t')
            if hasattr(t, n):
                return getattr(t, n)
            return getattr(object.__getattribute__(self, '_hb'), n)
    VW = {(h, par): View(TB[(h, par)], HB[h]) for h in heads for par in range(2)}

    def st1(h, g, tt):
        b = VW[(h, tt % 2)]
        c = g * 4 + tt
        ts_, tok = sl(tt), slice(c * 128, (c + 1) * 128)
        bk, v_ = b.bk, b.v_[g % 2]
        z = b.z
        K.dma('sp', z[:], d['GZ'][tok, h * 128:(h + 1) * 128], writes=[z])
        K.op('pool', lambda e: e.tensor_tensor(out=b.nz[:], in0=z[:], in1=nwb[:], op=ALU.mult), reads=[z, nwb], writes=[b.nz])
        beta_c, be_c, e2_c, gc_c = tokS[:, c, h:h + 1], tokS[:, c, 4 + h:5 + h], tokS[:, c, 8 + h:9 + h], tokS[:, c, 12 + h:13 + h]
        knf, knb, qnb, gcb = b.knf[g % 2], b.knb[g % 2], b.qnb[g % 2], b.gcb[g % 2]
        K.tr(bk, bk[:, 0:128], knf, knf[:, ts_], identf, identf[:])
        K.tr(bk, bk[:, 128:256], v_, v_[:, ts_], identf, identf[:])
        K.mm(bk, bk[:, 256:384], knb, knb[:, ts_], knb, knb[:, ts_])
        K.mm(bk, bk[:, 384:512], knb, knb[:, ts_], qnb, qnb[:, ts_])
        K.op('dve', lambda e: e.tensor_scalar(out=b.Kbg[:], in0=bk[:, 0:128], scalar1=be_c, scalar2=None, op0=ALU.mult), reads=[bk, tokS], writes=[b.Kbg])
        K.op('dve', lambda e: e.tensor_scalar(out=b.Kd[:], in0=bk[:, 0:128], scalar1=e2_c, scalar2=None, op0=ALU.mult), reads=[bk, tokS], writes=[b.Kd])
        K.op('dve', lambda e: e.tensor_scalar(out=b.Vb[:], in0=bk[:, 128:256], scalar1=beta_c, scalar2=None, op0=ALU.mult), reads=[bk, tokS], writes=[b.Vb])
        K.op('dve', lambda e: e.scalar_tensor_tensor(out=b.dm1[:], in0=gcb[:, ts_], scalar=gc_c, in1=maskT[:], op0=ALU.subtract, op1=ALU.add),
             reads=[gcb, tokS, maskT], writes=[b.dm1])
        K.op('dve', lambda e: e.scalar_tensor_tensor(out=b.dm2[:], in0=gcb[:, ts_], scalar=gc_c, in1=maskTs[:], op0=ALU.subtract, op1=ALU.add),
             reads=[gcb, tokS, maskTs], writes=[b.dm2])

    def st1b(h, g, tt):
        b = VW[(h, tt % 2)]
        bk = b.bk
        K.op('act', lambda e: e.activation(out=b.dm1[:], in_=b.dm1[:], func=AF.Exp), reads=[b.dm1], writes=[b.dm1])
        K.op('act', lambda e: e.activation(out=b.dm2[:], in_=b.dm2[:], func=AF.Exp), reads=[b.dm2], writes=[b.dm2])
        K.op('dve', lambda e: e.tensor_tensor(out=b.attnT[:], in0=bk[:, 384:512], in1=b.dm1[:], op=ALU.mult), reads=[bk, b.dm1], writes=[b.attnT])
        K.op('dve', lambda e: e.tensor_tensor(out=b.Mt[:], in0=bk[:, 256:384], in1=b.dm2[:], op=ALU.mult), reads=[bk, b.dm2], writes=[b.Mt])

    def st2(h, g, tt):
        b = VW[(h, tt % 2)]
        c = g * 4 + tt
        bk = b.bk
        beta_c = tokS[:, c, h:h + 1]
        K.tr(bk, bk[:, 0:128], b.Mt, b.Mt[:], identf, identf[:])
        K.op('dve', lambda e: e.tensor_scalar(out=b.P[0][:], in0=bk[:, 0:128], scalar1=beta_c, scalar2=-1.0, op0=ALU.mult, op1=ALU.mult),
             reads=[bk, tokS], writes=[b.P[0]])

    def st2b(h, g, tt):
        b = VW[(h, tt % 2)]
        bk = b.bk
        K.tr(bk, bk[:, 128:256], b.P[0], b.P[0][:], identf, identf[:])
        K.op('act', lambda e: e.copy(out=b.RX[:, 0:128], in_=bk[:, 128:256]), reads=[bk], writes=[b.RX])
        K.op('dve', lambda e: e.tensor_tensor(out=b.RX[:, 128:256], in0=bk[:, 128:256], in1=identf[:], op=ALU.add), reads=[bk, identf],
             writes=[b.RX], nowaw=[b.RX])

    def lvl_first(h, tt):
        b = VW[(h, tt % 2)]
        bk = b.bk
        K.mm(bk, bk[:, 256:384], b.RX, b.RX[:, 0:128], b.P[0], b.P[0][:])
        K.mm(bk, bk[:, 384:512], b.P[0], b.P[0][:], b.RX, b.RX[:, 0:128])
        K.op('act', lambda e: e.copy(out=b.P[1][:], in_=bk[:, 256:384]), reads=[bk], writes=[b.P[1]])
        K.op('act', lambda e: e.copy(out=b.RX[:, 0:128], in_=bk[:, 384:512]), reads=[bk], writes=[b.RX], nowaw=[b.RX])

    def lvl_mid(h, n, tt):
        b = VW[(h, tt % 2)]
        bk = b.bk
        Pn, Pn1 = b.P[n % 2], b.P[(n + 1) % 2]
        if n < 5:
            K.mm(bk, bk[:, 0:256], Pn, Pn[:], b.RX, b.RX[:, 0:256])
        else:
            K.mm(bk, bk[:, 128:256], Pn, Pn[:], b.RX, b.RX[:, 128:256])
        K.mm(bk, bk[:, 256:384], b.RX, b.RX[:, 0:128], Pn, Pn[:])
        K.op('dve', lambda e: e.tensor_tensor(out=b.RX[:, 128:256], in0=b.RX[:, 128:256], in1=bk[:, 128:256], op=ALU.add),
             reads=[b.RX, bk], writes=[b.RX], nowaw=[b.RX])
        if n < 5:
            K.op('act', lambda e: e.copy(out=b.RX[:, 0:128], in_=bk[:, 0:128]), reads=[bk], writes=[b.RX], nowaw=[b.RX])
        K.op('act', lambda e: e.copy(out=Pn1[:], in_=bk[:, 256:384]), reads=[bk], writes=[Pn1])

    def lvl_last(h, tt):
        b = VW[(h, tt % 2)]
        bk = b.bk
        K.mm(bk, bk[:, 0:128], b.P[0], b.P[0][:], b.RX, b.RX[:, 128:256])
        K.op('dve', lambda e: e.tensor_tensor(out=b.RX[:, 128:256], in0=b.RX[:, 128:256], in1=bk[:, 0:128], op=ALU.add),
             reads=[b.RX, bk], writes=[b.RX], nowaw=[b.RX])

    def st3(h, g, tt):
        b = VW[(h, tt % 2)]
        bk = b.bk
        K.op('pool', lambda e: e.tensor_copy(out=b.Xb[:], in_=b.RX[:, 128:256]), reads=[b.RX], writes=[b.Xb])
        K.mm(bk, bk[:, 0:128], b.Xb, b.Xb[:], b.Vb, b.Vb[:])
        K.mm(bk, bk[:, 128:256], b.Kbg, b.Kbg[:], b.Xb, b.Xb[:])
        K.op('act', lambda e: e.copy(out=b.U[:], in_=bk[:, 0:128]), reads=[bk], writes=[b.U])
        K.op('act', lambda e: e.copy(out=b.WT[:], in_=bk[:, 128:256]), reads=[bk], writes=[b.WT])

    def st4(h, g, tt):
        b = VW[(h, tt % 2)]
        c = g * 4 + tt
        ts_ = sl(tt)
        bk = b.bk
        K.mm(bk, bk[:, 256:384], b.WT, b.WT[:], b.Sb, b.Sb[:])
        K.op('dve', lambda e: e.tensor_tensor(out=b.vnew[:], in0=b.U[:], in1=bk[:, 256:384], op=ALU.subtract), reads=[b.U, bk], writes=[b.vnew])
        K.mm(bk, bk[:, 384:512], b.qgb[g % 2], b.qgb[g % 2][:, ts_], b.Sb, b.Sb[:], start=True, stop=False)
        K.mm(bk, bk[:, 384:512], b.attnT, b.attnT[:], b.vnew, b.vnew[:], start=False, stop=True)
        K.mm(bk, bk[:, 0:128], b.Kd, b.Kd[:], b.vnew, b.vnew[:])
        K.op('dve', lambda e: e.scalar_tensor_tensor(out=b.Sb[:], in0=b.Sf[:], scalar=b.eglb[:, c:c + 1], in1=bk[:, 0:128],
                                                     op0=ALU.mult, op1=ALU.add), reads=[b.Sf, b.eglb, bk], writes=[b.Sb])
        K.op('dve', lambda e: e.scalar_tensor_tensor(out=b.Sf[:], in0=b.Sf[:], scalar=b.eglb[:, c:c + 1], in1=bk[:, 0:128],
                                                     op0=ALU.mult, op1=ALU.add), reads=[b.Sf, b.eglb, bk], writes=[b.Sf])
        K.op('act', lambda e: e.copy(out=b.osb[:], in_=bk[:, 384:512]), reads=[bk], writes=[b.osb])

    def st5(h, g, tt):
        b = VW[(h, tt % 2)]
        c = g * 4 + tt
        tok = slice(c * 128, (c + 1) * 128)
        K.op('dve', lambda e: e.scalar_tensor_tensor(out=b.ojunk[:], in0=b.osb[:], scalar=1.0 / 128, in1=b.osb[:], op0=ALU.mult, op1=ALU.mult,
                                                     accum_out=b.ost[:, 0:1]), reads=[b.osb], writes=[b.ojunk, b.ost])
        K.op('act', lambda e: e.activation(out=b.ost[:, 1:2], in_=b.ost[:, 0:1], func=AF.Ln, bias=K.eps_t[:, 0:1]), reads=[b.ost, K.eps_t], writes=[b.ost])
        K.op('act', lambda e: e.activation(out=b.ost[:, 2:3], in_=b.ost[:, 1:2], func=AF.Exp, scale=-0.5), reads=[b.ost], writes=[b.ost])
        o_b = b.ob
        K.op('dve', lambda e: e.scalar_tensor_tensor(out=o_b[:], in0=b.osb[:], scalar=b.ost[:, 2:3], in1=b.nz[:], op0=ALU.mult, op1=ALU.mult),
             reads=[b.osb, b.ost, b.nz], writes=[o_b])
        K.dma('pool', d['mixed'][tok, 512 + h * 128:512 + (h + 1) * 128], o_b[:], reads=[o_b])

    for h in heads:
        prep_a(h, 0)
    for h in heads:
        prep_b(h, 0)
    for g in range(NG):
        if g + 1 < NG:
            for h in heads:
                prep_a(h, g + 1)
        for tp in range(0, 4, 2):
            tts = (tp, tp + 1)
            if tp == 2 and g + 1 < NG:
                for h in heads:
                    prep_b(h, g + 1)
            for stage in (st1, st1b, st2, st2b):
                for tt in tts:
                    for h in heads:
                        stage(h, g, tt)
            for tt in tts:
                for h in heads:
                    lvl_first(h, tt)
            for n in range(1, 6):
                for tt in tts:
                    for h in heads:
                        lvl_mid(h, n, tt)
            for tt in tts:
                for h in heads:
                    lvl_last(h, tt)
            for tt in tts:
                for h in heads:
                    st3(h, g, tt)
            for tt in tts:
                for h in heads:
                    st4(h, g, tt)
            for tt in tts:
                for h in heads:
                    st5(h, g, tt)
    K.phase_end()
```

```python
from contextlib import ExitStack
import concourse.bass as bass
import concourse.mybir as mybir

F32 = mybir.dt.float32
BF16 = mybir.dt.bfloat16
I32 = mybir.dt.int32
ALU = mybir.AluOpType
AF = mybir.ActivationFunctionType
AX = mybir.AxisListType
ENG = ['pe', 'act', 'dve', 'pool', 'sp']
NDS = 90


class Buf:
    def __init__(self, t, name):
        self.t = t
        self.name = name
        self.w = {}
        self.r = {}
        self.dsem = None
        self.psum = False
        self.wr = {}

    def __getitem__(self, k):
        return self.t[k]


class KCtx:
    def __init__(self, nc):
        self.nc = nc
        self.e = dict(pe=nc.tensor, act=nc.scalar, dve=nc.vector, pool=nc.gpsimd, sp=nc.sync)
        self.gstack = ExitStack()
        self.csem = {n: self.gstack.enter_context(nc.semaphore('c_' + n)) for n in ENG}
        self.cnt = {n: 0 for n in ENG}
        self.dsems = [self.gstack.enter_context(nc.semaphore('d%d' % i)) for i in range(NDS)]
        self.dcnt = [0] * NDS
        self.free_ds = list(range(NDS))
        self.seen = {n: {} for n in ENG}
        self.pstack = None
        self.phase_bufs = []
        self.nwaits = 0
        self.uid = 0

    def phase_begin(self):
        self.pstack = ExitStack()
        self.phase_bufs = []

    def phase_end(self):
        self.barrier()
        for b in self.phase_bufs:
            if b.dsem is not None:
                self.free_ds.append(b.dsem)
                b.dsem = None
        self.pstack.close()
        self.pstack = None

    def sb(self, name, shape, dt):
        self.uid += 1
        t = self.pstack.enter_context(self.nc.sbuf_tensor('%s_%d' % (name, self.uid), list(shape), dt))
        b = Buf(t, name)
        self.phase_bufs.append(b)
        return b

    def ps(self, name, shape, dt=F32):
        self.uid += 1
        t = self.pstack.enter_context(self.nc.psum_tensor('%s_%d' % (name, self.uid), list(shape), dt))
        b = Buf(t, name)
        b.psum = True
        self.phase_bufs.append(b)
        return b

    def _sem(self, k):
        return self.csem[k] if isinstance(k, str) else self.dsems[k]

    def _wait(self, eng, deps, force_self=False):
        for k, v in deps.items():
            if v <= self.seen[eng].get(k, 0):
                continue
            if k == eng and not force_self:
                if eng == 'pe':
                    continue
                if v < self.cnt[eng] - 1:
                    continue
            self.e[eng].wait_ge(self._sem(k), v)
            self.nwaits += 1
            self.seen[eng][k] = v

    @staticmethod
    def _merge(d, s):
        for k, v in s.items():
            if v > d.get(k, 0):
                d[k] = v

    def _deps(self, reads, writes, nowaw, eng=None):
        deps = {}
        for b in reads:
            self._merge(deps, b.w)
            if b.psum:
                self._merge(deps, {k: v for k, v in b.r.items() if k != eng})
        for b in writes:
            if b not in nowaw:
                self._merge(deps, b.w)
            else:
                self._merge(deps, b.wr)
            self._merge(deps, b.r)
        return deps

    def op(self, eng, fn, reads=(), writes=(), nowaw=()):
        self._wait(eng, self._deps(reads, writes, nowaw, eng))
        ins = fn(self.e[eng])
        self.cnt[eng] += 1
        c = self.cnt[eng]
        ins.then_inc(self.csem[eng], 1)
        for b in reads:
            if c > b.r.get(eng, 0):
                b.r[eng] = c
        for b in writes:
            if b in nowaw:
                b.w[eng] = c
            else:
                b.w = {eng: c}
                b.wr = {eng: c}
                b.r = {}
        return ins

    def dma(self, q, out, in_, reads=(), writes=(), nowaw=(), **kw):
        self._wait(q, self._deps(reads, writes, nowaw))
        b0 = (list(writes) + list(reads))[0]
        if b0.dsem is None:
            assert self.free_ds, "out of DMA semaphores"
            b0.dsem = self.free_ds.pop(0)
        i = b0.dsem
        self.e[q].dma_start(out=out, in_=in_, **kw).then_inc(self.dsems[i], 16)
        self.dcnt[i] += 16
        v = self.dcnt[i]
        for b in reads:
            b.r[i] = v
        for b in writes:
            if b in nowaw:
                b.w[i] = v
            else:
                b.w = {i: v}
                b.wr = {i: v}
                b.r = {}

    def barrier(self):
        deps = {n: self.cnt[n] for n in ENG if self.cnt[n] > 0}
        for i in range(NDS):
            if self.dcnt[i] > 0:
                deps[i] = self.dcnt[i]
        for eng in ENG:
            self._wait(eng, deps, force_self=True)

    def finish(self):
        self.barrier()
        self.gstack.close()

    def mm(self, out_b, out_ap, lhsT_b, lhsT_ap, rhs_b, rhs_ap, start=True, stop=True, extra_reads=()):
        return self.op('pe', lambda e: e.matmul(out_ap, lhsT_ap, rhs_ap, start=start, stop=stop),
                       reads=[lhsT_b, rhs_b] + list(extra_reads), writes=[out_b])

    def tr(self, out_b, out_ap, in_b, in_ap, id_b, id_ap):
        return self.op('pe', lambda e: e.transpose(out_ap, in_ap, id_ap), reads=[in_b, id_b], writes=[out_b])

import numpy as np
from concourse.bass_utils import run_bass_kernel_spmd

D = 1024
EPS = 1e-6


def make_ident(K, dt, name):
    ones = K.sb(name + '_ones', [128, 128], dt)
    ident = K.sb(name, [128, 128], dt)
    K.op('pool', lambda e: e.memset(ones[:], 1.0), writes=[ones])
    K.op('pool', lambda e: e.affine_select(out=ident[:], in_=ones[:], pattern=[[-1, 128]],
                                           compare_op=ALU.is_equal, fill=0.0, base=0, channel_multiplier=1),
         reads=[ones], writes=[ident])
    return ident


def load_weight_bf16(K, Wb, src, kchunks, ncols, gain=None, stage_name='wst', q='sp', cb=1024):
    cb = min(cb, ncols)
    st = [K.sb(stage_name + str(i), [128, cb], F32) for i in range(4)]
    n = 0
    for k in range(kchunks):
        for c0 in range(0, ncols, cb):
            c1 = min(ncols, c0 + cb)
            s = st[n % 4]
            eng = 'dve' if n % 2 == 0 else 'act'
            n += 1
            K.dma(q, s[:, 0:c1 - c0], src[k * 128:(k + 1) * 128, c0:c1], writes=[s])
            if eng == 'act':
                if gain is not None:
                    K.op('act', lambda e, s=s, k=k, c0=c0, c1=c1: e.activation(out=Wb[:, k, c0:c1], in_=s[:, 0:c1 - c0], func=AF.Copy,
                                                                               scale=gain[:, k:k + 1]),
                         reads=[s, gain], writes=[Wb], nowaw=[Wb])
                else:
                    K.op('act', lambda e, s=s, k=k, c0=c0, c1=c1: e.copy(out=Wb[:, k, c0:c1], in_=s[:, 0:c1 - c0]),
                         reads=[s], writes=[Wb], nowaw=[Wb])
            elif gain is not None:
                K.op(eng, lambda e, s=s, k=k, c0=c0, c1=c1: e.tensor_scalar(out=Wb[:, k, c0:c1], in0=s[:, 0:c1 - c0], scalar1=gain[:, k:k + 1],
                                                                scalar2=None, op0=ALU.mult),
                     reads=[s, gain], writes=[Wb], nowaw=[Wb])
            else:
                K.op(eng, lambda e, s=s, k=k, c0=c0, c1=c1: e.tensor_copy(out=Wb[:, k, c0:c1], in_=s[:, 0:c1 - c0]),
                     reads=[s], writes=[Wb], nowaw=[Wb])


class NormT:
    def __init__(self, K, identb, nbuf_x=3, gt=4, nbuf_n=2):
        self.K = K
        self.identb = identb
        self.nn = nbuf_n
        self.xb = [K.sb('xb%d' % i, [128, D], F32) for i in range(nbuf_x)]
        self.xn = [K.sb('xn%d' % i, [128, D], BF16) for i in range(nbuf_n)]
        self.st = [K.sb('nst%d' % i, [128, 4], F32) for i in range(nbuf_n)]
        self.pT = [K.ps('pT%d' % i, [128, D], BF16) for i in range(2)]
        self.hnT = [K.sb('hnT%d' % i, [128, 8, gt * 128], BF16) for i in range(2)]
        self.n = 0

    def prep(self, src_rows):
        K = self.K
        n = self.n
        self.n += 1
        xb = self.xb[n % len(self.xb)]
        xn = self.xn[n % self.nn]
        st = self.st[n % self.nn]
        K.dma('sp', xb[:], src_rows, writes=[xb])
        K.op('act', lambda e: e.activation(out=xn[:], in_=xb[:], func=AF.Square, accum_out=st[:, 0:1]),
             reads=[xb], writes=[xn, st])
        K.op('act', lambda e: e.activation(out=st[:, 1:2], in_=st[:, 0:1], func=AF.Sqrt, scale=1.0 / D, bias=K.eps_t[:, 0:1]),
             reads=[st, K.eps_t], writes=[st])
        K.op('dve', lambda e: e.reciprocal(out=st[:, 2:3], in_=st[:, 1:2]), reads=[st], writes=[st])
        K.op('dve', lambda e: e.tensor_scalar(out=xn[:], in0=xb[:], scalar1=st[:, 2:3], scalar2=None, op0=ALU.mult),
             reads=[xb, st], writes=[xn])
        return (n, xb)

    def finish(self, tok, hnT, j):
        K = self.K
        n, xb = tok
        xn = self.xn[n % self.nn]
        pT = self.pT[n % 2]
        for k in range(8):
            K.tr(pT, pT[:, k * 128:(k + 1) * 128], xn, xn[:, k * 128:(k + 1) * 128], self.identb, self.identb[:])
        K.op('act', lambda e: e.copy(out=hnT[:, :, j * 128:(j + 1) * 128],
                                     in_=pT[:].rearrange("p (k t) -> p k t", k=8)),
             reads=[pT], writes=[hnT], nowaw=[hnT])
        return xb

    def tile(self, src_rows, hnT, j, keep_x=None):
        return self.finish(self.prep(src_rows), hnT, j)


def consts(K):
    K.eps_t = K.sb('eps', [128, 1], F32)
    K.op('pool', lambda e: e.memset(K.eps_t[:], EPS), writes=[K.eps_t])
    K.one_t = K.sb('one', [128, 1], F32)
    K.op('pool', lambda e: e.memset(K.one_t[:], 1.0), writes=[K.one_t])
    K.mhalf_t = K.sb('mhalf', [128, 1], F32)
    K.op('pool', lambda e: e.memset(K.mhalf_t[:], -0.5), writes=[K.mhalf_t])


E_FQ, E_FK, E_FV, E_FF, E_GQ, E_GK, E_GV, E_GZ, E_GB, E_GA = 0, 512, 1024, 1536, 1544, 2056, 2568, 3080, 3592, 3596


def phase_A0(K, d, S):
    NG = S // 512
    K.phase_begin()
    consts(K)
    identb = make_ident(K, BF16, 'identb')
    gain = K.sb('gain', [128, 8], F32)
    K.dma('sp', gain[:], d['even_norm_mix'].rearrange("(k p) -> p k", p=128), writes=[gain],
          allow_slow_non_contiguous=True)
    cw = K.sb('cw', [128, 4, 12], F32)
    for j in range(4):
        K.dma('sp', cw[:, j, :], d['even_gdn_conv_w'][j, :].rearrange("(c p) -> p c", p=128), writes=[cw], nowaw=[cw],
              allow_slow_non_contiguous=True)
    W = K.sb('W', [128, 8, 3600], BF16)
    load_weight_bf16(K, W, d['even_w_in'], 8, 3600, gain=gain)
    NT = NormT(K, identb, nbuf_x=8, nbuf_n=8)
    xpad = [K.sb('xpad%d' % c, [128, 515], F32) for c in range(12)]
    for c in range(12):
        K.op('pool', lambda e, c=c: e.memset(xpad[c][:, 0:3], 0.0), writes=[xpad[c]])
    acc = [K.sb('acc%d' % i, [128, 512], F32) for i in range(2)]
    cout = [K.sb('cout%d' % i, [128, 512], F32) for i in range(3)]
    qk = [K.sb('qk%d' % i, [128, 512], BF16) for i in range(3)]
    vt = [K.sb('vt%d' % i, [128, 512], BF16) for i in range(2)]
    zt = [K.sb('zt%d' % i, [128, 512], F32) for i in range(2)]
    sm = [K.sb('sm%d' % i, [8, 512], F32) for i in range(2)]
    sm2 = [K.sb('smb%d' % i, [8, 512], F32) for i in range(2)]
    ptm = [K.ps('ptm%d' % i, [128, 512], F32) for i in range(2)]
    pfm = [K.ps('pfm%d' % i, [128, 512], F32) for i in range(3)]
    nfm = 0
    ntm = 0
    pend_silu = [None]
    toks = [NT.prep(d['x'][u * 128:(u + 1) * 128, :]) for u in range(4)]
    for g in range(NG):
        hnT = NT.hnT[g % 2]
        for j in range(4):
            t = g * 4 + j
            NT.finish(toks.pop(0), hnT, j)
            if j == 0:
                for u in range(t + 4, min(t + 8, NG * 4)):
                    toks.append(NT.prep(d['x'][u * 128:(u + 1) * 128, :]))
            for which, col0 in ((0, E_FV), (1, E_GZ)):
                p = ptm[ntm % 2]
                ntm += 1
                for k in range(8):
                    K.mm(p, p[:], hnT, hnT[:, k, j * 128:(j + 1) * 128], W, W[:, k, col0:col0 + 512],
                         start=(k == 0), stop=(k == 7))
                if which == 0:
                    o = vt[t % 2]
                    K.op('dve', lambda e, o=o, p=p: e.tensor_copy(out=o[:], in_=p[:]), reads=[p], writes=[o])
                    K.dma('pool', d['Vf'][t * 128:(t + 1) * 128, :], o[:], reads=[o])
                else:
                    o = zt[t % 2]
                    K.op('act', lambda e, o=o, p=p: e.activation(out=o[:], in_=p[:], func=AF.Silu), reads=[p], writes=[o])
                    K.dma('pool', d['GZ'][t * 128:(t + 1) * 128, :], o[:], reads=[o])
        chunks = ([('q', c, E_FQ + c * 128, 128) for c in range(4)] + [('k', c, E_FK + c * 128, 128) for c in range(4)]
                  + [('g', c, E_GQ + c * 128, 128) for c in range(12)] + [('s', 0, None, 16)])
        for kind, c, col0, M in chunks:
            p = pfm[nfm % 3]
            nfm += 1
            if kind == 's':
                for k in range(8):
                    K.mm(p, p[0:8, :], W, W[:, k, E_FF:E_FF + 8], hnT, hnT[:, k, :], start=(k == 0), stop=(k == 7))
                o = sm[g % 2]
                K.op('act', lambda e, o=o, p=p: e.copy(out=o[0:8, :], in_=p[0:8, :]), reads=[p], writes=[o])
                p2 = pfm[nfm % 3]
                nfm += 1
                for k in range(8):
                    K.mm(p2, p2[0:8, :], W, W[:, k, E_GB:E_GB + 8], hnT, hnT[:, k, :], start=(k == 0), stop=(k == 7))
                o2 = sm2[g % 2]
                K.op('act', lambda e, o2=o2, p2=p2: e.copy(out=o2[:], in_=p2[0:8, :]), reads=[p2], writes=[o2])
                K.dma('pool', d['smallT'][0:8, g * 512:(g + 1) * 512], o[0:8, :], reads=[o])
                K.dma('pool', d['smallT'][8:16, g * 512:(g + 1) * 512], o2[:], reads=[o2])
                continue
            for k in range(8):
                K.mm(p, p[:], W, W[:, k, col0:col0 + 128], hnT, hnT[:, k, :], start=(k == 0), stop=(k == 7))
            if kind in ('q', 'k'):
                o = qk[nfm % 3]
                K.op('dve', lambda e, o=o, p=p: e.tensor_copy(out=o[:], in_=p[:]), reads=[p], writes=[o])
                dst = d['QfT'] if kind == 'q' else d['KfT']
                K.dma('pool', dst[c * 128:(c + 1) * 128, g * 512:(g + 1) * 512], o[:], reads=[o])
            else:
                xp = xpad[c]
                K.op('act', lambda e, xp=xp, p=p: e.copy(out=xp[:, 3:515], in_=p[:]), reads=[p], writes=[xp], nowaw=[xp])
                a = acc[c % 2]
                K.op('dve', lambda e, a=a, xp=xp, c=c: e.tensor_scalar(out=a[:], in0=xp[:, 0:512], scalar1=cw[:, 0, c:c + 1],
                                                                        scalar2=None, op0=ALU.mult),
                     reads=[xp, cw], writes=[a])
                for j in range(1, 4):
                    K.op('dve', lambda e, a=a, xp=xp, c=c, j=j: e.scalar_tensor_tensor(
                        out=a[:], in0=xp[:, j:j + 512], scalar=cw[:, j, c:c + 1], in1=a[:], op0=ALU.mult, op1=ALU.add),
                        reads=[xp, cw, a], writes=[a])
                K.op('pool', lambda e, xp=xp: e.tensor_copy(out=xp[:, 0:3], in_=xp[:, 512:515]), reads=[xp], writes=[xp])
                o = cout[nfm % 3]

                def silu_store(o=o, a=a, c=c, g=g):
                    K.op('act', lambda e: e.activation(out=o[:], in_=a[:], func=AF.Silu), reads=[a], writes=[o])
                    K.dma('pool', d['GCT'][c * 128:(c + 1) * 128, g * 512:(g + 1) * 512], o[:], reads=[o])
                if pend_silu[0] is not None:
                    pend_silu[0]()
                pend_silu[0] = silu_store
    if pend_silu[0] is not None:
        pend_silu[0]()
    K.phase_end()


INPUT_SHAPES = {
    "even_norm_mix": [1024], "even_w_in": [1024, 3600], "even_fox_bf": [8], "even_gdn_conv_w": [4, 1536],
    "even_gdn_a_log": [4], "even_gdn_dt_bias": [4], "even_gdn_norm_w": [128], "even_w_out": [1024, 1024],
    "odd_norm_mix": [1024], "odd_w_in": [1024, 2328], "odd_cmp_k_pe": [32, 64], "odd_cmp_k_w1": [2048, 128],
    "odd_cmp_k_w2": [128, 64], "odd_cmp_v_pe": [32, 64], "odd_cmp_v_w1": [2048, 128], "odd_cmp_v_w2": [128, 64],
    "odd_rg_conv_w": [4, 512], "odd_rg_conv_b": [512], "odd_rg_wa": [8, 64, 64], "odd_rg_ba": [512],
    "odd_rg_wx": [8, 64, 64], "odd_rg_bx": [512], "odd_rg_lambda": [512], "odd_w_out": [1024, 1024],
    "mlp_norm": [2, 1024], "mlp_w_up": [2, 1024, 4096], "mlp_w_down": [2, 4096, 1024], "ple_norm": [2, 1024],
    "ple_w_gate": [2, 1024, 1024], "ple_w_proj": [2, 256, 1024], "final_norm": [1024],
}


def scratch_specs(S):
    return {
        "h": ([S, 1024], F32),
        "QfT": ([512, S], BF16), "KfT": ([512, S], BF16), "Vf": ([S, 512], BF16),
        "GZ": ([S, 512], F32), "GCT": ([1536, S], F32), "smallT": ([16, S], F32),
        "gcd": ([4, S], F32), "ngcd": ([4, S], F32), "e1d": ([4, S], F32), "egld": ([4, S // 128], F32),
        "tokSd": ([128, (S // 128) * 16], F32),
        "csd": ([128, (S // 128) * 16], F32), "QT1": ([512, S], BF16), "KcT": ([128, S], BF16), "VcT": ([128, S], BF16),
        "KsT": ([128, S], BF16), "KwT": ([128, S], BF16), "Vs": ([S, 128], BF16), "Vw": ([S, 128], BF16),
        "gates": ([S, 24], F32), "RGX": ([1024, S], F32),
        "KcmpT": ([128, S // 16], BF16), "Vcmp": ([S // 16, 128], BF16),
        "nsa0": ([S, 512], F32), "nsa1": ([S, 512], F32), "nsa2": ([S, 512], F32), "selT": ([2, 128, S], BF16),
        "FQ": ([8, 3, S], BF16), "nF": ([128, (S // 128) * 8], F32), "mixed": ([S, 1024], BF16),
    }


def build(S, phases, dbg=(), h_input=False):
    nc = bass.Bass("TRN2", target_bir_lowering=False)
    d = {}
    d['x'] = nc.dram_tensor("x", [S, 1024], F32, kind="ExternalInput").ap()
    d['p'] = nc.dram_tensor("p", [2, S, 256], F32, kind="ExternalInput").ap()
    d['positions'] = nc.dram_tensor("positions", [1, S], I32, kind="ExternalInput").ap()
    for n, shp in INPUT_SHAPES.items():
        d[n] = nc.dram_tensor(n, shp, F32, kind="ExternalInput").ap()
    d['out'] = nc.dram_tensor("out", [S, 1024], F32, kind="ExternalOutput").ap()
    for n, (shp, dt) in scratch_specs(S).items():
        kind = "ExternalOutput" if n in dbg else "Internal"
        if n == 'h' and h_input:
            kind = "ExternalInput"
        d[n] = nc.dram_tensor(n, shp, dt, kind=kind).ap()
    K = KCtx(nc)
    import os
    K.stop = int(os.environ.get('KSTOP', '0')) or None
    for ph in phases:
        ph(K, d, S)
    K.finish()
    return nc, K


def phase_Fprep(K, d, S):
    NT_ = S // 128
    K.phase_begin()
    consts(K)
    identf = make_ident(K, F32, 'identf')
    ff = K.sb('ff', [8, S], F32)
    sp = K.sb('spl', [8, S], F32)
    ones = K.sb('ones8', [8, S], F32)
    nF = K.sb('negF', [8, S], F32)
    bf = K.sb('bf', [8, 2], F32)
    K.dma('sp', ff[:], d['smallT'][0:8, :], writes=[ff])
    K.dma('sp', bf[:, 0:1], d['even_fox_bf'].rearrange("(h o) -> h o", o=1), writes=[bf])
    K.op('dve', lambda e: e.tensor_scalar(out=bf[:, 1:2], in0=bf[:, 0:1], scalar1=-1.0, scalar2=None, op0=ALU.mult),
         reads=[bf], writes=[bf])
    K.op('pool', lambda e: e.memset(ones[:], 1.0), writes=[ones])
    K.op('act', lambda e: e.activation(out=sp[:], in_=ff[:], func=AF.Exp, scale=-1.0, bias=bf[:, 1:2]),
         reads=[ff, bf], writes=[sp])
    K.op('act', lambda e: e.activation(out=sp[:], in_=sp[:], func=AF.Ln, scale=1.0, bias=K.one_t[0:8, 0:1]),
         reads=[sp, K.one_t], writes=[sp])
    K.op('dve', lambda e: e.tensor_tensor_scan(out=nF[:], data0=ones[:], data1=sp[:], initial=0.0,
                                               op0=ALU.mult, op1=ALU.add), reads=[ones, sp], writes=[nF])
    q8 = ff
    K.op('dve', lambda e: e.tensor_scalar(out=q8[:], in0=nF[:], scalar1=-8.0, scalar2=None, op0=ALU.mult),
         reads=[nF], writes=[q8])
    parts = [K.sb('fq%d' % i, [8, S], BF16) for i in range(3)]
    for i in range(3):
        K.op('dve', lambda e, i=i: e.tensor_copy(out=parts[i][:], in_=q8[:]), reads=[q8], writes=[parts[i]])
        if i < 2:
            K.op('dve', lambda e, i=i: e.tensor_tensor(out=q8[:], in0=q8[:], in1=parts[i][:], op=ALU.subtract),
                 reads=[q8, parts[i]], writes=[q8])
        K.dma('sp', d['FQ'][:, i, :], parts[i][:], reads=[parts[i]])
    pt = K.ps('ptr', [128, 512], F32)
    nft = K.sb('nft', [128, NT_ * 8], F32)
    for t0 in range(0, NT_, 64):
        n = min(64, NT_ - t0)
        for t in range(n):
            K.tr(pt, pt[:, t * 8:(t + 1) * 8], nF, nF[0:8, (t0 + t) * 128:(t0 + t + 1) * 128], identf, identf[0:8, 0:8])
        K.op('act', lambda e, t0=t0, n=n: e.copy(out=nft[:, t0 * 8:(t0 + n) * 8], in_=pt[:, 0:n * 8]), reads=[pt], writes=[nft])
    K.dma('sp', d['nF'][:, :], nft[:], reads=[nft])
    K.phase_end()


def phase_fox(K, d, S, heads=range(8), L=3):
    NT_ = S // 128
    NQ = S // 512
    K.phase_begin()
    consts(K)
    identf = make_ident(K, F32, 'identf')
    nF = K.sb('nF', [128, NT_ * 8], F32)
    K.dma('sp', nF[:], d['nF'][:, :], writes=[nF])
    KT = [K.sb('KT%d' % i, [67, S], BF16) for i in range(2)]
    QT = [K.sb('QT%d' % i, [67, S], BF16) for i in range(2)]
    V = [K.sb('V%d' % i, [128, NT_, 65], BF16) for i in range(2)]
    for i in range(2):
        K.op('pool', lambda e, i=i: e.memset(KT[i][64:67, :], 1.0), writes=[KT[i]])
        K.op('pool', lambda e, i=i: e.memset(V[i][:, :, 64:65], 1.0), writes=[V[i]])
    ps = [K.ps('ps%d' % i, [128, 512], F32) for i in range(3)]
    po = [K.ps('po%d' % i, [128, 512], F32) for i in range(2)]
    pq = [K.ps('pq%d' % i, [128, 4, 65], F32) for i in range(2)]
    pt = [K.sb('pt%d' % i, [128, 512], BF16) for i in range(4)]
    osb = [K.sb('osb%d' % i, [65, 512], F32) for i in range(2)]
    rc = [K.sb('rc%d' % i, [128, 4], F32) for i in range(2)]
    mo = [K.sb('mo%d' % i, [128, 4, 64], BF16) for i in range(2)]

    def load_head(hi, h):
        s = hi % 2
        K.dma('sp', KT[s][0:64, :], d['KfT'][h * 64:(h + 1) * 64, :], writes=[KT[s]])
        K.dma('sp', QT[s][0:64, :], d['QfT'][h * 64:(h + 1) * 64, :], writes=[QT[s]])
        K.dma('sp', QT[s][64:67, :], d['FQ'][h, :, :], writes=[QT[s]], nowaw=[QT[s]])
        K.dma('sp', V[s][:, :, 0:64], d['Vf'][:, h * 64:(h + 1) * 64].rearrange("(t p) c -> p t c", p=128),
              writes=[V[s]])

    heads = list(heads)
    units = []
    for hi, h in enumerate(heads):
        for qg in range(NQ):
            nk = 4 * qg + 4
            for kt in range(nk):
                units.append((hi, h, qg, kt, kt == nk - 1))
    load_head(0, heads[0])
    NU = len(units)
    for i in range(NU + L):
        if i < NU:
            hi, h, qg, kt, last = units[i]
            s = hi % 2
            r = kt - 4 * qg
            c0 = 128 * max(r, 0)
            p_s = ps[i % 3]
            p_t = pt[i % 4]
            K.mm(p_s, p_s[:, c0:512], KT[s], KT[s][0:67, kt * 128:(kt + 1) * 128], QT[s],
                 QT[s][0:67, qg * 512 + c0:(qg + 1) * 512])
            K.op('act', lambda e, p_s=p_s, p_t=p_t, c0=c0, kt=kt, h=h: e.activation(
                out=p_t[:, c0:512], in_=p_s[:, c0:512], func=AF.Exp, scale=0.125, bias=nF[:, kt * 8 + h:kt * 8 + h + 1]),
                reads=[p_s, nF], writes=[p_t])
            if r >= 0:
                K.op('pool', lambda e, p_t=p_t, c0=c0: e.affine_select(
                    out=p_t[:, c0:c0 + 128], in_=p_t[:, c0:c0 + 128], pattern=[[1, 128]], compare_op=ALU.is_ge,
                    fill=0.0, base=0, channel_multiplier=-1), reads=[p_t], writes=[p_t])
        if i - L >= 0:
            hi, h, qg, kt, last = units[i - L]
            if qg == 0 and kt == 0 and hi + 1 < len(heads):
                load_head(hi + 1, heads[hi + 1])
            s = hi % 2
            r = kt - 4 * qg
            c0 = 128 * max(r, 0)
            p_t = pt[(i - L) % 4]
            gi = hi * NQ + qg
            p_o = po[gi % 2]
            K.mm(p_o, p_o[0:65, c0:512], V[s], V[s][:, kt, 0:65], p_t, p_t[:, c0:512], start=(kt == 0), stop=last)
            if last:
                o_s = osb[gi % 2]
                p_q = pq[gi % 2]
                K.op('act', lambda e, o_s=o_s, p_o=p_o: e.copy(out=o_s[:], in_=p_o[0:65, :]), reads=[p_o], writes=[o_s])
                for j in range(4):
                    K.tr(p_q, p_q[:, j, :], o_s, o_s[0:65, j * 128:(j + 1) * 128], identf, identf[0:65, 0:65])
                r_c = rc[gi % 2]
                m_o = mo[gi % 2]
                K.op('dve', lambda e, r_c=r_c, p_q=p_q: e.reciprocal(out=r_c[:], in_=p_q[:, :, 64]), reads=[p_q], writes=[r_c])
                for j in range(4):
                    K.op('dve', lambda e, j=j, r_c=r_c, p_q=p_q, m_o=m_o: e.tensor_scalar(
                        out=m_o[:, j, :], in0=p_q[:, j, 0:64], scalar1=r_c[:, j:j + 1], scalar2=None, op0=ALU.mult),
                        reads=[p_q, r_c], writes=[m_o])
                K.dma('pool', d['mixed'][qg * 512:(qg + 1) * 512, h * 64:(h + 1) * 64].rearrange("(j p) c -> p j c", p=128),
                      m_o[:], reads=[m_o])
    K.phase_end()


NEGM = -30000.0


def phase_Gprep(K, d, S):
    NC = S // 128
    K.phase_begin()
    consts(K)
    identf = make_ident(K, F32, 'identf')
    A = K.sb('gpA', [4, S], F32)
    B = K.sb('gpB', [4, S], F32)
    C = K.sb('gpC', [4, S], F32)
    Dt = K.sb('gpD', [4, S], F32)
    E = K.sb('gpE', [4, S], F32)
    K.dma('sp', A[:], d['smallT'][8:12, :], writes=[A])
    K.dma('sp', B[:], d['smallT'][12:16, :], writes=[B])
    pr = K.sb('pr', [4, 4], F32)
    K.dma('sp', pr[:, 0:1], d['even_gdn_a_log'].rearrange("(h o) -> h o", o=1), writes=[pr])
    K.dma('sp', pr[:, 1:2], d['even_gdn_dt_bias'].rearrange("(h o) -> h o", o=1), writes=[pr], nowaw=[pr])
    K.op('act', lambda e: e.activation(out=pr[:, 2:3], in_=pr[:, 0:1], func=AF.Exp), reads=[pr], writes=[pr])
    K.op('pool', lambda e: e.memset(C[:], 1.0), writes=[C])
    K.op('pool', lambda e: e.memset(C[:].rearrange("p (c t) -> p c t", t=128)[:, :, 0:1], 0.0), writes=[C])
    K.op('act', lambda e: e.activation(out=Dt[:], in_=B[:], func=AF.Exp, bias=pr[:, 1:2]), reads=[B, pr], writes=[Dt])
    K.op('act', lambda e: e.activation(out=Dt[:], in_=Dt[:], func=AF.Ln, bias=K.one_t[0:4, 0:1]), reads=[Dt, K.one_t], writes=[Dt])
    K.op('dve', lambda e: e.tensor_scalar(out=B[:], in0=Dt[:], scalar1=pr[:, 2:3], scalar2=-1.0, op0=ALU.mult, op1=ALU.mult),
         reads=[Dt, pr], writes=[B])
    gc = E
    K.op('dve', lambda e: e.tensor_tensor_scan(out=gc[:], data0=C[:], data1=B[:], initial=0.0, op0=ALU.mult, op1=ALU.add),
         reads=[C, B], writes=[gc])
    K.dma('sp', d['gcd'][:, :], gc[:], reads=[gc])
    K.op('dve', lambda e: e.tensor_scalar(out=Dt[:], in0=gc[:], scalar1=-1.0, scalar2=None, op0=ALU.mult), reads=[gc], writes=[Dt])
    K.dma('sp', d['ngcd'][:, :], Dt[:], reads=[Dt])
    beta = A
    K.op('act', lambda e: e.activation(out=beta[:], in_=A[:], func=AF.Sigmoid), reads=[A], writes=[beta])
    e1 = B
    K.op('act', lambda e: e.activation(out=e1[:], in_=gc[:], func=AF.Exp), reads=[gc], writes=[e1])
    K.dma('sp', d['e1d'][:, :], e1[:], reads=[e1])
    be = C
    K.op('dve', lambda e: e.tensor_tensor(out=be[:], in0=beta[:], in1=e1[:], op=ALU.mult), reads=[beta, e1], writes=[be])
    gc3 = gc[:].rearrange("p (c t) -> p c t", t=128)
    e2 = Dt
    K.op('dve', lambda e: e.tensor_tensor(out=e2[:].rearrange("p (c t) -> p c t", t=128),
                                          in0=gc3[:, :, 127:128].to_broadcast([4, NC, 128]), in1=gc3, op=ALU.subtract),
         reads=[gc], writes=[e2])
    K.op('act', lambda e: e.activation(out=e2[:], in_=e2[:], func=AF.Exp), reads=[e2], writes=[e2])
    egl = K.sb('egl', [4, NC], F32)
    K.op('act', lambda e: e.activation(out=egl[:], in_=gc3[:, :, 127], func=AF.Exp), reads=[gc], writes=[egl])
    K.dma('sp', d['egld'][:, :], egl[:], reads=[egl])
    pt = [K.ps('ptg%d' % i, [128, 32, 16], F32) for i in range(2)]
    tokS = K.sb('tokS', [128, NC, 16], F32)
    K.op('pool', lambda e: e.memset(tokS[:], 0.0), writes=[tokS])
    for t0 in range(0, NC, 32):
        n = min(32, NC - t0)
        p = pt[(t0 // 32) % 2]
        for t in range(n):
            for qi, src in enumerate((beta, be, e2, gc)):
                K.tr(p, p[:, t, qi * 4:(qi + 1) * 4], src, src[0:4, (t0 + t) * 128:(t0 + t + 1) * 128], identf, identf[0:4, 0:4])
        K.op('act', lambda e, p=p, t0=t0, n=n: e.copy(out=tokS[:, t0:t0 + n, 0:16], in_=p[:, 0:n, 0:16]), reads=[p], writes=[tokS],
             nowaw=[tokS])
    K.dma('sp', d['tokSd'][:, :], tokS[:].rearrange("p c q -> p (c q)"), reads=[tokS])
    K.phase_end()


class StopPhase(Exception):
    pass


def chk(K, n):
    if getattr(K, 'stop', None) == n:
        raise StopPhase()


def phase_gdn(K, d, S, heads=range(4)):
    try:
        _phase_gdn(K, d, S, heads)
    except StopPhase:
        pass
    K.phase_end()


def _phase_gdn(K, d, S, heads=range(4)):
    NC = S // 128
    NG = S // 512
    K.phase_begin()
    consts(K)
    identf = make_ident(K, F32, 'identf')
    onesf = K.sb('onesf', [128, 128], F32)
    K.op('pool', lambda e: e.memset(onesf[:], 1.0), writes=[onesf])
    maskT = K.sb('maskT', [128, 128], F32)
    maskTs = K.sb('maskTs', [128, 128], F32)
    zer = K.sb('zer', [128, 128], F32)
    K.op('pool', lambda e: e.memset(zer[:], 0.0), writes=[zer])
    K.op('pool', lambda e: e.affine_select(out=maskT[:], in_=zer[:], pattern=[[1, 128]], compare_op=ALU.is_ge, fill=NEGM,
                                           base=0, channel_multiplier=-1), reads=[zer], writes=[maskT])
    K.op('pool', lambda e: e.affine_select(out=maskTs[:], in_=zer[:], pattern=[[1, 128]], compare_op=ALU.is_gt, fill=NEGM,
                                           base=0, channel_multiplier=-1), reads=[zer], writes=[maskTs])
    nwb = K.sb('nwb', [128, 128], F32)
    K.dma('sp', nwb[:], d['even_gdn_norm_w'].partition_broadcast(128), writes=[nwb])
    tokS = K.sb('tokS', [128, NC, 16], F32)
    K.dma('sp', tokS[:].rearrange("p c q -> p (c q)"), d['tokSd'][:, :], writes=[tokS])
    c128 = K.sb('c128', [128, 1], F32)
    K.op('pool', lambda e: e.memset(c128[:], 128.0 * 1e-6), writes=[c128])
    G2L = K.sb('G2L', [2, S], F32)
    G2R = K.sb('G2R', [2, S], F32)
    e1row = K.sb('e1row', [1, S], F32)
    eglrow = K.sb('eglrow', [1, NC], F32)
    eglb = K.sb('eglb', [128, NC], F32)
    Sst = K.sb('Sst', [128, 128], F32)
    qT = [K.sb('gqT%d' % i, [128, 512], F32) for i in range(2)]
    kT = [K.sb('gkT%d' % i, [128, 512], F32) for i in range(2)]
    vT = [K.sb('gvT%d' % i, [128, 512], F32) for i in range(2)]
    sq = K.sb('gsq', [128, 512], F32)
    rr = K.sb('grr', [128, 512], F32)
    knT = K.sb('knT', [128, 512], F32)
    qnT = K.sb('qnT', [128, 512], F32)
    qgT = K.sb('qgT', [128, 512], F32)
    zt = [K.sb('gz%d' % i, [128, 128], F32) for i in range(2)]
    nz = K.sb('nz', [128, 128], F32)
    Kbg = K.sb('Kbg', [128, 128], F32)
    Kd = K.sb('Kd', [128, 128], F32)
    Vb = K.sb('Vb', [128, 128], F32)
    dm1 = K.sb('dm1', [128, 128], F32)
    dm2 = K.sb('dm2', [128, 128], F32)
    attnT = K.sb('attnT', [128, 128], F32)
    Mt = K.sb('Mt', [128, 128], F32)
    P = [K.sb('Pn%d' % i, [128, 128], F32) for i in range(2)]
    PT = [K.sb('PTn%d' % i, [128, 128], F32) for i in range(2)]
    X = K.sb('Xn', [128, 128], F32)
    U = K.sb('Un', [128, 128], F32)
    WT = K.sb('WTn', [128, 128], F32)
    vnew = K.sb('vnew', [128, 128], F32)
    ost = K.sb('gost', [128, 4], F32)
    ojunk = K.sb('gojunk', [128, 128], F32)
    ob = [K.sb('gob%d' % i, [128, 128], BF16) for i in range(2)]
    bA = K.ps('bA', [128, 512], F32)
    bB = K.ps('bB', [128, 512], F32)
    bC = K.ps('bC', [128, 512], F32)
    bD = K.ps('bD', [128, 512], F32)
    bE = K.ps('bE', [128, 512], F32)
    bF = K.ps('bF', [128, 512], F32)
    bG = K.ps('bG', [128, 512], F32)
    bH = K.ps('bH', [128, 512], F32)
    sl = lambda i: slice(i * 128, (i + 1) * 128)
    chk(K, 1)
    for h in heads:
        K.op('pool', lambda e: e.memset(G2L[:], 1.0), writes=[G2L])
        K.op('pool', lambda e: e.memset(G2R[:], 1.0), writes=[G2R])
        K.dma('sp', G2L[0:1, :], d['ngcd'][h:h + 1, :], writes=[G2L])
        K.dma('sp', G2R[1:2, :], d['gcd'][h:h + 1, :], writes=[G2R])
        K.dma('sp', e1row[:], d['e1d'][h:h + 1, :], writes=[e1row])
        K.dma('sp', eglrow[:], d['egld'][h:h + 1, :], writes=[eglrow])
        K.mm(bA, bA[:, 0:NC], onesf, onesf[0:1, :], eglrow, eglrow[0:1, :])
        K.op('act', lambda e: e.copy(out=eglb[:], in_=bA[:, 0:NC]), reads=[bA], writes=[eglb])
        K.op('pool', lambda e: e.memset(Sst[:], 0.0), writes=[Sst])
        chk(K, 2)
        for g in range(NG):
            cs = slice(g * 512, (g + 1) * 512)
            q_, k_, v_ = qT[g % 2], kT[g % 2], vT[g % 2]
            K.dma('sp', q_[:], d['GCT'][h * 128:(h + 1) * 128, cs], writes=[q_])
            K.dma('sp', k_[:], d['GCT'][512 + h * 128:512 + (h + 1) * 128, cs], writes=[k_])
            K.dma('sp', v_[:], d['GCT'][1024 + h * 128:1024 + (h + 1) * 128, cs], writes=[v_])
            K.op('act', lambda e: e.activation(out=sq[:], in_=k_[:], func=AF.Square), reads=[k_], writes=[sq])
            K.mm(bA, bA[:], onesf, onesf[:], sq, sq[:])
            K.op('act', lambda e: e.activation(out=rr[:], in_=bA[:], func=AF.Sqrt, bias=K.eps_t[:, 0:1]), reads=[bA, K.eps_t], writes=[rr])
            K.op('dve', lambda e: e.reciprocal(out=rr[:], in_=rr[:]), reads=[rr], writes=[rr])
            K.op('dve', lambda e: e.tensor_tensor(out=knT[:], in0=k_[:], in1=rr[:], op=ALU.mult), reads=[k_, rr], writes=[knT])
            K.op('act', lambda e: e.activation(out=sq[:], in_=q_[:], func=AF.Square), reads=[q_], writes=[sq])
            K.mm(bA, bA[:], onesf, onesf[:], sq, sq[:])
            K.op('act', lambda e: e.activation(out=rr[:], in_=bA[:], func=AF.Sqrt, scale=128.0, bias=c128[:, 0:1]),
                 reads=[bA, c128], writes=[rr])
            K.op('dve', lambda e: e.reciprocal(out=rr[:], in_=rr[:]), reads=[rr], writes=[rr])
            K.op('dve', lambda e: e.tensor_tensor(out=qnT[:], in0=q_[:], in1=rr[:], op=ALU.mult), reads=[q_, rr], writes=[qnT])
            K.mm(bA, bA[:], onesf, onesf[0:1, :], e1row, e1row[0:1, cs])
            K.op('dve', lambda e: e.tensor_tensor(out=qgT[:], in0=qnT[:], in1=bA[:], op=ALU.mult), reads=[qnT, bA], writes=[qgT])
            chk(K, 3)
            for tt in range(4):
                c = g * 4 + tt
                ts_ = sl(tt)
                tok = slice(c * 128, (c + 1) * 128)
                z = zt[c % 2]
                K.dma('sp', z[:], d['GZ'][tok, h * 128:(h + 1) * 128], writes=[z])
                K.op('pool', lambda e, z=z: e.tensor_tensor(out=nz[:], in0=z[:], in1=nwb[:], op=ALU.mult), reads=[z, nwb], writes=[nz])
                beta_c = tokS[:, c, h:h + 1]
                be_c = tokS[:, c, 4 + h:5 + h]
                e2_c = tokS[:, c, 8 + h:9 + h]
                K.tr(bB, bB[:, 0:128], knT, knT[:, ts_], identf, identf[:])
                K.tr(bB, bB[:, 128:256], v_, v_[:, ts_], identf, identf[:])
                K.op('dve', lambda e: e.tensor_scalar(out=Kbg[:], in0=bB[:, 0:128], scalar1=be_c, scalar2=None, op0=ALU.mult),
                     reads=[bB, tokS], writes=[Kbg])
                K.op('act', lambda e: e.activation(out=Kd[:], in_=bB[:, 0:128], func=AF.Copy, scale=e2_c), reads=[bB, tokS], writes=[Kd])
                K.op('dve', lambda e: e.tensor_scalar(out=Vb[:], in0=bB[:, 128:256], scalar1=beta_c, scalar2=None, op0=ALU.mult),
                     reads=[bB, tokS], writes=[Vb])
                chk(K, 4)
                K.mm(bC, bC[:, 0:128], knT, knT[:, ts_], knT, knT[:, ts_])
                K.mm(bC, bC[:, 128:256], knT, knT[:, ts_], qnT, qnT[:, ts_])
                K.mm(bC, bC[:, 256:384], G2L, G2L[0:2, tok], G2R, G2R[0:2, tok])
                K.op('dve', lambda e: e.tensor_tensor(out=dm1[:], in0=bC[:, 256:384], in1=maskT[:], op=ALU.add), reads=[bC, maskT], writes=[dm1])
                K.op('dve', lambda e: e.tensor_tensor(out=dm2[:], in0=bC[:, 256:384], in1=maskTs[:], op=ALU.add), reads=[bC, maskTs], writes=[dm2])
                K.op('act', lambda e: e.activation(out=dm1[:], in_=dm1[:], func=AF.Exp), reads=[dm1], writes=[dm1])
                K.op('act', lambda e: e.activation(out=dm2[:], in_=dm2[:], func=AF.Exp), reads=[dm2], writes=[dm2])
                K.op('dve', lambda e: e.tensor_tensor(out=attnT[:], in0=bC[:, 128:256], in1=dm1[:], op=ALU.mult), reads=[bC, dm1], writes=[attnT])
                K.op('dve', lambda e: e.tensor_tensor(out=Mt[:], in0=bC[:, 0:128], in1=dm2[:], op=ALU.mult), reads=[bC, dm2], writes=[Mt])
                chk(K, 5)
                K.tr(bD, bD[:, 0:128], Mt, Mt[:], identf, identf[:])
                K.op('dve', lambda e: e.tensor_scalar(out=P[0][:], in0=bD[:, 0:128], scalar1=beta_c, scalar2=-1.0, op0=ALU.mult, op1=ALU.mult),
                     reads=[bD, tokS], writes=[P[0]])
                K.tr(bD, bD[:, 128:256], P[0], P[0][:], identf, identf[:])
                K.op('act', lambda e: e.copy(out=PT[0][:], in_=bD[:, 128:256]), reads=[bD], writes=[PT[0]])
                K.op('dve', lambda e: e.tensor_tensor(out=X[:], in0=bD[:, 128:256], in1=identf[:], op=ALU.add), reads=[bD, identf], writes=[X])
                for n in range(6):
                    a, b = n % 2, (n + 1) % 2
                    K.mm(bE, bE[:, 128:256], PT[a], PT[a][:], P[a], P[a][:])
                    K.mm(bE, bE[:, 256:384], P[a], P[a][:], PT[a], PT[a][:])
                    K.op('act', lambda e, b=b: e.copy(out=P[b][:], in_=bE[:, 128:256]), reads=[bE], writes=[P[b]])
                    K.op('act', lambda e, b=b: e.copy(out=PT[b][:], in_=bE[:, 256:384]), reads=[bE], writes=[PT[b]])
                    K.mm(bF, bF[:, 0:128], P[b], P[b][:], X, X[:])
                    K.op('dve', lambda e: e.tensor_tensor(out=X[:], in0=X[:], in1=bF[:, 0:128], op=ALU.add), reads=[X, bF], writes=[X])
                chk(K, 6)
                K.mm(bG, bG[:, 0:128], X, X[:], Vb, Vb[:])
                K.mm(bG, bG[:, 128:256], Kbg, Kbg[:], X, X[:])
                K.op('act', lambda e: e.copy(out=U[:], in_=bG[:, 0:128]), reads=[bG], writes=[U])
                K.op('act', lambda e: e.copy(out=WT[:], in_=bG[:, 128:256]), reads=[bG], writes=[WT])
                chk(K, 7)
                K.mm(bH, bH[:, 0:128], WT, WT[:], Sst, Sst[:])
                K.op('dve', lambda e: e.tensor_tensor(out=vnew[:], in0=U[:], in1=bH[:, 0:128], op=ALU.subtract), reads=[U, bH], writes=[vnew])
                K.mm(bH, bH[:, 128:256], qgT, qgT[:, ts_], Sst, Sst[:], start=True, stop=False)
                K.mm(bH, bH[:, 128:256], attnT, attnT[:], vnew, vnew[:], start=False, stop=True)
                K.mm(bH, bH[:, 256:384], Kd, Kd[:], vnew, vnew[:])
                K.op('dve', lambda e, c=c: e.scalar_tensor_tensor(out=Sst[:], in0=Sst[:], scalar=eglb[:, c:c + 1], in1=bH[:, 256:384],
                                                                   op0=ALU.mult, op1=ALU.add), reads=[Sst, eglb, bH], writes=[Sst])
                chk(K, 8)
                K.op('act', lambda e: e.activation(out=ojunk[:], in_=bH[:, 128:256], func=AF.Square, accum_out=ost[:, 0:1]),
                     reads=[bH], writes=[ojunk, ost])
                K.op('act', lambda e: e.activation(out=ost[:, 1:2], in_=ost[:, 0:1], func=AF.Sqrt, scale=1.0 / 128, bias=K.eps_t[:, 0:1]),
                     reads=[ost, K.eps_t], writes=[ost])
                K.op('dve', lambda e: e.reciprocal(out=ost[:, 2:3], in_=ost[:, 1:2]), reads=[ost], writes=[ost])
                o_b = ob[c % 2]
                K.op('dve', lambda e, o_b=o_b: e.scalar_tensor_tensor(out=o_b[:], in0=bH[:, 128:256], scalar=ost[:, 2:3], in1=nz[:],
                                                                       op0=ALU.mult, op1=ALU.mult), reads=[bH, ost, nz], writes=[o_b])
                K.dma('pool', d['mixed'][tok, 512 + h * 128:512 + (h + 1) * 128], o_b[:], reads=[o_b])


def load_gain(K, name, src1d):
    g = K.sb(name, [128, 8], F32)
    K.dma('sp', g[:], src1d.rearrange("(k p) -> p k", p=128), writes=[g], allow_slow_non_contiguous=True)
    return g


def phase_O(K, d, S, src, w_out, dst='h'):
    NT_ = S // 128
    K.phase_begin()
    consts(K)
    identb = make_ident(K, BF16, 'identb')
    W = K.sb('Wo', [128, 8, 1024], BF16)
    load_weight_bf16(K, W, w_out, 8, 1024)
    mx = [K.sb('mx%d' % i, [128, 1024], BF16) for i in range(2)]
    xr = [K.sb('xr%d' % i, [128, 1024], F32) for i in range(2)]
    mT = [K.sb('mT%d' % i, [128, 8, 128], BF16) for i in range(2)]
    ho = [K.sb('ho%d' % i, [128, 1024], F32) for i in range(2)]
    pT = [K.ps('opT%d' % i, [128, 1024], BF16) for i in range(2)]
    po = [K.ps('opo%d' % i, [128, 512], F32) for i in range(4)]
    for t in range(NT_):
        rows = slice(t * 128, (t + 1) * 128)
        m_, x_, mT_, h_, p_ = mx[t % 2], xr[t % 2], mT[t % 2], ho[t % 2], pT[t % 2]
        K.dma('sp', m_[:], d['mixed'][rows, :], writes=[m_])
        K.dma('sp', x_[:], src[rows, :], writes=[x_])
        for k in range(8):
            K.tr(p_, p_[:, k * 128:(k + 1) * 128], m_, m_[:, k * 128:(k + 1) * 128], identb, identb[:])
        K.op('act', lambda e: e.copy(out=mT_[:], in_=p_[:].rearrange("p (k t) -> p k t", k=8)), reads=[p_], writes=[mT_])
        for half in range(2):
            p2 = po[(t * 2 + half) % 4]
            for k in range(8):
                K.mm(p2, p2[:], mT_, mT_[:, k, :], W, W[:, k, half * 512:(half + 1) * 512], start=(k == 0), stop=(k == 7))
            K.op('dve', lambda e, p2=p2, half=half: e.tensor_tensor(out=h_[:, half * 512:(half + 1) * 512], in0=p2[:],
                                                                   in1=x_[:, half * 512:(half + 1) * 512], op=ALU.add),
                 reads=[p2, x_], writes=[h_], nowaw=[h_])
        K.dma('pool', d[dst][rows, :], h_[:], reads=[h_])
    K.phase_end()


def phase_MLP(K, d, S, li):
    GT = 2
    NG = S // (GT * 128)
    K.phase_begin()
    consts(K)
    identb = make_ident(K, BF16, 'identb')
    gain = load_gain(K, 'gain', d['mlp_norm'][li, :])
    Wu = K.sb('Wu', [128, 8, 4096], BF16)
    load_weight_bf16(K, Wu, d['mlp_w_up'][li], 8, 4096, gain=gain, stage_name='wsu', cb=512)
    Wd = K.sb('Wd', [128, 32, 1024], BF16)
    load_weight_bf16(K, Wd, d['mlp_w_down'][li], 32, 1024, stage_name='wsd', cb=512)
    NT = NormT(K, identb, nbuf_x=2 * GT, gt=GT)
    uT = K.sb('uT', [128, 32, GT * 128], BF16)
    rl = [K.sb('rl%d' % i, [128, GT * 128], F32) for i in range(2)]
    ho = [K.sb('mho%d' % i, [128, 1024], F32) for i in range(2)]
    pu = [K.ps('pu%d' % i, [128, 512], F32) for i in range(2)]
    pd = [K.ps('pd%d' % i, [128, 512], F32) for i in range(4)]
    toks = [NT.prep(d['h'][j * 128:(j + 1) * 128, :]) for j in range(GT)]
    for g in range(NG):
        hnT = NT.hnT[g % 2]
        xbs = []
        for j in range(GT):
            xbs.append(NT.finish(toks[j], hnT, j))
        for fc in range(32):
            if fc == 8 and g + 1 < NG:
                toks = [NT.prep(d['h'][((g + 1) * GT + j) * 128:((g + 1) * GT + j + 1) * 128, :]) for j in range(GT)]
            p = pu[fc % 2]
            for k in range(8):
                K.mm(p, p[:, 0:GT * 128], Wu, Wu[:, k, fc * 128:(fc + 1) * 128], hnT, hnT[:, k, :], start=(k == 0), stop=(k == 7))
            r = rl[fc % 2]
            K.op('act', lambda e, r=r, p=p: e.activation(out=r[:], in_=p[:, 0:GT * 128], func=AF.Relu), reads=[p], writes=[r])
            K.op('dve' if fc % 2 == 0 else 'pool', lambda e, r=r, fc=fc: e.tensor_tensor(out=uT[:, fc, :], in0=r[:], in1=r[:], op=ALU.mult),
                 reads=[r], writes=[uT], nowaw=[uT])
        for j in range(GT):
            t = g * GT + j
            h_ = ho[t % 2]
            for half in range(2):
                p2 = pd[(t * 2 + half) % 4]
                for fc in range(32):
                    K.mm(p2, p2[:], uT, uT[:, fc, j * 128:(j + 1) * 128], Wd, Wd[:, fc, half * 512:(half + 1) * 512],
                         start=(fc == 0), stop=(fc == 31))
                K.op('dve', lambda e, p2=p2, half=half, x_=xbs[j]: e.tensor_tensor(
                    out=h_[:, half * 512:(half + 1) * 512], in0=p2[:], in1=x_[:, half * 512:(half + 1) * 512], op=ALU.add),
                    reads=[p2, xbs[j]], writes=[h_], nowaw=[h_])
            K.dma('pool', d['h'][t * 128:(t + 1) * 128, :], h_[:], reads=[h_])
    K.phase_end()


def phase_PLE(K, d, S, li, final=False):
    NT_ = S // 128
    K.phase_begin()
    consts(K)
    identb = make_ident(K, BF16, 'identb')
    gain = load_gain(K, 'gain', d['ple_norm'][li, :])
    Wg = K.sb('Wg', [128, 8, 1024], BF16)
    load_weight_bf16(K, Wg, d['ple_w_gate'][li], 8, 1024, gain=gain)
    Wp = K.sb('Wp', [128, 2, 1024], BF16)
    load_weight_bf16(K, Wp, d['ple_w_proj'][li], 2, 1024)
    NT = NormT(K, identb, nbuf_x=9, gt=1, nbuf_n=8)
    pin = [K.sb('pin%d' % i, [128, 256], F32) for i in range(2)]
    pbf = [K.sb('pbf%d' % i, [128, 256], BF16) for i in range(2)]
    ppT = [K.sb('ppT%d' % i, [128, 2, 128], BF16) for i in range(2)]
    sg = [K.sb('sg%d' % i, [128, 1024], F32) for i in range(2)]
    ho = [K.sb('pho%d' % i, [128, 1024], F32) for i in range(3)]
    ptp = K.ps('ptp', [128, 1024], BF16)
    pg = [K.ps('pg%d' % i, [128, 512], F32) for i in range(2)]
    pp = [K.ps('pp%d' % i, [128, 512], F32) for i in range(2)]
    if final:
        fg = K.sb('fg', [128, 1024], F32)
        K.dma('sp', fg[:], d['final_norm'].partition_broadcast(128), writes=[fg])
        fst = [K.sb('fst%d' % i, [128, 4], F32) for i in range(2)]
        fo = [K.sb('fo%d' % i, [128, 1024], F32) for i in range(2)]
    pend_fin = [None]
    toks = [NT.prep(d['h'][u * 128:(u + 1) * 128, :]) for u in range(min(4, NT_))]
    for t in range(NT_):
        rows = slice(t * 128, (t + 1) * 128)
        hnT = NT.hnT[t % 2]
        x_ = NT.finish(toks.pop(0), hnT, 0)
        if t % 4 == 0:
            for u in range(t + 4, min(t + 8, NT_)):
                toks.append(NT.prep(d['h'][u * 128:(u + 1) * 128, :]))
        pi_, pb_, pT_, sg_, h_ = pin[t % 2], pbf[t % 2], ppT[t % 2], sg[t % 2], ho[t % 3]
        K.dma('sp', pi_[:], d['p'][li, rows, :], writes=[pi_])
        K.op('dve', lambda e: e.tensor_copy(out=pb_[:], in_=pi_[:]), reads=[pi_], writes=[pb_])
        for k in range(2):
            K.tr(ptp, ptp[:, k * 128:(k + 1) * 128], pb_, pb_[:, k * 128:(k + 1) * 128], identb, identb[:])
        K.op('act', lambda e: e.copy(out=pT_[:], in_=ptp[:, 0:256].rearrange("p (k t) -> p k t", k=2)), reads=[ptp], writes=[pT_])
        for half in range(2):
            hs = slice(half * 512, (half + 1) * 512)
            g_, p_ = pg[half], pp[half]
            for k in range(8):
                K.mm(g_, g_[:], hnT, hnT[:, k, :], Wg, Wg[:, k, hs], start=(k == 0), stop=(k == 7))
            for k in range(2):
                K.mm(p_, p_[:], pT_, pT_[:, k, :], Wp, Wp[:, k, hs], start=(k == 0), stop=(k == 1))
            K.op('act', lambda e, g_=g_, hs=hs: e.activation(out=sg_[:, hs], in_=g_[:], func=AF.Sigmoid), reads=[g_], writes=[sg_], nowaw=[sg_])
            K.op('dve', lambda e, p_=p_, hs=hs: e.tensor_tensor(out=sg_[:, hs], in0=sg_[:, hs], in1=p_[:], op=ALU.mult),
                 reads=[sg_, p_], writes=[sg_])
        K.op('pool', lambda e: e.tensor_tensor(out=h_[:], in0=sg_[:], in1=x_[:], op=ALU.add), reads=[sg_, x_], writes=[h_])
        if not final:
            K.dma('pool', d['h'][rows, :], h_[:], reads=[h_])
        else:
            def fin(h_=h_, st=fst[t % 2], o_=fo[t % 2], rows=rows):
                K.op('act', lambda e: e.activation(out=o_[:], in_=h_[:], func=AF.Square, accum_out=st[:, 0:1]), reads=[h_], writes=[o_, st])
                K.op('act', lambda e: e.activation(out=st[:, 1:2], in_=st[:, 0:1], func=AF.Sqrt, scale=1.0 / D, bias=K.eps_t[:, 0:1]),
                     reads=[st, K.eps_t], writes=[st])
                K.op('dve', lambda e: e.reciprocal(out=st[:, 2:3], in_=st[:, 1:2]), reads=[st], writes=[st])
                K.op('dve', lambda e: e.scalar_tensor_tensor(out=o_[:], in0=h_[:], scalar=st[:, 2:3], in1=fg[:], op0=ALU.mult, op1=ALU.mult),
                     reads=[h_, st, fg], writes=[o_])
                K.dma('pool', d['out'][rows, :], o_[:], reads=[o_])
            if pend_fin[0] is not None:
                pend_fin[0]()
            pend_fin[0] = fin
    if final and pend_fin[0] is not None:
        pend_fin[0]()
    K.phase_end()


O_NQ, O_KC, O_VC, O_KSL, O_VSL, O_KWN, O_VWN, O_NG, O_RG, O_RX = 0, 512, 640, 768, 896, 1024, 1152, 1280, 1304, 1816
TWO_PI = 6.283185307179586
CW1 = 6.28125
CW2 = TWO_PI - CW1
MAGIC = 12582912.0


def phase_Rprep(K, d, S):
    NT_ = S // 128
    inv_freq = (np.float32(500000.0) ** (-np.arange(8, dtype=np.float32) * np.float32(2.0 / 16))).astype(np.float32)
    K.phase_begin()
    consts(K)
    posi = K.sb('posi', [128, NT_], I32)
    K.dma('sp', posi[:], d['positions'].rearrange("o (t p) -> p (o t)", p=128), writes=[posi], allow_slow_non_contiguous=True)
    posf = K.sb('posf', [128, NT_], F32)
    K.op('dve', lambda e: e.tensor_copy(out=posf[:], in_=posi[:]), reads=[posi], writes=[posf])
    ang = K.sb('ang', [128, NT_, 8], F32)
    for i in range(8):
        K.op('dve', lambda e, i=i: e.tensor_scalar(out=ang[:, :, i], in0=posf[:], scalar1=float(inv_freq[i]), scalar2=None, op0=ALU.mult),
             reads=[posf], writes=[ang], nowaw=[ang])
    cs = K.sb('cs', [128, NT_, 16], F32)
    a2 = K.sb('a2', [128, NT_, 8], F32)
    kk = K.sb('kk', [128, NT_, 8], F32)
    rr = K.sb('rr', [128, NT_, 8], F32)
    for which in range(2):
        src = ang
        if which == 0:
            K.op('dve', lambda e: e.tensor_scalar(out=a2[:], in0=ang[:], scalar1=float(np.pi / 2), scalar2=None, op0=ALU.add), reads=[ang], writes=[a2])
            src = a2
        K.op('dve', lambda e, src=src: e.tensor_scalar(out=kk[:], in0=src[:], scalar1=float(1.0 / TWO_PI), scalar2=MAGIC, op0=ALU.mult, op1=ALU.add),
             reads=[src], writes=[kk])
        K.op('dve', lambda e: e.tensor_scalar(out=kk[:], in0=kk[:], scalar1=-MAGIC, scalar2=None, op0=ALU.add), reads=[kk], writes=[kk])
        K.op('dve', lambda e, src=src: e.scalar_tensor_tensor(out=rr[:], in0=kk[:], scalar=-CW1, in1=src[:], op0=ALU.mult, op1=ALU.add),
             reads=[kk, src], writes=[rr])
        K.op('dve', lambda e: e.scalar_tensor_tensor(out=rr[:], in0=kk[:], scalar=-CW2, in1=rr[:], op0=ALU.mult, op1=ALU.add),
             reads=[kk, rr], writes=[rr])
        K.op('dve', lambda e: e.tensor_scalar(out=rr[:], in0=rr[:], scalar1=3.141592, scalar2=-3.141592, op0=ALU.min, op1=ALU.max),
             reads=[rr], writes=[rr])
        K.op('act', lambda e, which=which: e.activation(out=cs[:, :, which * 8:(which + 1) * 8], in_=rr[:], func=AF.Sin),
             reads=[rr], writes=[cs], nowaw=[cs])
    K.dma('sp', d['csd'][:, :], cs[:].rearrange("p t c -> p (t c)"), reads=[cs])
    K.phase_end()


def phase_A1(K, d, S):
    NG = S // 512
    NT_ = S // 128
    K.phase_begin()
    consts(K)
    identb = make_ident(K, BF16, 'identb')
    gain = load_gain(K, 'gain', d['odd_norm_mix'])
    W = K.sb('W1', [128, 8, 2328], BF16)
    load_weight_bf16(K, W, d['odd_w_in'], 8, 2328, gain=gain)
    cs = K.sb('cs', [128, NT_, 16], F32)
    K.dma('sp', cs[:].rearrange("p t c -> p (t c)"), d['csd'][:, :], writes=[cs])
    NT = NormT(K, identb, nbuf_x=8, nbuf_n=8)
    yq = [K.sb('yq%d' % i, [128, 512], F32) for i in range(2)]
    yb = [K.sb('yb%d' % i, [128, 512], F32) for i in range(2)]
    yc = [K.sb('yc%d' % i, [128, 280], F32) for i in range(2)]
    gt_ = [K.sb('gt%d' % i, [128, 24], F32) for i in range(2)]
    qb = [K.sb('qb%d' % i, [128, 512], BF16) for i in range(2)]
    kb = [K.sb('kb%d' % i, [128, 4, 128], BF16) for i in range(2)]
    vb = [K.sb('vb%d' % i, [128, 2, 128], BF16) for i in range(2)]
    tA = K.sb('tA', [128, 8, 8], F32)
    tB = K.sb('tB', [128, 8, 8], F32)
    qTs = [K.sb('qTs%d' % i, [128, 4, 128], BF16) for i in range(2)]
    kTs = [K.sb('kTs%d' % i, [128, 4, 128], BF16) for i in range(2)]
    rgo = [K.sb('rgo%d' % i, [128, 512], F32) for i in range(3)]
    ptm = [K.ps('p1tm%d' % i, [128, 512], F32) for i in range(3)]
    pfm = [K.ps('p1fm%d' % i, [128, 512], F32) for i in range(2)]
    ptr = K.ps('p1tr', [128, 1024], BF16)
    nfm = 0

    def rope(eng, src, nh, dst, t):
        cosb = cs[:, t:t + 1, 0:8].to_broadcast([128, nh, 8])
        sinb = cs[:, t:t + 1, 8:16].to_broadcast([128, nh, 8])
        x1, x2 = src[:, :, 0:8], src[:, :, 8:16]
        a, b = tA[:, 0:nh, :], tB[:, 0:nh, :]
        return [
            lambda e: e.tensor_tensor(out=a, in0=x1, in1=cosb, op=ALU.mult),
            lambda e: e.tensor_tensor(out=b, in0=x2, in1=sinb, op=ALU.mult),
            lambda e: e.tensor_tensor(out=dst[:, :, 0:8], in0=a, in1=b, op=ALU.subtract),
            lambda e: e.tensor_tensor(out=a, in0=x2, in1=cosb, op=ALU.mult),
            lambda e: e.tensor_tensor(out=b, in0=x1, in1=sinb, op=ALU.mult),
            lambda e: e.tensor_tensor(out=dst[:, :, 8:16], in0=a, in1=b, op=ALU.add),
        ]

    toks = [NT.prep(d['h'][u * 128:(u + 1) * 128, :]) for u in range(4)]
    pending = [None]
    for g in range(NG):
        hnT = NT.hnT[g % 2]
        for j in range(4):
            t = g * 4 + j
            rows = slice(t * 128, (t + 1) * 128)
            NT.finish(toks.pop(0), hnT, j)
            if j == 0:
                for u in range(t + 4, min(t + 8, NG * 4)):
                    toks.append(NT.prep(d['h'][u * 128:(u + 1) * 128, :]))
            yq_, yb_, yc_, g_, qb_, kb_, vb_, qT_, kT_ = (yq[t % 2], yb[t % 2], yc[t % 2], gt_[t % 2], qb[t % 2], kb[t % 2], vb[t % 2],
                                                         qTs[t % 2], kTs[t % 2])
            for bi, (c0, c1, dst) in enumerate(((0, 512, yq_), (512, 1024, yb_), (1024, 1304, yc_))):
                p = ptm[bi]
                for k in range(8):
                    K.mm(p, p[:, 0:c1 - c0], hnT, hnT[:, k, j * 128:(j + 1) * 128], W, W[:, k, c0:c1], start=(k == 0), stop=(k == 7))
                K.op('act', lambda e, p=p, dst=dst, n=c1 - c0: e.copy(out=dst[:, 0:n], in_=p[:, 0:n]), reads=[p], writes=[dst])
            K.op('act', lambda e: e.activation(out=g_[:], in_=yc_[:, 256:280], func=AF.Sigmoid), reads=[yc_], writes=[g_])
            K.dma('pool', d['gates'][rows, :], g_[:], reads=[g_])
            K.op('dve', lambda e: e.tensor_copy(out=qb_[:], in_=yq_[:]), reads=[yq_], writes=[qb_])
            K.op('dve', lambda e: e.tensor_copy(out=kb_[:, 0:2, :], in_=yb_[:, 0:256].rearrange("p (a c) -> p a c", a=2)), reads=[yb_], writes=[kb_])
            K.op('dve', lambda e: e.tensor_copy(out=kb_[:, 2, :], in_=yb_[:, 256:384]), reads=[yb_], writes=[kb_])
            K.op('dve', lambda e: e.tensor_copy(out=kb_[:, 3, :], in_=yc_[:, 0:128]), reads=[yc_], writes=[kb_])
            K.op('pool', lambda e: e.tensor_copy(out=vb_[:, 0, :], in_=yb_[:, 384:512]), reads=[yb_], writes=[vb_])
            K.op('pool', lambda e: e.tensor_copy(out=vb_[:, 1, :], in_=yc_[:, 128:256]), reads=[yc_], writes=[vb_])
            K.dma('pool', d['Vs'][rows, :], vb_[:, 0, :], reads=[vb_])
            K.dma('pool', d['Vw'][rows, :], vb_[:, 1, :], reads=[vb_])
            jobs = [(yq_, yq_[:].rearrange("p (h c) -> p h c", c=64), 8, qb_, qb_[:].rearrange("p (h c) -> p h c", c=64)),
                    (yb_, yb_[:, 0:128].rearrange("p (h c) -> p h c", c=64), 2, kb_, kb_[:, 0, :].rearrange("p (h c) -> p h c", c=64)),
                    (yb_, yb_[:, 256:384].rearrange("p (h c) -> p h c", c=64), 2, kb_, kb_[:, 2, :].rearrange("p (h c) -> p h c", c=64)),
                    (yc_, yc_[:, 0:128].rearrange("p (h c) -> p h c", c=64), 2, kb_, kb_[:, 3, :].rearrange("p (h c) -> p h c", c=64))]
            for sb_, sap, nh, db_, dap in jobs:
                fns = rope('dve', sap, nh, dap, t)
                rw = [([sb_, cs], [tA]), ([sb_, cs], [tB]), ([tA, tB], [db_]), ([sb_, cs], [tA]), ([sb_, cs], [tB]), ([tA, tB], [db_])]
                for fn, (rd, wr) in zip(fns, rw):
                    K.op('dve', fn, reads=rd, writes=wr)
            def post(qb_=qb_, kb_=kb_, qT_=qT_, kT_=kT_, rows=rows):
                for c in range(4):
                    K.tr(ptr, ptr[:, c * 128:(c + 1) * 128], qb_, qb_[:, c * 128:(c + 1) * 128], identb, identb[:])
                for c in range(4):
                    K.tr(ptr, ptr[:, 512 + c * 128:512 + (c + 1) * 128], kb_, kb_[:, c, :], identb, identb[:])
                K.op('act', lambda e: e.copy(out=qT_[:], in_=ptr[:, 0:512].rearrange("p (c t) -> p c t", c=4)), reads=[ptr], writes=[qT_])
                K.op('act', lambda e: e.copy(out=kT_[:], in_=ptr[:, 512:1024].rearrange("p (c t) -> p c t", c=4)), reads=[ptr], writes=[kT_])
                K.dma('sp', d['QT1'][:, rows].rearrange("(c p) t -> p c t", p=128), qT_[:], reads=[qT_])
                for c, nm in enumerate(('KcT', 'VcT', 'KsT', 'KwT')):
                    K.dma('sp', d[nm][:, rows], kT_[:, c, :], reads=[kT_])
            if pending[0] is not None:
                pending[0]()
            pending[0] = post
        for c in range(8):
            p = pfm[nfm % 2]
            o = rgo[nfm % 3]
            nfm += 1
            for k in range(8):
                K.mm(p, p[:], W, W[:, k, O_RG + c * 128:O_RG + (c + 1) * 128], hnT, hnT[:, k, :], start=(k == 0), stop=(k == 7))
            K.op('act', lambda e, o=o, p=p: e.copy(out=o[:], in_=p[:]), reads=[p], writes=[o])
            K.dma('pool', d['RGX'][c * 128:(c + 1) * 128, g * 512:(g + 1) * 512], o[:], reads=[o])
    pending[0]()
    K.phase_end()


GELU_C = 1.5957691216057308


def gelu_tanh(K, xs, tmp, out_ap, out_b, eng2='pool'):
    K.op(eng2, lambda e: e.tensor_tensor(out=tmp[:], in0=xs[:], in1=xs[:], op=ALU.mult), reads=[xs], writes=[tmp])
    K.op('dve', lambda e: e.tensor_scalar(out=tmp[:], in0=tmp[:], scalar1=0.044715, scalar2=1.0, op0=ALU.mult, op1=ALU.add),
         reads=[tmp], writes=[tmp])
    K.op('dve', lambda e: e.tensor_tensor(out=tmp[:], in0=tmp[:], in1=xs[:], op=ALU.mult), reads=[tmp, xs], writes=[tmp])
    K.op('act', lambda e: e.activation(out=tmp[:], in_=tmp[:], func=AF.Sigmoid, scale=GELU_C), reads=[tmp], writes=[tmp])
    K.op('dve', lambda e: e.tensor_tensor(out=out_ap, in0=tmp[:], in1=xs[:], op=ALU.mult), reads=[tmp, xs], writes=[out_b])


def phase_cmp(K, d, S):
    NCP = S // 16
    NCMP = NCP - 1
    K.phase_begin()
    consts(K)
    XT = K.sb('XT', [128, S], BF16)
    w1s = K.sb('w1s', [128, 32, 128], F32)
    W1 = K.sb('W1c', [128, 32, 128], BF16)
    pes = K.sb('pes', [128, 32], F32)
    peT = K.sb('peT', [128, 32], BF16)
    w2s = K.sb('w2s', [128, 64], F32)
    W2 = K.sb('W2c', [128, 64], BF16)
    cvec = K.sb('cvec', [128, 1], F32)
    xs = K.sb('cxs', [128, NCP], F32)
    tmp = K.sb('ctmp', [128, NCP], F32)
    hidT = K.sb('hidT', [128, NCP], BF16)
    ko = K.sb('ko', [64, NCP], BF16)
    vo = K.sb('vo', [128, 64], BF16)
    ph = K.ps('cph', [128, 512], F32)
    pc = K.ps('cpc', [128, 512], F32)
    po = K.ps('cpo', [128, 512], F32)
    for which, (src, pe, w1, w2) in enumerate((('KcT', 'odd_cmp_k_pe', 'odd_cmp_k_w1', 'odd_cmp_k_w2'),
                                                ('VcT', 'odd_cmp_v_pe', 'odd_cmp_v_w1', 'odd_cmp_v_w2'))):
        K.dma('sp', XT[:], d[src][:, :], writes=[XT])
        for hf in range(2):
            K.dma('sp', w1s[hf * 64:(hf + 1) * 64, :, :], d[w1].rearrange("(l c) h -> c l h", c=64), writes=[w1s], nowaw=[w1s])
            K.dma('sp', pes[hf * 64:(hf + 1) * 64, :], d[pe].rearrange("l c -> c l"), writes=[pes], nowaw=[pes],
                  allow_slow_non_contiguous=True)
        K.dma('sp', w2s[:], d[w2][:, :], writes=[w2s])
        K.op('dve', lambda e: e.tensor_copy(out=W1[:], in_=w1s[:]), reads=[w1s], writes=[W1])
        K.op('dve', lambda e: e.tensor_copy(out=peT[:], in_=pes[:]), reads=[pes], writes=[peT])
        K.op('dve', lambda e: e.tensor_copy(out=W2[:], in_=w2s[:]), reads=[w2s], writes=[W2])
        for l in range(32):
            K.mm(pc, pc[:, 0:1], W1, W1[0:64, l, :], peT, peT[0:64, l:l + 1], start=(l == 0), stop=(l == 31))
        K.op('act', lambda e: e.copy(out=cvec[:], in_=pc[:, 0:1]), reads=[pc], writes=[cvec])
        X3 = XT[:].rearrange("p (n s) -> p n s", s=16)
        for g in range(2):
            gs = slice(g * 64, (g + 1) * 64)
            for l in range(32):
                rhs = X3[gs, 0:NCMP, l] if l < 16 else X3[gs, 1:NCP, l - 16]
                K.mm(ph, ph[:, 0:NCMP], W1, W1[gs, l, :], XT, rhs, start=(l == 0), stop=(l == 31))
            K.op('pool', lambda e: e.memset(xs[:], 0.0), writes=[xs])
            K.op('act', lambda e: e.activation(out=xs[:, 0:NCMP], in_=ph[:, 0:NCMP], func=AF.Identity, bias=cvec[:, 0:1]),
                 reads=[ph, cvec], writes=[xs])
            gelu_tanh(K, xs, tmp, hidT[:], hidT)
            if which == 0:
                for c0 in range(0, NCP, 512):
                    n = min(512, NCP - c0)
                    K.mm(po, po[0:64, 0:n], W2, W2[:, :], hidT, hidT[:, c0:c0 + n])
                    K.op('act', lambda e, c0=c0, n=n: e.copy(out=ko[:, c0:c0 + n], in_=po[0:64, 0:n]), reads=[po], writes=[ko])
                K.dma('sp', d['KcmpT'][gs, :], ko[:], reads=[ko])
            else:
                for c0 in range(0, NCP, 128):
                    K.mm(po, po[:, 0:64], hidT, hidT[:, c0:c0 + 128], W2, W2[:, :])
                    K.op('act', lambda e: e.copy(out=vo[:], in_=po[:, 0:64]), reads=[po], writes=[vo])
                    K.dma('sp', d['Vcmp'][c0:c0 + 128, gs], vo[:], reads=[vo])
    K.phase_end()


def attn_core(K, S, heads, B, load_head, units_fn, KR, s_extra, exp_bias, emit_masks, pv_extra, finalize, L=3):
    NQ = S // 512
    units = []
    for hi, h in enumerate(heads):
        for qg in range(NQ):
            us = units_fn(qg)
            for ui, u in enumerate(us):
                units.append((hi, h, qg, u, ui == 0, ui == len(us) - 1))
    load_head(0, heads[0])
    NU = len(units)
    deferred = []
    for i in range(NU + L):
        if deferred and deferred[0][0] <= i:
            deferred.pop(0)[1]()
        if i < NU:
            hi, h, qg, (kt, c0, c1), first, last = units[i]
            s = hi % 2
            p_s = B['ps'][i % len(B['ps'])]
            p_t = B['pt'][i % 4]
            KT = B['KT'][s]
            QT = B['QTsel'](s, kt) if 'QTsel' in B else B['QT'][s]
            ex = s_extra(h, qg, kt, c0, c1) if s_extra else None
            K.mm(p_s, p_s[:, c0:c1], KT, KT[0:KR, kt * 128:(kt + 1) * 128], QT, QT[0:KR, qg * 512 + c0:qg * 512 + c1],
                 start=True, stop=(ex is None))
            if ex is not None:
                K.mm(p_s, p_s[:, c0:c1], ex[0], ex[1], ex[2], ex[3], start=False, stop=True)
            bb, bap = exp_bias(h, kt) if exp_bias else (K.zero_t, K.zero_t[:, 0:1])
            K.op('act', lambda e, p_s=p_s, p_t=p_t, c0=c0, c1=c1, bap=bap: e.activation(
                out=p_t[:, c0:c1], in_=p_s[:, c0:c1], func=AF.Exp, scale=0.125, bias=bap), reads=[p_s, bb], writes=[p_t])
            emit_masks(p_t, h, qg, kt, c0, c1)
        if i - L >= 0:
            hi, h, qg, (kt, c0, c1), first, last = units[i - L]
            if first and qg == 0 and hi + 1 < len(heads):
                load_head(hi + 1, heads[hi + 1])
            s = hi % 2
            p_t = B['pt'][(i - L) % 4]
            gi = hi * NQ + qg
            p_o = B['po'][gi % 2]
            V = B['V'][s]
            K.mm(p_o, p_o[0:65, c0:c1], V, V[:, kt, 0:65], p_t, p_t[:, c0:c1], start=first, stop=last)
            if pv_extra:
                pv_extra(p_t, hi, h, qg, kt, c0, c1, first, last)
            if last:
                rest = finalize(hi, h, qg, gi, p_o)
                if rest is not None:
                    deferred.append((i + 2, rest))
    for _, rest in deferred:
        rest()


def causal_sel(K, p_t, c):
    K.op('pool', lambda e: e.affine_select(out=p_t[:, c:c + 128], in_=p_t[:, c:c + 128], pattern=[[1, 128]], compare_op=ALU.is_ge,
                                           fill=0.0, base=0, channel_multiplier=-1), reads=[p_t], writes=[p_t])


class NsaCommon:
    def __init__(self, K, d, S, nkt, br, out_name, n_ps=3):
        self.K, self.d, self.S, self.br, self.out_name = K, d, S, br, out_name
        NT_ = S // 128
        consts(K)
        K.zero_t = K.sb('zero', [128, 1], F32)
        K.op('pool', lambda e: e.memset(K.zero_t[:], 0.0), writes=[K.zero_t])
        self.identf = make_ident(K, F32, 'identf')
        self.gates = K.sb('gates', [128, NT_, 24], F32)
        K.dma('sp', self.gates[:], d['gates'].rearrange("(t p) c -> p t c", p=128), writes=[self.gates])
        self.B = dict(
            KT=[K.sb('nKT%d' % i, [128, nkt * 128], BF16) for i in range(2)],
            QT=[K.sb('nQT%d' % i, [128, S], BF16) for i in range(2)],
            V=[K.sb('nV%d' % i, [128, nkt, 65], BF16) for i in range(2)],
            ps=[K.ps('nps%d' % i, [128, 512], F32) for i in range(n_ps)],
            po=[K.ps('npo%d' % i, [128, 512], F32) for i in range(2)],
            pt=[K.sb('npt%d' % i, [128, 512], BF16) for i in range(4)],
        )
        for i in range(2):
            K.op('pool', lambda e, i=i: e.memset(self.B['V'][i][:, :, 64:65], 1.0), writes=[self.B['V'][i]])
            K.op('pool', lambda e, i=i: e.memset(self.B['KT'][i][64:128, :], 0.0), writes=[self.B['KT'][i]])
            K.op('pool', lambda e, i=i: e.memset(self.B['QT'][i][64:128, :], 0.0), writes=[self.B['QT'][i]])
        self.pq = K.ps('npq', [128, 4, 65], F32)
        self.osb = [K.sb('nosb%d' % i, [65, 512], F32) for i in range(2)]
        self.rc = [K.sb('nrc%d' % i, [128, 4], F32) for i in range(2)]
        self.mo = [K.sb('nmo%d' % i, [128, 4, 64], F32) for i in range(2)]

    def load_head(self, ksrc, vsrc, nk):
        K, d, B = self.K, self.d, self.B

        def f(hi, h):
            s, g = hi % 2, h // 4
            K.dma('sp', B['KT'][s][0:64, 0:nk], d[ksrc][g * 64:(g + 1) * 64, 0:nk], writes=[B['KT'][s]])
            K.dma('sp', B['QT'][s][0:64, :], d['QT1'][h * 64:(h + 1) * 64, :], writes=[B['QT'][s]])
            K.dma('sp', B['V'][s][:, :, 0:64], d[vsrc][0:nk, g * 64:(g + 1) * 64].rearrange("(t p) c -> p t c", p=128),
                  writes=[B['V'][s]])
        return f

    def finalize(self, hi, h, qg, gi, p_o, pre=None, defer_pre=False, rc_hook=None):
        K, d = self.K, self.d
        o_s, r_c, m_o, p_q = self.osb[gi % 2], self.rc[gi % 2], self.mo[gi % 2], self.pq
        K.op('act', lambda e: e.copy(out=o_s[:], in_=p_o[0:65, :]), reads=[p_o], writes=[o_s])
        if pre and not defer_pre:
            pre(o_s)

        def rest():
            if pre and defer_pre:
                pre(o_s)
            for j in range(4):
                K.tr(p_q, p_q[:, j, :], o_s, o_s[0:65, j * 128:(j + 1) * 128], self.identf, self.identf[0:65, 0:65])
            K.op('dve', lambda e: e.tensor_scalar(out=r_c[:], in0=p_q[:, :, 64], scalar1=1e-30, scalar2=None, op0=ALU.add),
                 reads=[p_q], writes=[r_c])
            K.op('dve', lambda e: e.reciprocal(out=r_c[:], in_=r_c[:]), reads=[r_c], writes=[r_c])
            if rc_hook:
                rc_hook(r_c)
            col = h * 3 + self.br
            K.op('dve', lambda e: e.tensor_tensor(out=r_c[:], in0=r_c[:], in1=self.gates[:, qg * 4:(qg + 1) * 4, col], op=ALU.mult),
                 reads=[r_c, self.gates], writes=[r_c])
            for j in range(4):
                K.op('dve', lambda e, j=j: e.tensor_scalar(out=m_o[:, j, :], in0=p_q[:, j, 0:64], scalar1=r_c[:, j:j + 1], scalar2=None,
                                                           op0=ALU.mult), reads=[p_q, r_c], writes=[m_o])
            K.dma('pool', d[self.out_name][qg * 512:(qg + 1) * 512, h * 64:(h + 1) * 64].rearrange("(j p) c -> p j c", p=128),
                  m_o[:], reads=[m_o])
        return rest


def phase_nsa_win(K, d, S, heads=range(8)):
    K.phase_begin()
    C = NsaCommon(K, d, S, S // 128, 2, 'nsa2')

    def units_fn(qg):
        us = []
        for kt in range(max(0, 4 * qg - 4), 4 * qg + 4):
            i0 = max(kt, 4 * qg)
            i1 = min(kt + 4, 4 * qg + 3)
            us.append((kt, (i0 - 4 * qg) * 128, (i1 - 4 * qg + 1) * 128))
        return us

    def masks(p_t, h, qg, kt, c0, c1):
        if kt >= 4 * qg:
            causal_sel(K, p_t, (kt - 4 * qg) * 128)
        if kt + 4 <= 4 * qg + 3 and kt + 4 >= 4 * qg:
            c = (kt + 4 - 4 * qg) * 128
            K.op('pool', lambda e: e.affine_select(out=p_t[:, c:c + 128], in_=p_t[:, c:c + 128], pattern=[[-1, 128]], compare_op=ALU.is_gt,
                                                   fill=0.0, base=0, channel_multiplier=1), reads=[p_t], writes=[p_t])

    attn_core(K, S, list(heads), C.B, C.load_head('KwT', 'Vw', S), units_fn, 128, None, None, masks, None, C.finalize)
    K.phase_end()


def phase_nsa_slc(K, d, S, heads=range(8)):
    NT_ = S // 128
    NA = max(1, S // 4096)
    K.phase_begin()
    C = NsaCommon(K, d, S, NT_, 1, 'nsa1')
    QT2 = [[C.B['QT'][i] for i in range(2)]] + [[K.sb('nQTa%d_%d' % (a, i), [128, S], BF16) for i in range(2)] for a in range(1, NA)]
    for i in range(2):
        kt_ = C.B['KT'][i]
        K.op('pool', lambda e, kt_=kt_: e.memset(kt_[64:128, :], 262144.0), writes=[kt_])
        MB = min(64, S // 64)
        K.op('pool', lambda e, kt_=kt_: e.affine_select(
            out=kt_[64:128, :].rearrange("p (a m k) -> p a m k", m=MB, k=64), in_=kt_[64:128, :].rearrange("p (a m k) -> p a m k", m=MB, k=64),
            pattern=[[0, S // (MB * 64)], [1, MB], [0, 64]], compare_op=ALU.is_equal, fill=0.0, base=0,
            channel_multiplier=-1), reads=[kt_], writes=[kt_])
    C.B['QTsel'] = lambda s_, kt: QT2[kt // 32][s_]

    def load_head(hi, h):
        s_, g = hi % 2, h // 4
        B = C.B
        K.dma('sp', B['KT'][s_][0:64, :], d['KsT'][g * 64:(g + 1) * 64, :], writes=[B['KT'][s_]], nowaw=[B['KT'][s_]])
        for a in range(NA):
            qt = QT2[a][s_]
            K.dma('sp', qt[0:64, :], d['QT1'][h * 64:(h + 1) * 64, :], writes=[qt])
            K.dma('sp', qt[64:128, :], d['selT'][g, 64 * a:64 * a + 64, :], writes=[qt], nowaw=[qt])
        K.dma('sp', B['V'][s_][:, :, 0:64], d['Vs'][:, g * 64:(g + 1) * 64].rearrange("(t p) c -> p t c", p=128), writes=[B['V'][s_]])

    def units_fn(qg):
        return [(kt, 128 * max(kt - 4 * qg, 0), 512) for kt in range(4 * qg + 4)]

    def masks(p_t, h, qg, kt, c0, c1):
        if kt >= 4 * qg:
            causal_sel(K, p_t, (kt - 4 * qg) * 128)

    attn_core(K, S, list(heads), C.B, load_head, units_fn, 128, None, None, masks, None, C.finalize)
    K.phase_end()


def phase_nsa_cmp(K, d, S, heads=range(8)):
    NT_ = S // 128
    NQ = S // 512
    NCP = S // 16
    NKT = (NCP + 127) // 128
    NKP = NKT * 128
    K.phase_begin()
    C = NsaCommon(K, d, S, NKT, 0, 'nsa0', n_ps=2)
    identf = C.identf
    identb = make_ident(K, BF16, 'identb')
    onesf = K.sb('onesf', [128, 128], F32)
    K.op('pool', lambda e: e.memset(onesf[:], 1.0), writes=[onesf])
    ovf = K.sb('ovf', [128, 128], F32)
    ovf2 = K.sb('ovf2', [128, 128], F32)
    ovl = K.sb('ovl', [128, NKT, 128], BF16)
    for kt in range(NKT):
        K.op('pool', lambda e, kt=kt: e.affine_select(out=ovf[:], in_=onesf[:], pattern=[[64, 128]], compare_op=ALU.is_ge, fill=0.0,
                                                       base=63 - 2048 * kt, channel_multiplier=-16), reads=[onesf], writes=[ovf])
        K.op('pool', lambda e, kt=kt: e.affine_select(out=ovf2[:], in_=ovf[:], pattern=[[-64, 128]], compare_op=ALU.is_ge, fill=0.0,
                                                       base=2048 * kt + 31, channel_multiplier=16), reads=[ovf], writes=[ovf2])
        K.op('pool', lambda e, kt=kt: e.tensor_copy(out=ovl[:, kt, :], in_=ovf2[:]), reads=[ovf2], writes=[ovl], nowaw=[ovl])
    cst = K.sb('cst', [128, 8], F32)
    K.op('pool', lambda e: e.memset(cst[:, 0:1], 1e30), writes=[cst])
    K.op('pool', lambda e: e.memset(cst[:, 1:2], 2e30), writes=[cst])
    K.op('pool', lambda e: e.memset(cst[0:64, 2:3], 3e30), writes=[cst])
    K.op('pool', lambda e: e.memset(cst[64:128, 2:3], 0.0), writes=[cst])
    K.op('pool', lambda e: e.memset(cst[0:64, 3:4], 0.0), writes=[cst])
    K.op('pool', lambda e: e.memset(cst[64:128, 3:4], 1.0), writes=[cst])
    K.op('pool', lambda e: e.memset(cst[0:64, 4:5], -1e30), writes=[cst])
    K.op('pool', lambda e: e.memset(cst[64:128, 4:5], 4e30), writes=[cst])
    po2s = [K.ps('npo2_%d' % i, [128, 4, 128], F32) for i in range(2)]
    pb = K.ps('npb', [128, 512], F32)
    imp_acc = K.sb('impacc', [128, NT_, 128], F32)
    rrow = K.sb('rrow', [65, 512], F32)
    rbs = K.sb('rbs', [128, 512], F32)
    tmpi = K.sb('tmpi', [128, 512], F32)
    work = K.sb('work', [128, 128], F32)
    work2 = K.sb('work2', [128, 128], F32)
    m8 = K.sb('m8', [128, 16], F32)
    selm = K.sb('selm', [128, 128], F32)
    selb = K.sb('selb', [128, 128], F32)
    sTs = [K.sb('sTs%d' % i, [128, 128], BF16) for i in range(2)]

    def units_fn(qg):
        kmax = min(NKT - 1, (32 * qg + 30) // 128)
        return [(kt, 0, 512) for kt in range(kmax + 1)]

    def masks(p_t, h, qg, kt, c0, c1):
        K.op('pool', lambda e: e.affine_select(out=p_t[:], in_=p_t[:], pattern=[[1, 512]], compare_op=ALU.is_ge, fill=0.0,
                                               base=512 * qg - 2048 * kt - 31, channel_multiplier=-16), reads=[p_t], writes=[p_t])

    def pv_extra(p_t, hi, h, qg, kt, c0, c1, first, last):
        po2 = po2s[(hi * NQ + qg) % 2]
        for j in range(4):
            K.mm(po2, po2[:, j, :], p_t, p_t[:, j * 128:(j + 1) * 128], ovl, ovl[:, kt, :], start=(first and j == 0), stop=(last and j == 3))

    def topk_pass(g):
        for i in range(NT_):
            W = 2 * i + 2
            cols = slice(i * 128, (i + 1) * 128)
            K.op('pool', lambda e: e.memset(selm[:], 0.0), writes=[selm])
            if W <= 16:
                K.op('pool', lambda e, W=W: e.memset(selm[:, 0:W], 1.0), writes=[selm])
            else:
                K.op('act', lambda e, W=W, i=i: e.copy(out=work[:, 0:W], in_=imp_acc[:, i, 0:W]), reads=[imp_acc], writes=[work])
                K.op('dve', lambda e: e.tensor_copy(out=work[:, 0:1], in_=cst[:, 0:1]), reads=[cst], writes=[work])
                K.op('dve', lambda e, i=i: e.tensor_copy(out=work[:, 2 * i:2 * i + 1], in_=cst[:, 1:2]), reads=[cst], writes=[work])
                K.op('dve', lambda e, i=i: e.tensor_scalar(out=work[:, 2 * i - 1:2 * i], in0=work[:, 2 * i - 1:2 * i], scalar1=cst[:, 3:4],
                                                           scalar2=cst[:, 2:3], op0=ALU.mult, op1=ALU.add), reads=[work, cst], writes=[work])
                K.op('dve', lambda e, i=i: e.tensor_copy(out=work[:, 2 * i + 1:2 * i + 2], in_=cst[:, 4:5]), reads=[cst], writes=[work])
                K.op('dve', lambda e, W=W: e.max(out=m8[:, 0:8], in_=work[:, 0:W]), reads=[work], writes=[m8])
                K.op('dve', lambda e, W=W: e.match_replace(out=work2[:, 0:W], in_to_replace=m8[:, 0:8], in_values=work[:, 0:W],
                                                           imm_value=-3.0e38), reads=[work, m8], writes=[work2])
                K.op('dve', lambda e, W=W: e.max(out=m8[:, 8:16], in_=work2[:, 0:W]), reads=[work2], writes=[m8])
                K.op('dve', lambda e, W=W: e.tensor_scalar(out=selm[:, 0:W], in0=work[:, 0:W], scalar1=m8[:, 15:16], scalar2=None,
                                                           op0=ALU.is_ge), reads=[work, m8], writes=[selm])
            K.op('dve', lambda e: e.tensor_scalar(out=selb[:], in0=selm[:], scalar1=-1.0, scalar2=None, op0=ALU.add), reads=[selm], writes=[selb])
            K.tr(pb, pb[:, 128:256], selb, selb[:], identf, identf[:])
            sT = sTs[i % 2]
            K.op('act', lambda e, sT=sT: e.copy(out=sT[:], in_=pb[:, 128:256]), reads=[pb], writes=[sT])
            K.dma('pool', d['selT'][g, :, cols], sT[:], reads=[sT])


    def finalize(hi, h, qg, gi, p_o):
        po2 = po2s[gi % 2]

        def rc_hook(r_c):
            for j in range(4):
                dst = imp_acc[:, qg * 4 + j, :]
                if h % 4 == 0:
                    K.op('dve', lambda e, j=j, dst=dst: e.tensor_scalar(out=dst, in0=po2[:, j, :], scalar1=r_c[:, j:j + 1], scalar2=None, op0=ALU.mult),
                         reads=[po2, r_c], writes=[imp_acc], nowaw=[imp_acc])
                else:
                    K.op('dve', lambda e, j=j, dst=dst: e.scalar_tensor_tensor(out=dst, in0=po2[:, j, :], scalar=r_c[:, j:j + 1], in1=dst,
                                                                                op0=ALU.mult, op1=ALU.add),
                         reads=[po2, r_c, imp_acc], writes=[imp_acc], nowaw=[imp_acc])
        rest0 = C.finalize(hi, h, qg, gi, p_o, rc_hook=rc_hook)

        def rest():
            rest0()
            if h % 4 == 3 and qg == NQ - 1:
                topk_pass(h // 4)
        return rest

    attn_core(K, S, list(heads), C.B, C.load_head('KcmpT', 'Vcmp', NCP), units_fn, 128, None, None, masks, pv_extra, finalize)
    K.phase_end()


def phase_nsa_combine(K, d, S):
    NT_ = S // 128
    K.phase_begin()
    a = [K.sb('ca%d' % i, [128, 512], F32) for i in range(2)]
    b = [K.sb('cb%d' % i, [128, 512], F32) for i in range(2)]
    c = [K.sb('cc%d' % i, [128, 512], F32) for i in range(2)]
    o = [K.sb('co%d' % i, [128, 512], BF16) for i in range(2)]
    for t in range(NT_):
        rows = slice(t * 128, (t + 1) * 128)
        a_, b_, c_, o_ = a[t % 2], b[t % 2], c[t % 2], o[t % 2]
        K.dma('sp', a_[:], d['nsa0'][rows, :], writes=[a_])
        K.dma('sp', b_[:], d['nsa1'][rows, :], writes=[b_])
        K.dma('sp', c_[:], d['nsa2'][rows, :], writes=[c_])
        K.op('dve', lambda e: e.tensor_tensor(out=a_[:], in0=a_[:], in1=b_[:], op=ALU.add), reads=[a_, b_], writes=[a_])
        K.op('dve', lambda e: e.tensor_tensor(out=o_[:], in0=a_[:], in1=c_[:], op=ALU.add), reads=[a_, c_], writes=[o_])
        K.dma('pool', d['mixed'][rows, 0:512], o_[:], reads=[o_])
    K.phase_end()


def phase_lru(K, d, S):
    SEG = min(S, 2048)
    NS = S // SEG
    K.phase_begin()
    consts(K)
    identb = make_ident(K, BF16, 'identb')
    prm = K.sb('lprm', [128, 12, 4], F32)
    for j in range(4):
        K.dma('sp', prm[:, j, :], d['odd_rg_conv_w'][j, :].rearrange("(c p) -> p c", p=128), writes=[prm], nowaw=[prm],
              allow_slow_non_contiguous=True)
    for idx, nm in ((4, 'odd_rg_conv_b'), (5, 'odd_rg_ba'), (6, 'odd_rg_bx'), (7, 'odd_rg_lambda')):
        K.dma('sp', prm[:, idx, :], d[nm].rearrange("(c p) -> p c", p=128), writes=[prm], nowaw=[prm], allow_slow_non_contiguous=True)
    K.op('act', lambda e: e.activation(out=prm[:, 8, :], in_=prm[:, 7, :], func=AF.Exp, scale=-1.0), reads=[prm], writes=[prm])
    K.op('act', lambda e: e.activation(out=prm[:, 8, :], in_=prm[:, 8, :], func=AF.Ln, bias=K.one_t[:, 0:1]), reads=[prm, K.one_t], writes=[prm])
    K.op('dve', lambda e: e.tensor_scalar(out=prm[:, 8, :], in0=prm[:, 8, :], scalar1=-8.0, scalar2=None, op0=ALU.mult), reads=[prm], writes=[prm])
    wst = K.sb('lwst', [128, 128], F32)
    WA = K.sb('WA', [128, 4, 128], BF16)
    WX = K.sb('WX', [128, 4, 128], BF16)
    for Wt, nm in ((WA, 'odd_rg_wa'), (WX, 'odd_rg_wx')):
        for c in range(4):
            K.op('pool', lambda e: e.memset(wst[:], 0.0), writes=[wst])
            K.dma('sp', wst[0:64, 0:64], d[nm][2 * c, :, :], writes=[wst])
            K.dma('sp', wst[64:128, 64:128], d[nm][2 * c + 1, :, :], writes=[wst])
            K.op('dve', lambda e, Wt=Wt, c=c: e.tensor_copy(out=Wt[:, c, :], in_=wst[:]), reads=[wst], writes=[Wt], nowaw=[Wt])
    names = [('xpad', SEG + 3, F32), ('x', SEG, F32), ('xbf', SEG, BF16), ('r', SEG, F32), ('ig', SEG, F32), ('a', SEG, F32), ('t', SEG, F32),
             ('hh', SEG, F32), ('rg', SEG, F32), ('tmp', SEG, F32), ('gg', SEG, F32), ('ybf', SEG, BF16)]
    TL = [{nm: K.sb('l%s%d' % (nm, i), [128, w], dt) for nm, w, dt in names} for i in range(2)]
    hc = K.sb('lhc', [128, 1], F32)
    yo = [K.sb('lyo%d' % i, [128, 4, 128], BF16) for i in range(2)]
    pr = [K.ps('lpr%d' % i, [128, 512], F32) for i in range(2)]
    pi = [K.ps('lpi%d' % i, [128, 512], F32) for i in range(2)]
    ptr = [K.ps('lptr%d' % i, [128, 1024], BF16) for i in range(2)]
    nt = 0
    it = 0
    for c in range(4):
        K.op('pool', lambda e: e.memset(hc[:], 0.0), writes=[hc])
        for s in range(NS):
            cols = slice(s * SEG, (s + 1) * SEG)
            T_ = TL[it % 2]
            Tn = TL[(it + 1) % 2]
            it += 1
            xpad, x, xbf, r, ig, a, t, hh, rg, tmp, gg, ybf = (T_[k] for k in ('xpad', 'x', 'xbf', 'r', 'ig', 'a', 't', 'hh', 'rg', 'tmp', 'gg', 'ybf'))
            if s == 0:
                K.op('pool', lambda e, xpad=xpad: e.memset(xpad[:, 0:3], 0.0), writes=[xpad])
            K.dma('sp', xpad[:, 3:3 + SEG], d['RGX'][512 + c * 128:512 + (c + 1) * 128, cols], writes=[xpad], nowaw=[xpad])
            K.dma('sp', rg[:], d['RGX'][c * 128:(c + 1) * 128, cols], writes=[rg])
            K.op('dve', lambda e: e.tensor_scalar(out=x[:], in0=xpad[:, 0:SEG], scalar1=prm[:, 0, c:c + 1], scalar2=prm[:, 4, c:c + 1],
                                                  op0=ALU.mult, op1=ALU.add), reads=[xpad, prm], writes=[x])
            for j in range(1, 4):
                K.op('dve', lambda e, j=j: e.scalar_tensor_tensor(out=x[:], in0=xpad[:, j:j + SEG], scalar=prm[:, j, c:c + 1], in1=x[:],
                                                                   op0=ALU.mult, op1=ALU.add), reads=[xpad, prm, x], writes=[x])
            K.op('pool', lambda e, xpad=xpad, xnx=Tn['xpad']: e.tensor_copy(out=xnx[:, 0:3], in_=xpad[:, SEG:SEG + 3]), reads=[xpad], writes=[Tn['xpad']])
            K.op('pool', lambda e: e.tensor_copy(out=xbf[:], in_=x[:]), reads=[x], writes=[xbf])
            for pc in range(SEG // 512):
                ps_ = slice(pc * 512, (pc + 1) * 512)
                p1, p2 = pr[pc % 2], pi[pc % 2]
                K.mm(p1, p1[:], WA, WA[:, c, :], xbf, xbf[:, ps_])
                K.mm(p2, p2[:], WX, WX[:, c, :], xbf, xbf[:, ps_])
                K.op('act', lambda e, p1=p1, ps_=ps_: e.activation(out=r[:, ps_], in_=p1[:], func=AF.Sigmoid, bias=prm[:, 5, c:c + 1]),
                     reads=[p1, prm], writes=[r], nowaw=[r])
                K.op('act', lambda e, p2=p2, ps_=ps_: e.activation(out=ig[:, ps_], in_=p2[:], func=AF.Sigmoid, bias=prm[:, 6, c:c + 1]),
                     reads=[p2, prm], writes=[ig], nowaw=[ig])
            K.op('act', lambda e: e.activation(out=a[:], in_=r[:], func=AF.Exp, scale=prm[:, 8, c:c + 1]), reads=[r, prm], writes=[a])
            K.op('dve', lambda e: e.tensor_tensor(out=t[:], in0=a[:], in1=a[:], op=ALU.mult), reads=[a], writes=[t])
            K.op('dve', lambda e: e.tensor_scalar(out=t[:], in0=t[:], scalar1=-1.0, scalar2=1.0, op0=ALU.mult, op1=ALU.add), reads=[t], writes=[t])
            K.op('act', lambda e: e.activation(out=t[:], in_=t[:], func=AF.Sqrt), reads=[t], writes=[t])
            K.op('dve', lambda e: e.tensor_tensor(out=t[:], in0=t[:], in1=ig[:], op=ALU.mult), reads=[t, ig], writes=[t])
            K.op('dve', lambda e: e.tensor_tensor(out=t[:], in0=t[:], in1=x[:], op=ALU.mult), reads=[t, x], writes=[t])
            K.op('dve', lambda e: e.tensor_tensor_scan(out=hh[:], data0=a[:], data1=t[:], initial=hc[:, 0:1], op0=ALU.mult, op1=ALU.add),
                 reads=[a, t, hc], writes=[hh])
            K.op('pool', lambda e: e.tensor_copy(out=hc[:], in_=hh[:, SEG - 1:SEG]), reads=[hh], writes=[hc])
            gelu_tanh(K, rg, tmp, gg[:], gg)
            K.op('dve', lambda e: e.tensor_tensor(out=ybf[:], in0=hh[:], in1=gg[:], op=ALU.mult), reads=[hh, gg], writes=[ybf])
            for q4 in range(SEG // 512):
                p_ = ptr[nt % 2]
                y_ = yo[nt % 2]
                nt += 1
                for j in range(4):
                    tc_ = slice(q4 * 512 + j * 128, q4 * 512 + (j + 1) * 128)
                    K.tr(p_, p_[:, j * 128:(j + 1) * 128], ybf, ybf[:, tc_], identb, identb[:])
                K.op('act', lambda e, p_=p_, y_=y_: e.copy(out=y_[:], in_=p_[:, 0:512].rearrange("p (j c) -> p j c", j=4)), reads=[p_], writes=[y_])
                t0 = s * SEG + q4 * 512
                K.dma('pool', d['mixed'][t0:t0 + 512, 512 + c * 128:512 + (c + 1) * 128].rearrange("(j p) c -> p j c", p=128), y_[:],
                      reads=[y_])
    K.phase_end()


def all_phases():
    return [
        phase_A0, phase_Fprep, phase_fox, phase_Gprep, phase_gdn2,
        lambda K, d, S: phase_O(K, d, S, d['x'], d['even_w_out']),
        lambda K, d, S: phase_MLP(K, d, S, 0),
        lambda K, d, S: phase_PLE(K, d, S, 0),
        phase_Rprep, phase_A1, phase_cmp, phase_nsa_cmp, phase_nsa_slc, phase_nsa_win, phase_nsa_combine, phase_lru,
        lambda K, d, S: phase_O(K, d, S, d['h'], d['odd_w_out']),
        lambda K, d, S: phase_MLP(K, d, S, 1),
        lambda K, d, S: phase_PLE(K, d, S, 1, final=True),
    ]


SEQ = 8192
NCORES = 8


def kernel(**inputs):
    S = SEQ
    nc, K = build(S, all_phases())
    in_maps = []
    shared = {}
    for n, shp in INPUT_SHAPES.items():
        a = np.asarray(inputs[n])
        if list(a.shape) != shp:
            a = a.reshape(shp)
        shared[n] = np.ascontiguousarray(a.astype(np.float32, copy=False))
    x = np.asarray(inputs['x'])
    p = np.asarray(inputs['p'])
    pos = np.asarray(inputs['positions'])
    for b in range(NCORES):
        m = dict(shared)
        m['x'] = np.ascontiguousarray(x[b])
        m['p'] = np.ascontiguousarray(p[:, b])
        m['positions'] = np.ascontiguousarray(pos[b:b + 1]).astype(np.int32, copy=False)
        in_maps.append(m)
    res = run_bass_kernel_spmd(nc, in_maps, core_ids=list(range(NCORES)))
    out = np.stack([np.asarray(res.results[b]['out']) for b in range(NCORES)], axis=0)
    return out.astype(np.float32, copy=False)


def phase_gdn2(K, d, S, heads=range(4)):
    NC = S // 128
    NG = S // 512
    heads = list(heads)
    K.phase_begin()
    consts(K)
    identf = make_ident(K, F32, 'identf')
    onesf = K.sb('onesf', [128, 128], F32)
    K.op('pool', lambda e: e.memset(onesf[:], 1.0), writes=[onesf])
    maskT = K.sb('maskT', [128, 128], F32)
    maskTs = K.sb('maskTs', [128, 128], F32)
    zer = K.sb('zer', [128, 128], F32)
    K.op('pool', lambda e: e.memset(zer[:], 0.0), writes=[zer])
    K.op('pool', lambda e: e.affine_select(out=maskT[:], in_=zer[:], pattern=[[1, 128]], compare_op=ALU.is_ge, fill=NEGM,
                                           base=0, channel_multiplier=-1), reads=[zer], writes=[maskT])
    K.op('pool', lambda e: e.affine_select(out=maskTs[:], in_=zer[:], pattern=[[1, 128]], compare_op=ALU.is_gt, fill=NEGM,
                                           base=0, channel_multiplier=-1), reads=[zer], writes=[maskTs])
    nwb = K.sb('nwb', [128, 128], F32)
    K.dma('sp', nwb[:], d['even_gdn_norm_w'].partition_broadcast(128), writes=[nwb])
    tokS = K.sb('tokS', [128, NC, 16], F32)
    K.dma('sp', tokS[:].rearrange("p c q -> p (c q)"), d['tokSd'][:, :], writes=[tokS])
    c128 = K.sb('c128', [128, 1], F32)
    K.op('pool', lambda e: e.memset(c128[:], 128.0 * 1e-6), writes=[c128])
    onesb = K.sb('onesb', [128, 128], BF16)
    K.op('pool', lambda e: e.memset(onesb[:], 1.0), writes=[onesb])
    lnt = [K.sb('gln%d' % i, [128, 512], F32) for i in range(2)]
    ngl = [0]

    class H:
        pass
    HB = {}
    f32t = lambda nm, h: K.sb('%s_h%d' % (nm, h), [128, 128], F32)
    bf_t = lambda nm, h: K.sb('%s_h%d' % (nm, h), [128, 128], BF16)
    TB = {}
    for h in heads:
        for par in range(2):
            t = H()
            t.bk = K.ps('gbk%d_%d' % (h, par), [128, 512], F32)
            t.z = K.sb('gz_%d_%d' % (h, par), [128, 128], F32)
            t.nz = K.sb('nz_%d_%d' % (h, par), [128, 128], F32)
            for nm in ('Kbg', 'Kd', 'Vb', 'attnT', 'Xb', 'WT', 'vnew', 'ob'):
                setattr(t, nm, K.sb('%s_%d_%d' % (nm, h, par), [128, 128], BF16))
            for nm in ('dm1', 'dm2', 'Mt', 'U', 'osb', 'ojunk'):
                setattr(t, nm, K.sb('%s_%d_%d' % (nm, h, par), [128, 128], F32))
            t.P = [K.sb('P%d_%d_%d' % (i, h, par), [128, 128], F32) for i in range(2)]
            t.RX = K.sb('RX_%d_%d' % (h, par), [128, 256], F32)
            t.ost = K.sb('ost_%d_%d' % (h, par), [128, 4], F32)
            TB[(h, par)] = t
    for h in heads:
        b = H()
        b.gcrow = K.sb('gcrow%d' % h, [1, 512], F32)
        b.e1row = K.sb('e1row%d' % h, [1, 512], F32)
        b.gcb = [K.sb('gcb%d_%d' % (h, i), [128, 512], F32) for i in range(2)]
        b.sqk = K.sb('sqk%d' % h, [128, 512], BF16)
        b.sqq = K.sb('sqq%d' % h, [128, 512], BF16)
        b.eglrow = K.sb('eglrow%d' % h, [1, NC], F32)
        b.eglb = K.sb('eglb%d' % h, [128, NC], F32)
        b.Sf = f32t('Sf', h)
        b.Sb = bf_t('Sb', h)
        b.q_ = K.sb('gq%d' % h, [128, 512], F32)
        b.k_ = K.sb('gk%d' % h, [128, 512], F32)
        b.v_ = [K.sb('gv%d_%d' % (h, i), [128, 512], F32) for i in range(2)]
        b.knf = [K.sb('knf%d_%d' % (h, i), [128, 512], F32) for i in range(2)]
        b.knb = [K.sb('knb%d_%d' % (h, i), [128, 512], BF16) for i in range(2)]
        b.qnf = K.sb('qnf%d' % h, [128, 512], F32)
        b.qnb = [K.sb('qnb%d_%d' % (h, i), [128, 512], BF16) for i in range(2)]
        b.qgb = [K.sb('qgb%d_%d' % (h, i), [128, 512], BF16) for i in range(2)]
        HB[h] = b
    sl = lambda i: slice(i * 128, (i + 1) * 128)
    import os
    F32R = mybir.dt.float32r
    R = (lambda ap: ap.bitcast(F32R)) if os.environ.get('GDN_F32R', '0') == '1' else (lambda ap: ap)

    for h in heads:
        b = HB[h]
        K.dma('sp', b.eglrow[:], d['egld'][h:h + 1, :], writes=[b.eglrow])
        pg0 = TB[(h, 0)].bk
        K.mm(pg0, pg0[:, 0:NC], onesf, onesf[0:1, :], b.eglrow, b.eglrow[0:1, :])
        K.op('act', lambda e: e.copy(out=b.eglb[:], in_=pg0[:, 0:NC]), reads=[pg0], writes=[b.eglb])
        K.op('pool', lambda e: e.memset(b.Sf[:], 0.0), writes=[b.Sf])
        K.op('pool', lambda e: e.memset(b.Sb[:], 0.0), writes=[b.Sb])

    def prep_a(h, g):
        b = HB[h]
        cs = slice(g * 512, (g + 1) * 512)
        v_ = b.v_[g % 2]
        K.dma('sp', b.q_[:], d['GCT'][h * 128:(h + 1) * 128, cs], writes=[b.q_])
        K.dma('sp', b.k_[:], d['GCT'][512 + h * 128:512 + (h + 1) * 128, cs], writes=[b.k_])
        K.dma('sp', v_[:], d['GCT'][1024 + h * 128:1024 + (h + 1) * 128, cs], writes=[v_])
        K.dma('sp', b.gcrow[:], d['gcd'][h:h + 1, cs], writes=[b.gcrow])
        K.dma('sp', b.e1row[:], d['e1d'][h:h + 1, cs], writes=[b.e1row])
        K.op('pool', lambda e: e.tensor_tensor(out=b.sqk[:], in0=b.k_[:], in1=b.k_[:], op=ALU.mult), reads=[b.k_], writes=[b.sqk])
        K.op('pool', lambda e: e.tensor_tensor(out=b.sqq[:], in0=b.q_[:], in1=b.q_[:], op=ALU.mult), reads=[b.q_], writes=[b.sqq])

    def prep_b(h, g):
        b = HB[h]
        b0, b1 = TB[(h, 0)].bk, TB[(h, 1)].bk
        i = ngl[0] % 2
        ngl[0] += 1
        lk, lq = lnt[0], lnt[1]
        knf, knb, qnb, qgb, gcb = b.knf[g % 2], b.knb[g % 2], b.qnb[g % 2], b.qgb[g % 2], b.gcb[g % 2]
        K.mm(b0, b0[:], onesb, onesb[:], b.sqk, b.sqk[:])
        K.mm(b1, b1[:], onesb, onesb[:], b.sqq, b.sqq[:])
        K.op('act', lambda e: e.activation(out=lk[:], in_=b0[:], func=AF.Ln, bias=K.eps_t[:, 0:1]), reads=[b0, K.eps_t], writes=[lk])
        K.op('act', lambda e: e.activation(out=lq[:], in_=b1[:], func=AF.Ln, scale=128.0, bias=c128[:, 0:1]), reads=[b1, c128], writes=[lq])
        K.mm(b0, b0[:], onesf, onesf[0:1, :], b.gcrow, b.gcrow[0:1, :])
        K.mm(b1, b1[:], onesf, onesf[0:1, :], b.e1row, b.e1row[0:1, :])
        K.op('act', lambda e: e.activation(out=lk[:], in_=lk[:], func=AF.Exp, scale=-0.5), reads=[lk], writes=[lk])
        K.op('act', lambda e: e.activation(out=lq[:], in_=lq[:], func=AF.Exp, scale=-0.5), reads=[lq], writes=[lq])
        K.op('act', lambda e: e.copy(out=gcb[:], in_=b0[:]), reads=[b0], writes=[gcb])
        K.op('dve', lambda e: e.tensor_tensor(out=knf[:], in0=b.k_[:], in1=lk[:], op=ALU.mult), reads=[b.k_, lk], writes=[knf])
        K.op('pool', lambda e: e.tensor_copy(out=knb[:], in_=knf[:]), reads=[knf], writes=[knb])
        K.op('dve', lambda e: e.tensor_tensor(out=b.qnf[:], in0=b.q_[:], in1=lq[:], op=ALU.mult), reads=[b.q_, lq], writes=[b.qnf])
        K.op('pool', lambda e: e.tensor_copy(out=qnb[:], in_=b.qnf[:]), reads=[b.qnf], writes=[qnb])
        K.op('dve', lambda e: e.tensor_tensor(out=qgb[:], in0=b.qnf[:], in1=b1[:], op=ALU.mult), reads=[b.qnf, b1], writes=[qgb])

    class View:
        def __init__(self, t, hb):
            self._t, self._hb = t, hb

        def __getattr__(self, n):
            t = object.__getattribute__(self, '_t')
            if hasattr(t, n):
                return getattr(t, n)
            return getattr(object.__getattribute__(self, '_hb'), n)
    VW = {(h, par): View(TB[(h, par)], HB[h]) for h in heads for par in range(2)}

    def st1(h, g, tt):
        b = VW[(h, tt % 2)]
        c = g * 4 + tt
        ts_, tok = sl(tt), slice(c * 128, (c + 1) * 128)
        bk, v_ = b.bk, b.v_[g % 2]
        z = b.z
        K.dma('sp', z[:], d['GZ'][tok, h * 128:(h + 1) * 128], writes=[z])
        K.op('pool', lambda e: e.tensor_tensor(out=b.nz[:], in0=z[:], in1=nwb[:], op=ALU.mult), reads=[z, nwb], writes=[b.nz])
        beta_c, be_c, e2_c, gc_c = tokS[:, c, h:h + 1], tokS[:, c, 4 + h:5 + h], tokS[:, c, 8 + h:9 + h], tokS[:, c, 12 + h:13 + h]
        knf, knb, qnb, gcb = b.knf[g % 2], b.knb[g % 2], b.qnb[g % 2], b.gcb[g % 2]
        K.tr(bk, bk[:, 0:128], knf, knf[:, ts_], identf, identf[:])
        K.tr(bk, bk[:, 128:256], v_, v_[:, ts_], identf, identf[:])
        K.mm(bk, bk[:, 256:384], knb, knb[:, ts_], knb, knb[:, ts_])
        K.mm(bk, bk[:, 384:512], knb, knb[:, ts_], qnb, qnb[:, ts_])
        K.op('dve', lambda e: e.tensor_scalar(out=b.Kbg[:], in0=bk[:, 0:128], scalar1=be_c, scalar2=None, op0=ALU.mult), reads=[bk, tokS], writes=[b.Kbg])
        K.op('dve', lambda e: e.tensor_scalar(out=b.Kd[:], in0=bk[:, 0:128], scalar1=e2_c, scalar2=None, op0=ALU.mult), reads=[bk, tokS], writes=[b.Kd])
        K.op('dve', lambda e: e.tensor_scalar(out=b.Vb[:], in0=bk[:, 128:256], scalar1=beta_c, scalar2=None, op0=ALU.mult), reads=[bk, tokS], writes=[b.Vb])
        K.op('dve', lambda e: e.scalar_tensor_tensor(out=b.dm1[:], in0=gcb[:, ts_], scalar=gc_c, in1=maskT[:], op0=ALU.subtract, op1=ALU.add),
             reads=[gcb, tokS, maskT], writes=[b.dm1])
        K.op('dve', lambda e: e.scalar_tensor_tensor(out=b.dm2[:], in0=gcb[:, ts_], scalar=gc_c, in1=maskTs[:], op0=ALU.subtract, op1=ALU.add),
             reads=[gcb, tokS, maskTs], writes=[b.dm2])

    def st1b(h, g, tt):
        b = VW[(h, tt % 2)]
        bk = b.bk
        K.op('act', lambda e: e.activation(out=b.dm1[:], in_=b.dm1[:], func=AF.Exp), reads=[b.dm1], writes=[b.dm1])
        K.op('act', lambda e: e.activation(out=b.dm2[:], in_=b.dm2[:], func=AF.Exp), reads=[b.dm2], writes=[b.dm2])
        K.op('dve', lambda e: e.tensor_tensor(out=b.attnT[:], in0=bk[:, 384:512], in1=b.dm1[:], op=ALU.mult), reads=[bk, b.dm1], writes=[b.attnT])
        K.op('dve', lambda e: e.tensor_tensor(out=b.Mt[:], in0=bk[:, 256:384], in1=b.dm2[:], op=ALU.mult), reads=[bk, b.dm2], writes=[b.Mt])

    def st2(h, g, tt):
        b = VW[(h, tt % 2)]
        c = g * 4 + tt
        bk = b.bk
        beta_c = tokS[:, c, h:h + 1]
        K.tr(bk, bk[:, 0:128], b.Mt, b.Mt[:], identf, identf[:])
        K.op('dve', lambda e: e.tensor_scalar(out=b.P[0][:], in0=bk[:, 0:128], scalar1=beta_c, scalar2=-1.0, op0=ALU.mult, op1=ALU.mult),
             reads=[bk, tokS], writes=[b.P[0]])

    def st2b(h, g, tt):
        b = VW[(h, tt % 2)]
        bk = b.bk
        K.tr(bk, bk[:, 128:256], b.P[0], b.P[0][:], identf, identf[:])
        K.op('act', lambda e: e.copy(out=b.RX[:, 0:128], in_=bk[:, 128:256]), reads=[bk], writes=[b.RX])
        K.op('dve', lambda e: e.tensor_tensor(out=b.RX[:, 128:256], in0=bk[:, 128:256], in1=identf[:], op=ALU.add), reads=[bk, identf],
             writes=[b.RX], nowaw=[b.RX])

    def lvl_first(h, tt):
        b = VW[(h, tt % 2)]
        bk = b.bk
        K.mm(bk, bk[:, 256:384], b.RX, b.RX[:, 0:128], b.P[0], b.P[0][:])
        K.mm(bk, bk[:, 384:512], b.P[0], b.P[0][:], b.RX, b.RX[:, 0:128])
        K.op('act', lambda e: e.copy(out=b.P[1][:], in_=bk[:, 256:384]), reads=[bk], writes=[b.P[1]])
        K.op('act', lambda e: e.copy(out=b.RX[:, 0:128], in_=bk[:, 384:512]), reads=[bk], writes=[b.RX], nowaw=[b.RX])

    def lvl_mid(h, n, tt):
        b = VW[(h, tt % 2)]
        bk = b.bk
        Pn, Pn1 = b.P[n % 2], b.P[(n + 1) % 2]
        if n < 5:
            K.mm(bk, bk[:, 0:256], Pn, Pn[:], b.RX, b.RX[:, 0:256])
        else:
            K.mm(bk, bk[:, 128:256], Pn, Pn[:], b.RX, b.RX[:, 128:256])
        K.mm(bk, bk[:, 256:384], b.RX, b.RX[:, 0:128], Pn, Pn[:])
        K.op('dve', lambda e: e.tensor_tensor(out=b.RX[:, 128:256], in0=b.RX[:, 128:256], in1=bk[:, 128:256], op=ALU.add),
             reads=[b.RX, bk], writes=[b.RX], nowaw=[b.RX])
        if n < 5:
            K.op('act', lambda e: e.copy(out=b.RX[:, 0:128], in_=bk[:, 0:128]), reads=[bk], writes=[b.RX], nowaw=[b.RX])
        K.op('act', lambda e: e.copy(out=Pn1[:], in_=bk[:, 256:384]), reads=[bk], writes=[Pn1])

    def lvl_last(h, tt):
        b = VW[(h, tt % 2)]
        bk = b.bk
        K.mm(bk, bk[:, 0:128], b.P[0], b.P[0][:], b.RX, b.RX[:, 128:256])
        K.op('dve', lambda e: e.tensor_tensor(out=b.RX[:, 128:256], in0=b.RX[:, 128:256], in1=bk[:, 0:128], op=ALU.add),
             reads=[b.RX, bk], writes=[b.RX], nowaw=[b.RX])

    def st3(h, g, tt):
        b = VW[(h, tt % 2)]
        bk = b.bk
        K.op('pool', lambda e: e.tensor_copy(out=b.Xb[:], in_=b.RX[:, 128:256]), reads=[b.RX], writes=[b.Xb])
        K.mm(bk, bk[:, 0:128], b.Xb, b.Xb[:], b.Vb, b.Vb[:])
        K.mm(bk, bk[:, 128:256], b.Kbg, b.Kbg[:], b.Xb, b.Xb[:])
        K.op('act', lambda e: e.copy(out=b.U[:], in_=bk[:, 0:128]), reads=[bk], writes=[b.U])
        K.op('act', lambda e: e.copy(out=b.WT[:], in_=bk[:, 128:256]), reads=[bk], writes=[b.WT])

    def st4(h, g, tt):
        b = VW[(h, tt % 2)]
        c = g * 4 + tt
        ts_ = sl(tt)
        bk = b.bk
        K.mm(bk, bk[:, 256:384], b.WT, b.WT[:], b.Sb, b.Sb[:])
        K.op('dve', lambda e: e.tensor_tensor(out=b.vnew[:], in0=b.U[:], in1=bk[:, 256:384], op=ALU.subtract), reads=[b.U, bk], writes=[b.vnew])
        K.mm(bk, bk[:, 384:512], b.qgb[g % 2], b.qgb[g % 2][:, ts_], b.Sb, b.Sb[:], start=True, stop=False)
        K.mm(bk, bk[:, 384:512], b.attnT, b.attnT[:], b.vnew, b.vnew[:], start=False, stop=True)
        K.mm(bk, bk[:, 0:128], b.Kd, b.Kd[:], b.vnew, b.vnew[:])
        K.op('dve', lambda e: e.scalar_tensor_tensor(out=b.Sb[:], in0=b.Sf[:], scalar=b.eglb[:, c:c + 1], in1=bk[:, 0:128],
                                                     op0=ALU.mult, op1=ALU.add), reads=[b.Sf, b.eglb, bk], writes=[b.Sb])
        K.op('dve', lambda e: e.scalar_tensor_tensor(out=b.Sf[:], in0=b.Sf[:], scalar=b.eglb[:, c:c + 1], in1=bk[:, 0:128],
                                                     op0=ALU.mult, op1=ALU.add), reads=[b.Sf, b.eglb, bk], writes=[b.Sf])
        K.op('act', lambda e: e.copy(out=b.osb[:], in_=bk[:, 384:512]), reads=[bk], writes=[b.osb])

    def st5(h, g, tt):
        b = VW[(h, tt % 2)]
        c = g * 4 + tt
        tok = slice(c * 128, (c + 1) * 128)
        K.op('dve', lambda e: e.scalar_tensor_tensor(out=b.ojunk[:], in0=b.osb[:], scalar=1.0 / 128, in1=b.osb[:], op0=ALU.mult, op1=ALU.mult,
                                                     accum_out=b.ost[:, 0:1]), reads=[b.osb], writes=[b.ojunk, b.ost])
        K.op('act', lambda e: e.activation(out=b.ost[:, 1:2], in_=b.ost[:, 0:1], func=AF.Ln, bias=K.eps_t[:, 0:1]), reads=[b.ost, K.eps_t], writes=[b.ost])
        K.op('act', lambda e: e.activation(out=b.ost[:, 2:3], in_=b.ost[:, 1:2], func=AF.Exp, scale=-0.5), reads=[b.ost], writes=[b.ost])
        o_b = b.ob
        K.op('dve', lambda e: e.scalar_tensor_tensor(out=o_b[:], in0=b.osb[:], scalar=b.ost[:, 2:3], in1=b.nz[:], op0=ALU.mult, op1=ALU.mult),
             reads=[b.osb, b.ost, b.nz], writes=[o_b])
        K.dma('pool', d['mixed'][tok, 512 + h * 128:512 + (h + 1) * 128], o_b[:], reads=[o_b])

    for h in heads:
        prep_a(h, 0)
    for h in heads:
        prep_b(h, 0)
    for g in range(NG):
        if g + 1 < NG:
            for h in heads:
                prep_a(h, g + 1)
        for tp in range(0, 4, 2):
            tts = (tp, tp + 1)
            if tp == 2 and g + 1 < NG:
                for h in heads:
                    prep_b(h, g + 1)
            for stage in (st1, st1b, st2, st2b):
                for tt in tts:
                    for h in heads:
                        stage(h, g, tt)
            for tt in tts:
                for h in heads:
                    lvl_first(h, tt)
            for n in range(1, 6):
                for tt in tts:
                    for h in heads:
                        lvl_mid(h, n, tt)
            for tt in tts:
                for h in heads:
                    lvl_last(h, tt)
            for tt in tts:
                for h in heads:
                    st3(h, g, tt)
            for tt in tts:
                for h in heads:
                    st4(h, g, tt)
            for tt in tts:
                for h in heads:
                    st5(h, g, tt)
    K.phase_end()
```

```python
from contextlib import ExitStack
import concourse.bass as bass
import concourse.mybir as mybir

F32 = mybir.dt.float32
BF16 = mybir.dt.bfloat16
I32 = mybir.dt.int32
ALU = mybir.AluOpType
AF = mybir.ActivationFunctionType
AX = mybir.AxisListType
ENG = ['pe', 'act', 'dve', 'pool', 'sp']
NDS = 90


class Buf:
    def __init__(self, t, name):
        self.t = t
        self.name = name
        self.w = {}
        self.r = {}
        self.dsem = None
        self.psum = False
        self.wr = {}

    def __getitem__(self, k):
        return self.t[k]


class KCtx:
    def __init__(self, nc):
        self.nc = nc
        self.e = dict(pe=nc.tensor, act=nc.scalar, dve=nc.vector, pool=nc.gpsimd, sp=nc.sync)
        self.gstack = ExitStack()
        self.csem = {n: self.gstack.enter_context(nc.semaphore('c_' + n)) for n in ENG}
        self.cnt = {n: 0 for n in ENG}
        self.dsems = [self.gstack.enter_context(nc.semaphore('d%d' % i)) for i in range(NDS)]
        self.dcnt = [0] * NDS
        self.free_ds = list(range(NDS))
        self.seen = {n: {} for n in ENG}
        self.pstack = None
        self.phase_bufs = []
        self.nwaits = 0
        self.uid = 0

    def phase_begin(self):
        self.pstack = ExitStack()
        self.phase_bufs = []

    def phase_end(self):
        self.barrier()
        for b in self.phase_bufs:
            if b.dsem is not None:
                self.free_ds.append(b.dsem)
                b.dsem = None
        self.pstack.close()
        self.pstack = None

    def sb(self, name, shape, dt):
        self.uid += 1
        t = self.pstack.enter_context(self.nc.sbuf_tensor('%s_%d' % (name, self.uid), list(shape), dt))
        b = Buf(t, name)
        self.phase_bufs.append(b)
        return b

    def ps(self, name, shape, dt=F32):
        self.uid += 1
        t = self.pstack.enter_context(self.nc.psum_tensor('%s_%d' % (name, self.uid), list(shape), dt))
        b = Buf(t, name)
        b.psum = True
        self.phase_bufs.append(b)
        return b

    def _sem(self, k):
        return self.csem[k] if isinstance(k, str) else self.dsems[k]

    def _wait(self, eng, deps, force_self=False):
        for k, v in deps.items():
            if v <= self.seen[eng].get(k, 0):
                continue
            if k == eng and not force_self:
                if eng == 'pe':
                    continue
                if v < self.cnt[eng] - 1:
                    continue
            self.e[eng].wait_ge(self._sem(k), v)
            self.nwaits += 1
            self.seen[eng][k] = v

    @staticmethod
    def _merge(d, s):
        for k, v in s.items():
            if v > d.get(k, 0):
                d[k] = v

    def _deps(self, reads, writes, nowaw, eng=None):
        deps = {}
        for b in reads:
            self._merge(deps, b.w)
            if b.psum:
                self._merge(deps, {k: v for k, v in b.r.items() if k != eng})
        for b in writes:
            if b not in nowaw:
                self._merge(deps, b.w)
            else:
                self._merge(deps, b.wr)
            self._merge(deps, b.r)
        return deps

    def op(self, eng, fn, reads=(), writes=(), nowaw=()):
        self._wait(eng, self._deps(reads, writes, nowaw, eng))
        ins = fn(self.e[eng])
        self.cnt[eng] += 1
        c = self.cnt[eng]
        ins.then_inc(self.csem[eng], 1)
        for b in reads:
            if c > b.r.get(eng, 0):
                b.r[eng] = c
        for b in writes:
            if b in nowaw:
                b.w[eng] = c
            else:
                b.w = {eng: c}
                b.wr = {eng: c}
                b.r = {}
        return ins

    def dma(self, q, out, in_, reads=(), writes=(), nowaw=(), **kw):
        self._wait(q, self._deps(reads, writes, nowaw))
        b0 = (list(writes) + list(reads))[0]
        if b0.dsem is None:
            assert self.free_ds, "out of DMA semaphores"
            b0.dsem = self.free_ds.pop(0)
        i = b0.dsem
        self.e[q].dma_start(out=out, in_=in_, **kw).then_inc(self.dsems[i], 16)
        self.dcnt[i] += 16
        v = self.dcnt[i]
        for b in reads:
            b.r[i] = v
        for b in writes:
            if b in nowaw:
                b.w[i] = v
            else:
                b.w = {i: v}
                b.wr = {i: v}
                b.r = {}

    def barrier(self):
        deps = {n: self.cnt[n] for n in ENG if self.cnt[n] > 0}
        for i in range(NDS):
            if self.dcnt[i] > 0:
                deps[i] = self.dcnt[i]
        for eng in ENG:
            self._wait(eng, deps, force_self=True)

    def finish(self):
        self.barrier()
        self.gstack.close()

    def mm(self, out_b, out_ap, lhsT_b, lhsT_ap, rhs_b, rhs_ap, start=True, stop=True, extra_reads=()):
        return self.op('pe', lambda e: e.matmul(out_ap, lhsT_ap, rhs_ap, start=start, stop=stop),
                       reads=[lhsT_b, rhs_b] + list(extra_reads), writes=[out_b])

    def tr(self, out_b, out_ap, in_b, in_ap, id_b, id_ap):
        return self.op('pe', lambda e: e.transpose(out_ap, in_ap, id_ap), reads=[in_b, id_b], writes=[out_b])

import numpy as np
from concourse.bass_utils import run_bass_kernel_spmd

D = 1024
EPS = 1e-6


def make_ident(K, dt, name):
    ones = K.sb(name + '_ones', [128, 128], dt)
    ident = K.sb(name, [128, 128], dt)
    K.op('pool', lambda e: e.memset(ones[:], 1.0), writes=[ones])
    K.op('pool', lambda e: e.affine_select(out=ident[:], in_=ones[:], pattern=[[-1, 128]],
                                           compare_op=ALU.is_equal, fill=0.0, base=0, channel_multiplier=1),
         reads=[ones], writes=[ident])
    return ident


def load_weight_bf16(K, Wb, src, kchunks, ncols, gain=None, stage_name='wst', q='sp', cb=1024):
    cb = min(cb, ncols)
    st = [K.sb(stage_name + str(i), [128, cb], F32) for i in range(4)]
    n = 0
    for k in range(kchunks):
        for c0 in range(0, ncols, cb):
            c1 = min(ncols, c0 + cb)
            s = st[n % 4]
            eng = 'dve' if n % 2 == 0 else 'act'
            n += 1
            K.dma(q, s[:, 0:c1 - c0], src[k * 128:(k + 1) * 128, c0:c1], writes=[s])
            if eng == 'act':
                if gain is not None:
                    K.op('act', lambda e, s=s, k=k, c0=c0, c1=c1: e.activation(out=Wb[:, k, c0:c1], in_=s[:, 0:c1 - c0], func=AF.Copy,
                                                                               scale=gain[:, k:k + 1]),
                         reads=[s, gain], writes=[Wb], nowaw=[Wb])
                else:
                    K.op('act', lambda e, s=s, k=k, c0=c0, c1=c1: e.copy(out=Wb[:, k, c0:c1], in_=s[:, 0:c1 - c0]),
                         reads=[s], writes=[Wb], nowaw=[Wb])
            elif gain is not None:
                K.op(eng, lambda e, s=s, k=k, c0=c0, c1=c1: e.tensor_scalar(out=Wb[:, k, c0:c1], in0=s[:, 0:c1 - c0], scalar1=gain[:, k:k + 1],
                                                                scalar2=None, op0=ALU.mult),
                     reads=[s, gain], writes=[Wb], nowaw=[Wb])
            else:
                K.op(eng, lambda e, s=s, k=k, c0=c0, c1=c1: e.tensor_copy(out=Wb[:, k, c0:c1], in_=s[:, 0:c1 - c0]),
                     reads=[s], writes=[Wb], nowaw=[Wb])


class NormT:
    def __init__(self, K, identb, nbuf_x=3, gt=4, nbuf_n=2):
        self.K = K
        self.identb = identb
        self.nn = nbuf_n
        self.xb = [K.sb('xb%d' % i, [128, D], F32) for i in range(nbuf_x)]
        self.xn = [K.sb('xn%d' % i, [128, D], BF16) for i in range(nbuf_n)]
        self.st = [K.sb('nst%d' % i, [128, 4], F32) for i in range(nbuf_n)]
        self.pT = [K.ps('pT%d' % i, [128, D], BF16) for i in range(2)]
        self.hnT = [K.sb('hnT%d' % i, [128, 8, gt * 128], BF16) for i in range(2)]
        self.n = 0

    def prep(self, src_rows):
        K = self.K
        n = self.n
        self.n += 1
        xb = self.xb[n % len(self.xb)]
        xn = self.xn[n % self.nn]
        st = self.st[n % self.nn]
        K.dma('sp', xb[:], src_rows, writes=[xb])
        K.op('act', lambda e: e.activation(out=xn[:], in_=xb[:], func=AF.Square, accum_out=st[:, 0:1]),
             reads=[xb], writes=[xn, st])
        K.op('act', lambda e: e.activation(out=st[:, 1:2], in_=st[:, 0:1], func=AF.Sqrt, scale=1.0 / D, bias=K.eps_t[:, 0:1]),
             reads=[st, K.eps_t], writes=[st])
        K.op('dve', lambda e: e.reciprocal(out=st[:, 2:3], in_=st[:, 1:2]), reads=[st], writes=[st])
        K.op('dve', lambda e: e.tensor_scalar(out=xn[:], in0=xb[:], scalar1=st[:, 2:3], scalar2=None, op0=ALU.mult),
             reads=[xb, st], writes=[xn])
        return (n, xb)

    def finish(self, tok, hnT, j):
        K = self.K
        n, xb = tok
        xn = self.xn[n % self.nn]
        pT = self.pT[n % 2]
        for k in range(8):
            K.tr(pT, pT[:, k * 128:(k + 1) * 128], xn, xn[:, k * 128:(k + 1) * 128], self.identb, self.identb[:])
        K.op('act', lambda e: e.copy(out=hnT[:, :, j * 128:(j + 1) * 128],
                                     in_=pT[:].rearrange("p (k t) -> p k t", k=8)),
             reads=[pT], writes=[hnT], nowaw=[hnT])
        return xb

    def tile(self, src_rows, hnT, j, keep_x=None):
        return self.finish(self.prep(src_rows), hnT, j)


def consts(K):
    K.eps_t = K.sb('eps', [128, 1], F32)
    K.op('pool', lambda e: e.memset(K.eps_t[:], EPS), writes=[K.eps_t])
    K.one_t = K.sb('one', [128, 1], F32)
    K.op('pool', lambda e: e.memset(K.one_t[:], 1.0), writes=[K.one_t])
    K.mhalf_t = K.sb('mhalf', [128, 1], F32)
    K.op('pool', lambda e: e.memset(K.mhalf_t[:], -0.5), writes=[K.mhalf_t])


E_FQ, E_FK, E_FV, E_FF, E_GQ, E_GK, E_GV, E_GZ, E_GB, E_GA = 0, 512, 1024, 1536, 1544, 2056, 2568, 3080, 3592, 3596


def phase_A0(K, d, S):
    NG = S // 512
    K.phase_begin()
    consts(K)
    identb = make_ident(K, BF16, 'identb')
    gain = K.sb('gain', [128, 8], F32)
    K.dma('sp', gain[:], d['even_norm_mix'].rearrange("(k p) -> p k", p=128), writes=[gain],
          allow_slow_non_contiguous=True)
    cw = K.sb('cw', [128, 4, 12], F32)
    for j in range(4):
        K.dma('sp', cw[:, j, :], d['even_gdn_conv_w'][j, :].rearrange("(c p) -> p c", p=128), writes=[cw], nowaw=[cw],
              allow_slow_non_contiguous=True)
    W = K.sb('W', [128, 8, 3600], BF16)
    load_weight_bf16(K, W, d['even_w_in'], 8, 3600, gain=gain)
    NT = NormT(K, identb, nbuf_x=8, nbuf_n=8)
    xpad = [K.sb('xpad%d' % c, [128, 515], F32) for c in range(12)]
    for c in range(12):
        K.op('pool', lambda e, c=c: e.memset(xpad[c][:, 0:3], 0.0), writes=[xpad[c]])
    acc = [K.sb('acc%d' % i, [128, 512], F32) for i in range(2)]
    cout = [K.sb('cout%d' % i, [128, 512], F32) for i in range(3)]
    qk = [K.sb('qk%d' % i, [128, 512], BF16) for i in range(3)]
    vt = [K.sb('vt%d' % i, [128, 512], BF16) for i in range(2)]
    zt = [K.sb('zt%d' % i, [128, 512], F32) for i in range(2)]
    sm = [K.sb('sm%d' % i, [8, 512], F32) for i in range(2)]
    sm2 = [K.sb('smb%d' % i, [8, 512], F32) for i in range(2)]
    ptm = [K.ps('ptm%d' % i, [128, 512], F32) for i in range(2)]
    pfm = [K.ps('pfm%d' % i, [128, 512], F32) for i in range(3)]
    nfm = 0
    ntm = 0
    pend_silu = [None]
    toks = [NT.prep(d['x'][u * 128:(u + 1) * 128, :]) for u in range(4)]
    for g in range(NG):
        hnT = NT.hnT[g % 2]
        for j in range(4):
            t = g * 4 + j
            NT.finish(toks.pop(0), hnT, j)
            if j == 0:
                for u in range(t + 4, min(t + 8, NG * 4)):
                    toks.append(NT.prep(d['x'][u * 128:(u + 1) * 128, :]))
            for which, col0 in ((0, E_FV), (1, E_GZ)):
                p = ptm[ntm % 2]
                ntm += 1
                for k in range(8):
                    K.mm(p, p[:], hnT, hnT[:, k, j * 128:(j + 1) * 128], W, W[:, k, col0:col0 + 512],
                         start=(k == 0), stop=(k == 7))
                if which == 0:
                    o = vt[t % 2]
                    K.op('dve', lambda e, o=o, p=p: e.tensor_copy(out=o[:], in_=p[:]), reads=[p], writes=[o])
                    K.dma('pool', d['Vf'][t * 128:(t + 1) * 128, :], o[:], reads=[o])
                else:
                    o = zt[t % 2]
                    K.op('act', lambda e, o=o, p=p: e.activation(out=o[:], in_=p[:], func=AF.Silu), reads=[p], writes=[o])
                    K.dma('pool', d['GZ'][t * 128:(t + 1) * 128, :], o[:], reads=[o])
        chunks = ([('q', c, E_FQ + c * 128, 128) for c in range(4)] + [('k', c, E_FK + c * 128, 128) for c in range(4)]
                  + [('g', c, E_GQ + c * 128, 128) for c in range(12)] + [('s', 0, None, 16)])
        for kind, c, col0, M in chunks:
            p = pfm[nfm % 3]
            nfm += 1
            if kind == 's':
                for k in range(8):
                    K.mm(p, p[0:8, :], W, W[:, k, E_FF:E_FF + 8], hnT, hnT[:, k, :], start=(k == 0), stop=(k == 7))
                o = sm[g % 2]
                K.op('act', lambda e, o=o, p=p: e.copy(out=o[0:8, :], in_=p[0:8, :]), reads=[p], writes=[o])
                p2 = pfm[nfm % 3]
                nfm += 1
                for k in range(8):
                    K.mm(p2, p2[0:8, :], W, W[:, k, E_GB:E_GB + 8], hnT, hnT[:, k, :], start=(k == 0), stop=(k == 7))
                o2 = sm2[g % 2]
                K.op('act', lambda e, o2=o2, p2=p2: e.copy(out=o2[:], in_=p2[0:8, :]), reads=[p2], writes=[o2])
                K.dma('pool', d['smallT'][0:8, g * 512:(g + 1) * 512], o[0:8, :], reads=[o])
                K.dma('pool', d['smallT'][8:16, g * 512:(g + 1) * 512], o2[:], reads=[o2])
                continue
            for k in range(8):
                K.mm(p, p[:], W, W[:, k, col0:col0 + 128], hnT, hnT[:, k, :], start=(k == 0), stop=(k == 7))
            if kind in ('q', 'k'):
                o = qk[nfm % 3]
                K.op('dve', lambda e, o=o, p=p: e.tensor_copy(out=o[:], in_=p[:]), reads=[p], writes=[o])
                dst = d['QfT'] if kind == 'q' else d['KfT']
                K.dma('pool', dst[c * 128:(c + 1) * 128, g * 512:(g + 1) * 512], o[:], reads=[o])
            else:
                xp = xpad[c]
                K.op('act', lambda e, xp=xp, p=p: e.copy(out=xp[:, 3:515], in_=p[:]), reads=[p], writes=[xp], nowaw=[xp])
                a = acc[c % 2]
                K.op('dve', lambda e, a=a, xp=xp, c=c: e.tensor_scalar(out=a[:], in0=xp[:, 0:512], scalar1=cw[:, 0, c:c + 1],
                                                                        scalar2=None, op0=ALU.mult),
                     reads=[xp, cw], writes=[a])
                for j in range(1, 4):
                    K.op('dve', lambda e, a=a, xp=xp, c=c, j=j: e.scalar_tensor_tensor(
                        out=a[:], in0=xp[:, j:j + 512], scalar=cw[:, j, c:c + 1], in1=a[:], op0=ALU.mult, op1=ALU.add),
                        reads=[xp, cw, a], writes=[a])
                K.op('pool', lambda e, xp=xp: e.tensor_copy(out=xp[:, 0:3], in_=xp[:, 512:515]), reads=[xp], writes=[xp])
                o = cout[nfm % 3]

                def silu_store(o=o, a=a, c=c, g=g):
                    K.op('act', lambda e: e.activation(out=o[:], in_=a[:], func=AF.Silu), reads=[a], writes=[o])
                    K.dma('pool', d['GCT'][c * 128:(c + 1) * 128, g * 512:(g + 1) * 512], o[:], reads=[o])
                if pend_silu[0] is not None:
                    pend_silu[0]()
                pend_silu[0] = silu_store
    if pend_silu[0] is not None:
        pend_silu[0]()
    K.phase_end()


INPUT_SHAPES = {
    "even_norm_mix": [1024], "even_w_in": [1024, 3600], "even_fox_bf": [8], "even_gdn_conv_w": [4, 1536],
    "even_gdn_a_log": [4], "even_gdn_dt_bias": [4], "even_gdn_norm_w": [128], "even_w_out": [1024, 1024],
    "odd_norm_mix": [1024], "odd_w_in": [1024, 2328], "odd_cmp_k_pe": [32, 64], "odd_cmp_k_w1": [2048, 128],
    "odd_cmp_k_w2": [128, 64], "odd_cmp_v_pe": [32, 64], "odd_cmp_v_w1": [2048, 128], "odd_cmp_v_w2": [128, 64],
    "odd_rg_conv_w": [4, 512], "odd_rg_conv_b": [512], "odd_rg_wa": [8, 64, 64], "odd_rg_ba": [512],
    "odd_rg_wx": [8, 64, 64], "odd_rg_bx": [512], "odd_rg_lambda": [512], "odd_w_out": [1024, 1024],
    "mlp_norm": [2, 1024], "mlp_w_up": [2, 1024, 4096], "mlp_w_down": [2, 4096, 1024], "ple_norm": [2, 1024],
    "ple_w_gate": [2, 1024, 1024], "ple_w_proj": [2, 256, 1024], "final_norm": [1024],
}


def scratch_specs(S):
    return {
        "h": ([S, 1024], F32),
        "QfT": ([512, S], BF16), "KfT": ([512, S], BF16), "Vf": ([S, 512], BF16),
        "GZ": ([S, 512], F32), "GCT": ([1536, S], F32), "smallT": ([16, S], F32),
        "gcd": ([4, S], F32), "ngcd": ([4, S], F32), "e1d": ([4, S], F32), "egld": ([4, S // 128], F32),
        "tokSd": ([128, (S // 128) * 16], F32),
        "csd": ([128, (S // 128) * 16], F32), "QT1": ([512, S], BF16), "KcT": ([128, S], BF16), "VcT": ([128, S], BF16),
        "KsT": ([128, S], BF16), "KwT": ([128, S], BF16), "Vs": ([S, 128], BF16), "Vw": ([S, 128], BF16),
        "gates": ([S, 24], F32), "RGX": ([1024, S], F32),
        "KcmpT": ([128, S // 16], BF16), "Vcmp": ([S // 16, 128], BF16),
        "nsa0": ([S, 512], F32), "nsa1": ([S, 512], F32), "nsa2": ([S, 512], F32), "selT": ([2, 128, S], BF16),
        "FQ": ([8, 3, S], BF16), "nF": ([128, (S // 128) * 8], F32), "mixed": ([S, 1024], BF16),
    }


def build(S, phases, dbg=(), h_input=False):
    nc = bass.Bass("TRN2", target_bir_lowering=False)
    d = {}
    d['x'] = nc.dram_tensor("x", [S, 1024], F32, kind="ExternalInput").ap()
    d['p'] = nc.dram_tensor("p", [2, S, 256], F32, kind="ExternalInput").ap()
    d['positions'] = nc.dram_tensor("positions", [1, S], I32, kind="ExternalInput").ap()
    for n, shp in INPUT_SHAPES.items():
        d[n] = nc.dram_tensor(n, shp, F32, kind="ExternalInput").ap()
    d['out'] = nc.dram_tensor("out", [S, 1024], F32, kind="ExternalOutput").ap()
    for n, (shp, dt) in scratch_specs(S).items():
        kind = "ExternalOutput" if n in dbg else "Internal"
        if n == 'h' and h_input:
            kind = "ExternalInput"
        d[n] = nc.dram_tensor(n, shp, dt, kind=kind).ap()
    K = KCtx(nc)
    import os
    K.stop = int(os.environ.get('KSTOP', '0')) or None
    for ph in phases:
        ph(K, d, S)
    K.finish()
    return nc, K


def phase_Fprep(K, d, S):
    NT_ = S // 128
    K.phase_begin()
    consts(K)
    identf = make_ident(K, F32, 'identf')
    ff = K.sb('ff', [8, S], F32)
    sp = K.sb('spl', [8, S], F32)
    ones = K.sb('ones8', [8, S], F32)
    nF = K.sb('negF', [8, S], F32)
    bf = K.sb('bf', [8, 2], F32)
    K.dma('sp', ff[:], d['smallT'][0:8, :], writes=[ff])
    K.dma('sp', bf[:, 0:1], d['even_fox_bf'].rearrange("(h o) -> h o", o=1), writes=[bf])
    K.op('dve', lambda e: e.tensor_scalar(out=bf[:, 1:2], in0=bf[:, 0:1], scalar1=-1.0, scalar2=None, op0=ALU.mult),
         reads=[bf], writes=[bf])
    K.op('pool', lambda e: e.memset(ones[:], 1.0), writes=[ones])
    K.op('act', lambda e: e.activation(out=sp[:], in_=ff[:], func=AF.Exp, scale=-1.0, bias=bf[:, 1:2]),
         reads=[ff, bf], writes=[sp])
    K.op('act', lambda e: e.activation(out=sp[:], in_=sp[:], func=AF.Ln, scale=1.0, bias=K.one_t[0:8, 0:1]),
         reads=[sp, K.one_t], writes=[sp])
    K.op('dve', lambda e: e.tensor_tensor_scan(out=nF[:], data0=ones[:], data1=sp[:], initial=0.0,
                                               op0=ALU.mult, op1=ALU.add), reads=[ones, sp], writes=[nF])
    q8 = ff
    K.op('dve', lambda e: e.tensor_scalar(out=q8[:], in0=nF[:], scalar1=-8.0, scalar2=None, op0=ALU.mult),
         reads=[nF], writes=[q8])
    parts = [K.sb('fq%d' % i, [8, S], BF16) for i in range(3)]
    for i in range(3):
        K.op('dve', lambda e, i=i: e.tensor_copy(out=parts[i][:], in_=q8[:]), reads=[q8], writes=[parts[i]])
        if i < 2:
            K.op('dve', lambda e, i=i: e.tensor_tensor(out=q8[:], in0=q8[:], in1=parts[i][:], op=ALU.subtract),
                 reads=[q8, parts[i]], writes=[q8])
        K.dma('sp', d['FQ'][:, i, :], parts[i][:], reads=[parts[i]])
    pt = K.ps('ptr', [128, 512], F32)
    nft = K.sb('nft', [128, NT_ * 8], F32)
    for t0 in range(0, NT_, 64):
        n = min(64, NT_ - t0)
        for t in range(n):
            K.tr(pt, pt[:, t * 8:(t + 1) * 8], nF, nF[0:8, (t0 + t) * 128:(t0 + t + 1) * 128], identf, identf[0:8, 0:8])
        K.op('act', lambda e, t0=t0, n=n: e.copy(out=nft[:, t0 * 8:(t0 + n) * 8], in_=pt[:, 0:n * 8]), reads=[pt], writes=[nft])
    K.dma('sp', d['nF'][:, :], nft[:], reads=[nft])
    K.phase_end()


def phase_fox(K, d, S, heads=range(8), L=3):
    NT_ = S // 128
    NQ = S // 512
    K.phase_begin()
    consts(K)
    identf = make_ident(K, F32, 'identf')
    nF = K.sb('nF', [128, NT_ * 8], F32)
    K.dma('sp', nF[:], d['nF'][:, :], writes=[nF])
    KT = [K.sb('KT%d' % i, [67, S], BF16) for i in range(2)]
    QT = [K.sb('QT%d' % i, [67, S], BF16) for i in range(2)]
    V = [K.sb('V%d' % i, [128, NT_, 65], BF16) for i in range(2)]
    for i in range(2):
        K.op('pool', lambda e, i=i: e.memset(KT[i][64:67, :], 1.0), writes=[KT[i]])
        K.op('pool', lambda e, i=i: e.memset(V[i][:, :, 64:65], 1.0), writes=[V[i]])
    ps = [K.ps('ps%d' % i, [128, 512], F32) for i in range(3)]
    po = [K.ps('po%d' % i, [128, 512], F32) for i in range(2)]
    pq = [K.ps('pq%d' % i, [128, 4, 65], F32) for i in range(2)]
    pt = [K.sb('pt%d' % i, [128, 512], BF16) for i in range(4)]
    osb = [K.sb('osb%d' % i, [65, 512], F32) for i in range(2)]
    rc = [K.sb('rc%d' % i, [128, 4], F32) for i in range(2)]
    mo = [K.sb('mo%d' % i, [128, 4, 64], BF16) for i in range(2)]

    def load_head(hi, h):
        s = hi % 2
        K.dma('sp', KT[s][0:64, :], d['KfT'][h * 64:(h + 1) * 64, :], writes=[KT[s]])
        K.dma('sp', QT[s][0:64, :], d['QfT'][h * 64:(h + 1) * 64, :], writes=[QT[s]])
        K.dma('sp', QT[s][64:67, :], d['FQ'][h, :, :], writes=[QT[s]], nowaw=[QT[s]])
        K.dma('sp', V[s][:, :, 0:64], d['Vf'][:, h * 64:(h + 1) * 64].rearrange("(t p) c -> p t c", p=128),
              writes=[V[s]])

    heads = list(heads)
    units = []
    for hi, h in enumerate(heads):
        for qg in range(NQ):
            nk = 4 * qg + 4
            for kt in range(nk):
                units.append((hi, h, qg, kt, kt == nk - 1))
    load_head(0, heads[0])
    NU = len(units)
    for i in range(NU + L):
        if i < NU:
            hi, h, qg, kt, last = units[i]
            s = hi % 2
            r = kt - 4 * qg
            c0 = 128 * max(r, 0)
            p_s = ps[i % 3]
            p_t = pt[i % 4]
            K.mm(p_s, p_s[:, c0:512], KT[s], KT[s][0:67, kt * 128:(kt + 1) * 128], QT[s],
                 QT[s][0:67, qg * 512 + c0:(qg + 1) * 512])
            K.op('act', lambda e, p_s=p_s, p_t=p_t, c0=c0, kt=kt, h=h: e.activation(
                out=p_t[:, c0:512], in_=p_s[:, c0:512], func=AF.Exp, scale=0.125, bias=nF[:, kt * 8 + h:kt * 8 + h + 1]),
                reads=[p_s, nF], writes=[p_t])
            if r >= 0:
                K.op('pool', lambda e, p_t=p_t, c0=c0: e.affine_select(
                    out=p_t[:, c0:c0 + 128], in_=p_t[:, c0:c0 + 128], pattern=[[1, 128]], compare_op=ALU.is_ge,
                    fill=0.0, base=0, channel_multiplier=-1), reads=[p_t], writes=[p_t])
        if i - L >= 0:
            hi, h, qg, kt, last = units[i - L]
            if qg == 0 and kt == 0 and hi + 1 < len(heads):
                load_head(hi + 1, heads[hi + 1])
            s = hi % 2
            r = kt - 4 * qg
            c0 = 128 * max(r, 0)
            p_t = pt[(i - L) % 4]
            gi = hi * NQ + qg
            p_o = po[gi % 2]
            K.mm(p_o, p_o[0:65, c0:512], V[s], V[s][:, kt, 0:65], p_t, p_t[:, c0:512], start=(kt == 0), stop=last)
            if last:
                o_s = osb[gi % 2]
                p_q = pq[gi % 2]
                K.op('act', lambda e, o_s=o_s, p_o=p_o: e.copy(out=o_s[:], in_=p_o[0:65, :]), reads=[p_o], writes=[o_s])
                for j in range(4):
                    K.tr(p_q, p_q[:, j, :], o_s, o_s[0:65, j * 128:(j + 1) * 128], identf, identf[0:65, 0:65])
                r_c = rc[gi % 2]
                m_o = mo[gi % 2]
                K.op('dve', lambda e, r_c=r_c, p_q=p_q: e.reciprocal(out=r_c[:], in_=p_q[:, :, 64]), reads=[p_q], writes=[r_c])
                for j in range(4):
                    K.op('dve', lambda e, j=j, r_c=r_c, p_q=p_q, m_o=m_o: e.tensor_scalar(
                        out=m_o[:, j, :], in0=p_q[:, j, 0:64], scalar1=r_c[:, j:j + 1], scalar2=None, op0=ALU.mult),
                        reads=[p_q, r_c], writes=[m_o])
                K.dma('pool', d['mixed'][qg * 512:(qg + 1) * 512, h * 64:(h + 1) * 64].rearrange("(j p) c -> p j c", p=128),
                      m_o[:], reads=[m_o])
    K.phase_end()


NEGM = -30000.0


def phase_Gprep(K, d, S):
    NC = S // 128
    K.phase_begin()
    consts(K)
    identf = make_ident(K, F32, 'identf')
    A = K.sb('gpA', [4, S], F32)
    B = K.sb('gpB', [4, S], F32)
    C = K.sb('gpC', [4, S], F32)
    Dt = K.sb('gpD', [4, S], F32)
    E = K.sb('gpE', [4, S], F32)
    K.dma('sp', A[:], d['smallT'][8:12, :], writes=[A])
    K.dma('sp', B[:], d['smallT'][12:16, :], writes=[B])
    pr = K.sb('pr', [4, 4], F32)
    K.dma('sp', pr[:, 0:1], d['even_gdn_a_log'].rearrange("(h o) -> h o", o=1), writes=[pr])
    K.dma('sp', pr[:, 1:2], d['even_gdn_dt_bias'].rearrange("(h o) -> h o", o=1), writes=[pr], nowaw=[pr])
    K.op('act', lambda e: e.activation(out=pr[:, 2:3], in_=pr[:, 0:1], func=AF.Exp), reads=[pr], writes=[pr])
    K.op('pool', lambda e: e.memset(C[:], 1.0), writes=[C])
    K.op('pool', lambda e: e.memset(C[:].rearrange("p (c t) -> p c t", t=128)[:, :, 0:1], 0.0), writes=[C])
    K.op('act', lambda e: e.activation(out=Dt[:], in_=B[:], func=AF.Exp, bias=pr[:, 1:2]), reads=[B, pr], writes=[Dt])
    K.op('act', lambda e: e.activation(out=Dt[:], in_=Dt[:], func=AF.Ln, bias=K.one_t[0:4, 0:1]), reads=[Dt, K.one_t], writes=[Dt])
    K.op('dve', lambda e: e.tensor_scalar(out=B[:], in0=Dt[:], scalar1=pr[:, 2:3], scalar2=-1.0, op0=ALU.mult, op1=ALU.mult),
         reads=[Dt, pr], writes=[B])
    gc = E
    K.op('dve', lambda e: e.tensor_tensor_scan(out=gc[:], data0=C[:], data1=B[:], initial=0.0, op0=ALU.mult, op1=ALU.add),
         reads=[C, B], writes=[gc])
    K.dma('sp', d['gcd'][:, :], gc[:], reads=[gc])
    K.op('dve', lambda e: e.tensor_scalar(out=Dt[:], in0=gc[:], scalar1=-1.0, scalar2=None, op0=ALU.mult), reads=[gc], writes=[Dt])
    K.dma('sp', d['ngcd'][:, :], Dt[:], reads=[Dt])
    beta = A
    K.op('act', lambda e: e.activation(out=beta[:], in_=A[:], func=AF.Sigmoid), reads=[A], writes=[beta])
    e1 = B
    K.op('act', lambda e: e.activation(out=e1[:], in_=gc[:], func=AF.Exp), reads=[gc], writes=[e1])
    K.dma('sp', d['e1d'][:, :], e1[:], reads=[e1])
    be = C
    K.op('dve', lambda e: e.tensor_tensor(out=be[:], in0=beta[:], in1=e1[:], op=ALU.mult), reads=[beta, e1], writes=[be])
    gc3 = gc[:].rearrange("p (c t) -> p c t", t=128)
    e2 = Dt
    K.op('dve', lambda e: e.tensor_tensor(out=e2[:].rearrange("p (c t) -> p c t", t=128),
                                          in0=gc3[:, :, 127:128].to_broadcast([4, NC, 128]), in1=gc3, op=ALU.subtract),
         reads=[gc], writes=[e2])
    K.op('act', lambda e: e.activation(out=e2[:], in_=e2[:], func=AF.Exp), reads=[e2], writes=[e2])
    egl = K.sb('egl', [4, NC], F32)
    K.op('act', lambda e: e.activation(out=egl[:], in_=gc3[:, :, 127], func=AF.Exp), reads=[gc], writes=[egl])
    K.dma('sp', d['egld'][:, :], egl[:], reads=[egl])
    pt = [K.ps('ptg%d' % i, [128, 32, 16], F32) for i in range(2)]
    tokS = K.sb('tokS', [128, NC, 16], F32)
    K.op('pool', lambda e: e.memset(tokS[:], 0.0), writes=[tokS])
    for t0 in range(0, NC, 32):
        n = min(32, NC - t0)
        p = pt[(t0 // 32) % 2]
        for t in range(n):
            for qi, src in enumerate((beta, be, e2, gc)):
                K.tr(p, p[:, t, qi * 4:(qi + 1) * 4], src, src[0:4, (t0 + t) * 128:(t0 + t + 1) * 128], identf, identf[0:4, 0:4])
        K.op('act', lambda e, p=p, t0=t0, n=n: e.copy(out=tokS[:, t0:t0 + n, 0:16], in_=p[:, 0:n, 0:16]), reads=[p], writes=[tokS],
             nowaw=[tokS])
    K.dma('sp', d['tokSd'][:, :], tokS[:].rearrange("p c q -> p (c q)"), reads=[tokS])
    K.phase_end()


class StopPhase(Exception):
    pass


def chk(K, n):
    if getattr(K, 'stop', None) == n:
        raise StopPhase()


def phase_gdn(K, d, S, heads=range(4)):
    try:
        _phase_gdn(K, d, S, heads)
    except StopPhase:
        pass
    K.phase_end()


def _phase_gdn(K, d, S, heads=range(4)):
    NC = S // 128
    NG = S // 512
    K.phase_begin()
    consts(K)
    identf = make_ident(K, F32, 'identf')
    onesf = K.sb('onesf', [128, 128], F32)
    K.op('pool', lambda e: e.memset(onesf[:], 1.0), writes=[onesf])
    maskT = K.sb('maskT', [128, 128], F32)
    maskTs = K.sb('maskTs', [128, 128], F32)
    zer = K.sb('zer', [128, 128], F32)
    K.op('pool', lambda e: e.memset(zer[:], 0.0), writes=[zer])
    K.op('pool', lambda e: e.affine_select(out=maskT[:], in_=zer[:], pattern=[[1, 128]], compare_op=ALU.is_ge, fill=NEGM,
                                           base=0, channel_multiplier=-1), reads=[zer], writes=[maskT])
    K.op('pool', lambda e: e.affine_select(out=maskTs[:], in_=zer[:], pattern=[[1, 128]], compare_op=ALU.is_gt, fill=NEGM,
                                           base=0, channel_multiplier=-1), reads=[zer], writes=[maskTs])
    nwb = K.sb('nwb', [128, 128], F32)
    K.dma('sp', nwb[:], d['even_gdn_norm_w'].partition_broadcast(128), writes=[nwb])
    tokS = K.sb('tokS', [128, NC, 16], F32)
    K.dma('sp', tokS[:].rearrange("p c q -> p (c q)"), d['tokSd'][:, :], writes=[tokS])
    c128 = K.sb('c128', [128, 1], F32)
    K.op('pool', lambda e: e.memset(c128[:], 128.0 * 1e-6), writes=[c128])
    G2L = K.sb('G2L', [2, S], F32)
    G2R = K.sb('G2R', [2, S], F32)
    e1row = K.sb('e1row', [1, S], F32)
    eglrow = K.sb('eglrow', [1, NC], F32)
    eglb = K.sb('eglb', [128, NC], F32)
    Sst = K.sb('Sst', [128, 128], F32)
    qT = [K.sb('gqT%d' % i, [128, 512], F32) for i in range(2)]
    kT = [K.sb('gkT%d' % i, [128, 512], F32) for i in range(2)]
    vT = [K.sb('gvT%d' % i, [128, 512], F32) for i in range(2)]
    sq = K.sb('gsq', [128, 512], F32)
    rr = K.sb('grr', [128, 512], F32)
    knT = K.sb('knT', [128, 512], F32)
    qnT = K.sb('qnT', [128, 512], F32)
    qgT = K.sb('qgT', [128, 512], F32)
    zt = [K.sb('gz%d' % i, [128, 128], F32) for i in range(2)]
    nz = K.sb('nz', [128, 128], F32)
    Kbg = K.sb('Kbg', [128, 128], F32)
    Kd = K.sb('Kd', [128, 128], F32)
    Vb = K.sb('Vb', [128, 128], F32)
    dm1 = K.sb('dm1', [128, 128], F32)
    dm2 = K.sb('dm2', [128, 128], F32)
    attnT = K.sb('attnT', [128, 128], F32)
    Mt = K.sb('Mt', [128, 128], F32)
    P = [K.sb('Pn%d' % i, [128, 128], F32) for i in range(2)]
    PT = [K.sb('PTn%d' % i, [128, 128], F32) for i in range(2)]
    X = K.sb('Xn', [128, 128], F32)
    U = K.sb('Un', [128, 128], F32)
    WT = K.sb('WTn', [128, 128], F32)
    vnew = K.sb('vnew', [128, 128], F32)
    ost = K.sb('gost', [128, 4], F32)
    ojunk = K.sb('gojunk', [128, 128], F32)
    ob = [K.sb('gob%d' % i, [128, 128], BF16) for i in range(2)]
    bA = K.ps('bA', [128, 512], F32)
    bB = K.ps('bB', [128, 512], F32)
    bC = K.ps('bC', [128, 512], F32)
    bD = K.ps('bD', [128, 512], F32)
    bE = K.ps('bE', [128, 512], F32)
    bF = K.ps('bF', [128, 512], F32)
    bG = K.ps('bG', [128, 512], F32)
    bH = K.ps('bH', [128, 512], F32)
    sl = lambda i: slice(i * 128, (i + 1) * 128)
    chk(K, 1)
    for h in heads:
        K.op('pool', lambda e: e.memset(G2L[:], 1.0), writes=[G2L])
        K.op('pool', lambda e: e.memset(G2R[:], 1.0), writes=[G2R])
        K.dma('sp', G2L[0:1, :], d['ngcd'][h:h + 1, :], writes=[G2L])
        K.dma('sp', G2R[1:2, :], d['gcd'][h:h + 1, :], writes=[G2R])
        K.dma('sp', e1row[:], d['e1d'][h:h + 1, :], writes=[e1row])
        K.dma('sp', eglrow[:], d['egld'][h:h + 1, :], writes=[eglrow])
        K.mm(bA, bA[:, 0:NC], onesf, onesf[0:1, :], eglrow, eglrow[0:1, :])
        K.op('act', lambda e: e.copy(out=eglb[:], in_=bA[:, 0:NC]), reads=[bA], writes=[eglb])
        K.op('pool', lambda e: e.memset(Sst[:], 0.0), writes=[Sst])
        chk(K, 2)
        for g in range(NG):
            cs = slice(g * 512, (g + 1) * 512)
            q_, k_, v_ = qT[g % 2], kT[g % 2], vT[g % 2]
            K.dma('sp', q_[:], d['GCT'][h * 128:(h + 1) * 128, cs], writes=[q_])
            K.dma('sp', k_[:], d['GCT'][512 + h * 128:512 + (h + 1) * 128, cs], writes=[k_])
            K.dma('sp', v_[:], d['GCT'][1024 + h * 128:1024 + (h + 1) * 128, cs], writes=[v_])
            K.op('act', lambda e: e.activation(out=sq[:], in_=k_[:], func=AF.Square), reads=[k_], writes=[sq])
            K.mm(bA, bA[:], onesf, onesf[:], sq, sq[:])
            K.op('act', lambda e: e.activation(out=rr[:], in_=bA[:], func=AF.Sqrt, bias=K.eps_t[:, 0:1]), reads=[bA, K.eps_t], writes=[rr])
            K.op('dve', lambda e: e.reciprocal(out=rr[:], in_=rr[:]), reads=[rr], writes=[rr])
            K.op('dve', lambda e: e.tensor_tensor(out=knT[:], in0=k_[:], in1=rr[:], op=ALU.mult), reads=[k_, rr], writes=[knT])
            K.op('act', lambda e: e.activation(out=sq[:], in_=q_[:], func=AF.Square), reads=[q_], writes=[sq])
            K.mm(bA, bA[:], onesf, onesf[:], sq, sq[:])
            K.op('act', lambda e: e.activation(out=rr[:], in_=bA[:], func=AF.Sqrt, scale=128.0, bias=c128[:, 0:1]),
                 reads=[bA, c128], writes=[rr])
            K.op('dve', lambda e: e.reciprocal(out=rr[:], in_=rr[:]), reads=[rr], writes=[rr])
            K.op('dve', lambda e: e.tensor_tensor(out=qnT[:], in0=q_[:], in1=rr[:], op=ALU.mult), reads=[q_, rr], writes=[qnT])
            K.mm(bA, bA[:], onesf, onesf[0:1, :], e1row, e1row[0:1, cs])
            K.op('dve', lambda e: e.tensor_tensor(out=qgT[:], in0=qnT[:], in1=bA[:], op=ALU.mult), reads=[qnT, bA], writes=[qgT])
            chk(K, 3)
            for tt in range(4):
                c = g * 4 + tt
                ts_ = sl(tt)
                tok = slice(c * 128, (c + 1) * 128)
                z = zt[c % 2]
                K.dma('sp', z[:], d['GZ'][tok, h * 128:(h + 1) * 128], writes=[z])
                K.op('pool', lambda e, z=z: e.tensor_tensor(out=nz[:], in0=z[:], in1=nwb[:], op=ALU.mult), reads=[z, nwb], writes=[nz])
                beta_c = tokS[:, c, h:h + 1]
                be_c = tokS[:, c, 4 + h:5 + h]
                e2_c = tokS[:, c, 8 + h:9 + h]
                K.tr(bB, bB[:, 0:128], knT, knT[:, ts_], identf, identf[:])
                K.tr(bB, bB[:, 128:256], v_, v_[:, ts_], identf, identf[:])
                K.op('dve', lambda e: e.tensor_scalar(out=Kbg[:], in0=bB[:, 0:128], scalar1=be_c, scalar2=None, op0=ALU.mult),
                     reads=[bB, tokS], writes=[Kbg])
                K.op('act', lambda e: e.activation(out=Kd[:], in_=bB[:, 0:128], func=AF.Copy, scale=e2_c), reads=[bB, tokS], writes=[Kd])
                K.op('dve', lambda e: e.tensor_scalar(out=Vb[:], in0=bB[:, 128:256], scalar1=beta_c, scalar2=None, op0=ALU.mult),
                     reads=[bB, tokS], writes=[Vb])
                chk(K, 4)
                K.mm(bC, bC[:, 0:128], knT, knT[:, ts_], knT, knT[:, ts_])
                K.mm(bC, bC[:, 128:256], knT, knT[:, ts_], qnT, qnT[:, ts_])
                K.mm(bC, bC[:, 256:384], G2L, G2L[0:2, tok], G2R, G2R[0:2, tok])
                K.op('dve', lambda e: e.tensor_tensor(out=dm1[:], in0=bC[:, 256:384], in1=maskT[:], op=ALU.add), reads=[bC, maskT], writes=[dm1])
                K.op('dve', lambda e: e.tensor_tensor(out=dm2[:], in0=bC[:, 256:384], in1=maskTs[:], op=ALU.add), reads=[bC, maskTs], writes=[dm2])
                K.op('act', lambda e: e.activation(out=dm1[:], in_=dm1[:], func=AF.Exp), reads=[dm1], writes=[dm1])
                K.op('act', lambda e: e.activation(out=dm2[:], in_=dm2[:], func=AF.Exp), reads=[dm2], writes=[dm2])
                K.op('dve', lambda e: e.tensor_tensor(out=attnT[:], in0=bC[:, 128:256], in1=dm1[:], op=ALU.mult), reads=[bC, dm1], writes=[attnT])
                K.op('dve', lambda e: e.tensor_tensor(out=Mt[:], in0=bC[:, 0:128], in1=dm2[:], op=ALU.mult), reads=[bC, dm2], writes=[Mt])
                chk(K, 5)
                K.tr(bD, bD[:, 0:128], Mt, Mt[:], identf, identf[:])
                K.op('dve', lambda e: e.tensor_scalar(out=P[0][:], in0=bD[:, 0:128], scalar1=beta_c, scalar2=-1.0, op0=ALU.mult, op1=ALU.mult),
                     reads=[bD, tokS], writes=[P[0]])
                K.tr(bD, bD[:, 128:256], P[0], P[0][:], identf, identf[:])
                K.op('act', lambda e: e.copy(out=PT[0][:], in_=bD[:, 128:256]), reads=[bD], writes=[PT[0]])
                K.op('dve', lambda e: e.tensor_tensor(out=X[:], in0=bD[:, 128:256], in1=identf[:], op=ALU.add), reads=[bD, identf], writes=[X])
                for n in range(6):
                    a, b = n % 2, (n + 1) % 2
                    K.mm(bE, bE[:, 128:256], PT[a], PT[a][:], P[a], P[a][:])
                    K.mm(bE, bE[:, 256:384], P[a], P[a][:], PT[a], PT[a][:])
                    K.op('act', lambda e, b=b: e.copy(out=P[b][:], in_=bE[:, 128:256]), reads=[bE], writes=[P[b]])
                    K.op('act', lambda e, b=b: e.copy(out=PT[b][:], in_=bE[:, 256:384]), reads=[bE], writes=[PT[b]])
                    K.mm(bF, bF[:, 0:128], P[b], P[b][:], X, X[:])
                    K.op('dve', lambda e: e.tensor_tensor(out=X[:], in0=X[:], in1=bF[:, 0:128], op=ALU.add), reads=[X, bF], writes=[X])
                chk(K, 6)
                K.mm(bG, bG[:, 0:128], X, X[:], Vb, Vb[:])
                K.mm(bG, bG[:, 128:256], Kbg, Kbg[:], X, X[:])
                K.op('act', lambda e: e.copy(out=U[:], in_=bG[:, 0:128]), reads=[bG], writes=[U])
                K.op('act', lambda e: e.copy(out=WT[:], in_=bG[:, 128:256]), reads=[bG], writes=[WT])
                chk(K, 7)
                K.mm(bH, bH[:, 0:128], WT, WT[:], Sst, Sst[:])
                K.op('dve', lambda e: e.tensor_tensor(out=vnew[:], in0=U[:], in1=bH[:, 0:128], op=ALU.subtract), reads=[U, bH], writes=[vnew])
                K.mm(bH, bH[:, 128:256], qgT, qgT[:, ts_], Sst, Sst[:], start=True, stop=False)
                K.mm(bH, bH[:, 128:256], attnT, attnT[:], vnew, vnew[:], start=False, stop=True)
                K.mm(bH, bH[:, 256:384], Kd, Kd[:], vnew, vnew[:])
                K.op('dve', lambda e, c=c: e.scalar_tensor_tensor(out=Sst[:], in0=Sst[:], scalar=eglb[:, c:c + 1], in1=bH[:, 256:384],
                                                                   op0=ALU.mult, op1=ALU.add), reads=[Sst, eglb, bH], writes=[Sst])
                chk(K, 8)
                K.op('act', lambda e: e.activation(out=ojunk[:], in_=bH[:, 128:256], func=AF.Square, accum_out=ost[:, 0:1]),
                     reads=[bH], writes=[ojunk, ost])
                K.op('act', lambda e: e.activation(out=ost[:, 1:2], in_=ost[:, 0:1], func=AF.Sqrt, scale=1.0 / 128, bias=K.eps_t[:, 0:1]),
                     reads=[ost, K.eps_t], writes=[ost])
                K.op('dve', lambda e: e.reciprocal(out=ost[:, 2:3], in_=ost[:, 1:2]), reads=[ost], writes=[ost])
                o_b = ob[c % 2]
                K.op('dve', lambda e, o_b=o_b: e.scalar_tensor_tensor(out=o_b[:], in0=bH[:, 128:256], scalar=ost[:, 2:3], in1=nz[:],
                                                                       op0=ALU.mult, op1=ALU.mult), reads=[bH, ost, nz], writes=[o_b])
                K.dma('pool', d['mixed'][tok, 512 + h * 128:512 + (h + 1) * 128], o_b[:], reads=[o_b])


def load_gain(K, name, src1d):
    g = K.sb(name, [128, 8], F32)
    K.dma('sp', g[:], src1d.rearrange("(k p) -> p k", p=128), writes=[g], allow_slow_non_contiguous=True)
    return g


def phase_O(K, d, S, src, w_out, dst='h'):
    NT_ = S // 128
    K.phase_begin()
    consts(K)
    identb = make_ident(K, BF16, 'identb')
    W = K.sb('Wo', [128, 8, 1024], BF16)
    load_weight_bf16(K, W, w_out, 8, 1024)
    mx = [K.sb('mx%d' % i, [128, 1024], BF16) for i in range(2)]
    xr = [K.sb('xr%d' % i, [128, 1024], F32) for i in range(2)]
    mT = [K.sb('mT%d' % i, [128, 8, 128], BF16) for i in range(2)]
    ho = [K.sb('ho%d' % i, [128, 1024], F32) for i in range(2)]
    pT = [K.ps('opT%d' % i, [128, 1024], BF16) for i in range(2)]
    po = [K.ps('opo%d' % i, [128, 512], F32) for i in range(4)]
    for t in range(NT_):
        rows = slice(t * 128, (t + 1) * 128)
        m_, x_, mT_, h_, p_ = mx[t % 2], xr[t % 2], mT[t % 2], ho[t % 2], pT[t % 2]
        K.dma('sp', m_[:], d['mixed'][rows, :], writes=[m_])
        K.dma('sp', x_[:], src[rows, :], writes=[x_])
        for k in range(8):
            K.tr(p_, p_[:, k * 128:(k + 1) * 128], m_, m_[:, k * 128:(k + 1) * 128], identb, identb[:])
        K.op('act', lambda e: e.copy(out=mT_[:], in_=p_[:].rearrange("p (k t) -> p k t", k=8)), reads=[p_], writes=[mT_])
        for half in range(2):
            p2 = po[(t * 2 + half) % 4]
            for k in range(8):
                K.mm(p2, p2[:], mT_, mT_[:, k, :], W, W[:, k, half * 512:(half + 1) * 512], start=(k == 0), stop=(k == 7))
            K.op('dve', lambda e, p2=p2, half=half: e.tensor_tensor(out=h_[:, half * 512:(half + 1) * 512], in0=p2[:],
                                                                   in1=x_[:, half * 512:(half + 1) * 512], op=ALU.add),
                 reads=[p2, x_], writes=[h_], nowaw=[h_])
        K.dma('pool', d[dst][rows, :], h_[:], reads=[h_])
    K.phase_end()


def phase_MLP(K, d, S, li):
    GT = 2
    NG = S // (GT * 128)
    K.phase_begin()
    consts(K)
    identb = make_ident(K, BF16, 'identb')
    gain = load_gain(K, 'gain', d['mlp_norm'][li, :])
    Wu = K.sb('Wu', [128, 8, 4096], BF16)
    load_weight_bf16(K, Wu, d['mlp_w_up'][li], 8, 4096, gain=gain, stage_name='wsu', cb=512)
    Wd = K.sb('Wd', [128, 32, 1024], BF16)
    load_weight_bf16(K, Wd, d['mlp_w_down'][li], 32, 1024, stage_name='wsd', cb=512)
    NT = NormT(K, identb, nbuf_x=2 * GT, gt=GT)
    uT = K.sb('uT', [128, 32, GT * 128], BF16)
    rl = [K.sb('rl%d' % i, [128, GT * 128], F32) for i in range(2)]
    ho = [K.sb('mho%d' % i, [128, 1024], F32) for i in range(2)]
    pu = [K.ps('pu%d' % i, [128, 512], F32) for i in range(2)]
    pd = [K.ps('pd%d' % i, [128, 512], F32) for i in range(4)]
    toks = [NT.prep(d['h'][j * 128:(j + 1) * 128, :]) for j in range(GT)]
    for g in range(NG):
        hnT = NT.hnT[g % 2]
        xbs = []
        for j in range(GT):
            xbs.append(NT.finish(toks[j], hnT, j))
        for fc in range(32):
            if fc == 8 and g + 1 < NG:
                toks = [NT.prep(d['h'][((g + 1) * GT + j) * 128:((g + 1) * GT + j + 1) * 128, :]) for j in range(GT)]
            p = pu[fc % 2]
            for k in range(8):
                K.mm(p, p[:, 0:GT * 128], Wu, Wu[:, k, fc * 128:(fc + 1) * 128], hnT, hnT[:, k, :], start=(k == 0), stop=(k == 7))
            r = rl[fc % 2]
            K.op('act', lambda e, r=r, p=p: e.activation(out=r[:], in_=p[:, 0:GT * 128], func=AF.Relu), reads=[p], writes=[r])
            K.op('dve' if fc % 2 == 0 else 'pool', lambda e, r=r, fc=fc: e.tensor_tensor(out=uT[:, fc, :], in0=r[:], in1=r[:], op=ALU.mult),
                 reads=[r], writes=[uT], nowaw=[uT])
        for j in range(GT):
            t = g * GT + j
            h_ = ho[t % 2]
            for half in range(2):
                p2 = pd[(t * 2 + half) % 4]
                for fc in range(32):
                    K.mm(p2, p2[:], uT, uT[:, fc, j * 128:(j + 1) * 128], Wd, Wd[:, fc, half * 512:(half + 1) * 512],
                         start=(fc == 0), stop=(fc == 31))
                K.op('dve', lambda e, p2=p2, half=half, x_=xbs[j]: e.tensor_tensor(
                    out=h_[:, half * 512:(half + 1) * 512], in0=p2[:], in1=x_[:, half * 512:(half + 1) * 512], op=ALU.add),
                    reads=[p2, xbs[j]], writes=[h_], nowaw=[h_])
            K.dma('pool', d['h'][t * 128:(t + 1) * 128, :], h_[:], reads=[h_])
    K.phase_end()


def phase_PLE(K, d, S, li, final=False):
    NT_ = S // 128
    K.phase_begin()
    consts(K)
    identb = make_ident(K, BF16, 'identb')
    gain = load_gain(K, 'gain', d['ple_norm'][li, :])
    Wg = K.sb('Wg', [128, 8, 1024], BF16)
    load_weight_bf16(K, Wg, d['ple_w_gate'][li], 8, 1024, gain=gain)
    Wp = K.sb('Wp', [128, 2, 1024], BF16)
    load_weight_bf16(K, Wp, d['ple_w_proj'][li], 2, 1024)
    NT = NormT(K, identb, nbuf_x=9, gt=1, nbuf_n=8)
    pin = [K.sb('pin%d' % i, [128, 256], F32) for i in range(2)]
    pbf = [K.sb('pbf%d' % i, [128, 256], BF16) for i in range(2)]
    ppT = [K.sb('ppT%d' % i, [128, 2, 128], BF16) for i in range(2)]
    sg = [K.sb('sg%d' % i, [128, 1024], F32) for i in range(2)]
    ho = [K.sb('pho%d' % i, [128, 1024], F32) for i in range(3)]
    ptp = K.ps('ptp', [128, 1024], BF16)
    pg = [K.ps('pg%d' % i, [128, 512], F32) for i in range(2)]
    pp = [K.ps('pp%d' % i, [128, 512], F32) for i in range(2)]
    if final:
        fg = K.sb('fg', [128, 1024], F32)
        K.dma('sp', fg[:], d['final_norm'].partition_broadcast(128), writes=[fg])
        fst = [K.sb('fst%d' % i, [128, 4], F32) for i in range(2)]
        fo = [K.sb('fo%d' % i, [128, 1024], F32) for i in range(2)]
    pend_fin = [None]
    toks = [NT.prep(d['h'][u * 128:(u + 1) * 128, :]) for u in range(min(4, NT_))]
    for t in range(NT_):
        rows = slice(t * 128, (t + 1) * 128)
        hnT = NT.hnT[t % 2]
        x_ = NT.finish(toks.pop(0), hnT, 0)
        if t % 4 == 0:
            for u in range(t + 4, min(t + 8, NT_)):
                toks.append(NT.prep(d['h'][u * 128:(u + 1) * 128, :]))
        pi_, pb_, pT_, sg_, h_ = pin[t % 2], pbf[t % 2], ppT[t % 2], sg[t % 2], ho[t % 3]
        K.dma('sp', pi_[:], d['p'][li, rows, :], writes=[pi_])
        K.op('dve', lambda e: e.tensor_copy(out=pb_[:], in_=pi_[:]), reads=[pi_], writes=[pb_])
        for k in range(2):
            K.tr(ptp, ptp[:, k * 128:(k + 1) * 128], pb_, pb_[:, k * 128:(k + 1) * 128], identb, identb[:])
        K.op('act', lambda e: e.copy(out=pT_[:], in_=ptp[:, 0:256].rearrange("p (k t) -> p k t", k=2)), reads=[ptp], writes=[pT_])
        for half in range(2):
            hs = slice(half * 512, (half + 1) * 512)
            g_, p_ = pg[half], pp[half]
            for k in range(8):
                K.mm(g_, g_[:], hnT, hnT[:, k, :], Wg, Wg[:, k, hs], start=(k == 0), stop=(k == 7))
            for k in range(2):
                K.mm(p_, p_[:], pT_, pT_[:, k, :], Wp, Wp[:, k, hs], start=(k == 0), stop=(k == 1))
            K.op('act', lambda e, g_=g_, hs=hs: e.activation(out=sg_[:, hs], in_=g_[:], func=AF.Sigmoid), reads=[g_], writes=[sg_], nowaw=[sg_])
            K.op('dve', lambda e, p_=p_, hs=hs: e.tensor_tensor(out=sg_[:, hs], in0=sg_[:, hs], in1=p_[:], op=ALU.mult),
                 reads=[sg_, p_], writes=[sg_])
        K.op('pool', lambda e: e.tensor_tensor(out=h_[:], in0=sg_[:], in1=x_[:], op=ALU.add), reads=[sg_, x_], writes=[h_])
        if not final:
            K.dma('pool', d['h'][rows, :], h_[:], reads=[h_])
        else:
            def fin(h_=h_, st=fst[t % 2], o_=fo[t % 2], rows=rows):
                K.op('act', lambda e: e.activation(out=o_[:], in_=h_[:], func=AF.Square, accum_out=st[:, 0:1]), reads=[h_], writes=[o_, st])
                K.op('act', lambda e: e.activation(out=st[:, 1:2], in_=st[:, 0:1], func=AF.Sqrt, scale=1.0 / D, bias=K.eps_t[:, 0:1]),
                     reads=[st, K.eps_t], writes=[st])
                K.op('dve', lambda e: e.reciprocal(out=st[:, 2:3], in_=st[:, 1:2]), reads=[st], writes=[st])
                K.op('dve', lambda e: e.scalar_tensor_tensor(out=o_[:], in0=h_[:], scalar=st[:, 2:3], in1=fg[:], op0=ALU.mult, op1=ALU.mult),
                     reads=[h_, st, fg], writes=[o_])
                K.dma('pool', d['out'][rows, :], o_[:], reads=[o_])
            if pend_fin[0] is not None:
                pend_fin[0]()
            pend_fin[0] = fin
    if final and pend_fin[0] is not None:
        pend_fin[0]()
    K.phase_end()


O_NQ, O_KC, O_VC, O_KSL, O_VSL, O_KWN, O_VWN, O_NG, O_RG, O_RX = 0, 512, 640, 768, 896, 1024, 1152, 1280, 1304, 1816
TWO_PI = 6.283185307179586
CW1 = 6.28125
CW2 = TWO_PI - CW1
MAGIC = 12582912.0


def phase_Rprep(K, d, S):
    NT_ = S // 128
    inv_freq = (np.float32(500000.0) ** (-np.arange(8, dtype=np.float32) * np.float32(2.0 / 16))).astype(np.float32)
    K.phase_begin()
    consts(K)
    posi = K.sb('posi', [128, NT_], I32)
    K.dma('sp', posi[:], d['positions'].rearrange("o (t p) -> p (o t)", p=128), writes=[posi], allow_slow_non_contiguous=True)
    posf = K.sb('posf', [128, NT_], F32)
    K.op('dve', lambda e: e.tensor_copy(out=posf[:], in_=posi[:]), reads=[posi], writes=[posf])
    ang = K.sb('ang', [128, NT_, 8], F32)
    for i in range(8):
        K.op('dve', lambda e, i=i: e.tensor_scalar(out=ang[:, :, i], in0=posf[:], scalar1=float(inv_freq[i]), scalar2=None, op0=ALU.mult),
             reads=[posf], writes=[ang], nowaw=[ang])
    cs = K.sb('cs', [128, NT_, 16], F32)
    a2 = K.sb('a2', [128, NT_, 8], F32)
    kk = K.sb('kk', [128, NT_, 8], F32)
    rr = K.sb('rr', [128, NT_, 8], F32)
    for which in range(2):
        src = ang
        if which == 0:
            K.op('dve', lambda e: e.tensor_scalar(out=a2[:], in0=ang[:], scalar1=float(np.pi / 2), scalar2=None, op0=ALU.add), reads=[ang], writes=[a2])
            src = a2
        K.op('dve', lambda e, src=src: e.tensor_scalar(out=kk[:], in0=src[:], scalar1=float(1.0 / TWO_PI), scalar2=MAGIC, op0=ALU.mult, op1=ALU.add),
             reads=[src], writes=[kk])
        K.op('dve', lambda e: e.tensor_scalar(out=kk[:], in0=kk[:], scalar1=-MAGIC, scalar2=None, op0=ALU.add), reads=[kk], writes=[kk])
        K.op('dve', lambda e, src=src: e.scalar_tensor_tensor(out=rr[:], in0=kk[:], scalar=-CW1, in1=src[:], op0=ALU.mult, op1=ALU.add),
             reads=[kk, src], writes=[rr])
        K.op('dve', lambda e: e.scalar_tensor_tensor(out=rr[:], in0=kk[:], scalar=-CW2, in1=rr[:], op0=ALU.mult, op1=ALU.add),
             reads=[kk, rr], writes=[rr])
        K.op('dve', lambda e: e.tensor_scalar(out=rr[:], in0=rr[:], scalar1=3.141592, scalar2=-3.141592, op0=ALU.min, op1=ALU.max),
             reads=[rr], writes=[rr])
        K.op('act', lambda e, which=which: e.activation(out=cs[:, :, which * 8:(which + 1) * 8], in_=rr[:], func=AF.Sin),
             reads=[rr], writes=[cs], nowaw=[cs])
    K.dma('sp', d['csd'][:, :], cs[:].rearrange("p t c -> p (t c)"), reads=[cs])
    K.phase_end()


def phase_A1(K, d, S):
    NG = S // 512
    NT_ = S // 128
    K.phase_begin()
    consts(K)
    identb = make_ident(K, BF16, 'identb')
    gain = load_gain(K, 'gain', d['odd_norm_mix'])
    W = K.sb('W1', [128, 8, 2328], BF16)
    load_weight_bf16(K, W, d['odd_w_in'], 8, 2328, gain=gain)
    cs = K.sb('cs', [128, NT_, 16], F32)
    K.dma('sp', cs[:].rearrange("p t c -> p (t c)"), d['csd'][:, :], writes=[cs])
    NT = NormT(K, identb, nbuf_x=8, nbuf_n=8)
    yq = [K.sb('yq%d' % i, [128, 512], F32) for i in range(2)]
    yb = [K.sb('yb%d' % i, [128, 512], F32) for i in range(2)]
    yc = [K.sb('yc%d' % i, [128, 280], F32) for i in range(2)]
    gt_ = [K.sb('gt%d' % i, [128, 24], F32) for i in range(2)]
    qb = [K.sb('qb%d' % i, [128, 512], BF16) for i in range(2)]
    kb = [K.sb('kb%d' % i, [128, 4, 128], BF16) for i in range(2)]
    vb = [K.sb('vb%d' % i, [128, 2, 128], BF16) for i in range(2)]
    tA = K.sb('tA', [128, 8, 8], F32)
    tB = K.sb('tB', [128, 8, 8], F32)
    qTs = [K.sb('qTs%d' % i, [128, 4, 128], BF16) for i in range(2)]
    kTs = [K.sb('kTs%d' % i, [128, 4, 128], BF16) for i in range(2)]
    rgo = [K.sb('rgo%d' % i, [128, 512], F32) for i in range(3)]
    ptm = [K.ps('p1tm%d' % i, [128, 512], F32) for i in range(3)]
    pfm = [K.ps('p1fm%d' % i, [128, 512], F32) for i in range(2)]
    ptr = K.ps('p1tr', [128, 1024], BF16)
    nfm = 0

    def rope(eng, src, nh, dst, t):
        cosb = cs[:, t:t + 1, 0:8].to_broadcast([128, nh, 8])
        sinb = cs[:, t:t + 1, 8:16].to_broadcast([128, nh, 8])
        x1, x2 = src[:, :, 0:8], src[:, :, 8:16]
        a, b = tA[:, 0:nh, :], tB[:, 0:nh, :]
        return [
            lambda e: e.tensor_tensor(out=a, in0=x1, in1=cosb, op=ALU.mult),
            lambda e: e.tensor_tensor(out=b, in0=x2, in1=sinb, op=ALU.mult),
            lambda e: e.tensor_tensor(out=dst[:, :, 0:8], in0=a, in1=b, op=ALU.subtract),
            lambda e: e.tensor_tensor(out=a, in0=x2, in1=cosb, op=ALU.mult),
            lambda e: e.tensor_tensor(out=b, in0=x1, in1=sinb, op=ALU.mult),
            lambda e: e.tensor_tensor(out=dst[:, :, 8:16], in0=a, in1=b, op=ALU.add),
        ]

    toks = [NT.prep(d['h'][u * 128:(u + 1) * 128, :]) for u in range(4)]
    pending = [None]
    for g in range(NG):
        hnT = NT.hnT[g % 2]
        for j in range(4):
            t = g * 4 + j
            rows = slice(t * 128, (t + 1) * 128)
            NT.finish(toks.pop(0), hnT, j)
            if j == 0:
                for u in range(t + 4, min(t + 8, NG * 4)):
                    toks.append(NT.prep(d['h'][u * 128:(u + 1) * 128, :]))
            yq_, yb_, yc_, g_, qb_, kb_, vb_, qT_, kT_ = (yq[t % 2], yb[t % 2], yc[t % 2], gt_[t % 2], qb[t % 2], kb[t % 2], vb[t % 2],
                                                         qTs[t % 2], kTs[t % 2])
            for bi, (c0, c1, dst) in enumerate(((0, 512, yq_), (512, 1024, yb_), (1024, 1304, yc_))):
                p = ptm[bi]
                for k in range(8):
                    K.mm(p, p[:, 0:c1 - c0], hnT, hnT[:, k, j * 128:(j + 1) * 128], W, W[:, k, c0:c1], start=(k == 0), stop=(k == 7))
                K.op('act', lambda e, p=p, dst=dst, n=c1 - c0: e.copy(out=dst[:, 0:n], in_=p[:, 0:n]), reads=[p], writes=[dst])
            K.op('act', lambda e: e.activation(out=g_[:], in_=yc_[:, 256:280], func=AF.Sigmoid), reads=[yc_], writes=[g_])
            K.dma('pool', d['gates'][rows, :], g_[:], reads=[g_])
            K.op('dve', lambda e: e.tensor_copy(out=qb_[:], in_=yq_[:]), reads=[yq_], writes=[qb_])
            K.op('dve', lambda e: e.tensor_copy(out=kb_[:, 0:2, :], in_=yb_[:, 0:256].rearrange("p (a c) -> p a c", a=2)), reads=[yb_], writes=[kb_])
            K.op('dve', lambda e: e.tensor_copy(out=kb_[:, 2, :], in_=yb_[:, 256:384]), reads=[yb_], writes=[kb_])
            K.op('dve', lambda e: e.tensor_copy(out=kb_[:, 3, :], in_=yc_[:, 0:128]), reads=[yc_], writes=[kb_])
            K.op('pool', lambda e: e.tensor_copy(out=vb_[:, 0, :], in_=yb_[:, 384:512]), reads=[yb_], writes=[vb_])
            K.op('pool', lambda e: e.tensor_copy(out=vb_[:, 1, :], in_=yc_[:, 128:256]), reads=[yc_], writes=[vb_])
            K.dma('pool', d['Vs'][rows, :], vb_[:, 0, :], reads=[vb_])
            K.dma('pool', d['Vw'][rows, :], vb_[:, 1, :], reads=[vb_])
            jobs = [(yq_, yq_[:].rearrange("p (h c) -> p h c", c=64), 8, qb_, qb_[:].rearrange("p (h c) -> p h c", c=64)),
                    (yb_, yb_[:, 0:128].rearrange("p (h c) -> p h c", c=64), 2, kb_, kb_[:, 0, :].rearrange("p (h c) -> p h c", c=64)),
                    (yb_, yb_[:, 256:384].rearrange("p (h c) -> p h c", c=64), 2, kb_, kb_[:, 2, :].rearrange("p (h c) -> p h c", c=64)),
                    (yc_, yc_[:, 0:128].rearrange("p (h c) -> p h c", c=64), 2, kb_, kb_[:, 3, :].rearrange("p (h c) -> p h c", c=64))]
            for sb_, sap, nh, db_, dap in jobs:
                fns = rope('dve', sap, nh, dap, t)
                rw = [([sb_, cs], [tA]), ([sb_, cs], [tB]), ([tA, tB], [db_]), ([sb_, cs], [tA]), ([sb_, cs], [tB]), ([tA, tB], [db_])]
                for fn, (rd, wr) in zip(fns, rw):
                    K.op('dve', fn, reads=rd, writes=wr)
            def post(qb_=qb_, kb_=kb_, qT_=qT_, kT_=kT_, rows=rows):
                for c in range(4):
                    K.tr(ptr, ptr[:, c * 128:(c + 1) * 128], qb_, qb_[:, c * 128:(c + 1) * 128], identb, identb[:])
                for c in range(4):
                    K.tr(ptr, ptr[:, 512 + c * 128:512 + (c + 1) * 128], kb_, kb_[:, c, :], identb, identb[:])
                K.op('act', lambda e: e.copy(out=qT_[:], in_=ptr[:, 0:512].rearrange("p (c t) -> p c t", c=4)), reads=[ptr], writes=[qT_])
                K.op('act', lambda e: e.copy(out=kT_[:], in_=ptr[:, 512:1024].rearrange("p (c t) -> p c t", c=4)), reads=[ptr], writes=[kT_])
                K.dma('sp', d['QT1'][:, rows].rearrange("(c p) t -> p c t", p=128), qT_[:], reads=[qT_])
                for c, nm in enumerate(('KcT', 'VcT', 'KsT', 'KwT')):
                    K.dma('sp', d[nm][:, rows], kT_[:, c, :], reads=[kT_])
            if pending[0] is not None:
                pending[0]()
            pending[0] = post
        for c in range(8):
            p = pfm[nfm % 2]
            o = rgo[nfm % 3]
            nfm += 1
            for k in range(8):
                K.mm(p, p[:], W, W[:, k, O_RG + c * 128:O_RG + (c + 1) * 128], hnT, hnT[:, k, :], start=(k == 0), stop=(k == 7))
            K.op('act', lambda e, o=o, p=p: e.copy(out=o[:], in_=p[:]), reads=[p], writes=[o])
            K.dma('pool', d['RGX'][c * 128:(c + 1) * 128, g * 512:(g + 1) * 512], o[:], reads=[o])
    pending[0]()
    K.phase_end()


GELU_C = 1.5957691216057308


def gelu_tanh(K, xs, tmp, out_ap, out_b, eng2='pool'):
    if eng2 == 'act_sq':
        K.op('act', lambda e: e.activation(out=tmp[:], in_=xs[:], func=AF.Square), reads=[xs], writes=[tmp])
    else:
        K.op(eng2, lambda e: e.tensor_tensor(out=tmp[:], in0=xs[:], in1=xs[:], op=ALU.mult), reads=[xs], writes=[tmp])
    K.op('dve', lambda e: e.tensor_scalar(out=tmp[:], in0=tmp[:], scalar1=0.044715, scalar2=1.0, op0=ALU.mult, op1=ALU.add),
         reads=[tmp], writes=[tmp])
    K.op('dve', lambda e: e.tensor_tensor(out=tmp[:], in0=tmp[:], in1=xs[:], op=ALU.mult), reads=[tmp, xs], writes=[tmp])
    K.op('act', lambda e: e.activation(out=tmp[:], in_=tmp[:], func=AF.Sigmoid, scale=GELU_C), reads=[tmp], writes=[tmp])
    K.op('dve', lambda e: e.tensor_tensor(out=out_ap, in0=tmp[:], in1=xs[:], op=ALU.mult), reads=[tmp, xs], writes=[out_b])


def phase_cmp(K, d, S):
    NCP = S // 16
    NCMP = NCP - 1
    K.phase_begin()
    consts(K)
    XT = K.sb('XT', [128, S], BF16)
    w1s = K.sb('w1s', [128, 32, 128], F32)
    W1 = K.sb('W1c', [128, 32, 128], BF16)
    pes = K.sb('pes', [128, 32], F32)
    peT = K.sb('peT', [128, 32], BF16)
    w2s = K.sb('w2s', [128, 64], F32)
    W2 = K.sb('W2c', [128, 64], BF16)
    cvec = K.sb('cvec', [128, 1], F32)
    xs = K.sb('cxs', [128, NCP], F32)
    tmp = K.sb('ctmp', [128, NCP], F32)
    hidT = K.sb('hidT', [128, NCP], BF16)
    ko = K.sb('ko', [64, NCP], BF16)
    vo = K.sb('vo', [128, 64], BF16)
    ph = K.ps('cph', [128, 512], F32)
    pc = K.ps('cpc', [128, 512], F32)
    po = K.ps('cpo', [128, 512], F32)
    for which, (src, pe, w1, w2) in enumerate((('KcT', 'odd_cmp_k_pe', 'odd_cmp_k_w1', 'odd_cmp_k_w2'),
                                                ('VcT', 'odd_cmp_v_pe', 'odd_cmp_v_w1', 'odd_cmp_v_w2'))):
        K.dma('sp', XT[:], d[src][:, :], writes=[XT])
        for hf in range(2):
            K.dma('sp', w1s[hf * 64:(hf + 1) * 64, :, :], d[w1].rearrange("(l c) h -> c l h", c=64), writes=[w1s], nowaw=[w1s])
            K.dma('sp', pes[hf * 64:(hf + 1) * 64, :], d[pe].rearrange("l c -> c l"), writes=[pes], nowaw=[pes],
                  allow_slow_non_contiguous=True)
        K.dma('sp', w2s[:], d[w2][:, :], writes=[w2s])
        K.op('dve', lambda e: e.tensor_copy(out=W1[:], in_=w1s[:]), reads=[w1s], writes=[W1])
        K.op('dve', lambda e: e.tensor_copy(out=peT[:], in_=pes[:]), reads=[pes], writes=[peT])
        K.op('dve', lambda e: e.tensor_copy(out=W2[:], in_=w2s[:]), reads=[w2s], writes=[W2])
        for l in range(32):
            K.mm(pc, pc[:, 0:1], W1, W1[0:64, l, :], peT, peT[0:64, l:l + 1], start=(l == 0), stop=(l == 31))
        K.op('act', lambda e: e.copy(out=cvec[:], in_=pc[:, 0:1]), reads=[pc], writes=[cvec])
        X3 = XT[:].rearrange("p (n s) -> p n s", s=16)
        for g in range(2):
            gs = slice(g * 64, (g + 1) * 64)
            for l in range(32):
                rhs = X3[gs, 0:NCMP, l] if l < 16 else X3[gs, 1:NCP, l - 16]
                K.mm(ph, ph[:, 0:NCMP], W1, W1[gs, l, :], XT, rhs, start=(l == 0), stop=(l == 31))
            K.op('pool', lambda e: e.memset(xs[:], 0.0), writes=[xs])
            K.op('act', lambda e: e.activation(out=xs[:, 0:NCMP], in_=ph[:, 0:NCMP], func=AF.Identity, bias=cvec[:, 0:1]),
                 reads=[ph, cvec], writes=[xs])
            gelu_tanh(K, xs, tmp, hidT[:], hidT)
            if which == 0:
                for c0 in range(0, NCP, 512):
                    n = min(512, NCP - c0)
                    K.mm(po, po[0:64, 0:n], W2, W2[:, :], hidT, hidT[:, c0:c0 + n])
                    K.op('act', lambda e, c0=c0, n=n: e.copy(out=ko[:, c0:c0 + n], in_=po[0:64, 0:n]), reads=[po], writes=[ko])
                K.dma('sp', d['KcmpT'][gs, :], ko[:], reads=[ko])
            else:
                for c0 in range(0, NCP, 128):
                    K.mm(po, po[:, 0:64], hidT, hidT[:, c0:c0 + 128], W2, W2[:, :])
                    K.op('act', lambda e: e.copy(out=vo[:], in_=po[:, 0:64]), reads=[po], writes=[vo])
                    K.dma('sp', d['Vcmp'][c0:c0 + 128, gs], vo[:], reads=[vo])
    K.phase_end()


def attn_core(K, S, heads, B, load_head, units_fn, KR, s_extra, exp_bias, emit_masks, pv_extra, finalize, L=3):
    NQ = S // 512
    units = []
    for hi, h in enumerate(heads):
        for qg in range(NQ):
            us = units_fn(qg)
            for ui, u in enumerate(us):
                units.append((hi, h, qg, u, ui == 0, ui == len(us) - 1))
    load_head(0, heads[0])
    NU = len(units)
    deferred = []
    for i in range(NU + L):
        if deferred and deferred[0][0] <= i:
            deferred.pop(0)[1]()
        if i < NU:
            hi, h, qg, (kt, c0, c1), first, last = units[i]
            s = hi % 2
            p_s = B['ps'][i % len(B['ps'])]
            p_t = B['pt'][i % 4]
            KT = B['KT'][s]
            QT = B['QTsel'](s, kt) if 'QTsel' in B else B['QT'][s]
            ex = s_extra(h, qg, kt, c0, c1) if s_extra else None
            K.mm(p_s, p_s[:, c0:c1], KT, KT[0:KR, kt * 128:(kt + 1) * 128], QT, QT[0:KR, qg * 512 + c0:qg * 512 + c1],
                 start=True, stop=(ex is None))
            if ex is not None:
                K.mm(p_s, p_s[:, c0:c1], ex[0], ex[1], ex[2], ex[3], start=False, stop=True)
            bb, bap = exp_bias(h, kt) if exp_bias else (K.zero_t, K.zero_t[:, 0:1])
            K.op('act', lambda e, p_s=p_s, p_t=p_t, c0=c0, c1=c1, bap=bap: e.activation(
                out=p_t[:, c0:c1], in_=p_s[:, c0:c1], func=AF.Exp, scale=0.125, bias=bap), reads=[p_s, bb], writes=[p_t])
            emit_masks(p_t, h, qg, kt, c0, c1)
        if i - L >= 0:
            hi, h, qg, (kt, c0, c1), first, last = units[i - L]
            if first and qg == 0 and hi + 1 < len(heads):
                load_head(hi + 1, heads[hi + 1])
            s = hi % 2
            p_t = B['pt'][(i - L) % 4]
            gi = hi * NQ + qg
            p_o = B['po'][gi % 2]
            V = B['V'][s]
            K.mm(p_o, p_o[0:65, c0:c1], V, V[:, kt, 0:65], p_t, p_t[:, c0:c1], start=first, stop=last)
            if pv_extra:
                pv_extra(p_t, hi, h, qg, kt, c0, c1, first, last)
            if last:
                rest = finalize(hi, h, qg, gi, p_o)
                if rest is not None:
                    deferred.append((i + 2, rest))
    for _, rest in deferred:
        rest()


def causal_sel(K, p_t, c):
    K.op('pool', lambda e: e.affine_select(out=p_t[:, c:c + 128], in_=p_t[:, c:c + 128], pattern=[[1, 128]], compare_op=ALU.is_ge,
                                           fill=0.0, base=0, channel_multiplier=-1), reads=[p_t], writes=[p_t])


class NsaCommon:
    def __init__(self, K, d, S, nkt, br, out_name, n_ps=3):
        self.K, self.d, self.S, self.br, self.out_name = K, d, S, br, out_name
        NT_ = S // 128
        consts(K)
        K.zero_t = K.sb('zero', [128, 1], F32)
        K.op('pool', lambda e: e.memset(K.zero_t[:], 0.0), writes=[K.zero_t])
        self.identf = make_ident(K, F32, 'identf')
        self.gates = K.sb('gates', [128, NT_, 24], F32)
        K.dma('sp', self.gates[:], d['gates'].rearrange("(t p) c -> p t c", p=128), writes=[self.gates])
        self.B = dict(
            KT=[K.sb('nKT%d' % i, [128, nkt * 128], BF16) for i in range(2)],
            QT=[K.sb('nQT%d' % i, [128, S], BF16) for i in range(2)],
            V=[K.sb('nV%d' % i, [128, nkt, 65], BF16) for i in range(2)],
            ps=[K.ps('nps%d' % i, [128, 512], F32) for i in range(n_ps)],
            po=[K.ps('npo%d' % i, [128, 512], F32) for i in range(2)],
            pt=[K.sb('npt%d' % i, [128, 512], BF16) for i in range(4)],
        )
        for i in range(2):
            K.op('pool', lambda e, i=i: e.memset(self.B['V'][i][:, :, 64:65], 1.0), writes=[self.B['V'][i]])
            K.op('pool', lambda e, i=i: e.memset(self.B['KT'][i][64:128, :], 0.0), writes=[self.B['KT'][i]])
            K.op('pool', lambda e, i=i: e.memset(self.B['QT'][i][64:128, :], 0.0), writes=[self.B['QT'][i]])
        self.pq = K.ps('npq', [128, 4, 65], F32)
        self.osb = [K.sb('nosb%d' % i, [65, 512], F32) for i in range(2)]
        self.rc = [K.sb('nrc%d' % i, [128, 4], F32) for i in range(2)]
        self.mo = [K.sb('nmo%d' % i, [128, 4, 64], F32) for i in range(2)]

    def load_head(self, ksrc, vsrc, nk):
        K, d, B = self.K, self.d, self.B

        def f(hi, h):
            s, g = hi % 2, h // 4
            K.dma('sp', B['KT'][s][0:64, 0:nk], d[ksrc][g * 64:(g + 1) * 64, 0:nk], writes=[B['KT'][s]])
            K.dma('sp', B['QT'][s][0:64, :], d['QT1'][h * 64:(h + 1) * 64, :], writes=[B['QT'][s]])
            K.dma('sp', B['V'][s][:, :, 0:64], d[vsrc][0:nk, g * 64:(g + 1) * 64].rearrange("(t p) c -> p t c", p=128),
                  writes=[B['V'][s]])
        return f

    def finalize(self, hi, h, qg, gi, p_o, pre=None, defer_pre=False, rc_hook=None):
        K, d = self.K, self.d
        o_s, r_c, m_o, p_q = self.osb[gi % 2], self.rc[gi % 2], self.mo[gi % 2], self.pq
        K.op('act', lambda e: e.copy(out=o_s[:], in_=p_o[0:65, :]), reads=[p_o], writes=[o_s])
        if pre and not defer_pre:
            pre(o_s)

        def rest():
            if pre and defer_pre:
                pre(o_s)
            for j in range(4):
                K.tr(p_q, p_q[:, j, :], o_s, o_s[0:65, j * 128:(j + 1) * 128], self.identf, self.identf[0:65, 0:65])
            K.op('dve', lambda e: e.tensor_scalar(out=r_c[:], in0=p_q[:, :, 64], scalar1=1e-30, scalar2=None, op0=ALU.add),
                 reads=[p_q], writes=[r_c])
            K.op('dve', lambda e: e.reciprocal(out=r_c[:], in_=r_c[:]), reads=[r_c], writes=[r_c])
            if rc_hook:
                rc_hook(r_c)
            col = h * 3 + self.br
            K.op('dve', lambda e: e.tensor_tensor(out=r_c[:], in0=r_c[:], in1=self.gates[:, qg * 4:(qg + 1) * 4, col], op=ALU.mult),
                 reads=[r_c, self.gates], writes=[r_c])
            for j in range(4):
                K.op('dve', lambda e, j=j: e.tensor_scalar(out=m_o[:, j, :], in0=p_q[:, j, 0:64], scalar1=r_c[:, j:j + 1], scalar2=None,
                                                           op0=ALU.mult), reads=[p_q, r_c], writes=[m_o])
            K.dma('pool', d[self.out_name][qg * 512:(qg + 1) * 512, h * 64:(h + 1) * 64].rearrange("(j p) c -> p j c", p=128),
                  m_o[:], reads=[m_o])
        return rest


def phase_nsa_win(K, d, S, heads=range(8)):
    K.phase_begin()
    C = NsaCommon(K, d, S, S // 128, 2, 'nsa2')

    def units_fn(qg):
        us = []
        for kt in range(max(0, 4 * qg - 4), 4 * qg + 4):
            i0 = max(kt, 4 * qg)
            i1 = min(kt + 4, 4 * qg + 3)
            us.append((kt, (i0 - 4 * qg) * 128, (i1 - 4 * qg + 1) * 128))
        return us

    def masks(p_t, h, qg, kt, c0, c1):
        if kt >= 4 * qg:
            causal_sel(K, p_t, (kt - 4 * qg) * 128)
        if kt + 4 <= 4 * qg + 3 and kt + 4 >= 4 * qg:
            c = (kt + 4 - 4 * qg) * 128
            K.op('pool', lambda e: e.affine_select(out=p_t[:, c:c + 128], in_=p_t[:, c:c + 128], pattern=[[-1, 128]], compare_op=ALU.is_gt,
                                                   fill=0.0, base=0, channel_multiplier=1), reads=[p_t], writes=[p_t])

    attn_core(K, S, list(heads), C.B, C.load_head('KwT', 'Vw', S), units_fn, 128, None, None, masks, None, C.finalize)
    K.phase_end()


def phase_nsa_slc(K, d, S, heads=range(8)):
    NT_ = S // 128
    NA = max(1, S // 4096)
    K.phase_begin()
    C = NsaCommon(K, d, S, NT_, 1, 'nsa1')
    QT2 = [[C.B['QT'][i] for i in range(2)]] + [[K.sb('nQTa%d_%d' % (a, i), [128, S], BF16) for i in range(2)] for a in range(1, NA)]
    for i in range(2):
        kt_ = C.B['KT'][i]
        K.op('pool', lambda e, kt_=kt_: e.memset(kt_[64:128, :], 262144.0), writes=[kt_])
        MB = min(64, S // 64)
        K.op('pool', lambda e, kt_=kt_: e.affine_select(
            out=kt_[64:128, :].rearrange("p (a m k) -> p a m k", m=MB, k=64), in_=kt_[64:128, :].rearrange("p (a m k) -> p a m k", m=MB, k=64),
            pattern=[[0, S // (MB * 64)], [1, MB], [0, 64]], compare_op=ALU.is_equal, fill=0.0, base=0,
            channel_multiplier=-1), reads=[kt_], writes=[kt_])
    C.B['QTsel'] = lambda s_, kt: QT2[kt // 32][s_]

    def load_head(hi, h):
        s_, g = hi % 2, h // 4
        B = C.B
        K.dma('sp', B['KT'][s_][0:64, :], d['KsT'][g * 64:(g + 1) * 64, :], writes=[B['KT'][s_]], nowaw=[B['KT'][s_]])
        for a in range(NA):
            qt = QT2[a][s_]
            K.dma('sp', qt[0:64, :], d['QT1'][h * 64:(h + 1) * 64, :], writes=[qt])
            K.dma('sp', qt[64:128, :], d['selT'][g, 64 * a:64 * a + 64, :], writes=[qt], nowaw=[qt])
        K.dma('sp', B['V'][s_][:, :, 0:64], d['Vs'][:, g * 64:(g + 1) * 64].rearrange("(t p) c -> p t c", p=128), writes=[B['V'][s_]])

    def units_fn(qg):
        return [(kt, 128 * max(kt - 4 * qg, 0), 512) for kt in range(4 * qg + 4)]

    def masks(p_t, h, qg, kt, c0, c1):
        if kt >= 4 * qg:
            causal_sel(K, p_t, (kt - 4 * qg) * 128)

    attn_core(K, S, list(heads), C.B, load_head, units_fn, 128, None, None, masks, None, C.finalize)
    K.phase_end()


def phase_nsa_cmp(K, d, S, heads=range(8)):
    NT_ = S // 128
    NQ = S // 512
    NCP = S // 16
    NKT = (NCP + 127) // 128
    NKP = NKT * 128
    K.phase_begin()
    C = NsaCommon(K, d, S, NKT, 0, 'nsa0', n_ps=2)
    identf = C.identf
    identb = make_ident(K, BF16, 'identb')
    onesf = K.sb('onesf', [128, 128], F32)
    K.op('pool', lambda e: e.memset(onesf[:], 1.0), writes=[onesf])
    ovf = K.sb('ovf', [128, 128], F32)
    ovf2 = K.sb('ovf2', [128, 128], F32)
    ovl = K.sb('ovl', [128, NKT, 128], BF16)
    for kt in range(NKT):
        K.op('pool', lambda e, kt=kt: e.affine_select(out=ovf[:], in_=onesf[:], pattern=[[64, 128]], compare_op=ALU.is_ge, fill=0.0,
                                                       base=63 - 2048 * kt, channel_multiplier=-16), reads=[onesf], writes=[ovf])
        K.op('pool', lambda e, kt=kt: e.affine_select(out=ovf2[:], in_=ovf[:], pattern=[[-64, 128]], compare_op=ALU.is_ge, fill=0.0,
                                                       base=2048 * kt + 31, channel_multiplier=16), reads=[ovf], writes=[ovf2])
        K.op('pool', lambda e, kt=kt: e.tensor_copy(out=ovl[:, kt, :], in_=ovf2[:]), reads=[ovf2], writes=[ovl], nowaw=[ovl])
    cst = K.sb('cst', [128, 8], F32)
    K.op('pool', lambda e: e.memset(cst[:, 0:1], 1e30), writes=[cst])
    K.op('pool', lambda e: e.memset(cst[:, 1:2], 2e30), writes=[cst])
    K.op('pool', lambda e: e.memset(cst[0:64, 2:3], 3e30), writes=[cst])
    K.op('pool', lambda e: e.memset(cst[64:128, 2:3], 0.0), writes=[cst])
    K.op('pool', lambda e: e.memset(cst[0:64, 3:4], 0.0), writes=[cst])
    K.op('pool', lambda e: e.memset(cst[64:128, 3:4], 1.0), writes=[cst])
    K.op('pool', lambda e: e.memset(cst[0:64, 4:5], -1e30), writes=[cst])
    K.op('pool', lambda e: e.memset(cst[64:128, 4:5], 4e30), writes=[cst])
    po2s = [K.ps('npo2_%d' % i, [128, 4, 128], F32) for i in range(2)]
    pb = K.ps('npb', [128, 512], F32)
    imp_acc = K.sb('impacc', [128, NT_, 128], F32)
    rrow = K.sb('rrow', [65, 512], F32)
    rbs = K.sb('rbs', [128, 512], F32)
    tmpi = K.sb('tmpi', [128, 512], F32)
    work = K.sb('work', [128, 128], F32)
    work2 = K.sb('work2', [128, 128], F32)
    m8 = K.sb('m8', [128, 16], F32)
    selm = K.sb('selm', [128, 128], F32)
    selb = K.sb('selb', [128, 128], F32)
    sTs = [K.sb('sTs%d' % i, [128, 128], BF16) for i in range(2)]

    def units_fn(qg):
        kmax = min(NKT - 1, (32 * qg + 30) // 128)
        return [(kt, 0, 512) for kt in range(kmax + 1)]

    def masks(p_t, h, qg, kt, c0, c1):
        K.op('pool', lambda e: e.affine_select(out=p_t[:], in_=p_t[:], pattern=[[1, 512]], compare_op=ALU.is_ge, fill=0.0,
                                               base=512 * qg - 2048 * kt - 31, channel_multiplier=-16), reads=[p_t], writes=[p_t])

    def pv_extra(p_t, hi, h, qg, kt, c0, c1, first, last):
        po2 = po2s[(hi * NQ + qg) % 2]
        for j in range(4):
            K.mm(po2, po2[:, j, :], p_t, p_t[:, j * 128:(j + 1) * 128], ovl, ovl[:, kt, :], start=(first and j == 0), stop=(last and j == 3))

    def topk_pass(g):
        for i in range(NT_):
            W = 2 * i + 2
            cols = slice(i * 128, (i + 1) * 128)
            K.op('pool', lambda e: e.memset(selm[:], 0.0), writes=[selm])
            if W <= 16:
                K.op('pool', lambda e, W=W: e.memset(selm[:, 0:W], 1.0), writes=[selm])
            else:
                K.op('act', lambda e, W=W, i=i: e.copy(out=work[:, 0:W], in_=imp_acc[:, i, 0:W]), reads=[imp_acc], writes=[work])
                K.op('dve', lambda e: e.tensor_copy(out=work[:, 0:1], in_=cst[:, 0:1]), reads=[cst], writes=[work])
                K.op('dve', lambda e, i=i: e.tensor_copy(out=work[:, 2 * i:2 * i + 1], in_=cst[:, 1:2]), reads=[cst], writes=[work])
                K.op('dve', lambda e, i=i: e.tensor_scalar(out=work[:, 2 * i - 1:2 * i], in0=work[:, 2 * i - 1:2 * i], scalar1=cst[:, 3:4],
                                                           scalar2=cst[:, 2:3], op0=ALU.mult, op1=ALU.add), reads=[work, cst], writes=[work])
                K.op('dve', lambda e, i=i: e.tensor_copy(out=work[:, 2 * i + 1:2 * i + 2], in_=cst[:, 4:5]), reads=[cst], writes=[work])
                K.op('dve', lambda e, W=W: e.max(out=m8[:, 0:8], in_=work[:, 0:W]), reads=[work], writes=[m8])
                K.op('dve', lambda e, W=W: e.match_replace(out=work2[:, 0:W], in_to_replace=m8[:, 0:8], in_values=work[:, 0:W],
                                                           imm_value=-3.0e38), reads=[work, m8], writes=[work2])
                K.op('dve', lambda e, W=W: e.max(out=m8[:, 8:16], in_=work2[:, 0:W]), reads=[work2], writes=[m8])
                K.op('dve', lambda e, W=W: e.tensor_scalar(out=selm[:, 0:W], in0=work[:, 0:W], scalar1=m8[:, 15:16], scalar2=None,
                                                           op0=ALU.is_ge), reads=[work, m8], writes=[selm])
            K.op('dve', lambda e: e.tensor_scalar(out=selb[:], in0=selm[:], scalar1=-1.0, scalar2=None, op0=ALU.add), reads=[selm], writes=[selb])
            K.tr(pb, pb[:, 128:256], selb, selb[:], identf, identf[:])
            sT = sTs[i % 2]
            K.op('act', lambda e, sT=sT: e.copy(out=sT[:], in_=pb[:, 128:256]), reads=[pb], writes=[sT])
            K.dma('pool', d['selT'][g, :, cols], sT[:], reads=[sT])


    def finalize(hi, h, qg, gi, p_o):
        po2 = po2s[gi % 2]

        def rc_hook(r_c):
            for j in range(4):
                dst = imp_acc[:, qg * 4 + j, :]
                if h % 4 == 0:
                    K.op('dve', lambda e, j=j, dst=dst: e.tensor_scalar(out=dst, in0=po2[:, j, :], scalar1=r_c[:, j:j + 1], scalar2=None, op0=ALU.mult),
                         reads=[po2, r_c], writes=[imp_acc], nowaw=[imp_acc])
                else:
                    K.op('dve', lambda e, j=j, dst=dst: e.scalar_tensor_tensor(out=dst, in0=po2[:, j, :], scalar=r_c[:, j:j + 1], in1=dst,
                                                                                op0=ALU.mult, op1=ALU.add),
                         reads=[po2, r_c, imp_acc], writes=[imp_acc], nowaw=[imp_acc])
        rest0 = C.finalize(hi, h, qg, gi, p_o, rc_hook=rc_hook)

        def rest():
            rest0()
            if h % 4 == 3 and qg == NQ - 1:
                topk_pass(h // 4)
        return rest

    attn_core(K, S, list(heads), C.B, C.load_head('KcmpT', 'Vcmp', NCP), units_fn, 128, None, None, masks, pv_extra, finalize)
    K.phase_end()


def phase_nsa_combine(K, d, S):
    NT_ = S // 128
    K.phase_begin()
    a = [K.sb('ca%d' % i, [128, 512], F32) for i in range(2)]
    b = [K.sb('cb%d' % i, [128, 512], F32) for i in range(2)]
    c = [K.sb('cc%d' % i, [128, 512], F32) for i in range(2)]
    o = [K.sb('co%d' % i, [128, 512], BF16) for i in range(2)]
    for t in range(NT_):
        rows = slice(t * 128, (t + 1) * 128)
        a_, b_, c_, o_ = a[t % 2], b[t % 2], c[t % 2], o[t % 2]
        K.dma('sp', a_[:], d['nsa0'][rows, :], writes=[a_])
        K.dma('sp', b_[:], d['nsa1'][rows, :], writes=[b_])
        K.dma('sp', c_[:], d['nsa2'][rows, :], writes=[c_])
        K.op('dve', lambda e: e.tensor_tensor(out=a_[:], in0=a_[:], in1=b_[:], op=ALU.add), reads=[a_, b_], writes=[a_])
        K.op('dve', lambda e: e.tensor_tensor(out=o_[:], in0=a_[:], in1=c_[:], op=ALU.add), reads=[a_, c_], writes=[o_])
        K.dma('pool', d['mixed'][rows, 0:512], o_[:], reads=[o_])
    K.phase_end()


def phase_lru(K, d, S):
    SEG = min(S, 2048)
    NS = S // SEG
    K.phase_begin()
    consts(K)
    identb = make_ident(K, BF16, 'identb')
    prm = K.sb('lprm', [128, 12, 4], F32)
    for j in range(4):
        K.dma('sp', prm[:, j, :], d['odd_rg_conv_w'][j, :].rearrange("(c p) -> p c", p=128), writes=[prm], nowaw=[prm],
              allow_slow_non_contiguous=True)
    for idx, nm in ((4, 'odd_rg_conv_b'), (5, 'odd_rg_ba'), (6, 'odd_rg_bx'), (7, 'odd_rg_lambda')):
        K.dma('sp', prm[:, idx, :], d[nm].rearrange("(c p) -> p c", p=128), writes=[prm], nowaw=[prm], allow_slow_non_contiguous=True)
    K.op('act', lambda e: e.activation(out=prm[:, 8, :], in_=prm[:, 7, :], func=AF.Exp, scale=-1.0), reads=[prm], writes=[prm])
    K.op('act', lambda e: e.activation(out=prm[:, 8, :], in_=prm[:, 8, :], func=AF.Ln, bias=K.one_t[:, 0:1]), reads=[prm, K.one_t], writes=[prm])
    K.op('dve', lambda e: e.tensor_scalar(out=prm[:, 8, :], in0=prm[:, 8, :], scalar1=-8.0, scalar2=None, op0=ALU.mult), reads=[prm], writes=[prm])
    wst = K.sb('lwst', [128, 128], F32)
    WA = K.sb('WA', [128, 4, 128], BF16)
    WX = K.sb('WX', [128, 4, 128], BF16)
    for Wt, nm in ((WA, 'odd_rg_wa'), (WX, 'odd_rg_wx')):
        for c in range(4):
            K.op('pool', lambda e: e.memset(wst[:], 0.0), writes=[wst])
            K.dma('sp', wst[0:64, 0:64], d[nm][2 * c, :, :], writes=[wst])
            K.dma('sp', wst[64:128, 64:128], d[nm][2 * c + 1, :, :], writes=[wst])
            K.op('dve', lambda e, Wt=Wt, c=c: e.tensor_copy(out=Wt[:, c, :], in_=wst[:]), reads=[wst], writes=[Wt], nowaw=[Wt])
    names = [('xpad', SEG + 3, F32), ('x', SEG, F32), ('xbf', SEG, BF16), ('r', SEG, F32), ('ig', SEG, F32), ('a', SEG, F32), ('t', SEG, F32),
             ('hh', SEG, F32), ('rg', SEG, F32), ('tmp', SEG, F32), ('gg', SEG, F32), ('ybf', SEG, BF16)]
    TL = [{nm: K.sb('l%s%d' % (nm, i), [128, w], dt) for nm, w, dt in names} for i in range(2)]
    hc = K.sb('lhc', [128, 1], F32)
    yo = [K.sb('lyo%d' % i, [128, 4, 128], BF16) for i in range(2)]
    pr = [K.ps('lpr%d' % i, [128, 512], F32) for i in range(2)]
    pi = [K.ps('lpi%d' % i, [128, 512], F32) for i in range(2)]
    ptr = [K.ps('lptr%d' % i, [128, 1024], BF16) for i in range(2)]
    nt = 0
    it = 0
    for c in range(4):
        K.op('pool', lambda e: e.memset(hc[:], 0.0), writes=[hc])
        for s in range(NS):
            cols = slice(s * SEG, (s + 1) * SEG)
            T_ = TL[it % 2]
            Tn = TL[(it + 1) % 2]
            it += 1
            xpad, x, xbf, r, ig, a, t, hh, rg, tmp, gg, ybf = (T_[k] for k in ('xpad', 'x', 'xbf', 'r', 'ig', 'a', 't', 'hh', 'rg', 'tmp', 'gg', 'ybf'))
            if s == 0:
                K.op('pool', lambda e, xpad=xpad: e.memset(xpad[:, 0:3], 0.0), writes=[xpad])
            K.dma('sp', xpad[:, 3:3 + SEG], d['RGX'][512 + c * 128:512 + (c + 1) * 128, cols], writes=[xpad], nowaw=[xpad])
            K.dma('sp', rg[:], d['RGX'][c * 128:(c + 1) * 128, cols], writes=[rg])
            K.op('dve', lambda e: e.tensor_scalar(out=x[:], in0=xpad[:, 0:SEG], scalar1=prm[:, 0, c:c + 1], scalar2=prm[:, 4, c:c + 1],
                                                  op0=ALU.mult, op1=ALU.add), reads=[xpad, prm], writes=[x])
            for j in range(1, 4):
                K.op('dve', lambda e, j=j: e.scalar_tensor_tensor(out=x[:], in0=xpad[:, j:j + SEG], scalar=prm[:, j, c:c + 1], in1=x[:],
                                                                   op0=ALU.mult, op1=ALU.add), reads=[xpad, prm, x], writes=[x])
            K.op('pool', lambda e, xpad=xpad, xnx=Tn['xpad']: e.tensor_copy(out=xnx[:, 0:3], in_=xpad[:, SEG:SEG + 3]), reads=[xpad], writes=[Tn['xpad']])
            K.op('act', lambda e: e.copy(out=xbf[:], in_=x[:]), reads=[x], writes=[xbf])
            for pc in range(SEG // 512):
                ps_ = slice(pc * 512, (pc + 1) * 512)
                p1, p2 = pr[pc % 2], pi[pc % 2]
                K.mm(p1, p1[:], WA, WA[:, c, :], xbf, xbf[:, ps_])
                K.mm(p2, p2[:], WX, WX[:, c, :], xbf, xbf[:, ps_])
                K.op('act', lambda e, p1=p1, ps_=ps_: e.activation(out=r[:, ps_], in_=p1[:], func=AF.Sigmoid, bias=prm[:, 5, c:c + 1]),
                     reads=[p1, prm], writes=[r], nowaw=[r])
                K.op('act', lambda e, p2=p2, ps_=ps_: e.activation(out=ig[:, ps_], in_=p2[:], func=AF.Sigmoid, bias=prm[:, 6, c:c + 1]),
                     reads=[p2, prm], writes=[ig], nowaw=[ig])
            K.op('act', lambda e: e.activation(out=a[:], in_=r[:], func=AF.Exp, scale=prm[:, 8, c:c + 1]), reads=[r, prm], writes=[a])
            K.op('dve', lambda e: e.tensor_tensor(out=t[:], in0=a[:], in1=a[:], op=ALU.mult), reads=[a], writes=[t])
            K.op('dve', lambda e: e.tensor_scalar(out=t[:], in0=t[:], scalar1=-1.0, scalar2=1.0, op0=ALU.mult, op1=ALU.add), reads=[t], writes=[t])
            K.op('act', lambda e: e.activation(out=t[:], in_=t[:], func=AF.Sqrt), reads=[t], writes=[t])
            K.op('dve', lambda e: e.tensor_tensor(out=t[:], in0=t[:], in1=ig[:], op=ALU.mult), reads=[t, ig], writes=[t])
            K.op('dve', lambda e: e.tensor_tensor(out=t[:], in0=t[:], in1=x[:], op=ALU.mult), reads=[t, x], writes=[t])
            K.op('dve', lambda e: e.tensor_tensor_scan(out=hh[:], data0=a[:], data1=t[:], initial=hc[:, 0:1], op0=ALU.mult, op1=ALU.add),
                 reads=[a, t, hc], writes=[hh])
            K.op('pool', lambda e: e.tensor_copy(out=hc[:], in_=hh[:, SEG - 1:SEG]), reads=[hh], writes=[hc])
            gelu_tanh(K, rg, tmp, gg[:], gg, eng2='act_sq')
            K.op('dve', lambda e: e.tensor_tensor(out=ybf[:], in0=hh[:], in1=gg[:], op=ALU.mult), reads=[hh, gg], writes=[ybf])
            for q4 in range(SEG // 512):
                p_ = ptr[nt % 2]
                y_ = yo[nt % 2]
                nt += 1
                for j in range(4):
                    tc_ = slice(q4 * 512 + j * 128, q4 * 512 + (j + 1) * 128)
                    K.tr(p_, p_[:, j * 128:(j + 1) * 128], ybf, ybf[:, tc_], identb, identb[:])
                K.op('act', lambda e, p_=p_, y_=y_: e.copy(out=y_[:], in_=p_[:, 0:512].rearrange("p (j c) -> p j c", j=4)), reads=[p_], writes=[y_])
                t0 = s * SEG + q4 * 512
                K.dma('pool', d['mixed'][t0:t0 + 512, 512 + c * 128:512 + (c + 1) * 128].rearrange("(j p) c -> p j c", p=128), y_[:],
                      reads=[y_])
    K.phase_end()


def all_phases():
    return [
        phase_A0, phase_Fprep, phase_fox, phase_Gprep, phase_gdn2,
        lambda K, d, S: phase_O(K, d, S, d['x'], d['even_w_out']),
        lambda K, d, S: phase_MLP(K, d, S, 0),
        lambda K, d, S: phase_PLE(K, d, S, 0),
        phase_Rprep, phase_A1, phase_cmp, phase_nsa_cmp, phase_nsa_slc, phase_nsa_win, phase_nsa_combine, phase_lru,
        lambda K, d, S: phase_O(K, d, S, d['h'], d['odd_w_out']),
        lambda K, d, S: phase_MLP(K, d, S, 1),
        lambda K, d, S: phase_PLE(K, d, S, 1, final=True),
    ]


SEQ = 8192
NCORES = 8


def kernel(**inputs):
    S = SEQ
    nc, K = build(S, all_phases())
    in_maps = []
    shared = {}
    for n, shp in INPUT_SHAPES.items():
        a = np.asarray(inputs[n])
        if list(a.shape) != shp:
            a = a.reshape(shp)
        shared[n] = np.ascontiguousarray(a.astype(np.float32, copy=False))
    x = np.asarray(inputs['x'])
    p = np.asarray(inputs['p'])
    pos = np.asarray(inputs['positions'])
    for b in range(NCORES):
        m = dict(shared)
        m['x'] = np.ascontiguousarray(x[b])
        m['p'] = np.ascontiguousarray(p[:, b])
        m['positions'] = np.ascontiguousarray(pos[b:b + 1]).astype(np.int32, copy=False)
        in_maps.append(m)
    res = run_bass_kernel_spmd(nc, in_maps, core_ids=list(range(NCORES)))
    out = np.stack([np.asarray(res.results[b]['out']) for b in range(NCORES)], axis=0)
    return out.astype(np.float32, copy=False)


def phase_gdn2(K, d, S, heads=range(4)):
    NC = S // 128
    NG = S // 512
    heads = list(heads)
    K.phase_begin()
    consts(K)
    identf = make_ident(K, F32, 'identf')
    onesf = K.sb('onesf', [128, 128], F32)
    K.op('pool', lambda e: e.memset(onesf[:], 1.0), writes=[onesf])
    maskT = K.sb('maskT', [128, 128], F32)
    maskTs = K.sb('maskTs', [128, 128], F32)
    zer = K.sb('zer', [128, 128], F32)
    K.op('pool', lambda e: e.memset(zer[:], 0.0), writes=[zer])
    K.op('pool', lambda e: e.affine_select(out=maskT[:], in_=zer[:], pattern=[[1, 128]], compare_op=ALU.is_ge, fill=NEGM,
                                           base=0, channel_multiplier=-1), reads=[zer], writes=[maskT])
    K.op('pool', lambda e: e.affine_select(out=maskTs[:], in_=zer[:], pattern=[[1, 128]], compare_op=ALU.is_gt, fill=NEGM,
                                           base=0, channel_multiplier=-1), reads=[zer], writes=[maskTs])
    nwb = K.sb('nwb', [128, 128], F32)
    K.dma('sp', nwb[:], d['even_gdn_norm_w'].partition_broadcast(128), writes=[nwb])
    tokS = K.sb('tokS', [128, NC, 16], F32)
    K.dma('sp', tokS[:].rearrange("p c q -> p (c q)"), d['tokSd'][:, :], writes=[tokS])
    c128 = K.sb('c128', [128, 1], F32)
    K.op('pool', lambda e: e.memset(c128[:], 128.0 * 1e-6), writes=[c128])
    onesb = K.sb('onesb', [128, 128], BF16)
    K.op('pool', lambda e: e.memset(onesb[:], 1.0), writes=[onesb])
    lnt = [K.sb('gln%d' % i, [128, 512], F32) for i in range(2)]
    ngl = [0]

    class H:
        pass
    HB = {}
    f32t = lambda nm, h: K.sb('%s_h%d' % (nm, h), [128, 128], F32)
    bf_t = lambda nm, h: K.sb('%s_h%d' % (nm, h), [128, 128], BF16)
    TB = {}
    for h in heads:
        for par in range(2):
            t = H()
            t.bk = K.ps('gbk%d_%d' % (h, par), [128, 512], F32)
            t.z = K.sb('gz_%d_%d' % (h, par), [128, 128], F32)
            t.nz = K.sb('nz_%d_%d' % (h, par), [128, 128], F32)
            for nm in ('Kbg', 'Kd', 'Vb', 'attnT', 'Xb', 'WT', 'vnew', 'ob'):
                setattr(t, nm, K.sb('%s_%d_%d' % (nm, h, par), [128, 128], BF16))
            for nm in ('dm1', 'dm2', 'Mt', 'U', 'osb', 'ojunk'):
                setattr(t, nm, K.sb('%s_%d_%d' % (nm, h, par), [128, 128], F32))
            t.P = [K.sb('P%d_%d_%d' % (i, h, par), [128, 128], F32) for i in range(2)]
            t.RX = K.sb('RX_%d_%d' % (h, par), [128, 256], F32)
            t.ost = K.sb('ost_%d_%d' % (h, par), [128, 4], F32)
            TB[(h, par)] = t
    for h in heads:
        b = H()
        b.gcrow = K.sb('gcrow%d' % h, [1, 512], F32)
        b.e1row = K.sb('e1row%d' % h, [1, 512], F32)
        b.gcb = [K.sb('gcb%d_%d' % (h, i), [128, 512], F32) for i in range(2)]
        b.sqk = K.sb('sqk%d' % h, [128, 512], BF16)
        b.sqq = K.sb('sqq%d' % h, [128, 512], BF16)
        b.eglrow = K.sb('eglrow%d' % h, [1, NC], F32)
        b.eglb = K.sb('eglb%d' % h, [128, NC], F32)
        b.Sf = f32t('Sf', h)
        b.Sb = bf_t('Sb', h)
        b.q_ = K.sb('gq%d' % h, [128, 512], F32)
        b.k_ = K.sb('gk%d' % h, [128, 512], F32)
        b.v_ = [K.sb('gv%d_%d' % (h, i), [128, 512], F32) for i in range(2)]
        b.knf = [K.sb('knf%d_%d' % (h, i), [128, 512], F32) for i in range(2)]
        b.knb = [K.sb('knb%d_%d' % (h, i), [128, 512], BF16) for i in range(2)]
        b.qnf = K.sb('qnf%d' % h, [128, 512], F32)
        b.qnb = [K.sb('qnb%d_%d' % (h, i), [128, 512], BF16) for i in range(2)]
        b.qgb = [K.sb('qgb%d_%d' % (h, i), [128, 512], BF16) for i in range(2)]
        HB[h] = b
    sl = lambda i: slice(i * 128, (i + 1) * 128)
    import os
    F32R = mybir.dt.float32r
    R = (lambda ap: ap.bitcast(F32R)) if os.environ.get('GDN_F32R', '0') == '1' else (lambda ap: ap)

    for h in heads:
        b = HB[h]
        K.dma('sp', b.eglrow[:], d['egld'][h:h + 1, :], writes=[b.eglrow])
        pg0 = TB[(h, 0)].bk
        K.mm(pg0, pg0[:, 0:NC], onesf, onesf[0:1, :], b.eglrow, b.eglrow[0:1, :])
        K.op('act', lambda e: e.copy(out=b.eglb[:], in_=pg0[:, 0:NC]), reads=[pg0], writes=[b.eglb])
        K.op('pool', lambda e: e.memset(b.Sf[:], 0.0), writes=[b.Sf])
        K.op('pool', lambda e: e.memset(b.Sb[:], 0.0), writes=[b.Sb])

    def prep_a(h, g):
        b = HB[h]
        cs = slice(g * 512, (g + 1) * 512)
        v_ = b.v_[g % 2]
        K.dma('sp', b.q_[:], d['GCT'][h * 128:(h + 1) * 128, cs], writes=[b.q_])
        K.dma('sp', b.k_[:], d['GCT'][512 + h * 128:512 + (h + 1) * 128, cs], writes=[b.k_])
        K.dma('sp', v_[:], d['GCT'][1024 + h * 128:1024 + (h + 1) * 128, cs], writes=[v_])
        K.dma('sp', b.gcrow[:], d['gcd'][h:h + 1, cs], writes=[b.gcrow])
        K.dma('sp', b.e1row[:], d['e1d'][h:h + 1, cs], writes=[b.e1row])
        K.op('pool', lambda e: e.tensor_tensor(out=b.sqk[:], in0=b.k_[:], in1=b.k_[:], op=ALU.mult), reads=[b.k_], writes=[b.sqk])
        K.op('pool', lambda e: e.tensor_tensor(out=b.sqq[:], in0=b.q_[:], in1=b.q_[:], op=ALU.mult), reads=[b.q_], writes=[b.sqq])

    def prep_b(h, g):
        b = HB[h]
        b0, b1 = TB[(h, 0)].bk, TB[(h, 1)].bk
        i = ngl[0] % 2
        ngl[0] += 1
        lk, lq = lnt[0], lnt[1]
        knf, knb, qnb, qgb, gcb = b.knf[g % 2], b.knb[g % 2], b.qnb[g % 2], b.qgb[g % 2], b.gcb[g % 2]
        K.mm(b0, b0[:], onesb, onesb[:], b.sqk, b.sqk[:])
        K.mm(b1, b1[:], onesb, onesb[:], b.sqq, b.sqq[:])
        K.op('act', lambda e: e.activation(out=lk[:], in_=b0[:], func=AF.Ln, bias=K.eps_t[:, 0:1]), reads=[b0, K.eps_t], writes=[lk])
        K.op('act', lambda e: e.activation(out=lq[:], in_=b1[:], func=AF.Ln, scale=128.0, bias=c128[:, 0:1]), reads=[b1, c128], writes=[lq])
        K.mm(b0, b0[:], onesf, onesf[0:1, :], b.gcrow, b.gcrow[0:1, :])
        K.mm(b1, b1[:], onesf, onesf[0:1, :], b.e1row, b.e1row[0:1, :])
        K.op('act', lambda e: e.activation(out=lk[:], in_=lk[:], func=AF.Exp, scale=-0.5), reads=[lk], writes=[lk])
        K.op('act', lambda e: e.activation(out=lq[:], in_=lq[:], func=AF.Exp, scale=-0.5), reads=[lq], writes=[lq])
        K.op('act', lambda e: e.copy(out=gcb[:], in_=b0[:]), reads=[b0], writes=[gcb])
        K.op('dve', lambda e: e.tensor_tensor(out=knf[:], in0=b.k_[:], in1=lk[:], op=ALU.mult), reads=[b.k_, lk], writes=[knf])
        K.op('pool', lambda e: e.tensor_copy(out=knb[:], in_=knf[:]), reads=[knf], writes=[knb])
        K.op('dve', lambda e: e.tensor_tensor(out=b.qnf[:], in0=b.q_[:], in1=lq[:], op=ALU.mult), reads=[b.q_, lq], writes=[b.qnf])
        K.op('pool', lambda e: e.tensor_copy(out=qnb[:], in_=b.qnf[:]), reads=[b.qnf], writes=[qnb])
        K.op('dve', lambda e: e.tensor_tensor(out=qgb[:], in0=b.qnf[:], in1=b1[:], op=ALU.mult), reads=[b.qnf, b1], writes=[qgb])

    class View:
        def __init__(self, t, hb):
            self._t, self._hb = t, hb

        def __getattr__(self, n):
            t = object.__getattribute__(self, '_t')
            if hasattr(t, n):
                return getattr(t, n)
            return getattr(object.__getattribute__(self, '_hb'), n)
    VW = {(h, par): View(TB[(h, par)], HB[h]) for h in heads for par in range(2)}

    def st1(h, g, tt):
        b = VW[(h, tt % 2)]
        c = g * 4 + tt
        ts_, tok = sl(tt), slice(c * 128, (c + 1) * 128)
        bk, v_ = b.bk, b.v_[g % 2]
        z = b.z
        K.dma('sp', z[:], d['GZ'][tok, h * 128:(h + 1) * 128], writes=[z])
        K.op('pool', lambda e: e.tensor_tensor(out=b.nz[:], in0=z[:], in1=nwb[:], op=ALU.mult), reads=[z, nwb], writes=[b.nz])
        beta_c, be_c, e2_c, gc_c = tokS[:, c, h:h + 1], tokS[:, c, 4 + h:5 + h], tokS[:, c, 8 + h:9 + h], tokS[:, c, 12 + h:13 + h]
        knf, knb, qnb, gcb = b.knf[g % 2], b.knb[g % 2], b.qnb[g % 2], b.gcb[g % 2]
        K.tr(bk, bk[:, 0:128], knf, knf[:, ts_], identf, identf[:])
        K.tr(bk, bk[:, 128:256], v_, v_[:, ts_], identf, identf[:])
        K.mm(bk, bk[:, 256:384], knb, knb[:, ts_], knb, knb[:, ts_])
        K.mm(bk, bk[:, 384:512], knb, knb[:, ts_], qnb, qnb[:, ts_])
        K.op('dve', lambda e: e.tensor_scalar(out=b.Kbg[:], in0=bk[:, 0:128], scalar1=be_c, scalar2=None, op0=ALU.mult), reads=[bk, tokS], writes=[b.Kbg])
        K.op('dve', lambda e: e.tensor_scalar(out=b.Kd[:], in0=bk[:, 0:128], scalar1=e2_c, scalar2=None, op0=ALU.mult), reads=[bk, tokS], writes=[b.Kd])
        K.op('dve', lambda e: e.tensor_scalar(out=b.Vb[:], in0=bk[:, 128:256], scalar1=beta_c, scalar2=None, op0=ALU.mult), reads=[bk, tokS], writes=[b.Vb])
        K.op('dve', lambda e: e.scalar_tensor_tensor(out=b.dm1[:], in0=gcb[:, ts_], scalar=gc_c, in1=maskT[:], op0=ALU.subtract, op1=ALU.add),
             reads=[gcb, tokS, maskT], writes=[b.dm1])
        K.op('dve', lambda e: e.scalar_tensor_tensor(out=b.dm2[:], in0=gcb[:, ts_], scalar=gc_c, in1=maskTs[:], op0=ALU.subtract, op1=ALU.add),
             reads=[gcb, tokS, maskTs], writes=[b.dm2])

    def st1b(h, g, tt):
        b = VW[(h, tt % 2)]
        bk = b.bk
        K.op('act', lambda e: e.activation(out=b.dm1[:], in_=b.dm1[:], func=AF.Exp), reads=[b.dm1], writes=[b.dm1])
        K.op('act', lambda e: e.activation(out=b.dm2[:], in_=b.dm2[:], func=AF.Exp), reads=[b.dm2], writes=[b.dm2])
        K.op('dve', lambda e: e.tensor_tensor(out=b.attnT[:], in0=bk[:, 384:512], in1=b.dm1[:], op=ALU.mult), reads=[bk, b.dm1], writes=[b.attnT])
        K.op('dve', lambda e: e.tensor_tensor(out=b.Mt[:], in0=bk[:, 256:384], in1=b.dm2[:], op=ALU.mult), reads=[bk, b.dm2], writes=[b.Mt])

    def st2(h, g, tt):
        b = VW[(h, tt % 2)]
        c = g * 4 + tt
        bk = b.bk
        beta_c = tokS[:, c, h:h + 1]
        K.tr(bk, bk[:, 0:128], b.Mt, b.Mt[:], identf, identf[:])
        K.op('dve', lambda e: e.tensor_scalar(out=b.P[0][:], in0=bk[:, 0:128], scalar1=beta_c, scalar2=-1.0, op0=ALU.mult, op1=ALU.mult),
             reads=[bk, tokS], writes=[b.P[0]])

    def st2b(h, g, tt):
        b = VW[(h, tt % 2)]
        bk = b.bk
        K.tr(bk, bk[:, 128:256], b.P[0], b.P[0][:], identf, identf[:])
        K.op('act', lambda e: e.copy(out=b.RX[:, 0:128], in_=bk[:, 128:256]), reads=[bk], writes=[b.RX])
        K.op('dve', lambda e: e.tensor_tensor(out=b.RX[:, 128:256], in0=bk[:, 128:256], in1=identf[:], op=ALU.add), reads=[bk, identf],
             writes=[b.RX], nowaw=[b.RX])

    def lvl_first(h, tt):
        b = VW[(h, tt % 2)]
        bk = b.bk
        K.mm(bk, bk[:, 256:384], b.RX, b.RX[:, 0:128], b.P[0], b.P[0][:])
        K.mm(bk, bk[:, 384:512], b.P[0], b.P[0][:], b.RX, b.RX[:, 0:128])
        K.op('act', lambda e: e.copy(out=b.P[1][:], in_=bk[:, 256:384]), reads=[bk], writes=[b.P[1]])
        K.op('act', lambda e: e.copy(out=b.RX[:, 0:128], in_=bk[:, 384:512]), reads=[bk], writes=[b.RX], nowaw=[b.RX])

    def lvl_mid(h, n, tt):
        b = VW[(h, tt % 2)]
        bk = b.bk
        Pn, Pn1 = b.P[n % 2], b.P[(n + 1) % 2]
        if n < 5:
            K.mm(bk, bk[:, 0:256], Pn, Pn[:], b.RX, b.RX[:, 0:256])
        else:
            K.mm(bk, bk[:, 128:256], Pn, Pn[:], b.RX, b.RX[:, 128:256])
        K.mm(bk, bk[:, 256:384], b.RX, b.RX[:, 0:128], Pn, Pn[:])
        K.op('dve', lambda e: e.tensor_tensor(out=b.RX[:, 128:256], in0=b.RX[:, 128:256], in1=bk[:, 128:256], op=ALU.add),
             reads=[b.RX, bk], writes=[b.RX], nowaw=[b.RX])
        if n < 5:
            K.op('act', lambda e: e.copy(out=b.RX[:, 0:128], in_=bk[:, 0:128]), reads=[bk], writes=[b.RX], nowaw=[b.RX])
        K.op('act', lambda e: e.copy(out=Pn1[:], in_=bk[:, 256:384]), reads=[bk], writes=[Pn1])

    def lvl_last(h, tt):
        b = VW[(h, tt % 2)]
        bk = b.bk
        K.mm(bk, bk[:, 0:128], b.P[0], b.P[0][:], b.RX, b.RX[:, 128:256])
        K.op('dve', lambda e: e.tensor_tensor(out=b.RX[:, 128:256], in0=b.RX[:, 128:256], in1=bk[:, 0:128], op=ALU.add),
             reads=[b.RX, bk], writes=[b.RX], nowaw=[b.RX])

    def st3(h, g, tt):
        b = VW[(h, tt % 2)]
        bk = b.bk
        K.op('pool', lambda e: e.tensor_copy(out=b.Xb[:], in_=b.RX[:, 128:256]), reads=[b.RX], writes=[b.Xb])
        K.mm(bk, bk[:, 0:128], b.Xb, b.Xb[:], b.Vb, b.Vb[:])
        K.mm(bk, bk[:, 128:256], b.Kbg, b.Kbg[:], b.Xb, b.Xb[:])
        K.op('act', lambda e: e.copy(out=b.U[:], in_=bk[:, 0:128]), reads=[bk], writes=[b.U])
        K.op('act', lambda e: e.copy(out=b.WT[:], in_=bk[:, 128:256]), reads=[bk], writes=[b.WT])

    def st4(h, g, tt):
        b = VW[(h, tt % 2)]
        c = g * 4 + tt
        ts_ = sl(tt)
        bk = b.bk
        K.mm(bk, bk[:, 256:384], b.WT, b.WT[:], b.Sb, b.Sb[:])
        K.op('dve', lambda e: e.tensor_tensor(out=b.vnew[:], in0=b.U[:], in1=bk[:, 256:384], op=ALU.subtract), reads=[b.U, bk], writes=[b.vnew])
        K.mm(bk, bk[:, 384:512], b.qgb[g % 2], b.qgb[g % 2][:, ts_], b.Sb, b.Sb[:], start=True, stop=False)
        K.mm(bk, bk[:, 384:512], b.attnT, b.attnT[:], b.vnew, b.vnew[:], start=False, stop=True)
        K.mm(bk, bk[:, 0:128], b.Kd, b.Kd[:], b.vnew, b.vnew[:])
        K.op('dve', lambda e: e.scalar_tensor_tensor(out=b.Sb[:], in0=b.Sf[:], scalar=b.eglb[:, c:c + 1], in1=bk[:, 0:128],
                                                     op0=ALU.mult, op1=ALU.add), reads=[b.Sf, b.eglb, bk], writes=[b.Sb])
        K.op('dve', lambda e: e.scalar_tensor_tensor(out=b.Sf[:], in0=b.Sf[:], scalar=b.eglb[:, c:c + 1], in1=bk[:, 0:128],
                                                     op0=ALU.mult, op1=ALU.add), reads=[b.Sf, b.eglb, bk], writes=[b.Sf])
        K.op('act', lambda e: e.copy(out=b.osb[:], in_=bk[:, 384:512]), reads=[bk], writes=[b.osb])

    def st5(h, g, tt):
        b = VW[(h, tt % 2)]
        c = g * 4 + tt
        tok = slice(c * 128, (c + 1) * 128)
        K.op('dve', lambda e: e.scalar_tensor_tensor(out=b.ojunk[:], in0=b.osb[:], scalar=1.0 / 128, in1=b.osb[:], op0=ALU.mult, op1=ALU.mult,
                                                     accum_out=b.ost[:, 0:1]), reads=[b.osb], writes=[b.ojunk, b.ost])
        K.op('act', lambda e: e.activation(out=b.ost[:, 1:2], in_=b.ost[:, 0:1], func=AF.Ln, bias=K.eps_t[:, 0:1]), reads=[b.ost, K.eps_t], writes=[b.ost])
        K.op('act', lambda e: e.activation(out=b.ost[:, 2:3], in_=b.ost[:, 1:2], func=AF.Exp, scale=-0.5), reads=[b.ost], writes=[b.ost])
        o_b = b.ob
        K.op('dve', lambda e: e.scalar_tensor_tensor(out=o_b[:], in0=b.osb[:], scalar=b.ost[:, 2:3], in1=b.nz[:], op0=ALU.mult, op1=ALU.mult),
             reads=[b.osb, b.ost, b.nz], writes=[o_b])
        K.dma('pool', d['mixed'][tok, 512 + h * 128:512 + (h + 1) * 128], o_b[:], reads=[o_b])

    for h in heads:
        prep_a(h, 0)
    for h in heads:
        prep_b(h, 0)
    for g in range(NG):
        if g + 1 < NG:
            for h in heads:
                prep_a(h, g + 1)
        for tp in range(0, 4, 2):
            tts = (tp, tp + 1)
            if tp == 2 and g + 1 < NG:
                for h in heads:
                    prep_b(h, g + 1)
            for stage in (st1, st1b, st2, st2b):
                for tt in tts:
                    for h in heads:
                        stage(h, g, tt)
            for tt in tts:
                for h in heads:
                    lvl_first(h, tt)
            for n in range(1, 6):
                for tt in tts:
                    for h in heads:
                        lvl_mid(h, n, tt)
            for tt in tts:
                for h in heads:
                    lvl_last(h, tt)
            for tt in tts:
                for h in heads:
                    st3(h, g, tt)
            for tt in tts:
                for h in heads:
                    st4(h, g, tt)
            for tt in tts:
                for h in heads:
                    st5(h, g, tt)
    K.phase_end()
```

```python
from contextlib import ExitStack
import concourse.bass as bass
import concourse.mybir as mybir

F32 = mybir.dt.float32
BF16 = mybir.dt.bfloat16
I32 = mybir.dt.int32
ALU = mybir.AluOpType
AF = mybir.ActivationFunctionType
AX = mybir.AxisListType
ENG = ['pe', 'act', 'dve', 'pool', 'sp']
NDS = 90


class Buf:
    def __init__(self, t, name):
        self.t = t
        self.name = name
        self.w = {}
        self.r = {}
        self.dsem = None
        self.psum = False
        self.wr = {}

    def __getitem__(self, k):
        return self.t[k]


class KCtx:
    def __init__(self, nc):
        self.nc = nc
        self.e = dict(pe=nc.tensor, act=nc.scalar, dve=nc.vector, pool=nc.gpsimd, sp=nc.sync)
        self.gstack = ExitStack()
        self.csem = {n: self.gstack.enter_context(nc.semaphore('c_' + n)) for n in ENG}
        self.cnt = {n: 0 for n in ENG}
        self.dsems = [self.gstack.enter_context(nc.semaphore('d%d' % i)) for i in range(NDS)]
        self.dcnt = [0] * NDS
        self.free_ds = list(range(NDS))
        self.seen = {n: {} for n in ENG}
        self.pstack = None
        self.phase_bufs = []
        self.nwaits = 0
        self.uid = 0

    def phase_begin(self):
        self.pstack = ExitStack()
        self.phase_bufs = []

    def phase_end(self):
        self.barrier()
        for b in self.phase_bufs:
            if b.dsem is not None:
                self.free_ds.append(b.dsem)
                b.dsem = None
        self.pstack.close()
        self.pstack = None

    def sb(self, name, shape, dt):
        self.uid += 1
        t = self.pstack.enter_context(self.nc.sbuf_tensor('%s_%d' % (name, self.uid), list(shape), dt))
        b = Buf(t, name)
        self.phase_bufs.append(b)
        return b

    def ps(self, name, shape, dt=F32):
        self.uid += 1
        t = self.pstack.enter_context(self.nc.psum_tensor('%s_%d' % (name, self.uid), list(shape), dt))
        b = Buf(t, name)
        b.psum = True
        self.phase_bufs.append(b)
        return b

    def _sem(self, k):
        return self.csem[k] if isinstance(k, str) else self.dsems[k]

    def _wait(self, eng, deps, force_self=False):
        for k, v in deps.items():
            if v <= self.seen[eng].get(k, 0):
                continue
            if k == eng and not force_self:
                if eng == 'pe':
                    continue
                if v < self.cnt[eng] - 1:
                    continue
            self.e[eng].wait_ge(self._sem(k), v)
            self.nwaits += 1
            self.seen[eng][k] = v

    @staticmethod
    def _merge(d, s):
        for k, v in s.items():
            if v > d.get(k, 0):
                d[k] = v

    def _deps(self, reads, writes, nowaw, eng=None):
        deps = {}
        for b in reads:
            self._merge(deps, b.w)
            if b.psum:
                self._merge(deps, {k: v for k, v in b.r.items() if k != eng})
        for b in writes:
            if b not in nowaw:
                self._merge(deps, b.w)
            else:
                self._merge(deps, b.wr)
            self._merge(deps, b.r)
        return deps

    def op(self, eng, fn, reads=(), writes=(), nowaw=()):
        self._wait(eng, self._deps(reads, writes, nowaw, eng))
        ins = fn(self.e[eng])
        self.cnt[eng] += 1
        c = self.cnt[eng]
        ins.then_inc(self.csem[eng], 1)
        for b in reads:
            if c > b.r.get(eng, 0):
                b.r[eng] = c
        for b in writes:
            if b in nowaw:
                b.w[eng] = c
            else:
                b.w = {eng: c}
                b.wr = {eng: c}
                b.r = {}
        return ins

    def dma(self, q, out, in_, reads=(), writes=(), nowaw=(), **kw):
        self._wait(q, self._deps(reads, writes, nowaw))
        b0 = (list(writes) + list(reads))[0]
        if b0.dsem is None:
            assert self.free_ds, "out of DMA semaphores"
            b0.dsem = self.free_ds.pop(0)
        i = b0.dsem
        self.e[q].dma_start(out=out, in_=in_, **kw).then_inc(self.dsems[i], 16)
        self.dcnt[i] += 16
        v = self.dcnt[i]
        for b in reads:
            b.r[i] = v
        for b in writes:
            if b in nowaw:
                b.w[i] = v
            else:
                b.w = {i: v}
                b.wr = {i: v}
                b.r = {}

    def barrier(self):
        deps = {n: self.cnt[n] for n in ENG if self.cnt[n] > 0}
        for i in range(NDS):
            if self.dcnt[i] > 0:
                deps[i] = self.dcnt[i]
        for eng in ENG:
            self._wait(eng, deps, force_self=True)

    def finish(self):
        self.barrier()
        self.gstack.close()

    def mm(self, out_b, out_ap, lhsT_b, lhsT_ap, rhs_b, rhs_ap, start=True, stop=True, extra_reads=()):
        return self.op('pe', lambda e: e.matmul(out_ap, lhsT_ap, rhs_ap, start=start, stop=stop),
                       reads=[lhsT_b, rhs_b] + list(extra_reads), writes=[out_b])

    def tr(self, out_b, out_ap, in_b, in_ap, id_b, id_ap):
        return self.op('pe', lambda e: e.transpose(out_ap, in_ap, id_ap), reads=[in_b, id_b], writes=[out_b])

import numpy as np
from concourse.bass_utils import run_bass_kernel_spmd

D = 1024
EPS = 1e-6


def make_ident(K, dt, name):
    ones = K.sb(name + '_ones', [128, 128], dt)
    ident = K.sb(name, [128, 128], dt)
    K.op('pool', lambda e: e.memset(ones[:], 1.0), writes=[ones])
    K.op('pool', lambda e: e.affine_select(out=ident[:], in_=ones[:], pattern=[[-1, 128]],
                                           compare_op=ALU.is_equal, fill=0.0, base=0, channel_multiplier=1),
         reads=[ones], writes=[ident])
    return ident


def load_weight_bf16(K, Wb, src, kchunks, ncols, gain=None, stage_name='wst', q='sp', cb=1024):
    cb = min(cb, ncols)
    st = [K.sb(stage_name + str(i), [128, cb], F32) for i in range(4)]
    n = 0
    for k in range(kchunks):
        for c0 in range(0, ncols, cb):
            c1 = min(ncols, c0 + cb)
            s = st[n % 4]
            eng = 'dve' if n % 2 == 0 else 'act'
            n += 1
            K.dma(q, s[:, 0:c1 - c0], src[k * 128:(k + 1) * 128, c0:c1], writes=[s])
            if eng == 'act':
                if gain is not None:
                    K.op('act', lambda e, s=s, k=k, c0=c0, c1=c1: e.activation(out=Wb[:, k, c0:c1], in_=s[:, 0:c1 - c0], func=AF.Copy,
                                                                               scale=gain[:, k:k + 1]),
                         reads=[s, gain], writes=[Wb], nowaw=[Wb])
                else:
                    K.op('act', lambda e, s=s, k=k, c0=c0, c1=c1: e.copy(out=Wb[:, k, c0:c1], in_=s[:, 0:c1 - c0]),
                         reads=[s], writes=[Wb], nowaw=[Wb])
            elif gain is not None:
                K.op(eng, lambda e, s=s, k=k, c0=c0, c1=c1: e.tensor_scalar(out=Wb[:, k, c0:c1], in0=s[:, 0:c1 - c0], scalar1=gain[:, k:k + 1],
                                                                scalar2=None, op0=ALU.mult),
                     reads=[s, gain], writes=[Wb], nowaw=[Wb])
            else:
                K.op(eng, lambda e, s=s, k=k, c0=c0, c1=c1: e.tensor_copy(out=Wb[:, k, c0:c1], in_=s[:, 0:c1 - c0]),
                     reads=[s], writes=[Wb], nowaw=[Wb])


class NormT:
    def __init__(self, K, identb, nbuf_x=3, gt=4, nbuf_n=2):
        self.K = K
        self.identb = identb
        self.nn = nbuf_n
        self.xb = [K.sb('xb%d' % i, [128, D], F32) for i in range(nbuf_x)]
        self.xn = [K.sb('xn%d' % i, [128, D], BF16) for i in range(nbuf_n)]
        self.st = [K.sb('nst%d' % i, [128, 4], F32) for i in range(nbuf_n)]
        self.pT = [K.ps('pT%d' % i, [128, D], BF16) for i in range(2)]
        self.hnT = [K.sb('hnT%d' % i, [128, 8, gt * 128], BF16) for i in range(2)]
        self.n = 0

    def prep(self, src_rows):
        K = self.K
        n = self.n
        self.n += 1
        xb = self.xb[n % len(self.xb)]
        xn = self.xn[n % self.nn]
        st = self.st[n % self.nn]
        K.dma('sp', xb[:], src_rows, writes=[xb])
        K.op('act', lambda e: e.activation(out=xn[:], in_=xb[:], func=AF.Square, accum_out=st[:, 0:1]),
             reads=[xb], writes=[xn, st])
        K.op('act', lambda e: e.activation(out=st[:, 1:2], in_=st[:, 0:1], func=AF.Sqrt, scale=1.0 / D, bias=K.eps_t[:, 0:1]),
             reads=[st, K.eps_t], writes=[st])
        K.op('dve', lambda e: e.reciprocal(out=st[:, 2:3], in_=st[:, 1:2]), reads=[st], writes=[st])
        K.op('dve', lambda e: e.tensor_scalar(out=xn[:], in0=xb[:], scalar1=st[:, 2:3], scalar2=None, op0=ALU.mult),
             reads=[xb, st], writes=[xn])
        return (n, xb)

    def finish(self, tok, hnT, j):
        K = self.K
        n, xb = tok
        xn = self.xn[n % self.nn]
        pT = self.pT[n % 2]
        for k in range(8):
            K.tr(pT, pT[:, k * 128:(k + 1) * 128], xn, xn[:, k * 128:(k + 1) * 128], self.identb, self.identb[:])
        K.op('act', lambda e: e.copy(out=hnT[:, :, j * 128:(j + 1) * 128],
                                     in_=pT[:].rearrange("p (k t) -> p k t", k=8)),
             reads=[pT], writes=[hnT], nowaw=[hnT])
        return xb

    def tile(self, src_rows, hnT, j, keep_x=None):
        return self.finish(self.prep(src_rows), hnT, j)


def consts(K):
    K.eps_t = K.sb('eps', [128, 1], F32)
    K.op('pool', lambda e: e.memset(K.eps_t[:], EPS), writes=[K.eps_t])
    K.one_t = K.sb('one', [128, 1], F32)
    K.op('pool', lambda e: e.memset(K.one_t[:], 1.0), writes=[K.one_t])
    K.mhalf_t = K.sb('mhalf', [128, 1], F32)
    K.op('pool', lambda e: e.memset(K.mhalf_t[:], -0.5), writes=[K.mhalf_t])


E_FQ, E_FK, E_FV, E_FF, E_GQ, E_GK, E_GV, E_GZ, E_GB, E_GA = 0, 512, 1024, 1536, 1544, 2056, 2568, 3080, 3592, 3596


def phase_A0(K, d, S):
    NG = S // 512
    K.phase_begin()
    consts(K)
    identb = make_ident(K, BF16, 'identb')
    gain = K.sb('gain', [128, 8], F32)
    K.dma('sp', gain[:], d['even_norm_mix'].rearrange("(k p) -> p k", p=128), writes=[gain],
          allow_slow_non_contiguous=True)
    cw = K.sb('cw', [128, 4, 12], F32)
    for j in range(4):
        K.dma('sp', cw[:, j, :], d['even_gdn_conv_w'][j, :].rearrange("(c p) -> p c", p=128), writes=[cw], nowaw=[cw],
              allow_slow_non_contiguous=True)
    W = K.sb('W', [128, 8, 3600], BF16)
    load_weight_bf16(K, W, d['even_w_in'], 8, 3600, gain=gain)
    NT = NormT(K, identb, nbuf_x=8, nbuf_n=8)
    xpad = [K.sb('xpad%d' % c, [128, 515], F32) for c in range(12)]
    for c in range(12):
        K.op('pool', lambda e, c=c: e.memset(xpad[c][:, 0:3], 0.0), writes=[xpad[c]])
    acc = [K.sb('acc%d' % i, [128, 512], F32) for i in range(2)]
    cout = [K.sb('cout%d' % i, [128, 512], F32) for i in range(3)]
    qk = [K.sb('qk%d' % i, [128, 512], BF16) for i in range(3)]
    vt = [K.sb('vt%d' % i, [128, 512], BF16) for i in range(2)]
    zt = [K.sb('zt%d' % i, [128, 512], F32) for i in range(2)]
    sm = [K.sb('sm%d' % i, [8, 512], F32) for i in range(2)]
    sm2 = [K.sb('smb%d' % i, [8, 512], F32) for i in range(2)]
    ptm = [K.ps('ptm%d' % i, [128, 512], F32) for i in range(2)]
    pfm = [K.ps('pfm%d' % i, [128, 512], F32) for i in range(3)]
    nfm = 0
    ntm = 0
    pend_silu = [None]
    toks = [NT.prep(d['x'][u * 128:(u + 1) * 128, :]) for u in range(4)]
    for g in range(NG):
        hnT = NT.hnT[g % 2]
        for j in range(4):
            t = g * 4 + j
            NT.finish(toks.pop(0), hnT, j)
            if j == 0:
                for u in range(t + 4, min(t + 8, NG * 4)):
                    toks.append(NT.prep(d['x'][u * 128:(u + 1) * 128, :]))
            for which, col0 in ((0, E_FV), (1, E_GZ)):
                p = ptm[ntm % 2]
                ntm += 1
                for k in range(8):
                    K.mm(p, p[:], hnT, hnT[:, k, j * 128:(j + 1) * 128], W, W[:, k, col0:col0 + 512],
                         start=(k == 0), stop=(k == 7))
                if which == 0:
                    o = vt[t % 2]
                    K.op('dve', lambda e, o=o, p=p: e.tensor_copy(out=o[:], in_=p[:]), reads=[p], writes=[o])
                    K.dma('pool', d['Vf'][t * 128:(t + 1) * 128, :], o[:], reads=[o])
                else:
                    o = zt[t % 2]
                    K.op('act', lambda e, o=o, p=p: e.activation(out=o[:], in_=p[:], func=AF.Silu), reads=[p], writes=[o])
                    K.dma('pool', d['GZ'][t * 128:(t + 1) * 128, :], o[:], reads=[o])
        chunks = ([('q', c, E_FQ + c * 128, 128) for c in range(4)] + [('k', c, E_FK + c * 128, 128) for c in range(4)]
                  + [('g', c, E_GQ + c * 128, 128) for c in range(12)] + [('s', 0, None, 16)])
        for kind, c, col0, M in chunks:
            p = pfm[nfm % 3]
            nfm += 1
            if kind == 's':
                for k in range(8):
                    K.mm(p, p[0:8, :], W, W[:, k, E_FF:E_FF + 8], hnT, hnT[:, k, :], start=(k == 0), stop=(k == 7))
                o = sm[g % 2]
                K.op('act', lambda e, o=o, p=p: e.copy(out=o[0:8, :], in_=p[0:8, :]), reads=[p], writes=[o])
                p2 = pfm[nfm % 3]
                nfm += 1
                for k in range(8):
                    K.mm(p2, p2[0:8, :], W, W[:, k, E_GB:E_GB + 8], hnT, hnT[:, k, :], start=(k == 0), stop=(k == 7))
                o2 = sm2[g % 2]
                K.op('act', lambda e, o2=o2, p2=p2: e.copy(out=o2[:], in_=p2[0:8, :]), reads=[p2], writes=[o2])
                K.dma('pool', d['smallT'][0:8, g * 512:(g + 1) * 512], o[0:8, :], reads=[o])
                K.dma('pool', d['smallT'][8:16, g * 512:(g + 1) * 512], o2[:], reads=[o2])
                continue
            for k in range(8):
                K.mm(p, p[:], W, W[:, k, col0:col0 + 128], hnT, hnT[:, k, :], start=(k == 0), stop=(k == 7))
            if kind in ('q', 'k'):
                o = qk[nfm % 3]
                K.op('dve', lambda e, o=o, p=p: e.tensor_copy(out=o[:], in_=p[:]), reads=[p], writes=[o])
                dst = d['QfT'] if kind == 'q' else d['KfT']
                K.dma('pool', dst[c * 128:(c + 1) * 128, g * 512:(g + 1) * 512], o[:], reads=[o])
            else:
                xp = xpad[c]
                K.op('act', lambda e, xp=xp, p=p: e.copy(out=xp[:, 3:515], in_=p[:]), reads=[p], writes=[xp], nowaw=[xp])
                a = acc[c % 2]
                K.op('dve', lambda e, a=a, xp=xp, c=c: e.tensor_scalar(out=a[:], in0=xp[:, 0:512], scalar1=cw[:, 0, c:c + 1],
                                                                        scalar2=None, op0=ALU.mult),
                     reads=[xp, cw], writes=[a])
                for j in range(1, 4):
                    K.op('dve', lambda e, a=a, xp=xp, c=c, j=j: e.scalar_tensor_tensor(
                        out=a[:], in0=xp[:, j:j + 512], scalar=cw[:, j, c:c + 1], in1=a[:], op0=ALU.mult, op1=ALU.add),
                        reads=[xp, cw, a], writes=[a])
                K.op('pool', lambda e, xp=xp: e.tensor_copy(out=xp[:, 0:3], in_=xp[:, 512:515]), reads=[xp], writes=[xp])
                o = cout[nfm % 3]

                def silu_store(o=o, a=a, c=c, g=g):
                    K.op('act', lambda e: e.activation(out=o[:], in_=a[:], func=AF.Silu), reads=[a], writes=[o])
                    K.dma('pool', d['GCT'][c * 128:(c + 1) * 128, g * 512:(g + 1) * 512], o[:], reads=[o])
                if pend_silu[0] is not None:
                    pend_silu[0]()
                pend_silu[0] = silu_store
    if pend_silu[0] is not None:
        pend_silu[0]()
    K.phase_end()


INPUT_SHAPES = {
    "even_norm_mix": [1024], "even_w_in": [1024, 3600], "even_fox_bf": [8], "even_gdn_conv_w": [4, 1536],
    "even_gdn_a_log": [4], "even_gdn_dt_bias": [4], "even_gdn_norm_w": [128], "even_w_out": [1024, 1024],
    "odd_norm_mix": [1024], "odd_w_in": [1024, 2328], "odd_cmp_k_pe": [32, 64], "odd_cmp_k_w1": [2048, 128],
    "odd_cmp_k_w2": [128, 64], "odd_cmp_v_pe": [32, 64], "odd_cmp_v_w1": [2048, 128], "odd_cmp_v_w2": [128, 64],
    "odd_rg_conv_w": [4, 512], "odd_rg_conv_b": [512], "odd_rg_wa": [8, 64, 64], "odd_rg_ba": [512],
    "odd_rg_wx": [8, 64, 64], "odd_rg_bx": [512], "odd_rg_lambda": [512], "odd_w_out": [1024, 1024],
    "mlp_norm": [2, 1024], "mlp_w_up": [2, 1024, 4096], "mlp_w_down": [2, 4096, 1024], "ple_norm": [2, 1024],
    "ple_w_gate": [2, 1024, 1024], "ple_w_proj": [2, 256, 1024], "final_norm": [1024],
}


def scratch_specs(S):
    return {
        "h": ([S, 1024], F32),
        "QfT": ([512, S], BF16), "KfT": ([512, S], BF16), "Vf": ([S, 512], BF16),
        "GZ": ([S, 512], F32), "GCT": ([1536, S], F32), "smallT": ([16, S], F32),
        "gcd": ([4, S], F32), "ngcd": ([4, S], F32), "e1d": ([4, S], F32), "egld": ([4, S // 128], F32),
        "tokSd": ([128, (S // 128) * 16], F32),
        "csd": ([128, (S // 128) * 16], F32), "QT1": ([512, S], BF16), "KcT": ([128, S], BF16), "VcT": ([128, S], BF16),
        "KsT": ([128, S], BF16), "KwT": ([128, S], BF16), "Vs": ([S, 128], BF16), "Vw": ([S, 128], BF16),
        "gates": ([S, 24], F32), "RGX": ([1024, S], F32),
        "KcmpT": ([128, S // 16], BF16), "Vcmp": ([S // 16, 128], BF16),
        "nsa0": ([S, 512], F32), "nsa1": ([S, 512], F32), "nsa2": ([S, 512], F32), "selT": ([2, 128, S], BF16),
        "FQ": ([8, 3, S], BF16), "nF": ([128, (S // 128) * 8], F32), "mixed": ([S, 1024], BF16),
    }


def build(S, phases, dbg=(), h_input=False):
    nc = bass.Bass("TRN2", target_bir_lowering=False)
    d = {}
    d['x'] = nc.dram_tensor("x", [S, 1024], F32, kind="ExternalInput").ap()
    d['p'] = nc.dram_tensor("p", [2, S, 256], F32, kind="ExternalInput").ap()
    d['positions'] = nc.dram_tensor("positions", [1, S], I32, kind="ExternalInput").ap()
    for n, shp in INPUT_SHAPES.items():
        d[n] = nc.dram_tensor(n, shp, F32, kind="ExternalInput").ap()
    d['out'] = nc.dram_tensor("out", [S, 1024], F32, kind="ExternalOutput").ap()
    for n, (shp, dt) in scratch_specs(S).items():
        kind = "ExternalOutput" if n in dbg else "Internal"
        if n == 'h' and h_input:
            kind = "ExternalInput"
        d[n] = nc.dram_tensor(n, shp, dt, kind=kind).ap()
    K = KCtx(nc)
    import os
    K.stop = int(os.environ.get('KSTOP', '0')) or None
    for ph in phases:
        ph(K, d, S)
    K.finish()
    return nc, K


def phase_Fprep(K, d, S):
    NT_ = S // 128
    K.phase_begin()
    consts(K)
    identf = make_ident(K, F32, 'identf')
    ff = K.sb('ff', [8, S], F32)
    sp = K.sb('spl', [8, S], F32)
    ones = K.sb('ones8', [8, S], F32)
    nF = K.sb('negF', [8, S], F32)
    bf = K.sb('bf', [8, 2], F32)
    K.dma('sp', ff[:], d['smallT'][0:8, :], writes=[ff])
    K.dma('sp', bf[:, 0:1], d['even_fox_bf'].rearrange("(h o) -> h o", o=1), writes=[bf])
    K.op('dve', lambda e: e.tensor_scalar(out=bf[:, 1:2], in0=bf[:, 0:1], scalar1=-1.0, scalar2=None, op0=ALU.mult),
         reads=[bf], writes=[bf])
    K.op('pool', lambda e: e.memset(ones[:], 1.0), writes=[ones])
    K.op('act', lambda e: e.activation(out=sp[:], in_=ff[:], func=AF.Exp, scale=-1.0, bias=bf[:, 1:2]),
         reads=[ff, bf], writes=[sp])
    K.op('act', lambda e: e.activation(out=sp[:], in_=sp[:], func=AF.Ln, scale=1.0, bias=K.one_t[0:8, 0:1]),
         reads=[sp, K.one_t], writes=[sp])
    K.op('dve', lambda e: e.tensor_tensor_scan(out=nF[:], data0=ones[:], data1=sp[:], initial=0.0,
                                               op0=ALU.mult, op1=ALU.add), reads=[ones, sp], writes=[nF])
    q8 = ff
    K.op('dve', lambda e: e.tensor_scalar(out=q8[:], in0=nF[:], scalar1=-8.0, scalar2=None, op0=ALU.mult),
         reads=[nF], writes=[q8])
    parts = [K.sb('fq%d' % i, [8, S], BF16) for i in range(3)]
    for i in range(3):
        K.op('dve', lambda e, i=i: e.tensor_copy(out=parts[i][:], in_=q8[:]), reads=[q8], writes=[parts[i]])
        if i < 2:
            K.op('dve', lambda e, i=i: e.tensor_tensor(out=q8[:], in0=q8[:], in1=parts[i][:], op=ALU.subtract),
                 reads=[q8, parts[i]], writes=[q8])
        K.dma('sp', d['FQ'][:, i, :], parts[i][:], reads=[parts[i]])
    pt = K.ps('ptr', [128, 512], F32)
    nft = K.sb('nft', [128, NT_ * 8], F32)
    for t0 in range(0, NT_, 64):
        n = min(64, NT_ - t0)
        for t in range(n):
            K.tr(pt, pt[:, t * 8:(t + 1) * 8], nF, nF[0:8, (t0 + t) * 128:(t0 + t + 1) * 128], identf, identf[0:8, 0:8])
        K.op('act', lambda e, t0=t0, n=n: e.copy(out=nft[:, t0 * 8:(t0 + n) * 8], in_=pt[:, 0:n * 8]), reads=[pt], writes=[nft])
    K.dma('sp', d['nF'][:, :], nft[:], reads=[nft])
    K.phase_end()


def phase_fox(K, d, S, heads=range(8), L=3):
    NT_ = S // 128
    NQ = S // 512
    K.phase_begin()
    consts(K)
    identf = make_ident(K, F32, 'identf')
    nF = K.sb('nF', [128, NT_ * 8], F32)
    K.dma('sp', nF[:], d['nF'][:, :], writes=[nF])
    KT = [K.sb('KT%d' % i, [67, S], BF16) for i in range(2)]
    QT = [K.sb('QT%d' % i, [67, S], BF16) for i in range(2)]
    V = [K.sb('V%d' % i, [128, NT_, 65], BF16) for i in range(2)]
    for i in range(2):
        K.op('pool', lambda e, i=i: e.memset(KT[i][64:67, :], 1.0), writes=[KT[i]])
        K.op('pool', lambda e, i=i: e.memset(V[i][:, :, 64:65], 1.0), writes=[V[i]])
    ps = [K.ps('ps%d' % i, [128, 512], F32) for i in range(3)]
    po = [K.ps('po%d' % i, [128, 512], F32) for i in range(2)]
    pq = [K.ps('pq%d' % i, [128, 4, 65], F32) for i in range(2)]
    pt = [K.sb('pt%d' % i, [128, 512], BF16) for i in range(4)]
    osb = [K.sb('osb%d' % i, [65, 512], F32) for i in range(2)]
    rc = [K.sb('rc%d' % i, [128, 4], F32) for i in range(2)]
    mo = [K.sb('mo%d' % i, [128, 4, 64], BF16) for i in range(2)]

    def load_head(hi, h):
        s = hi % 2
        K.dma('sp', KT[s][0:64, :], d['KfT'][h * 64:(h + 1) * 64, :], writes=[KT[s]])
        K.dma('sp', QT[s][0:64, :], d['QfT'][h * 64:(h + 1) * 64, :], writes=[QT[s]])
        K.dma('sp', QT[s][64:67, :], d['FQ'][h, :, :], writes=[QT[s]], nowaw=[QT[s]])
        K.dma('sp', V[s][:, :, 0:64], d['Vf'][:, h * 64:(h + 1) * 64].rearrange("(t p) c -> p t c", p=128),
              writes=[V[s]])

    heads = list(heads)
    units = []
    for hi, h in enumerate(heads):
        for qg in range(NQ):
            nk = 4 * qg + 4
            for kt in range(nk):
                units.append((hi, h, qg, kt, kt == nk - 1))
    load_head(0, heads[0])
    NU = len(units)
    for i in range(NU + L):
        if i < NU:
            hi, h, qg, kt, last = units[i]
            s = hi % 2
            r = kt - 4 * qg
            c0 = 128 * max(r, 0)
            p_s = ps[i % 3]
            p_t = pt[i % 4]
            K.mm(p_s, p_s[:, c0:512], KT[s], KT[s][0:67, kt * 128:(kt + 1) * 128], QT[s],
                 QT[s][0:67, qg * 512 + c0:(qg + 1) * 512])
            K.op('act', lambda e, p_s=p_s, p_t=p_t, c0=c0, kt=kt, h=h: e.activation(
                out=p_t[:, c0:512], in_=p_s[:, c0:512], func=AF.Exp, scale=0.125, bias=nF[:, kt * 8 + h:kt * 8 + h + 1]),
                reads=[p_s, nF], writes=[p_t])
            if r >= 0:
                K.op('pool', lambda e, p_t=p_t, c0=c0: e.affine_select(
                    out=p_t[:, c0:c0 + 128], in_=p_t[:, c0:c0 + 128], pattern=[[1, 128]], compare_op=ALU.is_ge,
                    fill=0.0, base=0, channel_multiplier=-1), reads=[p_t], writes=[p_t])
        if i - L >= 0:
            hi, h, qg, kt, last = units[i - L]
            if qg == 0 and kt == 0 and hi + 1 < len(heads):
                load_head(hi + 1, heads[hi + 1])
            s = hi % 2
            r = kt - 4 * qg
            c0 = 128 * max(r, 0)
            p_t = pt[(i - L) % 4]
            gi = hi * NQ + qg
            p_o = po[gi % 2]
            K.mm(p_o, p_o[0:65, c0:512], V[s], V[s][:, kt, 0:65], p_t, p_t[:, c0:512], start=(kt == 0), stop=last)
            if last:
                o_s = osb[gi % 2]
                p_q = pq[gi % 2]
                K.op('act', lambda e, o_s=o_s, p_o=p_o: e.copy(out=o_s[:], in_=p_o[0:65, :]), reads=[p_o], writes=[o_s])
                for j in range(4):
                    K.tr(p_q, p_q[:, j, :], o_s, o_s[0:65, j * 128:(j + 1) * 128], identf, identf[0:65, 0:65])
                r_c = rc[gi % 2]
                m_o = mo[gi % 2]
                K.op('dve', lambda e, r_c=r_c, p_q=p_q: e.reciprocal(out=r_c[:], in_=p_q[:, :, 64]), reads=[p_q], writes=[r_c])
                for j in range(4):
                    K.op('dve', lambda e, j=j, r_c=r_c, p_q=p_q, m_o=m_o: e.tensor_scalar(
                        out=m_o[:, j, :], in0=p_q[:, j, 0:64], scalar1=r_c[:, j:j + 1], scalar2=None, op0=ALU.mult),
                        reads=[p_q, r_c], writes=[m_o])
                K.dma('pool', d['mixed'][qg * 512:(qg + 1) * 512, h * 64:(h + 1) * 64].rearrange("(j p) c -> p j c", p=128),
                      m_o[:], reads=[m_o])
    K.phase_end()


NEGM = -30000.0


def phase_Gprep(K, d, S):
    NC = S // 128
    K.phase_begin()
    consts(K)
    identf = make_ident(K, F32, 'identf')
    A = K.sb('gpA', [4, S], F32)
    B = K.sb('gpB', [4, S], F32)
    C = K.sb('gpC', [4, S], F32)
    Dt = K.sb('gpD', [4, S], F32)
    E = K.sb('gpE', [4, S], F32)
    K.dma('sp', A[:], d['smallT'][8:12, :], writes=[A])
    K.dma('sp', B[:], d['smallT'][12:16, :], writes=[B])
    pr = K.sb('pr', [4, 4], F32)
    K.dma('sp', pr[:, 0:1], d['even_gdn_a_log'].rearrange("(h o) -> h o", o=1), writes=[pr])
    K.dma('sp', pr[:, 1:2], d['even_gdn_dt_bias'].rearrange("(h o) -> h o", o=1), writes=[pr], nowaw=[pr])
    K.op('act', lambda e: e.activation(out=pr[:, 2:3], in_=pr[:, 0:1], func=AF.Exp), reads=[pr], writes=[pr])
    K.op('pool', lambda e: e.memset(C[:], 1.0), writes=[C])
    K.op('pool', lambda e: e.memset(C[:].rearrange("p (c t) -> p c t", t=128)[:, :, 0:1], 0.0), writes=[C])
    K.op('act', lambda e: e.activation(out=Dt[:], in_=B[:], func=AF.Exp, bias=pr[:, 1:2]), reads=[B, pr], writes=[Dt])
    K.op('act', lambda e: e.activation(out=Dt[:], in_=Dt[:], func=AF.Ln, bias=K.one_t[0:4, 0:1]), reads=[Dt, K.one_t], writes=[Dt])
    K.op('dve', lambda e: e.tensor_scalar(out=B[:], in0=Dt[:], scalar1=pr[:, 2:3], scalar2=-1.0, op0=ALU.mult, op1=ALU.mult),
         reads=[Dt, pr], writes=[B])
    gc = E
    K.op('dve', lambda e: e.tensor_tensor_scan(out=gc[:], data0=C[:], data1=B[:], initial=0.0, op0=ALU.mult, op1=ALU.add),
         reads=[C, B], writes=[gc])
    K.dma('sp', d['gcd'][:, :], gc[:], reads=[gc])
    K.op('dve', lambda e: e.tensor_scalar(out=Dt[:], in0=gc[:], scalar1=-1.0, scalar2=None, op0=ALU.mult), reads=[gc], writes=[Dt])
    K.dma('sp', d['ngcd'][:, :], Dt[:], reads=[Dt])
    beta = A
    K.op('act', lambda e: e.activation(out=beta[:], in_=A[:], func=AF.Sigmoid), reads=[A], writes=[beta])
    e1 = B
    K.op('act', lambda e: e.activation(out=e1[:], in_=gc[:], func=AF.Exp), reads=[gc], writes=[e1])
    K.dma('sp', d['e1d'][:, :], e1[:], reads=[e1])
    be = C
    K.op('dve', lambda e: e.tensor_tensor(out=be[:], in0=beta[:], in1=e1[:], op=ALU.mult), reads=[beta, e1], writes=[be])
    gc3 = gc[:].rearrange("p (c t) -> p c t", t=128)
    e2 = Dt
    K.op('dve', lambda e: e.tensor_tensor(out=e2[:].rearrange("p (c t) -> p c t", t=128),
                                          in0=gc3[:, :, 127:128].to_broadcast([4, NC, 128]), in1=gc3, op=ALU.subtract),
         reads=[gc], writes=[e2])
    K.op('act', lambda e: e.activation(out=e2[:], in_=e2[:], func=AF.Exp), reads=[e2], writes=[e2])
    egl = K.sb('egl', [4, NC], F32)
    K.op('act', lambda e: e.activation(out=egl[:], in_=gc3[:, :, 127], func=AF.Exp), reads=[gc], writes=[egl])
    K.dma('sp', d['egld'][:, :], egl[:], reads=[egl])
    pt = [K.ps('ptg%d' % i, [128, 32, 16], F32) for i in range(2)]
    tokS = K.sb('tokS', [128, NC, 16], F32)
    K.op('pool', lambda e: e.memset(tokS[:], 0.0), writes=[tokS])
    for t0 in range(0, NC, 32):
        n = min(32, NC - t0)
        p = pt[(t0 // 32) % 2]
        for t in range(n):
            for qi, src in enumerate((beta, be, e2, gc)):
                K.tr(p, p[:, t, qi * 4:(qi + 1) * 4], src, src[0:4, (t0 + t) * 128:(t0 + t + 1) * 128], identf, identf[0:4, 0:4])
        K.op('act', lambda e, p=p, t0=t0, n=n: e.copy(out=tokS[:, t0:t0 + n, 0:16], in_=p[:, 0:n, 0:16]), reads=[p], writes=[tokS],
             nowaw=[tokS])
    K.dma('sp', d['tokSd'][:, :], tokS[:].rearrange("p c q -> p (c q)"), reads=[tokS])
    K.phase_end()


class StopPhase(Exception):
    pass


def chk(K, n):
    if getattr(K, 'stop', None) == n:
        raise StopPhase()


def phase_gdn(K, d, S, heads=range(4)):
    try:
        _phase_gdn(K, d, S, heads)
    except StopPhase:
        pass
    K.phase_end()


def _phase_gdn(K, d, S, heads=range(4)):
    NC = S // 128
    NG = S // 512
    K.phase_begin()
    consts(K)
    identf = make_ident(K, F32, 'identf')
    onesf = K.sb('onesf', [128, 128], F32)
    K.op('pool', lambda e: e.memset(onesf[:], 1.0), writes=[onesf])
    maskT = K.sb('maskT', [128, 128], F32)
    maskTs = K.sb('maskTs', [128, 128], F32)
    zer = K.sb('zer', [128, 128], F32)
    K.op('pool', lambda e: e.memset(zer[:], 0.0), writes=[zer])
    K.op('pool', lambda e: e.affine_select(out=maskT[:], in_=zer[:], pattern=[[1, 128]], compare_op=ALU.is_ge, fill=NEGM,
                                           base=0, channel_multiplier=-1), reads=[zer], writes=[maskT])
    K.op('pool', lambda e: e.affine_select(out=maskTs[:], in_=zer[:], pattern=[[1, 128]], compare_op=ALU.is_gt, fill=NEGM,
                                           base=0, channel_multiplier=-1), reads=[zer], writes=[maskTs])
    nwb = K.sb('nwb', [128, 128], F32)
    K.dma('sp', nwb[:], d['even_gdn_norm_w'].partition_broadcast(128), writes=[nwb])
    tokS = K.sb('tokS', [128, NC, 16], F32)
    K.dma('sp', tokS[:].rearrange("p c q -> p (c q)"), d['tokSd'][:, :], writes=[tokS])
    c128 = K.sb('c128', [128, 1], F32)
    K.op('pool', lambda e: e.memset(c128[:], 128.0 * 1e-6), writes=[c128])
    G2L = K.sb('G2L', [2, S], F32)
    G2R = K.sb('G2R', [2, S], F32)
    e1row = K.sb('e1row', [1, S], F32)
    eglrow = K.sb('eglrow', [1, NC], F32)
    eglb = K.sb('eglb', [128, NC], F32)
    Sst = K.sb('Sst', [128, 128], F32)
    qT = [K.sb('gqT%d' % i, [128, 512], F32) for i in range(2)]
    kT = [K.sb('gkT%d' % i, [128, 512], F32) for i in range(2)]
    vT = [K.sb('gvT%d' % i, [128, 512], F32) for i in range(2)]
    sq = K.sb('gsq', [128, 512], F32)
    rr = K.sb('grr', [128, 512], F32)
    knT = K.sb('knT', [128, 512], F32)
    qnT = K.sb('qnT', [128, 512], F32)
    qgT = K.sb('qgT', [128, 512], F32)
    zt = [K.sb('gz%d' % i, [128, 128], F32) for i in range(2)]
    nz = K.sb('nz', [128, 128], F32)
    Kbg = K.sb('Kbg', [128, 128], F32)
    Kd = K.sb('Kd', [128, 128], F32)
    Vb = K.sb('Vb', [128, 128], F32)
    dm1 = K.sb('dm1', [128, 128], F32)
    dm2 = K.sb('dm2', [128, 128], F32)
    attnT = K.sb('attnT', [128, 128], F32)
    Mt = K.sb('Mt', [128, 128], F32)
    P = [K.sb('Pn%d' % i, [128, 128], F32) for i in range(2)]
    PT = [K.sb('PTn%d' % i, [128, 128], F32) for i in range(2)]
    X = K.sb('Xn', [128, 128], F32)
    U = K.sb('Un', [128, 128], F32)
    WT = K.sb('WTn', [128, 128], F32)
    vnew = K.sb('vnew', [128, 128], F32)
    ost = K.sb('gost', [128, 4], F32)
    ojunk = K.sb('gojunk', [128, 128], F32)
    ob = [K.sb('gob%d' % i, [128, 128], BF16) for i in range(2)]
    bA = K.ps('bA', [128, 512], F32)
    bB = K.ps('bB', [128, 512], F32)
    bC = K.ps('bC', [128, 512], F32)
    bD = K.ps('bD', [128, 512], F32)
    bE = K.ps('bE', [128, 512], F32)
    bF = K.ps('bF', [128, 512], F32)
    bG = K.ps('bG', [128, 512], F32)
    bH = K.ps('bH', [128, 512], F32)
    sl = lambda i: slice(i * 128, (i + 1) * 128)
    chk(K, 1)
    for h in heads:
        K.op('pool', lambda e: e.memset(G2L[:], 1.0), writes=[G2L])
        K.op('pool', lambda e: e.memset(G2R[:], 1.0), writes=[G2R])
        K.dma('sp', G2L[0:1, :], d['ngcd'][h:h + 1, :], writes=[G2L])
        K.dma('sp', G2R[1:2, :], d['gcd'][h:h + 1, :], writes=[G2R])
        K.dma('sp', e1row[:], d['e1d'][h:h + 1, :], writes=[e1row])
        K.dma('sp', eglrow[:], d['egld'][h:h + 1, :], writes=[eglrow])
        K.mm(bA, bA[:, 0:NC], onesf, onesf[0:1, :], eglrow, eglrow[0:1, :])
        K.op('act', lambda e: e.copy(out=eglb[:], in_=bA[:, 0:NC]), reads=[bA], writes=[eglb])
        K.op('pool', lambda e: e.memset(Sst[:], 0.0), writes=[Sst])
        chk(K, 2)
        for g in range(NG):
            cs = slice(g * 512, (g + 1) * 512)
            q_, k_, v_ = qT[g % 2], kT[g % 2], vT[g % 2]
            K.dma('sp', q_[:], d['GCT'][h * 128:(h + 1) * 128, cs], writes=[q_])
            K.dma('sp', k_[:], d['GCT'][512 + h * 128:512 + (h + 1) * 128, cs], writes=[k_])
            K.dma('sp', v_[:], d['GCT'][1024 + h * 128:1024 + (h + 1) * 128, cs], writes=[v_])
            K.op('act', lambda e: e.activation(out=sq[:], in_=k_[:], func=AF.Square), reads=[k_], writes=[sq])
            K.mm(bA, bA[:], onesf, onesf[:], sq, sq[:])
            K.op('act', lambda e: e.activation(out=rr[:], in_=bA[:], func=AF.Sqrt, bias=K.eps_t[:, 0:1]), reads=[bA, K.eps_t], writes=[rr])
            K.op('dve', lambda e: e.reciprocal(out=rr[:], in_=rr[:]), reads=[rr], writes=[rr])
            K.op('dve', lambda e: e.tensor_tensor(out=knT[:], in0=k_[:], in1=rr[:], op=ALU.mult), reads=[k_, rr], writes=[knT])
            K.op('act', lambda e: e.activation(out=sq[:], in_=q_[:], func=AF.Square), reads=[q_], writes=[sq])
            K.mm(bA, bA[:], onesf, onesf[:], sq, sq[:])
            K.op('act', lambda e: e.activation(out=rr[:], in_=bA[:], func=AF.Sqrt, scale=128.0, bias=c128[:, 0:1]),
                 reads=[bA, c128], writes=[rr])
            K.op('dve', lambda e: e.reciprocal(out=rr[:], in_=rr[:]), reads=[rr], writes=[rr])
            K.op('dve', lambda e: e.tensor_tensor(out=qnT[:], in0=q_[:], in1=rr[:], op=ALU.mult), reads=[q_, rr], writes=[qnT])
            K.mm(bA, bA[:], onesf, onesf[0:1, :], e1row, e1row[0:1, cs])
            K.op('dve', lambda e: e.tensor_tensor(out=qgT[:], in0=qnT[:], in1=bA[:], op=ALU.mult), reads=[qnT, bA], writes=[qgT])
            chk(K, 3)
            for tt in range(4):
                c = g * 4 + tt
                ts_ = sl(tt)
                tok = slice(c * 128, (c + 1) * 128)
                z = zt[c % 2]
                K.dma('sp', z[:], d['GZ'][tok, h * 128:(h + 1) * 128], writes=[z])
                K.op('pool', lambda e, z=z: e.tensor_tensor(out=nz[:], in0=z[:], in1=nwb[:], op=ALU.mult), reads=[z, nwb], writes=[nz])
                beta_c = tokS[:, c, h:h + 1]
                be_c = tokS[:, c, 4 + h:5 + h]
                e2_c = tokS[:, c, 8 + h:9 + h]
                K.tr(bB, bB[:, 0:128], knT, knT[:, ts_], identf, identf[:])
                K.tr(bB, bB[:, 128:256], v_, v_[:, ts_], identf, identf[:])
                K.op('dve', lambda e: e.tensor_scalar(out=Kbg[:], in0=bB[:, 0:128], scalar1=be_c, scalar2=None, op0=ALU.mult),
                     reads=[bB, tokS], writes=[Kbg])
                K.op('act', lambda e: e.activation(out=Kd[:], in_=bB[:, 0:128], func=AF.Copy, scale=e2_c), reads=[bB, tokS], writes=[Kd])
                K.op('dve', lambda e: e.tensor_scalar(out=Vb[:], in0=bB[:, 128:256], scalar1=beta_c, scalar2=None, op0=ALU.mult),
                     reads=[bB, tokS], writes=[Vb])
                chk(K, 4)
                K.mm(bC, bC[:, 0:128], knT, knT[:, ts_], knT, knT[:, ts_])
                K.mm(bC, bC[:, 128:256], knT, knT[:, ts_], qnT, qnT[:, ts_])
                K.mm(bC, bC[:, 256:384], G2L, G2L[0:2, tok], G2R, G2R[0:2, tok])
                K.op('dve', lambda e: e.tensor_tensor(out=dm1[:], in0=bC[:, 256:384], in1=maskT[:], op=ALU.add), reads=[bC, maskT], writes=[dm1])
                K.op('dve', lambda e: e.tensor_tensor(out=dm2[:], in0=bC[:, 256:384], in1=maskTs[:], op=ALU.add), reads=[bC, maskTs], writes=[dm2])
                K.op('act', lambda e: e.activation(out=dm1[:], in_=dm1[:], func=AF.Exp), reads=[dm1], writes=[dm1])
                K.op('act', lambda e: e.activation(out=dm2[:], in_=dm2[:], func=AF.Exp), reads=[dm2], writes=[dm2])
                K.op('dve', lambda e: e.tensor_tensor(out=attnT[:], in0=bC[:, 128:256], in1=dm1[:], op=ALU.mult), reads=[bC, dm1], writes=[attnT])
                K.op('dve', lambda e: e.tensor_tensor(out=Mt[:], in0=bC[:, 0:128], in1=dm2[:], op=ALU.mult), reads=[bC, dm2], writes=[Mt])
                chk(K, 5)
                K.tr(bD, bD[:, 0:128], Mt, Mt[:], identf, identf[:])
                K.op('dve', lambda e: e.tensor_scalar(out=P[0][:], in0=bD[:, 0:128], scalar1=beta_c, scalar2=-1.0, op0=ALU.mult, op1=ALU.mult),
                     reads=[bD, tokS], writes=[P[0]])
                K.tr(bD, bD[:, 128:256], P[0], P[0][:], identf, identf[:])
                K.op('act', lambda e: e.copy(out=PT[0][:], in_=bD[:, 128:256]), reads=[bD], writes=[PT[0]])
                K.op('dve', lambda e: e.tensor_tensor(out=X[:], in0=bD[:, 128:256], in1=identf[:], op=ALU.add), reads=[bD, identf], writes=[X])
                for n in range(6):
                    a, b = n % 2, (n + 1) % 2
                    K.mm(bE, bE[:, 128:256], PT[a], PT[a][:], P[a], P[a][:])
                    K.mm(bE, bE[:, 256:384], P[a], P[a][:], PT[a], PT[a][:])
                    K.op('act', lambda e, b=b: e.copy(out=P[b][:], in_=bE[:, 128:256]), reads=[bE], writes=[P[b]])
                    K.op('act', lambda e, b=b: e.copy(out=PT[b][:], in_=bE[:, 256:384]), reads=[bE], writes=[PT[b]])
                    K.mm(bF, bF[:, 0:128], P[b], P[b][:], X, X[:])
                    K.op('dve', lambda e: e.tensor_tensor(out=X[:], in0=X[:], in1=bF[:, 0:128], op=ALU.add), reads=[X, bF], writes=[X])
                chk(K, 6)
                K.mm(bG, bG[:, 0:128], X, X[:], Vb, Vb[:])
                K.mm(bG, bG[:, 128:256], Kbg, Kbg[:], X, X[:])
                K.op('act', lambda e: e.copy(out=U[:], in_=bG[:, 0:128]), reads=[bG], writes=[U])
                K.op('act', lambda e: e.copy(out=WT[:], in_=bG[:, 128:256]), reads=[bG], writes=[WT])
                chk(K, 7)
                K.mm(bH, bH[:, 0:128], WT, WT[:], Sst, Sst[:])
                K.op('dve', lambda e: e.tensor_tensor(out=vnew[:], in0=U[:], in1=bH[:, 0:128], op=ALU.subtract), reads=[U, bH], writes=[vnew])
                K.mm(bH, bH[:, 128:256], qgT, qgT[:, ts_], Sst, Sst[:], start=True, stop=False)
                K.mm(bH, bH[:, 128:256], attnT, attnT[:], vnew, vnew[:], start=False, stop=True)
                K.mm(bH, bH[:, 256:384], Kd, Kd[:], vnew, vnew[:])
                K.op('dve', lambda e, c=c: e.scalar_tensor_tensor(out=Sst[:], in0=Sst[:], scalar=eglb[:, c:c + 1], in1=bH[:, 256:384],
                                                                   op0=ALU.mult, op1=ALU.add), reads=[Sst, eglb, bH], writes=[Sst])
                chk(K, 8)
                K.op('act', lambda e: e.activation(out=ojunk[:], in_=bH[:, 128:256], func=AF.Square, accum_out=ost[:, 0:1]),
                     reads=[bH], writes=[ojunk, ost])
                K.op('act', lambda e: e.activation(out=ost[:, 1:2], in_=ost[:, 0:1], func=AF.Sqrt, scale=1.0 / 128, bias=K.eps_t[:, 0:1]),
                     reads=[ost, K.eps_t], writes=[ost])
                K.op('dve', lambda e: e.reciprocal(out=ost[:, 2:3], in_=ost[:, 1:2]), reads=[ost], writes=[ost])
                o_b = ob[c % 2]
                K.op('dve', lambda e, o_b=o_b: e.scalar_tensor_tensor(out=o_b[:], in0=bH[:, 128:256], scalar=ost[:, 2:3], in1=nz[:],
                                                                       op0=ALU.mult, op1=ALU.mult), reads=[bH, ost, nz], writes=[o_b])
                K.dma('pool', d['mixed'][tok, 512 + h * 128:512 + (h + 1) * 128], o_b[:], reads=[o_b])


def load_gain(K, name, src1d):
    g = K.sb(name, [128, 8], F32)
    K.dma('sp', g[:], src1d.rearrange("(k p) -> p k", p=128), writes=[g], allow_slow_non_contiguous=True)
    return g


def phase_O(K, d, S, src, w_out, dst='h'):
    NT_ = S // 128
    K.phase_begin()
    consts(K)
    identb = make_ident(K, BF16, 'identb')
    W = K.sb('Wo', [128, 8, 1024], BF16)
    load_weight_bf16(K, W, w_out, 8, 1024)
    mx = [K.sb('mx%d' % i, [128, 1024], BF16) for i in range(2)]
    xr = [K.sb('xr%d' % i, [128, 1024], F32) for i in range(2)]
    mT = [K.sb('mT%d' % i, [128, 8, 128], BF16) for i in range(2)]
    ho = [K.sb('ho%d' % i, [128, 1024], F32) for i in range(2)]
    pT = [K.ps('opT%d' % i, [128, 1024], BF16) for i in range(2)]
    po = [K.ps('opo%d' % i, [128, 512], F32) for i in range(4)]
    for t in range(NT_):
        rows = slice(t * 128, (t + 1) * 128)
        m_, x_, mT_, h_, p_ = mx[t % 2], xr[t % 2], mT[t % 2], ho[t % 2], pT[t % 2]
        K.dma('sp', m_[:], d['mixed'][rows, :], writes=[m_])
        K.dma('sp', x_[:], src[rows, :], writes=[x_])
        for k in range(8):
            K.tr(p_, p_[:, k * 128:(k + 1) * 128], m_, m_[:, k * 128:(k + 1) * 128], identb, identb[:])
        K.op('act', lambda e: e.copy(out=mT_[:], in_=p_[:].rearrange("p (k t) -> p k t", k=8)), reads=[p_], writes=[mT_])
        for half in range(2):
            p2 = po[(t * 2 + half) % 4]
            for k in range(8):
                K.mm(p2, p2[:], mT_, mT_[:, k, :], W, W[:, k, half * 512:(half + 1) * 512], start=(k == 0), stop=(k == 7))
            K.op('dve', lambda e, p2=p2, half=half: e.tensor_tensor(out=h_[:, half * 512:(half + 1) * 512], in0=p2[:],
                                                                   in1=x_[:, half * 512:(half + 1) * 512], op=ALU.add),
                 reads=[p2, x_], writes=[h_], nowaw=[h_])
        K.dma('pool', d[dst][rows, :], h_[:], reads=[h_])
    K.phase_end()


def phase_MLP(K, d, S, li):
    GT = 2
    NG = S // (GT * 128)
    K.phase_begin()
    consts(K)
    identb = make_ident(K, BF16, 'identb')
    gain = load_gain(K, 'gain', d['mlp_norm'][li, :])
    Wu = K.sb('Wu', [128, 8, 4096], BF16)
    load_weight_bf16(K, Wu, d['mlp_w_up'][li], 8, 4096, gain=gain, stage_name='wsu', cb=512)
    Wd = K.sb('Wd', [128, 32, 1024], BF16)
    load_weight_bf16(K, Wd, d['mlp_w_down'][li], 32, 1024, stage_name='wsd', cb=512)
    NT = NormT(K, identb, nbuf_x=2 * GT, gt=GT)
    uT = K.sb('uT', [128, 32, GT * 128], BF16)
    rl = [K.sb('rl%d' % i, [128, GT * 128], F32) for i in range(2)]
    ho = [K.sb('mho%d' % i, [128, 1024], F32) for i in range(2)]
    pu = [K.ps('pu%d' % i, [128, 512], F32) for i in range(2)]
    pd = [K.ps('pd%d' % i, [128, 512], F32) for i in range(4)]
    toks = [NT.prep(d['h'][j * 128:(j + 1) * 128, :]) for j in range(GT)]
    for g in range(NG):
        hnT = NT.hnT[g % 2]
        xbs = []
        for j in range(GT):
            xbs.append(NT.finish(toks[j], hnT, j))
        for fc in range(32):
            if fc == 8 and g + 1 < NG:
                toks = [NT.prep(d['h'][((g + 1) * GT + j) * 128:((g + 1) * GT + j + 1) * 128, :]) for j in range(GT)]
            p = pu[fc % 2]
            for k in range(8):
                K.mm(p, p[:, 0:GT * 128], Wu, Wu[:, k, fc * 128:(fc + 1) * 128], hnT, hnT[:, k, :], start=(k == 0), stop=(k == 7))
            r = rl[fc % 2]
            K.op('act', lambda e, r=r, p=p: e.activation(out=r[:], in_=p[:, 0:GT * 128], func=AF.Relu), reads=[p], writes=[r])
            K.op('dve' if fc % 2 == 0 else 'pool', lambda e, r=r, fc=fc: e.tensor_tensor(out=uT[:, fc, :], in0=r[:], in1=r[:], op=ALU.mult),
                 reads=[r], writes=[uT], nowaw=[uT])
        for j in range(GT):
            t = g * GT + j
            h_ = ho[t % 2]
            for half in range(2):
                p2 = pd[(t * 2 + half) % 4]
                for fc in range(32):
                    K.mm(p2, p2[:], uT, uT[:, fc, j * 128:(j + 1) * 128], Wd, Wd[:, fc, half * 512:(half + 1) * 512],
                         start=(fc == 0), stop=(fc == 31))
                K.op('dve', lambda e, p2=p2, half=half, x_=xbs[j]: e.tensor_tensor(
                    out=h_[:, half * 512:(half + 1) * 512], in0=p2[:], in1=x_[:, half * 512:(half + 1) * 512], op=ALU.add),
                    reads=[p2, xbs[j]], writes=[h_], nowaw=[h_])
            K.dma('pool', d['h'][t * 128:(t + 1) * 128, :], h_[:], reads=[h_])
    K.phase_end()


def phase_PLE(K, d, S, li, final=False):
    NT_ = S // 128
    K.phase_begin()
    consts(K)
    identb = make_ident(K, BF16, 'identb')
    gain = load_gain(K, 'gain', d['ple_norm'][li, :])
    Wg = K.sb('Wg', [128, 8, 1024], BF16)
    load_weight_bf16(K, Wg, d['ple_w_gate'][li], 8, 1024, gain=gain)
    Wp = K.sb('Wp', [128, 2, 1024], BF16)
    load_weight_bf16(K, Wp, d['ple_w_proj'][li], 2, 1024)
    NT = NormT(K, identb, nbuf_x=9, gt=1, nbuf_n=8)
    pin = [K.sb('pin%d' % i, [128, 256], F32) for i in range(2)]
    pbf = [K.sb('pbf%d' % i, [128, 256], BF16) for i in range(2)]
    ppT = [K.sb('ppT%d' % i, [128, 2, 128], BF16) for i in range(2)]
    sg = [K.sb('sg%d' % i, [128, 1024], F32) for i in range(2)]
    ho = [K.sb('pho%d' % i, [128, 1024], F32) for i in range(3)]
    ptp = K.ps('ptp', [128, 1024], BF16)
    pg = [K.ps('pg%d' % i, [128, 512], F32) for i in range(2)]
    pp = [K.ps('pp%d' % i, [128, 512], F32) for i in range(2)]
    if final:
        fg = K.sb('fg', [128, 1024], F32)
        K.dma('sp', fg[:], d['final_norm'].partition_broadcast(128), writes=[fg])
        fst = [K.sb('fst%d' % i, [128, 4], F32) for i in range(2)]
        fo = [K.sb('fo%d' % i, [128, 1024], F32) for i in range(2)]
    pend_fin = [None]
    toks = [NT.prep(d['h'][u * 128:(u + 1) * 128, :]) for u in range(min(4, NT_))]
    for t in range(NT_):
        rows = slice(t * 128, (t + 1) * 128)
        hnT = NT.hnT[t % 2]
        x_ = NT.finish(toks.pop(0), hnT, 0)
        if t % 4 == 0:
            for u in range(t + 4, min(t + 8, NT_)):
                toks.append(NT.prep(d['h'][u * 128:(u + 1) * 128, :]))
        pi_, pb_, pT_, sg_, h_ = pin[t % 2], pbf[t % 2], ppT[t % 2], sg[t % 2], ho[t % 3]
        K.dma('sp', pi_[:], d['p'][li, rows, :], writes=[pi_])
        K.op('dve', lambda e: e.tensor_copy(out=pb_[:], in_=pi_[:]), reads=[pi_], writes=[pb_])
        for k in range(2):
            K.tr(ptp, ptp[:, k * 128:(k + 1) * 128], pb_, pb_[:, k * 128:(k + 1) * 128], identb, identb[:])
        K.op('act', lambda e: e.copy(out=pT_[:], in_=ptp[:, 0:256].rearrange("p (k t) -> p k t", k=2)), reads=[ptp], writes=[pT_])
        for half in range(2):
            hs = slice(half * 512, (half + 1) * 512)
            g_, p_ = pg[half], pp[half]
            for k in range(8):
                K.mm(g_, g_[:], hnT, hnT[:, k, :], Wg, Wg[:, k, hs], start=(k == 0), stop=(k == 7))
            for k in range(2):
                K.mm(p_, p_[:], pT_, pT_[:, k, :], Wp, Wp[:, k, hs], start=(k == 0), stop=(k == 1))
            K.op('act', lambda e, g_=g_, hs=hs: e.activation(out=sg_[:, hs], in_=g_[:], func=AF.Sigmoid), reads=[g_], writes=[sg_], nowaw=[sg_])
            K.op('dve', lambda e, p_=p_, hs=hs: e.tensor_tensor(out=sg_[:, hs], in0=sg_[:, hs], in1=p_[:], op=ALU.mult),
                 reads=[sg_, p_], writes=[sg_])
        K.op('pool', lambda e: e.tensor_tensor(out=h_[:], in0=sg_[:], in1=x_[:], op=ALU.add), reads=[sg_, x_], writes=[h_])
        if not final:
            K.dma('pool', d['h'][rows, :], h_[:], reads=[h_])
        else:
            def fin(h_=h_, st=fst[t % 2], o_=fo[t % 2], rows=rows):
                K.op('act', lambda e: e.activation(out=o_[:], in_=h_[:], func=AF.Square, accum_out=st[:, 0:1]), reads=[h_], writes=[o_, st])
                K.op('act', lambda e: e.activation(out=st[:, 1:2], in_=st[:, 0:1], func=AF.Sqrt, scale=1.0 / D, bias=K.eps_t[:, 0:1]),
                     reads=[st, K.eps_t], writes=[st])
                K.op('dve', lambda e: e.reciprocal(out=st[:, 2:3], in_=st[:, 1:2]), reads=[st], writes=[st])
                K.op('dve', lambda e: e.scalar_tensor_tensor(out=o_[:], in0=h_[:], scalar=st[:, 2:3], in1=fg[:], op0=ALU.mult, op1=ALU.mult),
                     reads=[h_, st, fg], writes=[o_])
                K.dma('pool', d['out'][rows, :], o_[:], reads=[o_])
            if pend_fin[0] is not None:
                pend_fin[0]()
            pend_fin[0] = fin
    if final and pend_fin[0] is not None:
        pend_fin[0]()
    K.phase_end()


O_NQ, O_KC, O_VC, O_KSL, O_VSL, O_KWN, O_VWN, O_NG, O_RG, O_RX = 0, 512, 640, 768, 896, 1024, 1152, 1280, 1304, 1816
TWO_PI = 6.283185307179586
CW1 = 6.28125
CW2 = TWO_PI - CW1
MAGIC = 12582912.0


def phase_Rprep(K, d, S):
    NT_ = S // 128
    inv_freq = (np.float32(500000.0) ** (-np.arange(8, dtype=np.float32) * np.float32(2.0 / 16))).astype(np.float32)
    K.phase_begin()
    consts(K)
    posi = K.sb('posi', [128, NT_], I32)
    K.dma('sp', posi[:], d['positions'].rearrange("o (t p) -> p (o t)", p=128), writes=[posi], allow_slow_non_contiguous=True)
    posf = K.sb('posf', [128, NT_], F32)
    K.op('dve', lambda e: e.tensor_copy(out=posf[:], in_=posi[:]), reads=[posi], writes=[posf])
    ang = K.sb('ang', [128, NT_, 8], F32)
    for i in range(8):
        K.op('dve', lambda e, i=i: e.tensor_scalar(out=ang[:, :, i], in0=posf[:], scalar1=float(inv_freq[i]), scalar2=None, op0=ALU.mult),
             reads=[posf], writes=[ang], nowaw=[ang])
    cs = K.sb('cs', [128, NT_, 16], F32)
    a2 = K.sb('a2', [128, NT_, 8], F32)
    kk = K.sb('kk', [128, NT_, 8], F32)
    rr = K.sb('rr', [128, NT_, 8], F32)
    for which in range(2):
        src = ang
        if which == 0:
            K.op('dve', lambda e: e.tensor_scalar(out=a2[:], in0=ang[:], scalar1=float(np.pi / 2), scalar2=None, op0=ALU.add), reads=[ang], writes=[a2])
            src = a2
        K.op('dve', lambda e, src=src: e.tensor_scalar(out=kk[:], in0=src[:], scalar1=float(1.0 / TWO_PI), scalar2=MAGIC, op0=ALU.mult, op1=ALU.add),
             reads=[src], writes=[kk])
        K.op('dve', lambda e: e.tensor_scalar(out=kk[:], in0=kk[:], scalar1=-MAGIC, scalar2=None, op0=ALU.add), reads=[kk], writes=[kk])
        K.op('dve', lambda e, src=src: e.scalar_tensor_tensor(out=rr[:], in0=kk[:], scalar=-CW1, in1=src[:], op0=ALU.mult, op1=ALU.add),
             reads=[kk, src], writes=[rr])
        K.op('dve', lambda e: e.scalar_tensor_tensor(out=rr[:], in0=kk[:], scalar=-CW2, in1=rr[:], op0=ALU.mult, op1=ALU.add),
             reads=[kk, rr], writes=[rr])
        K.op('dve', lambda e: e.tensor_scalar(out=rr[:], in0=rr[:], scalar1=3.141592, scalar2=-3.141592, op0=ALU.min, op1=ALU.max),
             reads=[rr], writes=[rr])
        K.op('act', lambda e, which=which: e.activation(out=cs[:, :, which * 8:(which + 1) * 8], in_=rr[:], func=AF.Sin),
             reads=[rr], writes=[cs], nowaw=[cs])
    K.dma('sp', d['csd'][:, :], cs[:].rearrange("p t c -> p (t c)"), reads=[cs])
    K.phase_end()


def phase_A1(K, d, S):
    NG = S // 512
    NT_ = S // 128
    K.phase_begin()
    consts(K)
    identb = make_ident(K, BF16, 'identb')
    gain = load_gain(K, 'gain', d['odd_norm_mix'])
    W = K.sb('W1', [128, 8, 2328], BF16)
    load_weight_bf16(K, W, d['odd_w_in'], 8, 2328, gain=gain)
    cs = K.sb('cs', [128, NT_, 16], F32)
    K.dma('sp', cs[:].rearrange("p t c -> p (t c)"), d['csd'][:, :], writes=[cs])
    NT = NormT(K, identb, nbuf_x=8, nbuf_n=8)
    yq = [K.sb('yq%d' % i, [128, 512], F32) for i in range(2)]
    yb = [K.sb('yb%d' % i, [128, 512], F32) for i in range(2)]
    yc = [K.sb('yc%d' % i, [128, 280], F32) for i in range(2)]
    gt_ = [K.sb('gt%d' % i, [128, 24], F32) for i in range(2)]
    qb = [K.sb('qb%d' % i, [128, 512], BF16) for i in range(2)]
    kb = [K.sb('kb%d' % i, [128, 4, 128], BF16) for i in range(2)]
    vb = [K.sb('vb%d' % i, [128, 2, 128], BF16) for i in range(2)]
    tA = K.sb('tA', [128, 8, 8], F32)
    tB = K.sb('tB', [128, 8, 8], F32)
    qTs = [K.sb('qTs%d' % i, [128, 4, 128], BF16) for i in range(2)]
    kTs = [K.sb('kTs%d' % i, [128, 4, 128], BF16) for i in range(2)]
    rgo = [K.sb('rgo%d' % i, [128, 512], F32) for i in range(3)]
    ptm = [K.ps('p1tm%d' % i, [128, 512], F32) for i in range(3)]
    pfm = [K.ps('p1fm%d' % i, [128, 512], F32) for i in range(2)]
    ptr = K.ps('p1tr', [128, 1024], BF16)
    nfm = 0

    def rope(eng, src, nh, dst, t):
        cosb = cs[:, t:t + 1, 0:8].to_broadcast([128, nh, 8])
        sinb = cs[:, t:t + 1, 8:16].to_broadcast([128, nh, 8])
        x1, x2 = src[:, :, 0:8], src[:, :, 8:16]
        a, b = tA[:, 0:nh, :], tB[:, 0:nh, :]
        return [
            lambda e: e.tensor_tensor(out=a, in0=x1, in1=cosb, op=ALU.mult),
            lambda e: e.tensor_tensor(out=b, in0=x2, in1=sinb, op=ALU.mult),
            lambda e: e.tensor_tensor(out=dst[:, :, 0:8], in0=a, in1=b, op=ALU.subtract),
            lambda e: e.tensor_tensor(out=a, in0=x2, in1=cosb, op=ALU.mult),
            lambda e: e.tensor_tensor(out=b, in0=x1, in1=sinb, op=ALU.mult),
            lambda e: e.tensor_tensor(out=dst[:, :, 8:16], in0=a, in1=b, op=ALU.add),
        ]

    toks = [NT.prep(d['h'][u * 128:(u + 1) * 128, :]) for u in range(4)]
    pending = [None]
    for g in range(NG):
        hnT = NT.hnT[g % 2]
        for j in range(4):
            t = g * 4 + j
            rows = slice(t * 128, (t + 1) * 128)
            NT.finish(toks.pop(0), hnT, j)
            if j == 0:
                for u in range(t + 4, min(t + 8, NG * 4)):
                    toks.append(NT.prep(d['h'][u * 128:(u + 1) * 128, :]))
            yq_, yb_, yc_, g_, qb_, kb_, vb_, qT_, kT_ = (yq[t % 2], yb[t % 2], yc[t % 2], gt_[t % 2], qb[t % 2], kb[t % 2], vb[t % 2],
                                                         qTs[t % 2], kTs[t % 2])
            for bi, (c0, c1, dst) in enumerate(((0, 512, yq_), (512, 1024, yb_), (1024, 1304, yc_))):
                p = ptm[bi]
                for k in range(8):
                    K.mm(p, p[:, 0:c1 - c0], hnT, hnT[:, k, j * 128:(j + 1) * 128], W, W[:, k, c0:c1], start=(k == 0), stop=(k == 7))
                K.op('act', lambda e, p=p, dst=dst, n=c1 - c0: e.copy(out=dst[:, 0:n], in_=p[:, 0:n]), reads=[p], writes=[dst])
            K.op('act', lambda e: e.activation(out=g_[:], in_=yc_[:, 256:280], func=AF.Sigmoid), reads=[yc_], writes=[g_])
            K.dma('pool', d['gates'][rows, :], g_[:], reads=[g_])
            K.op('dve', lambda e: e.tensor_copy(out=qb_[:], in_=yq_[:]), reads=[yq_], writes=[qb_])
            K.op('dve', lambda e: e.tensor_copy(out=kb_[:, 0:2, :], in_=yb_[:, 0:256].rearrange("p (a c) -> p a c", a=2)), reads=[yb_], writes=[kb_])
            K.op('dve', lambda e: e.tensor_copy(out=kb_[:, 2, :], in_=yb_[:, 256:384]), reads=[yb_], writes=[kb_])
            K.op('dve', lambda e: e.tensor_copy(out=kb_[:, 3, :], in_=yc_[:, 0:128]), reads=[yc_], writes=[kb_])
            K.op('pool', lambda e: e.tensor_copy(out=vb_[:, 0, :], in_=yb_[:, 384:512]), reads=[yb_], writes=[vb_])
            K.op('pool', lambda e: e.tensor_copy(out=vb_[:, 1, :], in_=yc_[:, 128:256]), reads=[yc_], writes=[vb_])
            K.dma('pool', d['Vs'][rows, :], vb_[:, 0, :], reads=[vb_])
            K.dma('pool', d['Vw'][rows, :], vb_[:, 1, :], reads=[vb_])
            jobs = [(yq_, yq_[:].rearrange("p (h c) -> p h c", c=64), 8, qb_, qb_[:].rearrange("p (h c) -> p h c", c=64)),
                    (yb_, yb_[:, 0:128].rearrange("p (h c) -> p h c", c=64), 2, kb_, kb_[:, 0, :].rearrange("p (h c) -> p h c", c=64)),
                    (yb_, yb_[:, 256:384].rearrange("p (h c) -> p h c", c=64), 2, kb_, kb_[:, 2, :].rearrange("p (h c) -> p h c", c=64)),
                    (yc_, yc_[:, 0:128].rearrange("p (h c) -> p h c", c=64), 2, kb_, kb_[:, 3, :].rearrange("p (h c) -> p h c", c=64))]
            for sb_, sap, nh, db_, dap in jobs:
                fns = rope('dve', sap, nh, dap, t)
                rw = [([sb_, cs], [tA]), ([sb_, cs], [tB]), ([tA, tB], [db_]), ([sb_, cs], [tA]), ([sb_, cs], [tB]), ([tA, tB], [db_])]
                for fn, (rd, wr) in zip(fns, rw):
                    K.op('dve', fn, reads=rd, writes=wr)
            def post(qb_=qb_, kb_=kb_, qT_=qT_, kT_=kT_, rows=rows):
                for c in range(4):
                    K.tr(ptr, ptr[:, c * 128:(c + 1) * 128], qb_, qb_[:, c * 128:(c + 1) * 128], identb, identb[:])
                for c in range(4):
                    K.tr(ptr, ptr[:, 512 + c * 128:512 + (c + 1) * 128], kb_, kb_[:, c, :], identb, identb[:])
                K.op('act', lambda e: e.copy(out=qT_[:], in_=ptr[:, 0:512].rearrange("p (c t) -> p c t", c=4)), reads=[ptr], writes=[qT_])
                K.op('act', lambda e: e.copy(out=kT_[:], in_=ptr[:, 512:1024].rearrange("p (c t) -> p c t", c=4)), reads=[ptr], writes=[kT_])
                K.dma('sp', d['QT1'][:, rows].rearrange("(c p) t -> p c t", p=128), qT_[:], reads=[qT_])
                for c, nm in enumerate(('KcT', 'VcT', 'KsT', 'KwT')):
                    K.dma('sp', d[nm][:, rows], kT_[:, c, :], reads=[kT_])
            if pending[0] is not None:
                pending[0]()
            pending[0] = post
        for c in range(8):
            p = pfm[nfm % 2]
            o = rgo[nfm % 3]
            nfm += 1
            for k in range(8):
                K.mm(p, p[:], W, W[:, k, O_RG + c * 128:O_RG + (c + 1) * 128], hnT, hnT[:, k, :], start=(k == 0), stop=(k == 7))
            K.op('act', lambda e, o=o, p=p: e.copy(out=o[:], in_=p[:]), reads=[p], writes=[o])
            K.dma('pool', d['RGX'][c * 128:(c + 1) * 128, g * 512:(g + 1) * 512], o[:], reads=[o])
    pending[0]()
    K.phase_end()


GELU_C = 1.5957691216057308


def gelu_tanh(K, xs, tmp, out_ap, out_b, eng2='pool'):
    if eng2 == 'act_sq':
        K.op('act', lambda e: e.activation(out=tmp[:], in_=xs[:], func=AF.Square), reads=[xs], writes=[tmp])
    else:
        K.op(eng2, lambda e: e.tensor_tensor(out=tmp[:], in0=xs[:], in1=xs[:], op=ALU.mult), reads=[xs], writes=[tmp])
    K.op('dve', lambda e: e.tensor_scalar(out=tmp[:], in0=tmp[:], scalar1=0.044715, scalar2=1.0, op0=ALU.mult, op1=ALU.add),
         reads=[tmp], writes=[tmp])
    K.op('dve', lambda e: e.tensor_tensor(out=tmp[:], in0=tmp[:], in1=xs[:], op=ALU.mult), reads=[tmp, xs], writes=[tmp])
    K.op('act', lambda e: e.activation(out=tmp[:], in_=tmp[:], func=AF.Sigmoid, scale=GELU_C), reads=[tmp], writes=[tmp])
    K.op('dve', lambda e: e.tensor_tensor(out=out_ap, in0=tmp[:], in1=xs[:], op=ALU.mult), reads=[tmp, xs], writes=[out_b])


def phase_cmp(K, d, S):
    NCP = S // 16
    NCMP = NCP - 1
    K.phase_begin()
    consts(K)
    XT = K.sb('XT', [128, S], BF16)
    w1s = K.sb('w1s', [128, 32, 128], F32)
    W1 = K.sb('W1c', [128, 32, 128], BF16)
    pes = K.sb('pes', [128, 32], F32)
    peT = K.sb('peT', [128, 32], BF16)
    w2s = K.sb('w2s', [128, 64], F32)
    W2 = K.sb('W2c', [128, 64], BF16)
    cvec = K.sb('cvec', [128, 1], F32)
    xs = K.sb('cxs', [128, NCP], F32)
    tmp = K.sb('ctmp', [128, NCP], F32)
    hidT = K.sb('hidT', [128, NCP], BF16)
    ko = K.sb('ko', [64, NCP], BF16)
    vo = K.sb('vo', [128, 64], BF16)
    ph = K.ps('cph', [128, 512], F32)
    pc = K.ps('cpc', [128, 512], F32)
    po = K.ps('cpo', [128, 512], F32)
    for which, (src, pe, w1, w2) in enumerate((('KcT', 'odd_cmp_k_pe', 'odd_cmp_k_w1', 'odd_cmp_k_w2'),
                                                ('VcT', 'odd_cmp_v_pe', 'odd_cmp_v_w1', 'odd_cmp_v_w2'))):
        K.dma('sp', XT[:], d[src][:, :], writes=[XT])
        for hf in range(2):
            K.dma('sp', w1s[hf * 64:(hf + 1) * 64, :, :], d[w1].rearrange("(l c) h -> c l h", c=64), writes=[w1s], nowaw=[w1s])
            K.dma('sp', pes[hf * 64:(hf + 1) * 64, :], d[pe].rearrange("l c -> c l"), writes=[pes], nowaw=[pes],
                  allow_slow_non_contiguous=True)
        K.dma('sp', w2s[:], d[w2][:, :], writes=[w2s])
        K.op('dve', lambda e: e.tensor_copy(out=W1[:], in_=w1s[:]), reads=[w1s], writes=[W1])
        K.op('dve', lambda e: e.tensor_copy(out=peT[:], in_=pes[:]), reads=[pes], writes=[peT])
        K.op('dve', lambda e: e.tensor_copy(out=W2[:], in_=w2s[:]), reads=[w2s], writes=[W2])
        for l in range(32):
            K.mm(pc, pc[:, 0:1], W1, W1[0:64, l, :], peT, peT[0:64, l:l + 1], start=(l == 0), stop=(l == 31))
        K.op('act', lambda e: e.copy(out=cvec[:], in_=pc[:, 0:1]), reads=[pc], writes=[cvec])
        X3 = XT[:].rearrange("p (n s) -> p n s", s=16)
        for g in range(2):
            gs = slice(g * 64, (g + 1) * 64)
            for l in range(32):
                rhs = X3[gs, 0:NCMP, l] if l < 16 else X3[gs, 1:NCP, l - 16]
                K.mm(ph, ph[:, 0:NCMP], W1, W1[gs, l, :], XT, rhs, start=(l == 0), stop=(l == 31))
            K.op('pool', lambda e: e.memset(xs[:], 0.0), writes=[xs])
            K.op('act', lambda e: e.activation(out=xs[:, 0:NCMP], in_=ph[:, 0:NCMP], func=AF.Identity, bias=cvec[:, 0:1]),
                 reads=[ph, cvec], writes=[xs])
            gelu_tanh(K, xs, tmp, hidT[:], hidT)
            if which == 0:
                for c0 in range(0, NCP, 512):
                    n = min(512, NCP - c0)
                    K.mm(po, po[0:64, 0:n], W2, W2[:, :], hidT, hidT[:, c0:c0 + n])
                    K.op('act', lambda e, c0=c0, n=n: e.copy(out=ko[:, c0:c0 + n], in_=po[0:64, 0:n]), reads=[po], writes=[ko])
                K.dma('sp', d['KcmpT'][gs, :], ko[:], reads=[ko])
            else:
                for c0 in range(0, NCP, 128):
                    K.mm(po, po[:, 0:64], hidT, hidT[:, c0:c0 + 128], W2, W2[:, :])
                    K.op('act', lambda e: e.copy(out=vo[:], in_=po[:, 0:64]), reads=[po], writes=[vo])
                    K.dma('sp', d['Vcmp'][c0:c0 + 128, gs], vo[:], reads=[vo])
    K.phase_end()


def attn_core(K, S, heads, B, load_head, units_fn, KR, s_extra, exp_bias, emit_masks, pv_extra, finalize, L=3):
    NQ = S // 512
    units = []
    for hi, h in enumerate(heads):
        for qg in range(NQ):
            us = units_fn(qg)
            for ui, u in enumerate(us):
                units.append((hi, h, qg, u, ui == 0, ui == len(us) - 1))
    load_head(0, heads[0])
    NU = len(units)
    deferred = []
    for i in range(NU + L):
        if deferred and deferred[0][0] <= i:
            deferred.pop(0)[1]()
        if i < NU:
            hi, h, qg, (kt, c0, c1), first, last = units[i]
            s = hi % 2
            p_s = B['ps'][i % len(B['ps'])]
            p_t = B['pt'][i % 4]
            KT = B['KT'][s]
            QT = B['QTsel'](s, kt) if 'QTsel' in B else B['QT'][s]
            ex = s_extra(h, qg, kt, c0, c1) if s_extra else None
            mms = B['mask_mm'](h, qg, kt, c0, c1) if 'mask_mm' in B else []
            K.mm(p_s, p_s[:, c0:c1], KT, KT[0:KR, kt * 128:(kt + 1) * 128], QT, QT[0:KR, qg * 512 + c0:qg * 512 + c1],
                 start=True, stop=(ex is None and not mms))
            if ex is not None:
                K.mm(p_s, p_s[:, c0:c1], ex[0], ex[1], ex[2], ex[3], start=False, stop=(not mms))
            for mi, (mc, mb) in enumerate(mms):
                K.mm(p_s, p_s[:, mc:mc + 128], B['identb'], B['identb'][:], mb, mb[:], start=False, stop=(mi == len(mms) - 1))
            bb, bap = exp_bias(h, kt) if exp_bias else (K.zero_t, K.zero_t[:, 0:1])
            K.op('act', lambda e, p_s=p_s, p_t=p_t, c0=c0, c1=c1, bap=bap: e.activation(
                out=p_t[:, c0:c1], in_=p_s[:, c0:c1], func=AF.Exp, scale=0.125, bias=bap), reads=[p_s, bb], writes=[p_t])
            emit_masks(p_t, h, qg, kt, c0, c1)
        if i - L >= 0:
            hi, h, qg, (kt, c0, c1), first, last = units[i - L]
            if first and qg == 0 and hi + 1 < len(heads):
                load_head(hi + 1, heads[hi + 1])
            s = hi % 2
            p_t = B['pt'][(i - L) % 4]
            gi = hi * NQ + qg
            p_o = B['po'][gi % 2]
            V = B['V'][s]
            K.mm(p_o, p_o[0:65, c0:c1], V, V[:, kt, 0:65], p_t, p_t[:, c0:c1], start=first, stop=last)
            if pv_extra:
                pv_extra(p_t, hi, h, qg, kt, c0, c1, first, last)
            if last:
                rest = finalize(hi, h, qg, gi, p_o)
                if rest is not None:
                    deferred.append((i + 2, rest))
    for _, rest in deferred:
        rest()


def causal_sel(K, p_t, c):
    K.op('pool', lambda e: e.affine_select(out=p_t[:, c:c + 128], in_=p_t[:, c:c + 128], pattern=[[1, 128]], compare_op=ALU.is_ge,
                                           fill=0.0, base=0, channel_multiplier=-1), reads=[p_t], writes=[p_t])


class NsaCommon:
    def __init__(self, K, d, S, nkt, br, out_name, n_ps=3):
        self.K, self.d, self.S, self.br, self.out_name = K, d, S, br, out_name
        NT_ = S // 128
        consts(K)
        K.zero_t = K.sb('zero', [128, 1], F32)
        K.op('pool', lambda e: e.memset(K.zero_t[:], 0.0), writes=[K.zero_t])
        self.identf = make_ident(K, F32, 'identf')
        self.gates = K.sb('gates', [128, NT_, 24], F32)
        K.dma('sp', self.gates[:], d['gates'].rearrange("(t p) c -> p t c", p=128), writes=[self.gates])
        self.B = dict(
            KT=[K.sb('nKT%d' % i, [128, nkt * 128], BF16) for i in range(2)],
            QT=[K.sb('nQT%d' % i, [128, S], BF16) for i in range(2)],
            V=[K.sb('nV%d' % i, [128, nkt, 65], BF16) for i in range(2)],
            ps=[K.ps('nps%d' % i, [128, 512], F32) for i in range(n_ps)],
            po=[K.ps('npo%d' % i, [128, 512], F32) for i in range(2)],
            pt=[K.sb('npt%d' % i, [128, 512], BF16) for i in range(4)],
        )
        for i in range(2):
            K.op('pool', lambda e, i=i: e.memset(self.B['V'][i][:, :, 64:65], 1.0), writes=[self.B['V'][i]])
            K.op('pool', lambda e, i=i: e.memset(self.B['KT'][i][64:128, :], 0.0), writes=[self.B['KT'][i]])
            K.op('pool', lambda e, i=i: e.memset(self.B['QT'][i][64:128, :], 0.0), writes=[self.B['QT'][i]])
        self.pq = K.ps('npq', [128, 4, 65], F32)
        self.osb = [K.sb('nosb%d' % i, [65, 512], F32) for i in range(2)]
        self.rc = [K.sb('nrc%d' % i, [128, 4], F32) for i in range(2)]
        self.mo = [K.sb('nmo%d' % i, [128, 4, 64], F32) for i in range(2)]

    def load_head(self, ksrc, vsrc, nk):
        K, d, B = self.K, self.d, self.B

        def f(hi, h):
            s, g = hi % 2, h // 4
            K.dma('sp', B['KT'][s][0:64, 0:nk], d[ksrc][g * 64:(g + 1) * 64, 0:nk], writes=[B['KT'][s]])
            K.dma('sp', B['QT'][s][0:64, :], d['QT1'][h * 64:(h + 1) * 64, :], writes=[B['QT'][s]])
            K.dma('sp', B['V'][s][:, :, 0:64], d[vsrc][0:nk, g * 64:(g + 1) * 64].rearrange("(t p) c -> p t c", p=128),
                  writes=[B['V'][s]])
        return f

    def finalize(self, hi, h, qg, gi, p_o, pre=None, defer_pre=False, rc_hook=None):
        K, d = self.K, self.d
        o_s, r_c, m_o, p_q = self.osb[gi % 2], self.rc[gi % 2], self.mo[gi % 2], self.pq
        K.op('act', lambda e: e.copy(out=o_s[:], in_=p_o[0:65, :]), reads=[p_o], writes=[o_s])
        if pre and not defer_pre:
            pre(o_s)

        def rest():
            if pre and defer_pre:
                pre(o_s)
            for j in range(4):
                K.tr(p_q, p_q[:, j, :], o_s, o_s[0:65, j * 128:(j + 1) * 128], self.identf, self.identf[0:65, 0:65])
            K.op('dve', lambda e: e.tensor_scalar(out=r_c[:], in0=p_q[:, :, 64], scalar1=1e-30, scalar2=None, op0=ALU.add),
                 reads=[p_q], writes=[r_c])
            K.op('dve', lambda e: e.reciprocal(out=r_c[:], in_=r_c[:]), reads=[r_c], writes=[r_c])
            if rc_hook:
                rc_hook(r_c)
            col = h * 3 + self.br
            K.op('dve', lambda e: e.tensor_tensor(out=r_c[:], in0=r_c[:], in1=self.gates[:, qg * 4:(qg + 1) * 4, col], op=ALU.mult),
                 reads=[r_c, self.gates], writes=[r_c])
            for j in range(4):
                K.op('dve', lambda e, j=j: e.tensor_scalar(out=m_o[:, j, :], in0=p_q[:, j, 0:64], scalar1=r_c[:, j:j + 1], scalar2=None,
                                                           op0=ALU.mult), reads=[p_q, r_c], writes=[m_o])
            K.dma('pool', d[self.out_name][qg * 512:(qg + 1) * 512, h * 64:(h + 1) * 64].rearrange("(j p) c -> p j c", p=128),
                  m_o[:], reads=[m_o])
        return rest


def phase_nsa_win(K, d, S, heads=range(8)):
    K.phase_begin()
    C = NsaCommon(K, d, S, S // 128, 2, 'nsa2')
    C.B['identb'] = make_ident(K, BF16, 'identb')
    zb = K.sb('zb', [128, 128], BF16)
    K.op('pool', lambda e: e.memset(zb[:], 0.0), writes=[zb])
    tri = K.sb('trimask', [128, 128], BF16)
    low = K.sb('lowmask', [128, 128], BF16)
    NEGB = -240000.0
    K.op('pool', lambda e: e.affine_select(out=tri[:], in_=zb[:], pattern=[[1, 128]], compare_op=ALU.is_ge, fill=NEGB, base=0,
                                           channel_multiplier=-1), reads=[zb], writes=[tri])
    K.op('pool', lambda e: e.affine_select(out=low[:], in_=zb[:], pattern=[[-1, 128]], compare_op=ALU.is_gt, fill=NEGB, base=0,
                                           channel_multiplier=1), reads=[zb], writes=[low])

    def mask_mm(h, qg, kt, c0, c1):
        ms = []
        if kt >= 4 * qg:
            ms.append(((kt - 4 * qg) * 128, tri))
        if 4 * qg <= kt + 4 <= 4 * qg + 3:
            ms.append(((kt + 4 - 4 * qg) * 128, low))
        return ms
    C.B['mask_mm'] = mask_mm

    def units_fn(qg):
        us = []
        for kt in range(max(0, 4 * qg - 4), 4 * qg + 4):
            i0 = max(kt, 4 * qg)
            i1 = min(kt + 4, 4 * qg + 3)
            us.append((kt, (i0 - 4 * qg) * 128, (i1 - 4 * qg + 1) * 128))
        return us

    def masks(p_t, h, qg, kt, c0, c1):
        pass

    attn_core(K, S, list(heads), C.B, C.load_head('KwT', 'Vw', S), units_fn, 128, None, None, masks, None, C.finalize)
    K.phase_end()


def phase_nsa_slc(K, d, S, heads=range(8)):
    NT_ = S // 128
    NA = max(1, S // 4096)
    K.phase_begin()
    C = NsaCommon(K, d, S, NT_, 1, 'nsa1')
    C.B['identb'] = make_ident(K, BF16, 'identb')
    zb = K.sb('zb', [128, 128], BF16)
    K.op('pool', lambda e: e.memset(zb[:], 0.0), writes=[zb])
    tri = K.sb('trimask', [128, 128], BF16)
    K.op('pool', lambda e: e.affine_select(out=tri[:], in_=zb[:], pattern=[[1, 128]], compare_op=ALU.is_ge, fill=-240000.0, base=0,
                                           channel_multiplier=-1), reads=[zb], writes=[tri])
    C.B['mask_mm'] = lambda h, qg, kt, c0, c1: ([((kt - 4 * qg) * 128, tri)] if kt >= 4 * qg else [])
    QT2 = [[C.B['QT'][i] for i in range(2)]] + [[K.sb('nQTa%d_%d' % (a, i), [128, S], BF16) for i in range(2)] for a in range(1, NA)]
    for i in range(2):
        kt_ = C.B['KT'][i]
        K.op('pool', lambda e, kt_=kt_: e.memset(kt_[64:128, :], 262144.0), writes=[kt_])
        MB = min(64, S // 64)
        K.op('pool', lambda e, kt_=kt_: e.affine_select(
            out=kt_[64:128, :].rearrange("p (a m k) -> p a m k", m=MB, k=64), in_=kt_[64:128, :].rearrange("p (a m k) -> p a m k", m=MB, k=64),
            pattern=[[0, S // (MB * 64)], [1, MB], [0, 64]], compare_op=ALU.is_equal, fill=0.0, base=0,
            channel_multiplier=-1), reads=[kt_], writes=[kt_])
    C.B['QTsel'] = lambda s_, kt: QT2[kt // 32][s_]

    def load_head(hi, h):
        s_, g = hi % 2, h // 4
        B = C.B
        K.dma('sp', B['KT'][s_][0:64, :], d['KsT'][g * 64:(g + 1) * 64, :], writes=[B['KT'][s_]], nowaw=[B['KT'][s_]])
        for a in range(NA):
            qt = QT2[a][s_]
            K.dma('sp', qt[0:64, :], d['QT1'][h * 64:(h + 1) * 64, :], writes=[qt])
            K.dma('sp', qt[64:128, :], d['selT'][g, 64 * a:64 * a + 64, :], writes=[qt], nowaw=[qt])
        K.dma('sp', B['V'][s_][:, :, 0:64], d['Vs'][:, g * 64:(g + 1) * 64].rearrange("(t p) c -> p t c", p=128), writes=[B['V'][s_]])

    def units_fn(qg):
        return [(kt, 128 * max(kt - 4 * qg, 0), 512) for kt in range(4 * qg + 4)]

    def masks(p_t, h, qg, kt, c0, c1):
        pass

    attn_core(K, S, list(heads), C.B, load_head, units_fn, 128, None, None, masks, None, C.finalize)
    K.phase_end()


def phase_nsa_cmp(K, d, S, heads=range(8)):
    NT_ = S // 128
    NQ = S // 512
    NCP = S // 16
    NKT = (NCP + 127) // 128
    NKP = NKT * 128
    K.phase_begin()
    C = NsaCommon(K, d, S, NKT, 0, 'nsa0', n_ps=2)
    identf = C.identf
    identb = make_ident(K, BF16, 'identb')
    onesf = K.sb('onesf', [128, 128], F32)
    K.op('pool', lambda e: e.memset(onesf[:], 1.0), writes=[onesf])
    ovf = K.sb('ovf', [128, 128], F32)
    ovf2 = K.sb('ovf2', [128, 128], F32)
    ovl = K.sb('ovl', [128, NKT, 128], BF16)
    for kt in range(NKT):
        K.op('pool', lambda e, kt=kt: e.affine_select(out=ovf[:], in_=onesf[:], pattern=[[64, 128]], compare_op=ALU.is_ge, fill=0.0,
                                                       base=63 - 2048 * kt, channel_multiplier=-16), reads=[onesf], writes=[ovf])
        K.op('pool', lambda e, kt=kt: e.affine_select(out=ovf2[:], in_=ovf[:], pattern=[[-64, 128]], compare_op=ALU.is_ge, fill=0.0,
                                                       base=2048 * kt + 31, channel_multiplier=16), reads=[ovf], writes=[ovf2])
        K.op('pool', lambda e, kt=kt: e.tensor_copy(out=ovl[:, kt, :], in_=ovf2[:]), reads=[ovf2], writes=[ovl], nowaw=[ovl])
    cst = K.sb('cst', [128, 8], F32)
    K.op('pool', lambda e: e.memset(cst[:, 0:1], 1e30), writes=[cst])
    K.op('pool', lambda e: e.memset(cst[:, 1:2], 2e30), writes=[cst])
    K.op('pool', lambda e: e.memset(cst[0:64, 2:3], 3e30), writes=[cst])
    K.op('pool', lambda e: e.memset(cst[64:128, 2:3], 0.0), writes=[cst])
    K.op('pool', lambda e: e.memset(cst[0:64, 3:4], 0.0), writes=[cst])
    K.op('pool', lambda e: e.memset(cst[64:128, 3:4], 1.0), writes=[cst])
    K.op('pool', lambda e: e.memset(cst[0:64, 4:5], -1e30), writes=[cst])
    K.op('pool', lambda e: e.memset(cst[64:128, 4:5], 4e30), writes=[cst])
    po2s = [K.ps('npo2_%d' % i, [128, 4, 128], F32) for i in range(2)]
    pb = K.ps('npb', [128, 512], F32)
    imp_acc = K.sb('impacc', [128, NT_, 128], F32)
    rrow = K.sb('rrow', [65, 512], F32)
    rbs = K.sb('rbs', [128, 512], F32)
    tmpi = K.sb('tmpi', [128, 512], F32)
    work = K.sb('work', [128, 128], F32)
    work2 = K.sb('work2', [128, 128], F32)
    m8 = K.sb('m8', [128, 16], F32)
    selm = K.sb('selm', [128, 128], F32)
    selb = K.sb('selb', [128, 128], F32)
    sTs = [K.sb('sTs%d' % i, [128, 128], BF16) for i in range(2)]

    def units_fn(qg):
        kmax = min(NKT - 1, (32 * qg + 30) // 128)
        return [(kt, 0, 512) for kt in range(kmax + 1)]

    def masks(p_t, h, qg, kt, c0, c1):
        K.op('pool', lambda e: e.affine_select(out=p_t[:], in_=p_t[:], pattern=[[1, 512]], compare_op=ALU.is_ge, fill=0.0,
                                               base=512 * qg - 2048 * kt - 31, channel_multiplier=-16), reads=[p_t], writes=[p_t])

    def pv_extra(p_t, hi, h, qg, kt, c0, c1, first, last):
        po2 = po2s[(hi * NQ + qg) % 2]
        for j in range(4):
            K.mm(po2, po2[:, j, :], p_t, p_t[:, j * 128:(j + 1) * 128], ovl, ovl[:, kt, :], start=(first and j == 0), stop=(last and j == 3))

    def topk_pass(g):
        for i in range(NT_):
            W = 2 * i + 2
            cols = slice(i * 128, (i + 1) * 128)
            K.op('pool', lambda e: e.memset(selm[:], 0.0), writes=[selm])
            if W <= 16:
                K.op('pool', lambda e, W=W: e.memset(selm[:, 0:W], 1.0), writes=[selm])
            else:
                K.op('act', lambda e, W=W, i=i: e.copy(out=work[:, 0:W], in_=imp_acc[:, i, 0:W]), reads=[imp_acc], writes=[work])
                K.op('dve', lambda e: e.tensor_copy(out=work[:, 0:1], in_=cst[:, 0:1]), reads=[cst], writes=[work])
                K.op('dve', lambda e, i=i: e.tensor_copy(out=work[:, 2 * i:2 * i + 1], in_=cst[:, 1:2]), reads=[cst], writes=[work])
                K.op('dve', lambda e, i=i: e.tensor_scalar(out=work[:, 2 * i - 1:2 * i], in0=work[:, 2 * i - 1:2 * i], scalar1=cst[:, 3:4],
                                                           scalar2=cst[:, 2:3], op0=ALU.mult, op1=ALU.add), reads=[work, cst], writes=[work])
                K.op('dve', lambda e, i=i: e.tensor_copy(out=work[:, 2 * i + 1:2 * i + 2], in_=cst[:, 4:5]), reads=[cst], writes=[work])
                K.op('dve', lambda e, W=W: e.max(out=m8[:, 0:8], in_=work[:, 0:W]), reads=[work], writes=[m8])
                K.op('dve', lambda e, W=W: e.match_replace(out=work2[:, 0:W], in_to_replace=m8[:, 0:8], in_values=work[:, 0:W],
                                                           imm_value=-3.0e38), reads=[work, m8], writes=[work2])
                K.op('dve', lambda e, W=W: e.max(out=m8[:, 8:16], in_=work2[:, 0:W]), reads=[work2], writes=[m8])
                K.op('dve', lambda e, W=W: e.tensor_scalar(out=selm[:, 0:W], in0=work[:, 0:W], scalar1=m8[:, 15:16], scalar2=None,
                                                           op0=ALU.is_ge), reads=[work, m8], writes=[selm])
            K.op('dve', lambda e: e.tensor_scalar(out=selb[:], in0=selm[:], scalar1=-1.0, scalar2=None, op0=ALU.add), reads=[selm], writes=[selb])
            K.tr(pb, pb[:, 128:256], selb, selb[:], identf, identf[:])
            sT = sTs[i % 2]
            K.op('act', lambda e, sT=sT: e.copy(out=sT[:], in_=pb[:, 128:256]), reads=[pb], writes=[sT])
            K.dma('pool', d['selT'][g, :, cols], sT[:], reads=[sT])


    def finalize(hi, h, qg, gi, p_o):
        po2 = po2s[gi % 2]

        def rc_hook(r_c):
            for j in range(4):
                dst = imp_acc[:, qg * 4 + j, :]
                if h % 4 == 0:
                    K.op('dve', lambda e, j=j, dst=dst: e.tensor_scalar(out=dst, in0=po2[:, j, :], scalar1=r_c[:, j:j + 1], scalar2=None, op0=ALU.mult),
                         reads=[po2, r_c], writes=[imp_acc], nowaw=[imp_acc])
                else:
                    K.op('dve', lambda e, j=j, dst=dst: e.scalar_tensor_tensor(out=dst, in0=po2[:, j, :], scalar=r_c[:, j:j + 1], in1=dst,
                                                                                op0=ALU.mult, op1=ALU.add),
                         reads=[po2, r_c, imp_acc], writes=[imp_acc], nowaw=[imp_acc])
        rest0 = C.finalize(hi, h, qg, gi, p_o, rc_hook=rc_hook)

        def rest():
            rest0()
            if h % 4 == 3 and qg == NQ - 1:
                topk_pass(h // 4)
        return rest

    attn_core(K, S, list(heads), C.B, C.load_head('KcmpT', 'Vcmp', NCP), units_fn, 128, None, None, masks, pv_extra, finalize)
    K.phase_end()


def phase_nsa_combine(K, d, S):
    NT_ = S // 128
    K.phase_begin()
    a = [K.sb('ca%d' % i, [128, 512], F32) for i in range(2)]
    b = [K.sb('cb%d' % i, [128, 512], F32) for i in range(2)]
    c = [K.sb('cc%d' % i, [128, 512], F32) for i in range(2)]
    o = [K.sb('co%d' % i, [128, 512], BF16) for i in range(2)]
    for t in range(NT_):
        rows = slice(t * 128, (t + 1) * 128)
        a_, b_, c_, o_ = a[t % 2], b[t % 2], c[t % 2], o[t % 2]
        K.dma('sp', a_[:], d['nsa0'][rows, :], writes=[a_])
        K.dma('sp', b_[:], d['nsa1'][rows, :], writes=[b_])
        K.dma('sp', c_[:], d['nsa2'][rows, :], writes=[c_])
        K.op('dve', lambda e: e.tensor_tensor(out=a_[:], in0=a_[:], in1=b_[:], op=ALU.add), reads=[a_, b_], writes=[a_])
        K.op('dve', lambda e: e.tensor_tensor(out=o_[:], in0=a_[:], in1=c_[:], op=ALU.add), reads=[a_, c_], writes=[o_])
        K.dma('pool', d['mixed'][rows, 0:512], o_[:], reads=[o_])
    K.phase_end()


def phase_lru(K, d, S):
    SEG = min(S, 2048)
    NS = S // SEG
    K.phase_begin()
    consts(K)
    identb = make_ident(K, BF16, 'identb')
    prm = K.sb('lprm', [128, 12, 4], F32)
    for j in range(4):
        K.dma('sp', prm[:, j, :], d['odd_rg_conv_w'][j, :].rearrange("(c p) -> p c", p=128), writes=[prm], nowaw=[prm],
              allow_slow_non_contiguous=True)
    for idx, nm in ((4, 'odd_rg_conv_b'), (5, 'odd_rg_ba'), (6, 'odd_rg_bx'), (7, 'odd_rg_lambda')):
        K.dma('sp', prm[:, idx, :], d[nm].rearrange("(c p) -> p c", p=128), writes=[prm], nowaw=[prm], allow_slow_non_contiguous=True)
    K.op('act', lambda e: e.activation(out=prm[:, 8, :], in_=prm[:, 7, :], func=AF.Exp, scale=-1.0), reads=[prm], writes=[prm])
    K.op('act', lambda e: e.activation(out=prm[:, 8, :], in_=prm[:, 8, :], func=AF.Ln, bias=K.one_t[:, 0:1]), reads=[prm, K.one_t], writes=[prm])
    K.op('dve', lambda e: e.tensor_scalar(out=prm[:, 8, :], in0=prm[:, 8, :], scalar1=-8.0, scalar2=None, op0=ALU.mult), reads=[prm], writes=[prm])
    wst = K.sb('lwst', [128, 128], F32)
    WA = K.sb('WA', [128, 4, 128], BF16)
    WX = K.sb('WX', [128, 4, 128], BF16)
    for Wt, nm in ((WA, 'odd_rg_wa'), (WX, 'odd_rg_wx')):
        for c in range(4):
            K.op('pool', lambda e: e.memset(wst[:], 0.0), writes=[wst])
            K.dma('sp', wst[0:64, 0:64], d[nm][2 * c, :, :], writes=[wst])
            K.dma('sp', wst[64:128, 64:128], d[nm][2 * c + 1, :, :], writes=[wst])
            K.op('dve', lambda e, Wt=Wt, c=c: e.tensor_copy(out=Wt[:, c, :], in_=wst[:]), reads=[wst], writes=[Wt], nowaw=[Wt])
    names = [('xpad', SEG + 3, F32), ('x', SEG, F32), ('xbf', SEG, BF16), ('r', SEG, F32), ('ig', SEG, F32), ('a', SEG, F32), ('t', SEG, F32),
             ('hh', SEG, F32), ('rg', SEG, F32), ('tmp', SEG, F32), ('gg', SEG, F32), ('ybf', SEG, BF16)]
    TL = [{nm: K.sb('l%s%d' % (nm, i), [128, w], dt) for nm, w, dt in names} for i in range(2)]
    hc = K.sb('lhc', [128, 1], F32)
    yo = [K.sb('lyo%d' % i, [128, 4, 128], BF16) for i in range(2)]
    pr = [K.ps('lpr%d' % i, [128, 512], F32) for i in range(2)]
    pi = [K.ps('lpi%d' % i, [128, 512], F32) for i in range(2)]
    ptr = [K.ps('lptr%d' % i, [128, 1024], BF16) for i in range(2)]
    nt = 0
    it = 0
    for c in range(4):
        K.op('pool', lambda e: e.memset(hc[:], 0.0), writes=[hc])
        for s in range(NS):
            cols = slice(s * SEG, (s + 1) * SEG)
            T_ = TL[it % 2]
            Tn = TL[(it + 1) % 2]
            it += 1
            xpad, x, xbf, r, ig, a, t, hh, rg, tmp, gg, ybf = (T_[k] for k in ('xpad', 'x', 'xbf', 'r', 'ig', 'a', 't', 'hh', 'rg', 'tmp', 'gg', 'ybf'))
            if s == 0:
                K.op('pool', lambda e, xpad=xpad: e.memset(xpad[:, 0:3], 0.0), writes=[xpad])
            K.dma('sp', xpad[:, 3:3 + SEG], d['RGX'][512 + c * 128:512 + (c + 1) * 128, cols], writes=[xpad], nowaw=[xpad])
            K.dma('sp', rg[:], d['RGX'][c * 128:(c + 1) * 128, cols], writes=[rg])
            K.op('dve', lambda e: e.tensor_scalar(out=x[:], in0=xpad[:, 0:SEG], scalar1=prm[:, 0, c:c + 1], scalar2=prm[:, 4, c:c + 1],
                                                  op0=ALU.mult, op1=ALU.add), reads=[xpad, prm], writes=[x])
            for j in range(1, 4):
                K.op('dve', lambda e, j=j: e.scalar_tensor_tensor(out=x[:], in0=xpad[:, j:j + SEG], scalar=prm[:, j, c:c + 1], in1=x[:],
                                                                   op0=ALU.mult, op1=ALU.add), reads=[xpad, prm, x], writes=[x])
            K.op('pool', lambda e, xpad=xpad, xnx=Tn['xpad']: e.tensor_copy(out=xnx[:, 0:3], in_=xpad[:, SEG:SEG + 3]), reads=[xpad], writes=[Tn['xpad']])
            K.op('act', lambda e: e.copy(out=xbf[:], in_=x[:]), reads=[x], writes=[xbf])
            for pc in range(SEG // 512):
                ps_ = slice(pc * 512, (pc + 1) * 512)
                p1, p2 = pr[pc % 2], pi[pc % 2]
                K.mm(p1, p1[:], WA, WA[:, c, :], xbf, xbf[:, ps_])
                K.mm(p2, p2[:], WX, WX[:, c, :], xbf, xbf[:, ps_])
                K.op('act', lambda e, p1=p1, ps_=ps_: e.activation(out=r[:, ps_], in_=p1[:], func=AF.Sigmoid, bias=prm[:, 5, c:c + 1]),
                     reads=[p1, prm], writes=[r], nowaw=[r])
                K.op('act', lambda e, p2=p2, ps_=ps_: e.activation(out=ig[:, ps_], in_=p2[:], func=AF.Sigmoid, bias=prm[:, 6, c:c + 1]),
                     reads=[p2, prm], writes=[ig], nowaw=[ig])
            K.op('act', lambda e: e.activation(out=a[:], in_=r[:], func=AF.Exp, scale=prm[:, 8, c:c + 1]), reads=[r, prm], writes=[a])
            K.op('dve', lambda e: e.tensor_tensor(out=t[:], in0=a[:], in1=a[:], op=ALU.mult), reads=[a], writes=[t])
            K.op('dve', lambda e: e.tensor_scalar(out=t[:], in0=t[:], scalar1=-1.0, scalar2=1.0, op0=ALU.mult, op1=ALU.add), reads=[t], writes=[t])
            K.op('act', lambda e: e.activation(out=t[:], in_=t[:], func=AF.Sqrt), reads=[t], writes=[t])
            K.op('dve', lambda e: e.tensor_tensor(out=t[:], in0=t[:], in1=ig[:], op=ALU.mult), reads=[t, ig], writes=[t])
            K.op('dve', lambda e: e.tensor_tensor(out=t[:], in0=t[:], in1=x[:], op=ALU.mult), reads=[t, x], writes=[t])
            K.op('dve', lambda e: e.tensor_tensor_scan(out=hh[:], data0=a[:], data1=t[:], initial=hc[:, 0:1], op0=ALU.mult, op1=ALU.add),
                 reads=[a, t, hc], writes=[hh])
            K.op('pool', lambda e: e.tensor_copy(out=hc[:], in_=hh[:, SEG - 1:SEG]), reads=[hh], writes=[hc])
            gelu_tanh(K, rg, tmp, gg[:], gg, eng2='act_sq')
            K.op('dve', lambda e: e.tensor_tensor(out=ybf[:], in0=hh[:], in1=gg[:], op=ALU.mult), reads=[hh, gg], writes=[ybf])
            for q4 in range(SEG // 512):
                p_ = ptr[nt % 2]
                y_ = yo[nt % 2]
                nt += 1
                for j in range(4):
                    tc_ = slice(q4 * 512 + j * 128, q4 * 512 + (j + 1) * 128)
                    K.tr(p_, p_[:, j * 128:(j + 1) * 128], ybf, ybf[:, tc_], identb, identb[:])
                K.op('act', lambda e, p_=p_, y_=y_: e.copy(out=y_[:], in_=p_[:, 0:512].rearrange("p (j c) -> p j c", j=4)), reads=[p_], writes=[y_])
                t0 = s * SEG + q4 * 512
                K.dma('pool', d['mixed'][t0:t0 + 512, 512 + c * 128:512 + (c + 1) * 128].rearrange("(j p) c -> p j c", p=128), y_[:],
                      reads=[y_])
    K.phase_end()


def all_phases():
    return [
        phase_A0, phase_Fprep, phase_fox, phase_Gprep, phase_gdn2,
        lambda K, d, S: phase_O(K, d, S, d['x'], d['even_w_out']),
        lambda K, d, S: phase_MLP(K, d, S, 0),
        lambda K, d, S: phase_PLE(K, d, S, 0),
        phase_Rprep, phase_A1, phase_cmp, phase_nsa_cmp, phase_nsa_slc, phase_nsa_win, phase_nsa_combine, phase_lru,
        lambda K, d, S: phase_O(K, d, S, d['h'], d['odd_w_out']),
        lambda K, d, S: phase_MLP(K, d, S, 1),
        lambda K, d, S: phase_PLE(K, d, S, 1, final=True),
    ]


SEQ = 8192
NCORES = 8


def kernel(**inputs):
    S = SEQ
    nc, K = build(S, all_phases())
    in_maps = []
    shared = {}
    for n, shp in INPUT_SHAPES.items():
        a = np.asarray(inputs[n])
        if list(a.shape) != shp:
            a = a.reshape(shp)
        shared[n] = np.ascontiguousarray(a.astype(np.float32, copy=False))
    x = np.asarray(inputs['x'])
    p = np.asarray(inputs['p'])
    pos = np.asarray(inputs['positions'])
    for b in range(NCORES):
        m = dict(shared)
        m['x'] = np.ascontiguousarray(x[b])
        m['p'] = np.ascontiguousarray(p[:, b])
        m['positions'] = np.ascontiguousarray(pos[b:b + 1]).astype(np.int32, copy=False)
        in_maps.append(m)
    res = run_bass_kernel_spmd(nc, in_maps, core_ids=list(range(NCORES)))
    out = np.stack([np.asarray(res.results[b]['out']) for b in range(NCORES)], axis=0)
    return out.astype(np.float32, copy=False)


def phase_gdn2(K, d, S, heads=range(4)):
    NC = S // 128
    NG = S // 512
    heads = list(heads)
    K.phase_begin()
    consts(K)
    identf = make_ident(K, F32, 'identf')
    onesf = K.sb('onesf', [128, 128], F32)
    K.op('pool', lambda e: e.memset(onesf[:], 1.0), writes=[onesf])
    maskT = K.sb('maskT', [128, 128], F32)
    maskTs = K.sb('maskTs', [128, 128], F32)
    zer = K.sb('zer', [128, 128], F32)
    K.op('pool', lambda e: e.memset(zer[:], 0.0), writes=[zer])
    K.op('pool', lambda e: e.affine_select(out=maskT[:], in_=zer[:], pattern=[[1, 128]], compare_op=ALU.is_ge, fill=NEGM,
                                           base=0, channel_multiplier=-1), reads=[zer], writes=[maskT])
    K.op('pool', lambda e: e.affine_select(out=maskTs[:], in_=zer[:], pattern=[[1, 128]], compare_op=ALU.is_gt, fill=NEGM,
                                           base=0, channel_multiplier=-1), reads=[zer], writes=[maskTs])
    nwb = K.sb('nwb', [128, 128], F32)
    K.dma('sp', nwb[:], d['even_gdn_norm_w'].partition_broadcast(128), writes=[nwb])
    tokS = K.sb('tokS', [128, NC, 16], F32)
    K.dma('sp', tokS[:].rearrange("p c q -> p (c q)"), d['tokSd'][:, :], writes=[tokS])
    c128 = K.sb('c128', [128, 1], F32)
    K.op('pool', lambda e: e.memset(c128[:], 128.0 * 1e-6), writes=[c128])
    onesb = K.sb('onesb', [128, 128], BF16)
    K.op('pool', lambda e: e.memset(onesb[:], 1.0), writes=[onesb])
    lnt = [K.sb('gln%d' % i, [128, 512], F32) for i in range(2)]
    ngl = [0]

    class H:
        pass
    HB = {}
    f32t = lambda nm, h: K.sb('%s_h%d' % (nm, h), [128, 128], F32)
    bf_t = lambda nm, h: K.sb('%s_h%d' % (nm, h), [128, 128], BF16)
    TB = {}
    for h in heads:
        for par in range(2):
            t = H()
            t.bk = K.ps('gbk%d_%d' % (h, par), [128, 512], F32)
            t.z = K.sb('gz_%d_%d' % (h, par), [128, 128], F32)
            t.nz = K.sb('nz_%d_%d' % (h, par), [128, 128], F32)
            for nm in ('Kbg', 'Kd', 'Vb', 'attnT', 'Xb', 'WT', 'vnew', 'ob'):
                setattr(t, nm, K.sb('%s_%d_%d' % (nm, h, par), [128, 128], BF16))
            for nm in ('dm1', 'dm2', 'Mt', 'U', 'osb', 'ojunk'):
                setattr(t, nm, K.sb('%s_%d_%d' % (nm, h, par), [128, 128], F32))
            t.P = [K.sb('P%d_%d_%d' % (i, h, par), [128, 128], F32) for i in range(2)]
            t.RX = K.sb('RX_%d_%d' % (h, par), [128, 256], F32)
            t.ost = K.sb('ost_%d_%d' % (h, par), [128, 4], F32)
            TB[(h, par)] = t
    for h in heads:
        b = H()
        b.gcrow = K.sb('gcrow%d' % h, [1, 512], F32)
        b.e1row = K.sb('e1row%d' % h, [1, 512], F32)
        b.gcb = [K.sb('gcb%d_%d' % (h, i), [128, 512], F32) for i in range(2)]
        b.sqk = K.sb('sqk%d' % h, [128, 512], BF16)
        b.sqq = K.sb('sqq%d' % h, [128, 512], BF16)
        b.eglrow = K.sb('eglrow%d' % h, [1, NC], F32)
        b.eglb = K.sb('eglb%d' % h, [128, NC], F32)
        b.Sf = f32t('Sf', h)
        b.Sb = bf_t('Sb', h)
        b.q_ = K.sb('gq%d' % h, [128, 512], F32)
        b.k_ = K.sb('gk%d' % h, [128, 512], F32)
        b.v_ = [K.sb('gv%d_%d' % (h, i), [128, 512], F32) for i in range(2)]
        b.knf = [K.sb('knf%d_%d' % (h, i), [128, 512], F32) for i in range(2)]
        b.knb = [K.sb('knb%d_%d' % (h, i), [128, 512], BF16) for i in range(2)]
        b.qnf = K.sb('qnf%d' % h, [128, 512], F32)
        b.qnb = [K.sb('qnb%d_%d' % (h, i), [128, 512], BF16) for i in range(2)]
        b.qgb = [K.sb('qgb%d_%d' % (h, i), [128, 512], BF16) for i in range(2)]
        HB[h] = b
    sl = lambda i: slice(i * 128, (i + 1) * 128)
    import os
    F32R = mybir.dt.float32r
    R = (lambda ap: ap.bitcast(F32R)) if os.environ.get('GDN_F32R', '0') == '1' else (lambda ap: ap)

    for h in heads:
        b = HB[h]
        K.dma('sp', b.eglrow[:], d['egld'][h:h + 1, :], writes=[b.eglrow])
        pg0 = TB[(h, 0)].bk
        K.mm(pg0, pg0[:, 0:NC], onesf, onesf[0:1, :], b.eglrow, b.eglrow[0:1, :])
        K.op('act', lambda e: e.copy(out=b.eglb[:], in_=pg0[:, 0:NC]), reads=[pg0], writes=[b.eglb])
        K.op('pool', lambda e: e.memset(b.Sf[:], 0.0), writes=[b.Sf])
        K.op('pool', lambda e: e.memset(b.Sb[:], 0.0), writes=[b.Sb])

    def prep_a(h, g):
        b = HB[h]
        cs = slice(g * 512, (g + 1) * 512)
        v_ = b.v_[g % 2]
        K.dma('sp', b.q_[:], d['GCT'][h * 128:(h + 1) * 128, cs], writes=[b.q_])
        K.dma('sp', b.k_[:], d['GCT'][512 + h * 128:512 + (h + 1) * 128, cs], writes=[b.k_])
        K.dma('sp', v_[:], d['GCT'][1024 + h * 128:1024 + (h + 1) * 128, cs], writes=[v_])
        K.dma('sp', b.gcrow[:], d['gcd'][h:h + 1, cs], writes=[b.gcrow])
        K.dma('sp', b.e1row[:], d['e1d'][h:h + 1, cs], writes=[b.e1row])
        K.op('pool', lambda e: e.tensor_tensor(out=b.sqk[:], in0=b.k_[:], in1=b.k_[:], op=ALU.mult), reads=[b.k_], writes=[b.sqk])
        K.op('pool', lambda e: e.tensor_tensor(out=b.sqq[:], in0=b.q_[:], in1=b.q_[:], op=ALU.mult), reads=[b.q_], writes=[b.sqq])

    def prep_b(h, g):
        b = HB[h]
        b0, b1 = TB[(h, 0)].bk, TB[(h, 1)].bk
        i = ngl[0] % 2
        ngl[0] += 1
        lk, lq = lnt[0], lnt[1]
        knf, knb, qnb, qgb, gcb = b.knf[g % 2], b.knb[g % 2], b.qnb[g % 2], b.qgb[g % 2], b.gcb[g % 2]
        K.mm(b0, b0[:], onesb, onesb[:], b.sqk, b.sqk[:])
        K.mm(b1, b1[:], onesb, onesb[:], b.sqq, b.sqq[:])
        K.op('act', lambda e: e.activation(out=lk[:], in_=b0[:], func=AF.Ln, bias=K.eps_t[:, 0:1]), reads=[b0, K.eps_t], writes=[lk])
        K.op('act', lambda e: e.activation(out=lq[:], in_=b1[:], func=AF.Ln, scale=128.0, bias=c128[:, 0:1]), reads=[b1, c128], writes=[lq])
        K.mm(b0, b0[:], onesf, onesf[0:1, :], b.gcrow, b.gcrow[0:1, :])
        K.mm(b1, b1[:], onesf, onesf[0:1, :], b.e1row, b.e1row[0:1, :])
        K.op('act', lambda e: e.activation(out=lk[:], in_=lk[:], func=AF.Exp, scale=-0.5), reads=[lk], writes=[lk])
        K.op('act', lambda e: e.activation(out=lq[:], in_=lq[:], func=AF.Exp, scale=-0.5), reads=[lq], writes=[lq])
        K.op('act', lambda e: e.copy(out=gcb[:], in_=b0[:]), reads=[b0], writes=[gcb])
        K.op('dve', lambda e: e.tensor_tensor(out=knf[:], in0=b.k_[:], in1=lk[:], op=ALU.mult), reads=[b.k_, lk], writes=[knf])
        K.op('pool', lambda e: e.tensor_copy(out=knb[:], in_=knf[:]), reads=[knf], writes=[knb])
        K.op('dve', lambda e: e.tensor_tensor(out=b.qnf[:], in0=b.q_[:], in1=lq[:], op=ALU.mult), reads=[b.q_, lq], writes=[b.qnf])
        K.op('pool', lambda e: e.tensor_copy(out=qnb[:], in_=b.qnf[:]), reads=[b.qnf], writes=[qnb])
        K.op('dve', lambda e: e.tensor_tensor(out=qgb[:], in0=b.qnf[:], in1=b1[:], op=ALU.mult), reads=[b.qnf, b1], writes=[qgb])

    class View:
        def __init__(self, t, hb):
            self._t, self._hb = t, hb

        def __getattr__(self, n):
            t = object.__getattribute__(self, '_t')
            if hasattr(t, n):
                return getattr(t, n)
            return getattr(object.__getattribute__(self, '_hb'), n)
    VW = {(h, par): View(TB[(h, par)], HB[h]) for h in heads for par in range(2)}

    def st1(h, g, tt):
        b = VW[(h, tt % 2)]
        c = g * 4 + tt
        ts_, tok = sl(tt), slice(c * 128, (c + 1) * 128)
        bk, v_ = b.bk, b.v_[g % 2]
        z = b.z
        K.dma('sp', z[:], d['GZ'][tok, h * 128:(h + 1) * 128], writes=[z])
        K.op('pool', lambda e: e.tensor_tensor(out=b.nz[:], in0=z[:], in1=nwb[:], op=ALU.mult), reads=[z, nwb], writes=[b.nz])
        beta_c, be_c, e2_c, gc_c = tokS[:, c, h:h + 1], tokS[:, c, 4 + h:5 + h], tokS[:, c, 8 + h:9 + h], tokS[:, c, 12 + h:13 + h]
        knf, knb, qnb, gcb = b.knf[g % 2], b.knb[g % 2], b.qnb[g % 2], b.gcb[g % 2]
        K.tr(bk, bk[:, 0:128], knf, knf[:, ts_], identf, identf[:])
        K.tr(bk, bk[:, 128:256], v_, v_[:, ts_], identf, identf[:])
        K.mm(bk, bk[:, 256:384], knb, knb[:, ts_], knb, knb[:, ts_])
        K.mm(bk, bk[:, 384:512], knb, knb[:, ts_], qnb, qnb[:, ts_])
        K.op('dve', lambda e: e.tensor_scalar(out=b.Kbg[:], in0=bk[:, 0:128], scalar1=be_c, scalar2=None, op0=ALU.mult), reads=[bk, tokS], writes=[b.Kbg])
        K.op('dve', lambda e: e.tensor_scalar(out=b.Kd[:], in0=bk[:, 0:128], scalar1=e2_c, scalar2=None, op0=ALU.mult), reads=[bk, tokS], writes=[b.Kd])
        K.op('dve', lambda e: e.tensor_scalar(out=b.Vb[:], in0=bk[:, 128:256], scalar1=beta_c, scalar2=None, op0=ALU.mult), reads=[bk, tokS], writes=[b.Vb])
        K.op('dve', lambda e: e.scalar_tensor_tensor(out=b.dm1[:], in0=gcb[:, ts_], scalar=gc_c, in1=maskT[:], op0=ALU.subtract, op1=ALU.add),
             reads=[gcb, tokS, maskT], writes=[b.dm1])
        K.op('dve', lambda e: e.scalar_tensor_tensor(out=b.dm2[:], in0=gcb[:, ts_], scalar=gc_c, in1=maskTs[:], op0=ALU.subtract, op1=ALU.add),
             reads=[gcb, tokS, maskTs], writes=[b.dm2])

    def st1b(h, g, tt):
        b = VW[(h, tt % 2)]
        bk = b.bk
        K.op('act', lambda e: e.activation(out=b.dm1[:], in_=b.dm1[:], func=AF.Exp), reads=[b.dm1], writes=[b.dm1])
        K.op('act', lambda e: e.activation(out=b.dm2[:], in_=b.dm2[:], func=AF.Exp), reads=[b.dm2], writes=[b.dm2])
        K.op('dve', lambda e: e.tensor_tensor(out=b.attnT[:], in0=bk[:, 384:512], in1=b.dm1[:], op=ALU.mult), reads=[bk, b.dm1], writes=[b.attnT])
        K.op('dve', lambda e: e.tensor_tensor(out=b.Mt[:], in0=bk[:, 256:384], in1=b.dm2[:], op=ALU.mult), reads=[bk, b.dm2], writes=[b.Mt])

    def st2(h, g, tt):
        b = VW[(h, tt % 2)]
        c = g * 4 + tt
        bk = b.bk
        beta_c = tokS[:, c, h:h + 1]
        K.tr(bk, bk[:, 0:128], b.Mt, b.Mt[:], identf, identf[:])
        K.op('dve', lambda e: e.tensor_scalar(out=b.P[0][:], in0=bk[:, 0:128], scalar1=beta_c, scalar2=-1.0, op0=ALU.mult, op1=ALU.mult),
             reads=[bk, tokS], writes=[b.P[0]])

    def st2b(h, g, tt):
        b = VW[(h, tt % 2)]
        bk = b.bk
        K.tr(bk, bk[:, 128:256], b.P[0], b.P[0][:], identf, identf[:])
        K.op('act', lambda e: e.copy(out=b.RX[:, 0:128], in_=bk[:, 128:256]), reads=[bk], writes=[b.RX])
        K.op('dve', lambda e: e.tensor_tensor(out=b.RX[:, 128:256], in0=bk[:, 128:256], in1=identf[:], op=ALU.add), reads=[bk, identf],
             writes=[b.RX], nowaw=[b.RX])

    def lvl_first(h, tt):
        b = VW[(h, tt % 2)]
        bk = b.bk
        K.mm(bk, bk[:, 256:384], b.RX, b.RX[:, 0:128], b.P[0], b.P[0][:])
        K.mm(bk, bk[:, 384:512], b.P[0], b.P[0][:], b.RX, b.RX[:, 0:128])
        K.op('act', lambda e: e.copy(out=b.P[1][:], in_=bk[:, 256:384]), reads=[bk], writes=[b.P[1]])
        K.op('act', lambda e: e.copy(out=b.RX[:, 0:128], in_=bk[:, 384:512]), reads=[bk], writes=[b.RX], nowaw=[b.RX])

    def lvl_mid(h, n, tt):
        b = VW[(h, tt % 2)]
        bk = b.bk
        Pn, Pn1 = b.P[n % 2], b.P[(n + 1) % 2]
        if n < 5:
            K.mm(bk, bk[:, 0:256], Pn, Pn[:], b.RX, b.RX[:, 0:256])
        else:
            K.mm(bk, bk[:, 128:256], Pn, Pn[:], b.RX, b.RX[:, 128:256])
        K.mm(bk, bk[:, 256:384], b.RX, b.RX[:, 0:128], Pn, Pn[:])
        K.op('dve', lambda e: e.tensor_tensor(out=b.RX[:, 128:256], in0=b.RX[:, 128:256], in1=bk[:, 128:256], op=ALU.add),
             reads=[b.RX, bk], writes=[b.RX], nowaw=[b.RX])
        if n < 5:
            K.op('act', lambda e: e.copy(out=b.RX[:, 0:128], in_=bk[:, 0:128]), reads=[bk], writes=[b.RX], nowaw=[b.RX])
        K.op('act', lambda e: e.copy(out=Pn1[:], in_=bk[:, 256:384]), reads=[bk], writes=[Pn1])

    def lvl_last(h, tt):
        b = VW[(h, tt % 2)]
        bk = b.bk
        K.mm(bk, bk[:, 0:128], b.P[0], b.P[0][:], b.RX, b.RX[:, 128:256])
        K.op('dve', lambda e: e.tensor_tensor(out=b.RX[:, 128:256], in0=b.RX[:, 128:256], in1=bk[:, 0:128], op=ALU.add),
             reads=[b.RX, bk], writes=[b.RX], nowaw=[b.RX])

    def st3(h, g, tt):
        b = VW[(h, tt % 2)]
        bk = b.bk
        K.op('pool', lambda e: e.tensor_copy(out=b.Xb[:], in_=b.RX[:, 128:256]), reads=[b.RX], writes=[b.Xb])
        K.mm(bk, bk[:, 0:128], b.Xb, b.Xb[:], b.Vb, b.Vb[:])
        K.mm(bk, bk[:, 128:256], b.Kbg, b.Kbg[:], b.Xb, b.Xb[:])
        K.op('act', lambda e: e.copy(out=b.U[:], in_=bk[:, 0:128]), reads=[bk], writes=[b.U])
        K.op('act', lambda e: e.copy(out=b.WT[:], in_=bk[:, 128:256]), reads=[bk], writes=[b.WT])

    def st4(h, g, tt):
        b = VW[(h, tt % 2)]
        c = g * 4 + tt
        ts_ = sl(tt)
        bk = b.bk
        K.mm(bk, bk[:, 256:384], b.WT, b.WT[:], b.Sb, b.Sb[:])
        K.op('dve', lambda e: e.tensor_tensor(out=b.vnew[:], in0=b.U[:], in1=bk[:, 256:384], op=ALU.subtract), reads=[b.U, bk], writes=[b.vnew])
        K.mm(bk, bk[:, 384:512], b.qgb[g % 2], b.qgb[g % 2][:, ts_], b.Sb, b.Sb[:], start=True, stop=False)
        K.mm(bk, bk[:, 384:512], b.attnT, b.attnT[:], b.vnew, b.vnew[:], start=False, stop=True)
        K.mm(bk, bk[:, 0:128], b.Kd, b.Kd[:], b.vnew, b.vnew[:])
        K.op('dve', lambda e: e.scalar_tensor_tensor(out=b.Sb[:], in0=b.Sf[:], scalar=b.eglb[:, c:c + 1], in1=bk[:, 0:128],
                                                     op0=ALU.mult, op1=ALU.add), reads=[b.Sf, b.eglb, bk], writes=[b.Sb])
        K.op('dve', lambda e: e.scalar_tensor_tensor(out=b.Sf[:], in0=b.Sf[:], scalar=b.eglb[:, c:c + 1], in1=bk[:, 0:128],
                                                     op0=ALU.mult, op1=ALU.add), reads=[b.Sf, b.eglb, bk], writes=[b.Sf])
        K.op('act', lambda e: e.copy(out=b.osb[:], in_=bk[:, 384:512]), reads=[bk], writes=[b.osb])

    def st5(h, g, tt):
        b = VW[(h, tt % 2)]
        c = g * 4 + tt
        tok = slice(c * 128, (c + 1) * 128)
        K.op('dve', lambda e: e.scalar_tensor_tensor(out=b.ojunk[:], in0=b.osb[:], scalar=1.0 / 128, in1=b.osb[:], op0=ALU.mult, op1=ALU.mult,
                                                     accum_out=b.ost[:, 0:1]), reads=[b.osb], writes=[b.ojunk, b.ost])
        K.op('act', lambda e: e.activation(out=b.ost[:, 1:2], in_=b.ost[:, 0:1], func=AF.Ln, bias=K.eps_t[:, 0:1]), reads=[b.ost, K.eps_t], writes=[b.ost])
        K.op('act', lambda e: e.activation(out=b.ost[:, 2:3], in_=b.ost[:, 1:2], func=AF.Exp, scale=-0.5), reads=[b.ost], writes=[b.ost])
        o_b = b.ob
        K.op('dve', lambda e: e.scalar_tensor_tensor(out=o_b[:], in0=b.osb[:], scalar=b.ost[:, 2:3], in1=b.nz[:], op0=ALU.mult, op1=ALU.mult),
             reads=[b.osb, b.ost, b.nz], writes=[o_b])
        K.dma('pool', d['mixed'][tok, 512 + h * 128:512 + (h + 1) * 128], o_b[:], reads=[o_b])

    for h in heads:
        prep_a(h, 0)
    for h in heads:
        prep_b(h, 0)
    for g in range(NG):
        if g + 1 < NG:
            for h in heads:
                prep_a(h, g + 1)
        for tp in range(0, 4, 2):
            tts = (tp, tp + 1)
            if tp == 2 and g + 1 < NG:
                for h in heads:
                    prep_b(h, g + 1)
            for stage in (st1, st1b, st2, st2b):
                for tt in tts:
                    for h in heads:
                        stage(h, g, tt)
            for tt in tts:
                for h in heads:
                    lvl_first(h, tt)
            for n in range(1, 6):
                for tt in tts:
                    for h in heads:
                        lvl_mid(h, n, tt)
            for tt in tts:
                for h in heads:
                    lvl_last(h, tt)
            for tt in tts:
                for h in heads:
                    st3(h, g, tt)
            for tt in tts:
                for h in heads:
                    st4(h, g, tt)
            for tt in tts:
                for h in heads:
                    st5(h, g, tt)
    K.phase_end()
```
